# Optimizing a Trainium2 kernel written in Bass

```python
import math
import jax, jax.numpy as jnp
from jax import lax
import numpy as np

D_MODEL = 1024
BATCH = 16
SEQ = 2048
DEPTH = 1

A_HEADS = 8
A_HEAD_DIM = 64
A_KV_GROUPS = 2
A_HEADS_PER_GROUP = A_HEADS // A_KV_GROUPS
A_WIDTH = A_HEADS * A_HEAD_DIM
A_KV_WIDTH = A_KV_GROUPS * A_HEAD_DIM
CMP_BLOCK = 32
CMP_STRIDE = 16
CMP_HIDDEN = 256
SLC_BLOCK = 64
SLC_TOPN = 16
WINDOW = 512
NSA_Q_BLOCK = 32
B_HEADS = 8
B_HEAD_DIM = 64
B_WIDTH = B_HEADS * B_HEAD_DIM
DECAY_LORA = 64
ICLR_LORA = 64
LNX_EPS = 64e-5
REL_BUCKETS = 32
REL_MAX_EXACT = 16
REL_MAX_DIST = 128
NORM_EPS = 1e-6
NEG_INF = -1e30
FORCE_SCORE = 1e30
NSA_IN = 2 * A_WIDTH + 6 * A_KV_WIDTH + 3 * A_HEADS
SHIFT_WIDTH = 3 * B_WIDTH + DECAY_LORA + ICLR_LORA
REST_IN = B_WIDTH + 2 * D_MODEL
IN_WIDTH = NSA_IN + SHIFT_WIDTH + REST_IN

kernel_name = 'hybrid_nsa_rwkv7_block'


def split_last(t, sizes):
    outs, off = [], 0
    for s in sizes:
        outs.append(t[..., off:off + s])
        off += s
    return outs


def rms_norm(x, gain):
    xf = x.astype(jnp.float32)
    y = xf * lax.rsqrt(jnp.mean(xf * xf, axis=-1, keepdims=True) + NORM_EPS)
    return (y * gain.astype(jnp.float32)).astype(x.dtype)


def masked_softmax(logits, mask):
    logits = jnp.where(mask, logits.astype(jnp.float32), NEG_INF)
    return jnp.where(mask, jax.nn.softmax(logits, axis=-1), 0.0)


def t5_bucket(dist):
    n = jnp.maximum(dist, 0)
    nf = jnp.maximum(n, REL_MAX_EXACT).astype(jnp.float32)
    large = REL_MAX_EXACT + (jnp.log(nf / REL_MAX_EXACT) / math.log(REL_MAX_DIST / REL_MAX_EXACT)
                             * (REL_BUCKETS - REL_MAX_EXACT)).astype(jnp.int32)
    return jnp.where(n < REL_MAX_EXACT, n, jnp.minimum(large, REL_BUCKETS - 1))


def shared_rel_bias(rel_bias, dist):
    nq, nk = dist.shape
    bias = rel_bias[t5_bucket(dist)].astype(jnp.float32)
    return bias.transpose(2, 0, 1).reshape(A_KV_GROUPS, A_HEADS_PER_GROUP, nq, nk)


def token_shift(p, mu):
    prev = jnp.pad(p, ((0, 0), (1, 0), (0, 0)))[:, :-1]
    return p + (prev - p) * mu


def compress_blocks(kv, pos_emb, w1, w2):
    b, s, g, dh = kv.shape
    n_cmp = (s - CMP_BLOCK) // CMP_STRIDE + 1
    idx = jnp.arange(n_cmp)[:, None] * CMP_STRIDE + jnp.arange(CMP_BLOCK)[None, :]
    blocks = kv[:, idx] + pos_emb[:, None, :]
    flat = blocks.transpose(0, 3, 1, 2, 4).reshape(b, g, n_cmp, CMP_BLOCK * dh)
    return jax.nn.gelu(flat @ w1) @ w2


def nsa_mixer(q, k_cmp, v_cmp, k_slc, v_slc, k_win, v_win, gate_logits, rel_bias,
              q_norm_gain, k_norm_gain, cmp_pos_k, cmp_pos_v, cmp_k_w1, cmp_k_w2, cmp_v_w1, cmp_v_w2):
    b, s, _ = q.shape
    g, hpg, dh = A_KV_GROUPS, A_HEADS_PER_GROUP, A_HEAD_DIM
    scale = dh ** -0.5
    heads_kv = lambda t: t.reshape(b, s, g, dh)
    qh = rms_norm(q.reshape(b, s, g, hpg, dh), q_norm_gain).transpose(0, 2, 3, 1, 4)
    kc = rms_norm(compress_blocks(heads_kv(k_cmp), cmp_pos_k, cmp_k_w1, cmp_k_w2), k_norm_gain[0])
    vc = compress_blocks(heads_kv(v_cmp), cmp_pos_v, cmp_v_w1, cmp_v_w2)
    n_cmp = kc.shape[2]
    cmp_end = jnp.arange(n_cmp) * CMP_STRIDE + CMP_BLOCK - 1
    n_slc = s // SLC_BLOCK
    top_n = min(SLC_TOPN, n_slc)
    ks = rms_norm(heads_kv(k_slc), k_norm_gain[1]).transpose(0, 2, 1, 3).reshape(b, g, n_slc, SLC_BLOCK, dh)
    vs = heads_kv(v_slc).transpose(0, 2, 1, 3).reshape(b, g, n_slc, SLC_BLOCK, dh)
    r1, r2 = SLC_BLOCK // CMP_STRIDE, CMP_BLOCK // CMP_STRIDE
    imp_idx = (r1 * jnp.arange(n_slc)[:, None, None] + jnp.arange(r1)[None, :, None]
               - jnp.arange(r2)[None, None, :]).reshape(n_slc, r1 * r2)
    imp_valid = (imp_idx >= 0) & (imp_idx < n_cmp)
    imp_idx = jnp.clip(imp_idx, 0, n_cmp - 1)
    blk = jnp.arange(n_slc)
    pad = ((0, 0), (0, 0), (WINDOW, 0), (0, 0))
    kw = jnp.pad(rms_norm(heads_kv(k_win), k_norm_gain[2]).transpose(0, 2, 1, 3), pad)
    vw = jnp.pad(heads_kv(v_win).transpose(0, 2, 1, 3), pad)
    gates = jax.nn.sigmoid(gate_logits.astype(jnp.float32)).reshape(b, s, 3, g, hpg).transpose(2, 0, 3, 4, 1)
    tbl = rel_bias.reshape(REL_BUCKETS, g, hpg).astype(jnp.float32)
    b_ix = jnp.arange(b)[:, None, None, None]
    g_ix = jnp.arange(g)[None, :, None, None]

    def query_block(i):
        q0 = i * NSA_Q_BLOCK
        t = q0 + jnp.arange(NSA_Q_BLOCK)
        qb = lax.dynamic_slice_in_dim(qh, q0, NSA_Q_BLOCK, axis=3)
        gb = lax.dynamic_slice_in_dim(gates, q0, NSA_Q_BLOCK, axis=4)[..., None].astype(q.dtype)
        dist_c = t[:, None] - cmp_end[None, :]
        s_c = jnp.einsum('bghqd,bgnd->bghqn', qb, kc).astype(jnp.float32) * scale + shared_rel_bias(rel_bias, dist_c)
        p_c = masked_softmax(s_c, dist_c >= 0)
        o_c = jnp.einsum('bghqn,bgnd->bghqd', p_c.astype(vc.dtype), vc)
        p_grp = p_c.sum(axis=2)
        imp = jnp.sum(jnp.where(imp_valid, p_grp[..., imp_idx], 0.0), axis=-1)
        cur = t[:, None] // SLC_BLOCK
        forced = (blk[None, :] == 0) | (blk[None, :] == cur) | (blk[None, :] == cur - 1)
        causal = blk[None, :] * SLC_BLOCK <= t[:, None]
        imp = jnp.where(forced, FORCE_SCORE, jnp.where(causal, imp, NEG_INF))
        _, sel = lax.top_k(imp, top_n)
        k_sel = ks[b_ix, g_ix, sel].reshape(b, g, NSA_Q_BLOCK, top_n * SLC_BLOCK, dh)
        v_sel = vs[b_ix, g_ix, sel].reshape(b, g, NSA_Q_BLOCK, top_n * SLC_BLOCK, dh)
        key_pos = (sel[..., None] * SLC_BLOCK + jnp.arange(SLC_BLOCK)).reshape(b, g, NSA_Q_BLOCK, top_n * SLC_BLOCK)
        dist_s = t[:, None] - key_pos
        bias_s = tbl[t5_bucket(dist_s)[:, :, None], jnp.arange(g)[:, None, None, None], jnp.arange(hpg)[:, None, None]]
        s_s = jnp.einsum('bghqd,bgqkd->bghqk', qb, k_sel).astype(jnp.float32) * scale + bias_s
        p_s = masked_softmax(s_s, (dist_s >= 0)[:, :, None])
        o_s = jnp.einsum('bghqk,bgqkd->bghqd', p_s.astype(v_sel.dtype), v_sel)
        kwb = lax.dynamic_slice_in_dim(kw, q0, WINDOW + NSA_Q_BLOCK, axis=2)
        vwb = lax.dynamic_slice_in_dim(vw, q0, WINDOW + NSA_Q_BLOCK, axis=2)
        kpos = q0 - WINDOW + jnp.arange(WINDOW + NSA_Q_BLOCK)
        dist_w = t[:, None] - kpos[None, :]
        mask_w = (dist_w >= 0) & (dist_w < WINDOW) & (kpos[None, :] >= 0)
        s_w = jnp.einsum('bghqd,bgkd->bghqk', qb, kwb).astype(jnp.float32) * scale + shared_rel_bias(rel_bias, dist_w)
        p_w = masked_softmax(s_w, mask_w)
        o_w = jnp.einsum('bghqk,bgkd->bghqd', p_w.astype(vwb.dtype), vwb)
        return gb[0] * o_c + gb[1] * o_s + gb[2] * o_w

    out = lax.map(query_block, jnp.arange(s // NSA_Q_BLOCK))
    return out.transpose(1, 0, 4, 2, 3, 5).reshape(b, s, A_WIDTH)


def rwkv7_mixer(r, k, v, wd, ad, w0, w_lora_up, a0, a_lora_up, k_k, k_a, r_k, ln_x_w, ln_x_b):
    b, s, _ = r.shape
    h, n = B_HEADS, B_HEAD_DIM
    f32 = jnp.float32
    heads = lambda t: t.astype(f32).reshape(b, s, h, n)
    tm = lambda t: t.transpose(1, 0, 2, 3)
    log_w = -jax.nn.softplus(-(w0 + jnp.tanh(wd) @ w_lora_up).astype(f32)) - 0.5
    decay = jnp.exp(-jnp.exp(log_w))
    a = jax.nn.sigmoid((a0 + ad @ a_lora_up).astype(f32))
    kk = heads(k * k_k)
    kk = kk * lax.rsqrt(jnp.maximum(jnp.sum(kk * kk, axis=-1, keepdims=True), 1e-24))
    k_mod = heads(k.astype(f32) * (1.0 + (a - 1.0) * k_a.astype(f32)))
    rh, vh, ah = heads(r), heads(v), heads(a)

    def step(state, inp):
        r_t, w_t, k_t, v_t, a_t, b_t = inp
        sa = jnp.einsum('bhvk,bhk->bhv', state, a_t)
        state = state * w_t[:, :, None, :] + sa[..., None] * b_t[:, :, None, :] + v_t[..., None] * k_t[:, :, None, :]
        return state, jnp.einsum('bhvk,bhk->bhv', state, r_t)

    state0 = jnp.zeros((b, h, n, n), f32)
    _, y = lax.scan(step, state0, (tm(rh), tm(heads(decay)), tm(k_mod), tm(vh), tm(-kk), tm(kk * ah)))
    y = tm(y)
    mean = jnp.mean(y, axis=-1, keepdims=True)
    var = jnp.mean(jnp.square(y - mean), axis=-1, keepdims=True)
    y = ((y - mean) * lax.rsqrt(var + LNX_EPS)).reshape(b, s, B_WIDTH) * ln_x_w + ln_x_b
    bonus = jnp.sum(rh * k_mod * r_k.astype(f32), axis=-1, keepdims=True) * vh
    return (y + bonus.reshape(b, s, B_WIDTH)).astype(r.dtype)


def hybrid_layer(x, c, rel_bias, w_ada, b_ada, norm_gain, w_in, q_norm_gain, k_norm_gain,
                 cmp_pos_k, cmp_pos_v, cmp_k_w1, cmp_k_w2, cmp_v_w1, cmp_v_w2,
                 shift_mu, w0, w_lora_up, a0, a_lora_up, k_k, k_a, r_k, ln_x_w, ln_x_b,
                 w_out_a, w_out_b, w_o):
    shift, scale, gate = jnp.split(jax.nn.silu(c) @ w_ada + b_ada, 3, axis=-1)
    h = rms_norm(x, norm_gain) * (1.0 + scale[:, None, :]) + shift[:, None, :]
    cols = h @ w_in
    cols_a, cols_shift, cols_rest = split_last(cols, (NSA_IN, SHIFT_WIDTH, REST_IN))
    q, k_cmp, v_cmp, k_slc, v_slc, k_win, v_win, a_gate_logits, a_silu = split_last(
        cols_a, (A_WIDTH,) + (A_KV_WIDTH,) * 6 + (3 * A_HEADS, A_WIDTH))
    r, k, v, wd, ad = split_last(token_shift(cols_shift, shift_mu), (B_WIDTH,) * 3 + (DECAY_LORA, ICLR_LORA))
    b_silu, merge_a, merge_b = split_last(cols_rest, (B_WIDTH, D_MODEL, D_MODEL))
    y_a = nsa_mixer(q, k_cmp, v_cmp, k_slc, v_slc, k_win, v_win, a_gate_logits, rel_bias,
                    q_norm_gain, k_norm_gain, cmp_pos_k, cmp_pos_v, cmp_k_w1, cmp_k_w2, cmp_v_w1, cmp_v_w2)
    y_a = y_a * jax.nn.silu(a_silu)
    y_b = rwkv7_mixer(r, k, v, wd, ad, w0, w_lora_up, a0, a_lora_up, k_k, k_a, r_k, ln_x_w, ln_x_b)
    y_b = y_b * jax.nn.silu(b_silu)
    merged = jax.nn.sigmoid(merge_a) * (y_a @ w_out_a) + jax.nn.sigmoid(merge_b) * (y_b @ w_out_b)
    return x + gate[:, None, :] * (merged @ w_o)


def setup_inputs(seed: int = 0) -> dict:
    key = jax.random.key(seed)
    ks = jax.random.split(key, 32)
    nrm = lambda k, shape, s: jax.random.normal(k, shape, jnp.float32) * s
    L = DEPTH
    fan_cmp = CMP_BLOCK * A_HEAD_DIM
    return {
        'x': nrm(ks[0], (BATCH, SEQ, D_MODEL), 1.0),
        'c': nrm(ks[1], (BATCH, D_MODEL), 1.0),
        'w_ada': nrm(ks[2], (L, D_MODEL, 3 * D_MODEL), 0.2 * D_MODEL ** -0.5),
        'b_ada': nrm(ks[3], (L, 3 * D_MODEL), 0.01),
        'norm_gain': 1.0 + nrm(ks[4], (L, D_MODEL), 0.02),
        'w_in': nrm(ks[5], (L, D_MODEL, IN_WIDTH), D_MODEL ** -0.5),
        'q_norm_gain': 1.0 + nrm(ks[6], (L, A_HEAD_DIM), 0.02),
        'k_norm_gain': 1.0 + nrm(ks[7], (L, 3, A_HEAD_DIM), 0.02),
        'cmp_pos_k': nrm(ks[8], (L, CMP_BLOCK, A_HEAD_DIM), 0.1),
        'cmp_pos_v': nrm(ks[9], (L, CMP_BLOCK, A_HEAD_DIM), 0.1),
        'cmp_k_w1': nrm(ks[10], (L, fan_cmp, CMP_HIDDEN), fan_cmp ** -0.5),
        'cmp_k_w2': nrm(ks[11], (L, CMP_HIDDEN, A_HEAD_DIM), CMP_HIDDEN ** -0.5),
        'cmp_v_w1': nrm(ks[12], (L, fan_cmp, CMP_HIDDEN), fan_cmp ** -0.5),
        'cmp_v_w2': nrm(ks[13], (L, CMP_HIDDEN, A_HEAD_DIM), CMP_HIDDEN ** -0.5),
        'rel_bias': nrm(ks[14], (REL_BUCKETS, A_HEADS), 0.5),
        'shift_mu': jax.random.uniform(ks[15], (L, SHIFT_WIDTH), jnp.float32),
        'w0': (-6.0 + 5.0 * jnp.linspace(0.0, 1.0, B_WIDTH))[None, :] + nrm(ks[16], (L, B_WIDTH), 0.1),
        'w_lora_up': nrm(ks[17], (L, DECAY_LORA, B_WIDTH), 0.5 * DECAY_LORA ** -0.5),
        'a0': nrm(ks[18], (L, B_WIDTH), 0.1),
        'a_lora_up': nrm(ks[19], (L, ICLR_LORA, B_WIDTH), ICLR_LORA ** -0.5),
        'k_k': 0.85 + nrm(ks[20], (L, B_WIDTH), 0.02),
        'k_a': 1.0 + nrm(ks[21], (L, B_WIDTH), 0.02),
        'r_k': nrm(ks[22], (L, B_HEADS, B_HEAD_DIM), 0.1),
        'ln_x_w': 1.0 + nrm(ks[23], (L, B_WIDTH), 0.02),
        'ln_x_b': nrm(ks[24], (L, B_WIDTH), 0.01),
        'w_out_a': nrm(ks[25], (L, A_WIDTH, D_MODEL), A_WIDTH ** -0.5),
        'w_out_b': nrm(ks[26], (L, B_WIDTH, D_MODEL), B_WIDTH ** -0.5),
        'w_o': nrm(ks[27], (L, D_MODEL, D_MODEL), D_MODEL ** -0.5),
    }


def reference(x, c, w_ada, b_ada, norm_gain, w_in, q_norm_gain, k_norm_gain, cmp_pos_k, cmp_pos_v,
              cmp_k_w1, cmp_k_w2, cmp_v_w1, cmp_v_w2, rel_bias, shift_mu, w0, w_lora_up, a0, a_lora_up,
              k_k, k_a, r_k, ln_x_w, ln_x_b, w_out_a, w_out_b, w_o):
    for l in range(DEPTH):
        x = hybrid_layer(x, c, rel_bias, w_ada[l], b_ada[l], norm_gain[l], w_in[l], q_norm_gain[l], k_norm_gain[l],
                         cmp_pos_k[l], cmp_pos_v[l], cmp_k_w1[l], cmp_k_w2[l], cmp_v_w1[l], cmp_v_w2[l],
                         shift_mu[l], w0[l], w_lora_up[l], a0[l], a_lora_up[l], k_k[l], k_a[l], r_k[l],
                         ln_x_w[l], ln_x_b[l], w_out_a[l], w_out_b[l], w_o[l])
    return x
```

```python
import math
import numpy as np
import concourse.bass as bass
import concourse.mybir as mybir
from concourse.bass_utils import run_bass_kernel_spmd
from contextlib import ExitStack

F32 = mybir.dt.float32
BF16 = mybir.dt.bfloat16
AF = mybir.ActivationFunctionType
ALU = mybir.AluOpType
AX = mybir.AxisListType

COMPUTE = ("pe", "act", "dve", "pool")
NEGM = -4096.0
NRES = 2968
CQ, CKV, CG, CC, CS = 0, 512, 1024, 1048, 1304


class Buf:
    __slots__ = ("w", "r")

    def __init__(self):
        self.w = None
        self.r = {}


class Sched:
    def __init__(self, nc, es):
        self.nc = nc
        self.es = es
        self.prog = {e: [] for e in COMPUTE + ("sp",)}
        self.cnt = {e: 0 for e in COMPUTE}
        self.sems = {}
        for e in COMPUTE:
            self.sems[e] = es.enter_context(nc.semaphore("sem_" + e))
        self.known = {e: {} for e in self.prog}
        self.snap = {}
        self.dcnt = {}
        self.pending = {e: False for e in COMPUTE}
        self.last = {}

    def dma_sem(self, name):
        self.sems[name] = self.es.enter_context(self.nc.semaphore("sem_" + name))
        self.dcnt[name] = 0
        return name

    @staticmethod
    def _flat(bs):
        out = []
        for b in bs:
            if isinstance(b, (list, tuple)):
                out.extend(Sched._flat(b))
            else:
                out.append(b)
        return out

    def op(self, eng, fn, reads=(), writes=(), inc=True, dsem=None):
        reads = self._flat(reads)
        writes = self._flat(writes)
        need = {}

        def req(tok, same_ok):
            if tok is None:
                return
            k, v = tok
            if same_ok and k == eng and eng == "pe":
                return
            if need.get(k, 0) < v:
                need[k] = v

        for b in reads:
            req(b.w, False)
        for b in writes:
            req(b.w, True)
            for k, v in b.r.items():
                req((k, v), True)
        kn = self.known[eng]
        waits = []
        for k, v in need.items():
            if kn.get(k, 0) < v:
                waits.append((k, v))
                kn[k] = v
                sn = self.snap.get((k, v))
                if sn is not None:
                    for k2, v2 in sn.items():
                        if kn.get(k2, 0) < v2:
                            kn[k2] = v2
        if dsem is not None:
            self.dcnt[dsem] += 16
            tok = (dsem, self.dcnt[dsem])
            incspec = (dsem, 16)
        elif inc:
            self.cnt[eng] += 1
            tok = (eng, self.cnt[eng])
            incspec = (eng, 1)
            self.pending[eng] = False
            self.snap[tok] = dict(kn)
        else:
            tok = (eng, self.cnt[eng] + 1)
            incspec = None
            self.pending[eng] = True
        self.last[tok[0]] = tok[1]
        for b in writes:
            b.w = tok
            b.r = {}
        for b in reads:
            if b.w is tok:
                continue
            if b.r.get(tok[0], 0) < tok[1]:
                b.r[tok[0]] = tok[1]
        self.prog[eng].append((waits, fn, incspec))
        return tok

    def wait_all(self, eng, toks):
        kn = self.known[eng]
        waits = []
        mx = {}
        for k, v in toks:
            if mx.get(k, 0) < v:
                mx[k] = v
        for k, v in mx.items():
            if kn.get(k, 0) < v:
                waits.append((k, v))
                kn[k] = v
        self.prog[eng].append((waits, None, None))

    def barrier(self):
        for e in COMPUTE:
            if self.pending[e]:
                self.op(e, lambda en: en.nop(), (), ())
        toks = list(self.last.items())
        for e in self.prog:
            self.wait_all(e, toks)

    def emit(self):
        nc = self.nc
        for e in COMPUTE:
            if self.pending[e]:
                self.op(e, lambda en: en.nop(), (), ())
        sems = self.sems
        prog = self.prog

        def run(engname):
            def f(e):
                for waits, fn, incspec in prog[engname]:
                    for k, v in waits:
                        e.wait_ge(sems[k], v)
                    if fn is None:
                        continue
                    ins = fn(e)
                    if incspec is not None:
                        ins.then_inc(sems[incspec[0]], incspec[1])
            return f

        with nc.Block() as block:
            block.sync(run("sp"))
            block.tensor(run("pe"))
            block.scalar(run("act"))
            block.vector(run("dve"))
            block.gpsimd(run("pool"))


def _t5_bucket(dist):
    n = np.maximum(dist, 0)
    nf = np.maximum(n, 16).astype(np.float32)
    large = 16 + (np.log(nf / np.float32(16)) / np.float32(math.log(128 / 16)) * np.float32(16)).astype(np.int32)
    return np.where(n < 16, n, np.minimum(large, 31))


def _perms():
    r = lambda a, b: list(range(a, b))
    res = (r(0, 512)
           + r(768, 832) + r(1024, 1088) + r(832, 896) + r(1088, 1152) + r(896, 1024) + r(1152, 1280)
           + r(1280, 1304)
           + r(512, 576) + r(640, 704) + r(576, 640) + r(704, 768)
           + r(1816, 3480))
    rest = r(1304, 1816) + r(3480, 3992) + r(3992, 5016) + r(5016, 6040)
    assert len(res) == NRES and len(rest) == 3072
    return np.array(res), np.array(rest)


def host_prep(inp):
    f = lambda k: np.ascontiguousarray(np.asarray(inp[k], dtype=np.float32))
    sh = {}
    pres, prest = _perms()
    w_in = f("w_in")[0]
    sh["w_res"] = np.ascontiguousarray(w_in[:, pres])
    sh["w_rest"] = np.ascontiguousarray(w_in[:, prest])
    sh["w_ada"] = f("w_ada")[0]
    sh["w_out_a"] = f("w_out_a")[0]
    sh["w_out_b"] = f("w_out_b")[0]
    sh["w_o"] = f("w_o")[0]
    sh["w1k"] = f("cmp_k_w1")[0]
    sh["w1v"] = f("cmp_v_w1")[0]
    col = lambda v, n: np.ascontiguousarray(v.reshape(n, 128).T)
    sh["b_ada"] = col(f("b_ada")[0], 24)
    sh["g_norm"] = col(f("norm_gain")[0], 8)
    sh["mu"] = col(f("shift_mu")[0], 13)
    vec4 = np.stack([col(f(k)[0].reshape(-1), 4) for k in ("k_k", "k_a", "r_k", "ln_x_b")], 1)
    sh["vec4"] = np.ascontiguousarray(vec4)
    rep = lambda v: np.ascontiguousarray(np.broadcast_to(v[None, :], (128, v.shape[0])))
    kng = f("k_norm_gain")[0]
    sh["bc_small"] = np.concatenate([rep(f("q_norm_gain")[0]), rep(kng[1]), rep(kng[2])], 1)
    sh["ln_w_bc"] = rep(f("ln_x_w")[0])
    sh["kgc"] = np.ascontiguousarray(kng[0].reshape(64, 1))
    sh["w0a0"] = np.ascontiguousarray(np.stack([f("w0")[0], f("a0")[0]], 0))
    sh["lora"] = np.ascontiguousarray(np.concatenate([f("w_lora_up")[0], f("a_lora_up")[0]], 0))
    w2 = lambda k: f(k)[0].reshape(2, 128, 64).transpose(1, 0, 2)
    sh["w2"] = np.ascontiguousarray(np.stack([w2("cmp_k_w2"), w2("cmp_v_w2")], 1))
    sh["peT"] = np.ascontiguousarray(np.concatenate([f("cmp_pos_k")[0].T, f("cmp_pos_v")[0].T], 0))
    tbl = f("rel_bias")
    k = np.arange(128)[:, None]
    q = np.arange(128)[None, :]
    tb = np.zeros((2, 2, 128, 4, 128), np.float32)
    for v, dist in enumerate((q - k, 128 + q - k)):
        bk = _t5_bucket(dist)
        for g in range(2):
            for h in range(4):
                tb[v, g, :, h, :] = tbl[bk, g * 4 + h]
    sh["tblDS"] = tb.reshape(2, 2, 128, 512)
    mk = np.zeros((128, 4, 128), np.float32)
    mk[np.broadcast_to(((q - k) < 0)[:, None, :], mk.shape)] = NEGM
    sh["maskD"] = mk.reshape(128, 512)
    c31 = np.zeros((2, 128, 4, 128), np.float32)
    for g in range(2):
        for h in range(4):
            c31[g, :, h, :] = tbl[31, g * 4 + h]
    sh["c31"] = c31.reshape(2, 128, 512)
    p = np.arange(16)[:, None]
    distc = q - 16 * p + 113
    bkc = _t5_bucket(distc)
    tc = np.zeros((2, 16, 4, 128), np.float32)
    for g in range(2):
        for h in range(4):
            tc[g, :, h, :] = tbl[bkc, g * 4 + h]
    sh["tblC"] = tc.reshape(2, 16, 512)
    mc = np.zeros((16, 4, 128), np.float32)
    mc[np.broadcast_to((distc < 0)[:, None, :], mc.shape)] = NEGM
    sh["maskC"] = mc.reshape(16, 512)
    sh["ident"] = np.eye(128, dtype=np.float32)
    far = np.where(k <= q, NEGM, 0.0).astype(np.float32)
    mus = (k < q).astype(np.float32)
    mui = (k <= q).astype(np.float32)
    mls = (k > q).astype(np.float32)
    sh["masks"] = np.ascontiguousarray(np.stack([far, mus, mui, mls], 1))
    z = np.zeros((16, 256), np.float32)
    z[np.arange(16), np.arange(16) + 119] = 1.0
    sh["zsh"] = z
    e = np.zeros((32, 2048), np.float32)
    e[np.arange(2048) // 64, np.arange(2048)] = -NEGM
    sh["emat"] = e
    mi = np.zeros((128, 32), np.float32)
    for j in range(32):
        for a in range(4):
            for b in range(2):
                n = 4 * j + a - b
                if 0 <= n < 127:
                    mi[n, j] += 1.0
    sh["mimp"] = mi
    ka = np.zeros((128, 8, 2, 32), np.float32)
    for i in range(8, 16):
        for qq in range(128):
            cur = (128 * i + qq) // 64
            for j in range(32):
                forced = (j == 0) or (j == cur) or (j == cur - 1)
                causal = j <= cur
                if forced:
                    ka[qq, i - 8, 0, j] = 0.0
                    ka[qq, i - 8, 1, j] = 1e30
                elif causal:
                    ka[qq, i - 8, 0, j] = 1.0
                else:
                    ka[qq, i - 8, 1, j] = -1e30
    sh["keepadd"] = ka.reshape(128, 512)
    ind2 = np.zeros((128, 2), np.float32)
    ind2[:64, 0] = 1.0
    ind2[64:, 1] = 1.0
    sh["ind2"] = ind2
    indT = np.zeros((8, 4, 128), np.float32)
    for h in range(8):
        indT[h, h // 2, (h % 2) * 64:(h % 2) * 64 + 64] = 1.0
    sh["indT"] = indT.reshape(8, 512)
    x = f("x")
    c = f("c")
    per = []
    for core in range(8):
        d = {"x": np.ascontiguousarray(x[2 * core:2 * core + 2].reshape(4096, 1024)),
             "cT": np.ascontiguousarray(c[2 * core:2 * core + 2].reshape(2, 8, 128).transpose(2, 1, 0))}
        per.append(d)
    return sh, per


class _Stop(Exception):
    pass


def build(nseq=2, ntile=16, dbg=None, stage=9):
    nc = bass.Bass("TRN2", target_bir_lowering=False)
    dbg = dbg or {}
    di = lambda name, shape: nc.dram_tensor(name, shape, F32, kind="ExternalInput").ap()
    x_d = di("x", [4096, 1024])
    cT_d = di("cT", [128, 8, 2])
    w_res_d = di("w_res", [1024, NRES])
    w_rest_d = di("w_rest", [1024, 3072])
    w_ada_d = di("w_ada", [1024, 3072])
    w_out_a_d = di("w_out_a", [512, 1024])
    w_out_b_d = di("w_out_b", [512, 1024])
    w_o_d = di("w_o", [1024, 1024])
    w1k_d = di("w1k", [2048, 256])
    w1v_d = di("w1v", [2048, 256])
    b_ada_d = di("b_ada", [128, 24])
    g_norm_d = di("g_norm", [128, 8])
    mu_d = di("mu", [128, 13])
    vec4_d = di("vec4", [128, 4, 4])
    bc_small_d = di("bc_small", [128, 192])
    ln_w_bc_d = di("ln_w_bc", [128, 512])
    kgc_d = di("kgc", [64, 1])
    w0a0_d = di("w0a0", [2, 512])
    lora_d = di("lora", [128, 512])
    w2_d = di("w2", [128, 2, 2, 64])
    peT_d = di("peT", [128, 32])
    tblDS_d = di("tblDS", [2, 2, 128, 512])
    maskD_d = di("maskD", [128, 512])
    c31_d = di("c31", [2, 128, 512])
    tblC_d = di("tblC", [2, 16, 512])
    maskC_d = di("maskC", [16, 512])
    ident_d = di("ident", [128, 128])
    masks_d = di("masks", [128, 4, 128])
    zsh_d = di("zsh", [16, 256])
    emat_d = di("emat", [32, 2048])
    mimp_d = di("mimp", [128, 32])
    keepadd_d = di("keepadd", [128, 512])
    ind2_d = di("ind2", [128, 2])
    indT_d = di("indT", [8, 512])
    out_d = nc.dram_tensor("out", [4096, 1024], F32, kind="ExternalOutput").ap()
    wrest_s = nc.dram_tensor("wrest_s", [12, 128, 2048], BF16, kind="Internal").ap()
    wog_s = nc.dram_tensor("wog_s", [128, 8192], BF16, kind="Internal").ap()

    with ExitStack() as es:
        S = Sched(nc, es)
        _n = [0]

        def sb(shape, dt, name=None):
            _n[0] += 1
            return es.enter_context(nc.sbuf_tensor("s_" + (name or f"sb{_n[0]}"), shape, dt))

        def psb(name):
            return es.enter_context(nc.psum_tensor(name, [128, 512], F32))

        dbg_outs = []

        def dump(name, ap, reads, dt=F32):
            if name not in dbg:
                return
            d = nc.dram_tensor("dbg_" + name, list(ap.shape), dt, kind="ExternalOutput").ap()
            dbg_outs.append(S.op("sp", lambda e: e.dma_start(out=d, in_=ap), reads, (), dsem=sem_dbg))

        def MM(out, lhsT, rhs, start=True, stop=True, r=(), w=(), inc=True, sgc=False):
            if sgc:
                return S.op("pe", lambda e: e.matmul(out, lhsT=lhsT, rhs=rhs, start=start, stop=stop, skip_group_check=True), r, w, inc=inc)
            return S.op("pe", lambda e: e.matmul(out, lhsT=lhsT, rhs=rhs, start=start, stop=stop), r, w, inc=inc)

        def TR(out, in_, ident, r=(), w=(), inc=True):
            return S.op("pe", lambda e: e.transpose(out=out, in_=in_, identity=ident), r, w, inc=inc)

        def ACT(out, in_, func, r=(), w=(), bias=None, scale=None, accum=None):
            kw = {}
            if bias is not None:
                kw["bias"] = bias
            if scale is not None:
                kw["scale"] = scale
            if accum is not None:
                kw["accum_out"] = accum
            return S.op("act", lambda e: e.activation(out=out, in_=in_, func=func, **kw), r, w)

        def TS(eng, out, in0, s1, s2, op0, op1=None, r=(), w=()):
            if op1 is None:
                return S.op(eng, lambda e: e.tensor_scalar(out=out, in0=in0, scalar1=s1, scalar2=None, op0=op0), r, w)
            return S.op(eng, lambda e: e.tensor_scalar(out=out, in0=in0, scalar1=s1, scalar2=s2, op0=op0, op1=op1), r, w)

        def TT(eng, out, in0, in1, op, r=(), w=()):
            return S.op(eng, lambda e: e.tensor_tensor(out=out, in0=in0, in1=in1, op=op), r, w)

        def STT(out, in0, scalar, in1, op0, op1, r=(), w=()):
            return S.op("dve", lambda e: e.scalar_tensor_tensor(out=out, in0=in0, scalar=scalar, in1=in1, op0=op0, op1=op1), r, w)

        def CP(eng, out, in_, r=(), w=()):
            if eng == "act":
                return S.op("act", lambda e: e.copy(out=out, in_=in_), r, w)
            return S.op(eng, lambda e: e.tensor_copy(out=out, in_=in_), r, w)

        def MSET(eng, ap, val, w=()):
            return S.op(eng, lambda e: e.memset(ap, val), (), w)

        def DMA(out, in_, sem, r=(), w=(), eng="sp"):
            return S.op(eng, lambda e: e.dma_start(out=out, in_=in_), r, w, dsem=sem)

        def bcast(ap, shape, axis):
            return ap.unsqueeze(axis).to_broadcast(shape)

        sem_dbg = S.dma_sem("dbg")
        sem_stg = [S.dma_sem("stg0"), S.dma_sem("stg1")]
        sem_scr = S.dma_sem("scr")
        sem_x = [S.dma_sem("x0"), S.dma_sem("x1")]
        sem_xr = S.dma_sem("xr")
        sem_ws = [S.dma_sem(f"ws{i}") for i in range(3)]
        sem_out = S.dma_sem("out")
        sem_wog = S.dma_sem("wog")

        PS = [psb(f"ps{i}") for i in range(8)]
        PSB = [Buf() for _ in range(8)]
        rot = {"proj": [0, 1], "sc": [2, 3], "acc": [4, 5], "rw": [6, 7]}
        rotc = {k: 0 for k in rot}

        def bank(cls):
            i = rot[cls][rotc[cls] % len(rot[cls])]
            rotc[cls] += 1
            return PS[i], PSB[i]

        NSLOT = 35
        AR = sb([128, NSLOT * 256], F32, "arena")
        SLB = [Buf() for _ in range(NSLOT)]

        def slot(start, shape, dt, P0=0):
            el = 4 if dt == F32 else 2
            n = int(np.prod(shape[1:]))
            nsl = (n * el + 1023) // 1024
            assert start + nsl <= NSLOT
            base = AR[:] if dt == F32 else AR[:].bitcast(BF16)
            o = start * 1024 // el
            ap = base[P0:P0 + shape[0], o:o + n]
            if len(shape) > 2:
                names = " ".join(f"d{i}" for i in range(len(shape) - 1))
                kw = {f"d{i}": shape[i + 1] for i in range(len(shape) - 1)}
                ap = ap.rearrange(f"p ({names}) -> p {names}", **kw)
            return ap, SLB[start:start + nsl]

        Wres = sb([128, 8, NRES], BF16, "Wres"); bWres = Buf()
        Wouta = sb([128, 4, 1024], BF16, "Wouta"); bWouta = Buf()
        Woutb = sb([128, 4, 1024], BF16, "Woutb"); bWoutb = Buf()
        Wog = sb([128, 8, 1024], BF16, "Wog"); bWog = Buf()
        W1c = sb([128, 32, 256], BF16, "W1c"); bW1c = Buf()
        W2c = sb([128, 2, 2, 64], BF16, "W2c"); bW2c = Buf()
        Lora = sb([128, 512], BF16, "Lora"); bLora = Buf()
        identf = sb([128, 128], F32, "identf"); bidf = Buf()
        identb = sb([128, 128], BF16, "identb"); bidb = Buf()
        masks = sb([128, 4, 128], BF16, "masks"); bmasks = Buf()
        biasDS = sb([128, 2, 2, 512], BF16, "biasDS"); bbias = Buf()
        emat = sb([64, 2048], BF16, "emat"); bemat = Buf()
        zsh = sb([128, 256], BF16, "zsh"); bzsh = Buf()
        biasC = sb([128, 2, 512], BF16, "biasC"); bbiasC = Buf()
        w0a0 = sb([128, 512], F32, "w0a0"); bw0a0 = Buf()
        bmisc = Buf()
        mimp = sb([128, 32], F32, "mimp"); bmimp = Buf()
        keepadd = sb([128, 8, 2, 32], F32, "keepadd"); bka = Buf()
        ind2 = sb([128, 2], F32, "ind2"); bind2 = Buf()
        indT = sb([8, 4, 128], F32, "indT"); bindT = Buf()
        ones_f = sb([128, 128], F32, "ones_f"); bones = Buf()
        bc_small = sb([128, 192], F32, "bc_small"); bbcs = Buf()
        ln_w_bc = sb([128, 512], F32, "ln_w_bc"); blnw = Buf()
        vec4 = sb([128, 4, 4], F32, "vec4"); bvec4 = Buf()
        mucol = sb([128, 2, 13], F32, "mucol"); bmu = Buf()
        kgc = sb([64, 1], F32, "kgc"); bkgc = Buf()
        gcol = sb([128, 8], F32, "gcol"); bgcol = Buf()
        badaT = sb([128, 24], F32, "badaT"); bbada = Buf()
        cTt = sb([128, 8, 2], F32, "cTt"); bcT = Buf()
        modT = sb([128, 24, 2], F32, "modT"); bmod = Buf()
        gsT = sb([128, 2, 8], F32, "gsT"); bgs = Buf()
        hb2 = sb([128, 2, 2], F32, "hb2"); bhb2 = Buf()
        cneg = sb([128, 16], F32, "cneg"); bcneg = Buf()
        peTb = sb([128, 32], BF16, "peTb"); bpeT = Buf()
        siluc = sb([128, 8, 2], F32, "siluc"); bsc = Buf()
        gtmp = sb([128, 16], F32, "gtmp"); bgtmp = Buf()

        stg = []; bstg = []
        for i_ in range(2):
            a_, b_ = slot(16 * i_, [128, 4096], F32)
            stg.append(a_); bstg.append(b_)
        kT = sb([128, 2, 2048], BF16, "kT"); bkT = [Buf() for _ in range(16)]
        Vcf = sb([128, 4160], BF16, "Vc"); bVc = [Buf() for _ in range(16)]
        Vc = Vcf[:].rearrange("p (a b c d) -> p a b c d", a=16, b=2, c=2)
        stgb = kT[:].rearrange("p a b -> p (a b)"); bstgb = bkT
        gate_bc = Vcf[:].bitcast(F32)[:, 0:2048].rearrange("p (s n) -> p s n", s=2); bgbc = bVc

        ldn = [0]
        sem_lds = [S.dma_sem(f"ld{i}") for i in range(8)]

        def ld(out, in_, w):
            sm = sem_lds[ldn[0] % 8]
            ldn[0] += 1
            if S.dcnt[sm] > 0:
                S.wait_all("sp", [(sm, S.dcnt[sm])])
            return DMA(out, in_, sm, w=w)

        ld(identf[:], ident_d, [bidf])
        CP("dve", identb[:], identf[:], [bidf], [bidb])
        ld(stg[0][:, 0:512].rearrange("p (a b) -> p a b", a=4), masks_d, [bstg[0]])
        CP("dve", masks[:], stg[0][:, 0:512].rearrange("p (a b) -> p a b", a=4), [bstg[0]], [bmasks])
        ld(mimp[:], mimp_d, [bmimp])
        ld(keepadd[:].rearrange("p a b c -> p (a b c)"), keepadd_d, [bka])
        ld(ind2[:], ind2_d, [bind2])
        ld(indT[:].rearrange("p a b -> p (a b)"), indT_d, [bindT])
        ld(bc_small[:], bc_small_d, [bbcs])
        ld(ln_w_bc[:], ln_w_bc_d, [blnw])
        ld(vec4[:], vec4_d, [bvec4])
        ld(mucol[:, 0, :], mu_d, [bmu])
        TS("dve", mucol[:, 1, :], mucol[:, 0, :], -1.0, 1.0, ALU.mult, ALU.add, [bmu], [bmu])
        ld(kgc[:], kgc_d, [bkgc])
        MSET("pool", w0a0[:], 0.0, [bw0a0])
        ld(w0a0[0:1, :], w0a0_d[0:1, :], [bw0a0])
        ld(w0a0[64:65, :], w0a0_d[1:2, :], [bw0a0])
        MSET("pool", emat[:], 0.0, [bemat])
        MSET("pool", zsh[:], 0.0, [bzsh])
        MSET("pool", biasC[:].rearrange("p a b -> p (a b)"), 0.0, [bbiasC])
        ld(gcol[:], g_norm_d, [bgcol])
        ld(badaT[:], b_ada_d, [bbada])
        ld(cTt[:], cT_d, [bcT])
        MSET("pool", ones_f[:], 1.0, [bones])
        MSET("pool", cneg[:], -0.5, [bcneg])
        ld(stg[1][0:16, 0:256], zsh_d, [bstg[1]])
        CP("dve", zsh[0:16, :], stg[1][0:16, 0:256], [bstg[1]], [bzsh])
        ld(stg[1][0:32, 0:2048], emat_d, [bstg[1]])
        CP("dve", emat[0:32, :], stg[1][0:32, 0:2048], [bstg[1]], [bemat])
        ld(stg[1][:, 2048:2560], lora_d, [bstg[1]])
        CP("dve", Lora[:], stg[1][:, 2048:2560], [bstg[1]], [bLora])
        ld(stg[1][:, 2560:2816].rearrange("p (a b c) -> p a b c", a=2, b=2), w2_d, [bstg[1]])
        CP("dve", W2c[:], stg[1][:, 2560:2816].rearrange("p (a b c) -> p a b c", a=2, b=2), [bstg[1]], [bW2c])
        ld(stg[1][:, 2816:2848], peT_d, [bstg[1]])
        CP("dve", peTb[:], stg[1][:, 2816:2848], [bstg[1]], [bpeT])
        for g in range(2):
            ld(stg[0][:, 0:512], c31_d[g], [bstg[0]])
            for v in range(2):
                ld(stg[1][:, 0:512], tblDS_d[v, g], [bstg[1]])
                TT("dve", stg[1][:, 0:512], stg[1][:, 0:512], stg[0][:, 0:512], ALU.subtract, [bstg[0], bstg[1]], [bstg[1]])
                if v == 0:
                    ld(stg[1][:, 512:1024], maskD_d, [bstg[1]])
                    STT(biasDS[:, v, g, :], stg[1][:, 0:512], 8.0, stg[1][:, 512:1024], ALU.mult, ALU.add, [bstg[1]], [bbias])
                else:
                    TS("dve", biasDS[:, v, g, :], stg[1][:, 0:512], 8.0, None, ALU.mult, None, [bstg[1]], [bbias])
            ld(stg[1][0:16, 0:512], tblC_d[g], [bstg[1]])
            ld(stg[1][0:16, 512:1024], maskC_d, [bstg[1]])
            TT("dve", stg[1][0:16, 0:512], stg[1][0:16, 0:512], stg[0][0:16, 0:512], ALU.subtract, [bstg[0], bstg[1]], [bstg[1]])
            STT(biasC[0:16, g, :], stg[1][0:16, 0:512], 8.0, stg[1][0:16, 512:1024], ALU.mult, ALU.add, [bstg[1]], [bbiasC])

        def stage_load(i, src_ap, ncols, nk=8):
            view = stg[i][:, 0:nk * ncols].rearrange("p (k n) -> p k n", k=nk)
            DMA(view, src_ap, sem_stg[i], w=[bstg[i]])
            return view

        si = 0
        for c0 in range(0, NRES, 512):
            n = min(512, NRES - c0)
            v = stage_load(si, w_res_d[:, c0:c0 + n].rearrange("(k p) n -> p k n", p=128), n)
            CP("dve" if si == 0 else "pool", Wres[:, :, c0:c0 + n], v, [bstg[si]], [bWres])
            si ^= 1
        for c in range(6):
            v = stage_load(si, w_rest_d[:, c * 512:(c + 1) * 512].rearrange("(k p) n -> p k n", p=128), 512)
            sv = stgb[:, 0:4096].rearrange("p (k n) -> p k n", k=8)
            CP("dve" if si == 0 else "pool", sv, v, [bstg[si]], [bstgb])
            for j_ in range(2):
                DMA(wrest_s[2 * c + j_].rearrange("p (k n) -> p k n", k=8),
                    stgb[:, 0:4096].rearrange("p (k j n) -> p k j n", k=8, j=2)[:, :, j_, :], sem_scr, r=[bstgb], w=[Buf()])
            si ^= 1
        for (wd_, Wt, bW) in ((w_out_a_d, Wouta, bWouta), (w_out_b_d, Woutb, bWoutb)):
            v = stage_load(si, wd_.rearrange("(k p) n -> p k n", p=128), 1024, nk=4)
            CP("dve" if si == 0 else "pool", Wt[:], v, [bstg[si]], [bW])
            si ^= 1
        for (wd_, lo) in ((w1k_d, 0), (w1v_d, 64)):
            for hh in range(2):
                view = stg[si][lo:lo + 64, 0:4096].rearrange("p (k n) -> p k n", k=16)
                DMA(view, wd_[hh * 1024:(hh + 1) * 1024, :].rearrange("(k p) n -> p k n", p=64), sem_stg[si], w=[bstg[si]])
                CP("dve" if si == 0 else "pool", W1c[lo:lo + 64, hh * 16:(hh + 1) * 16, :], view, [bstg[si]], [bW1c])
                si ^= 1
        ACT(siluc[:], cTt[:], AF.Tanh, [bcT], [bsc], scale=0.5)
        TS("dve", siluc[:], siluc[:], 0.5, 0.5, ALU.mult, ALU.add, [bsc], [bsc])
        TT("dve", siluc[:], siluc[:], cTt[:], ALU.mult, [bsc, bcT], [bsc])
        pm, bpm = bank("proj")
        for c in range(6):
            v = stage_load(si, w_ada_d[:, c * 512:(c + 1) * 512].rearrange("(k p) n -> p k n", p=128), 512)
            for jj in range(4):
                j = c * 4 + jj
                for kc in range(8):
                    MM(pm[:, j * 2:j * 2 + 2], v[:, kc, jj * 128:(jj + 1) * 128], siluc[:, kc, :], start=(kc == 0), stop=(kc == 7),
                       r=[bstg[si], bsc], w=[bpm], inc=(kc == 7))
            si ^= 1
        TT("dve", modT[:], pm[:, 0:48].rearrange("p (j b) -> p j b", b=2), bcast(badaT[:], [128, 24, 2], 2), ALU.add, [bpm, bbada], [bmod])
        for s in range(2):
            STT(gsT[:, s, :], modT[:, 8:16, s], 1.0, gcol[:], ALU.add, ALU.mult, [bmod, bgcol], [bgs])
        CP("dve", gtmp[:].rearrange("p (s j) -> p s j", s=2), modT[:, 16:24, :].rearrange("p j s -> p s j"), [bmod], [bgtmp])
        for q4 in range(4):
            pg, bpg = bank("proj")
            for jq in range(4):
                qq = q4 * 4 + jq
                MM(pg[0:1, jq * 128:(jq + 1) * 128], gtmp[:, qq:qq + 1], identf[:], r=[bgtmp, bidf], w=[bpg], inc=(jq == 3))
            CP("dve", stg[1][0:1, q4 * 512:(q4 + 1) * 512], pg[0:1, 0:512], [bpg], [bstg[1]])
        for s in range(2):
            for hh in range(2):
                pb_, bpb_ = bank("proj")
                MM(pb_[:, :], ones_f[0:1, :], stg[1][0:1, s * 1024 + hh * 512: s * 1024 + hh * 512 + 512], r=[bones, bstg[1]], w=[bpb_])
                TS("dve", gate_bc[:, s, hh * 512:(hh + 1) * 512], pb_[:, :], 0.5, None, ALU.mult, None, [bpb_], [bgbc])
        for s in (1, 0):
            for hh in range(2):
                v = stage_load(0, w_o_d[:, hh * 512:(hh + 1) * 512].rearrange("(k p) n -> p k n", p=128), 512)
                TT("dve", Wog[:, :, hh * 512:(hh + 1) * 512], v, bcast(gate_bc[:, s, hh * 512:(hh + 1) * 512], [128, 8, 512], 1), ALU.mult,
                   [bstg[0], bgbc], [bWog])
            if s == 1:
                DMA(wog_s, Wog[:].rearrange("p k n -> p (k n)"), sem_scr, r=[bWog], w=[Buf()])
        for kv in range(2):
            lo = kv * 64
            ph, bph = bank("proj")
            for jh in range(2):
                for pos in range(32):
                    MM(ph[:, jh:jh + 1], W1c[lo:lo + 64, pos, jh * 128:(jh + 1) * 128], peTb[lo:lo + 64, pos:pos + 1],
                       start=(pos == 0), stop=(pos == 31), r=[bW1c, bpeT], w=[bph], inc=(pos == 31))
            CP("dve", hb2[:, kv, :], ph[:, 0:2], [bph], [bhb2])
        S.barrier()
        print("SBUF remaining before main alloc:", nc.sbuf_bytes_remaining)

        xt = [sb([128, 1024], F32, f"xt{i}") for i in range(2)]; bxt = [Buf(), Buf()]
        xs = sb([128, 1024], F32, "xs"); bxs = Buf()
        hT = [sb([128, 8, 130], BF16, f"hT{i}") for i in range(2)]; bhT = [Buf(), Buf()]
        for i_ in range(2):
            MSET("pool", hT[i_][:].rearrange("p a b -> p (a b)"), 0.0, [bhT[i_]])
        ynsa = sb([128, 512], F32, "ynsa"); bynsa = Buf()
        yn = sb([128, 512], F32, "yn"); byn = Buf()
        st12 = sb([128, 16], F32, "st12"); bst12 = Buf()
        rs12 = sb([128, 16], F32, "rs12"); brs12 = Buf()
        MSET("pool", Vcf[:], 1.0, bVc)
        gsig = sb([128, 3, 8], F32, "gsig"); bgsig = Buf()
        kvc = sb([128, 2, 144], BF16, "kvc"); bkvc = Buf()
        kcT = sb([64, 2, 128], BF16, "kcT"); bkcT = Buf()
        vcT = sb([64, 2, 128], F32, "vcT"); bvcT = Buf()
        vca = sb([128, 2, 65], F32, "vca"); bvca = Buf()
        MSET("pool", vca[:].rearrange("p a b -> p (a b)"), 1.0, [bvca])
        hu = sb([128, 64], F32, "hu"); bhu = Buf()
        hw_ = sb([128, 64], F32, "hw_"); bhw = Buf()
        hid = sb([128, 64], BF16, "hid"); bhid = Buf()
        kcs = sb([64, 48], F32, "kcs"); bkcs = Buf()
        coef = sb([128, 16], F32, "coef"); bcoef = Buf()
        impr = sb([128, 2, 32], F32, "impr"); bimpr = Buf()
        imp2 = sb([128, 32], F32, "imp2"); bimp2 = Buf()
        m8a = sb([128, 8], F32, "m8a"); bm8a = Buf()
        m8b = sb([128, 8], F32, "m8b"); bm8b = Buf()
        nsel = sb([128, 2, 32], F32, "nsel"); bnsel = Buf()
        nselT = sb([64, 2, 128], BF16, "nselT"); bnselT = Buf()
        MSET("pool", nselT[:].rearrange("p a b -> p (a b)"), 0.0, [bnselT])
        wdad = sb([128, 128], F32, "wdad"); bwdad = Buf()
        wdadb = sb([128, 128], BF16, "wdadb"); bwdadb = Buf()
        eLC = sb([128, 4], F32, "eLC"); beLC = Buf()
        rn8 = sb([128, 8], F32, "rn8"); brn8 = Buf()
        rn8T = sb([8, 128], F32, "rn8T"); brn8T = Buf()
        sbon = sb([128, 8], F32, "sbon"); bsbon = Buf()
        Pst = sb([128, 4, 64], F32, "Pst"); bPst = Buf()
        Pb = sb([128, 4, 64], BF16, "Pb"); bPb = Buf()
        lnst = sb([128, 32], F32, "lnst"); blnst = Buf()
        NWS = 2
        WS = [sb([128, 8, 256], BF16, f"WS{i}") for i in range(NWS)]; bWS = [Buf() for _ in range(NWS)]
        sq, bsq = slot(0, [128, 1024], F32)
        qn2, bqn2 = slot(4, [128, 8, 2, 64], BF16)
        qT2, bqT2 = slot(6, [128, 8, 128], BF16)
        kn2, bkn2 = slot(8, [128, 2, 2, 64], BF16)
        PT = []; bPT = []
        for i_ in range(3):
            a_, b_ = slot(9 + i_, [128, 512], BF16)
            PT.append(a_); bPT.append(b_)
        PcT, bPcT = slot(12, [128, 512], F32)
        ytmp, bytmp = slot(14, [128, 256], F32)
        silA, bsilA = slot(15, [128, 4, 128], F32)
        silB, bsilB = slot(17, [128, 4, 128], F32)
        thA, bthA = slot(19, [128, 8, 128], BF16)
        thB, bthB = slot(21, [128, 8, 128], BF16)
        mg1, bmg1 = slot(23, [128, 8, 128], F32)
        mg2, bmg2 = slot(27, [128, 8, 128], F32)
        mgT, bmgT = slot(31, [128, 8, 128], BF16)
        yaT, byaT = slot(33, [128, 4, 128], BF16)
        ybT, bybT = slot(34, [128, 4, 128], BF16)
        rT, brT = slot(4, [128, 4, 128], F32)
        kTr, bkTr = slot(6, [128, 4, 128], F32)
        vT, bvT = slot(8, [128, 4, 128], F32)
        lwT, blw = slot(10, [128, 4, 128], F32)
        LT, bLT = slot(12, [128, 4, 128], F32)
        asg, basg = slot(14, [128, 4, 128], F32)
        e1, be1 = slot(16, [128, 4, 128], F32)
        e2, be2 = slot(18, [128, 4, 128], F32)
        e3, be3 = slot(20, [128, 4, 128], F32)
        kkn, bkkn = slot(22, [128, 4, 128], F32)
        kmod, bkmod = slot(24, [128, 4, 128], F32)
        tmpA, btmpA = slot(26, [128, 4, 128], F32)
        At, bAt = slot(28, [128, 4, 128], BF16)
        Bt, bBt = slot(29, [128, 4, 128], BF16)
        Kt, bKt = slot(30, [128, 4, 128], BF16)
        Rt, bRt = slot(31, [128, 4, 128], BF16)
        Bh, bBh = slot(32, [128, 512], BF16)
        Kh, bKh = slot(33, [128, 512], BF16)
        Vt, bVt = slot(34, [128, 512], BF16)
        Qm = []; bQm = []; QmT = []; bQmT = []; Xm = []; bXm = []
        for st_ in (10, 12):
            a_, b_ = slot(st_, [128, 8, 128], BF16); Qm.append(a_); bQm.append(b_)
        for st_ in (14, 18):
            a_, b_ = slot(st_, [128, 8, 128], BF16); QmT.append(a_); bQmT.append(b_)
        for st_ in (20, 22):
            a_, b_ = slot(st_, [128, 8, 128], BF16); Xm.append(a_); bXm.append(b_)
        AakT, bAak = slot(24, [128, 8, 128], BF16)
        MrbT, bMrb = slot(4, [128, 8, 128], BF16)
        MrkT, bMrk = slot(6, [128, 8, 128], BF16)
        rhs0, brhs0 = slot(8, [128, 8, 64], BF16)
        Ub, bUb = slot(9, [128, 8, 64], BF16)

        def f4(t):
            return t.rearrange("p c t -> p (c t)")

        def REDUCE(out, in_, r, w):
            return S.op("dve", lambda e: e.tensor_reduce(out=out, in_=in_, axis=AX.X, op=ALU.add), r, w)

        def MAX8(out, in_, r, w):
            return S.op("dve", lambda e: e.max(out=out, in_=in_), r, w)

        def MREP(out, rep, vals, r, w):
            return S.op("dve", lambda e: e.match_replace(out=out, in_to_replace=rep, in_values=vals, imm_value=-3.0e38), r, w)

        def RECIP(out, in_, r, w):
            return S.op("dve", lambda e: e.reciprocal(out=out, in_=in_), r, w)

        def SCAN(out, d0, d1, r, w):
            return S.op("dve", lambda e: e.tensor_tensor_scan(out=out, data0=d0, data1=d1, initial=0.0, op0=ALU.mult, op1=ALU.add), r, w)

        print("SBUF remaining:", nc.sbuf_bytes_remaining)
        ws_i = [0]

        def chk(n):
            if stage <= n:
                raise _Stop()

        def tile_body(s, i):
            try:
                return tile_body2(s, i)
            except _Stop:
                T = s * 16 + i
                return DMA(out_d[T * 128:T * 128 + 128, :], xs[:, :], sem_out, r=[bxs], w=[])

        def tile_body2(s, i):
            T = s * 16 + i
            tok0 = T * 128
            xb_ = T % 2
            x_t = xt[xb_]; bx = bxt[xb_]
            hcur = hT[T % 2]; bh = bhT[T % 2]
            hprev = hT[(T + 1) % 2]; bhp = bhT[(T + 1) % 2]
            ACT(sq[:], x_t[:], AF.Square, [bx], [bsq, bst12], accum=st12[:, 0:1])
            TS("dve", st12[:, 0:1], st12[:, 0:1], 1.0 / 1024, 1e-6, ALU.mult, ALU.add, [bst12], [bst12])
            TT("pool", rs12[:, 0:1], st12[:, 0:1], cneg[:, 0:1], ALU.pow, [bst12, bcneg], [brs12])
            TS("dve", xs[:], x_t[:], rs12[:, 0:1], None, ALU.mult, None, [bx, brs12], [bxs])
            import os as _os
            _sk = _os.environ.get("SKIP", "")
            if i == 0:
                if "m" not in _sk:
                    MSET("pool", hcur[:, :, 0:1], 0.0, [bh])
            else:
                CP("pool", hcur[:, :, 0:1], hprev[:, :, 128:129], [bhp], [bh])
            for half in range(2):
                pp, bp = bank("proj")
                for j in range(4):
                    kc = half * 4 + j
                    TR(pp[:, j * 128:(j + 1) * 128], xs[:, kc * 128:(kc + 1) * 128], identf[:], [bxs, bidf], [bp], inc=(j == 3))
                for j in range(4):
                    kc = half * 4 + j
                    if "a" in _sk:
                        ACT(hcur[:, kc, 1:129], pp[:, j * 128:(j + 1) * 128], AF.Identity, [bp, bgs, bmod], [bh])
                    elif "b" in _sk:
                        ACT(hcur[:, kc, 2:130], pp[:, j * 128:(j + 1) * 128], AF.Identity, [bp, bgs, bmod], [bh],
                            bias=modT[:, kc, s:s + 1], scale=gsT[:, s, kc:kc + 1])
                    else:
                        ACT(hcur[:, kc, 1:129], pp[:, j * 128:(j + 1) * 128], AF.Identity, [bp, bgs, bmod], [bh],
                            bias=modT[:, kc, s:s + 1], scale=gsT[:, s, kc:kc + 1])
            dump(f"hT_{T}", hcur[:], [bh], BF16)
            chk(1)

            pq, bpq = bank("proj")
            for kc in range(8):
                MM(pq[:, :], hcur[:, kc, 1:129], Wres[:, kc, CQ:CQ + 512], start=(kc == 0), stop=(kc == 7), r=[bh, bWres], w=[bpq], inc=(kc == 7))
            ACT(sq[:, 0:512], pq[:, :], AF.Square, [bpq], [bsq])
            REDUCE(st12[:, 0:8], sq[:, 0:512].rearrange("p (h d) -> p h d", h=8), [bsq], [bst12])
            pkv, bpkv = bank("proj")
            for kc in range(8):
                MM(pkv[:, :], hcur[:, kc, 1:129], Wres[:, kc, CKV:CKV + 512], start=(kc == 0), stop=(kc == 7), r=[bh, bWres], w=[bpkv], inc=(kc == 7))
            ACT(sq[:, 512:768], pkv[:, 0:256], AF.Square, [bpkv], [bsq])
            REDUCE(st12[:, 8:12], sq[:, 512:768].rearrange("p (h d) -> p h d", h=4), [bsq], [bst12])
            TS("dve", st12[:, 0:12], st12[:, 0:12], 1.0 / 64, 1e-6, ALU.mult, ALU.add, [bst12], [bst12])
            TT("pool", rs12[:, 0:12], st12[:, 0:12], cneg[:, 0:12], ALU.pow, [bst12, bcneg], [brs12])
            chk(1.2)
            for h in range(8):
                STT(qn2[:, h, :, :], bcast(pq[:, h * 64:(h + 1) * 64], [128, 2, 64], 1), rs12[:, h:h + 1],
                    bcast(bc_small[:, 0:64], [128, 2, 64], 1), ALU.mult, ALU.mult, [bpq, brs12, bbcs], [bqn2])
            for gg in range(2):
                for br in range(2):
                    c0 = gg * 128 + br * 64
                    STT(kn2[:, gg, br, :], pkv[:, c0:c0 + 64], rs12[:, 8 + gg * 2 + br:9 + gg * 2 + br],
                        bc_small[:, 64 + br * 64:128 + br * 64], ALU.mult, ALU.mult, [bpkv, brs12, bbcs], [bkn2])
            CP("act", Vc[:, i, :, :, 0:64], pkv[:, 256:512].rearrange("p (b g d) -> p b g d", b=2, g=2), [bpkv], [bVc[i]])
            chk(1.4)
            pt, bpt = bank("proj")
            ptb = pt[:].bitcast(BF16)
            for h in range(8):
                TR(ptb[:, h * 128:(h + 1) * 128], qn2[:, h, :, :].rearrange("p c d -> p (c d)"), identb[:], [bqn2, bidb], [bpt], inc=(h == 7))
            CP("act", qT2[:].rearrange("p h q -> p (h q)"), ptb[:, 0:1024], [bpt], [bqT2])
            pt2, bpt2 = bank("proj")
            pt2b = pt2[:].bitcast(BF16)
            for gg in range(2):
                TR(pt2b[:, gg * 128:(gg + 1) * 128], kn2[:, gg, :, :].rearrange("p c d -> p (c d)"), identb[:], [bkn2, bidb], [bpt2], inc=(gg == 1))
            CP("dve", kT[:, :, i * 128:(i + 1) * 128], pt2b[:, 0:256].rearrange("p (g t) -> p g t", g=2), [bpt2], [bkT[i]])
            chk(1.6)
            pgt, bpgt = bank("proj")
            for kc in range(8):
                MM(pgt[:, 0:24], hcur[:, kc, 1:129], Wres[:, kc, CG:CG + 24], start=(kc == 0), stop=(kc == 7), r=[bh, bWres], w=[bpgt], inc=(kc == 7))
            ACT(gsig[:].rearrange("p a b -> p (a b)"), pgt[:, 0:24], AF.Tanh, [bpgt], [bgsig], scale=0.5)
            TS("dve", gsig[:].rearrange("p a b -> p (a b)"), gsig[:].rearrange("p a b -> p (a b)"), 0.5, 0.5, ALU.mult, ALU.add, [bgsig], [bgsig])
            pcm, bpcm = bank("proj")
            for gg in range(2):
                for kc in range(8):
                    MM(pcm[:, gg * 128:(gg + 1) * 128], Wres[:, kc, CC + gg * 128:CC + (gg + 1) * 128], hcur[:, kc, 1:129],
                       start=(kc == 0), stop=(kc == 7), r=[bh, bWres], w=[bpcm], inc=(kc == 7 and gg == 1))
            CP("pool", kvc[:, :, 0:16], kvc[:, :, 128:144], [bkvc], [bkvc])
            CP("act", kvc[:, :, 16:144], pcm[:, 0:256].rearrange("p (g t) -> p g t", g=2), [bpcm], [bkvc])
            chk(1.8)
            m0 = 1 if i == 0 else 0
            nm = 8 - m0
            for kv in range(2):
                lo = kv * 64
                phd, bphd = bank("proj")
                for jh in range(2):
                    for pos in range(32):
                        MM(phd[:, jh * 16:jh * 16 + 16].rearrange("p (g m) -> p g m", g=2), W1c[lo:lo + 64, pos, jh * 128:(jh + 1) * 128],
                           kvc[lo:lo + 64, :, pos:pos + 113:16], start=(pos == 0), stop=(pos == 31), r=[bW1c, bkvc], w=[bphd],
                           inc=(pos == 31))
                for jh in range(2):
                    reg = (kv * 2 + jh) * 16
                    ACT(hu[:, reg:reg + 16], phd[:, jh * 16:jh * 16 + 16], AF.Identity, [bphd, bhb2], [bhu], bias=hb2[:, kv, jh:jh + 1])
            chk(1.85)
            TT("dve", hw_[:], hu[:], hu[:], ALU.mult, [bhu], [bhw])
            TS("dve", hw_[:], hw_[:], 0.044715, 1.0, ALU.mult, ALU.add, [bhw], [bhw])
            TT("dve", hw_[:], hw_[:], hu[:], ALU.mult, [bhw, bhu], [bhw])
            ACT(hw_[:], hw_[:], AF.Tanh, [bhw], [bhw], scale=math.sqrt(2.0 / math.pi))
            STT(hid[:], hw_[:], 1.0, hu[:], ALU.add, ALU.mult, [bhw, bhu], [bhid])
            chk(1.9)
            pc2, bpc2 = bank("proj")
            for kv in range(2):
                for jh in range(2):
                    reg = (kv * 2 + jh) * 16
                    MM(pc2[0:64, kv * 16:(kv + 1) * 16], W2c[:, kv, jh, :], hid[:, reg:reg + 16], start=(jh == 0), stop=(jh == 1),
                       r=[bW2c, bhid], w=[bpc2], inc=(jh == 1))
            TS("dve", kcs[:, 0:16], pc2[0:64, 0:16], 0.5, None, ALU.mult, None, [bpc2], [bkcs])
            TT("dve", kcs[:, 16:32], kcs[:, 0:16], kcs[:, 0:16], ALU.mult, [bkcs], [bkcs])
            MM(pc2[0:64, 64:80], ones_f[0:64, 0:64], kcs[:, 16:32], r=[bones, bkcs], w=[bpc2])
            TS("dve", kcs[:, 32:48], pc2[0:64, 64:80], 1.0 / 64, 1e-6, ALU.mult, ALU.add, [bpc2], [bkcs])
            TT("pool", kcs[:, 16:32], kcs[:, 32:48], cneg[0:64, 0:16], ALU.pow, [bkcs, bcneg], [bkcs])
            TT("dve", kcs[:, 0:16], kcs[:, 0:16], kcs[:, 16:32], ALU.mult, [bkcs], [bkcs])
            n0 = 8 * i - 1 + m0
            TS("dve", kcT[:, :, n0:n0 + nm], kcs[:, 0:16].rearrange("p (g m) -> p g m", g=2)[:, :, m0:8], kgc[:, 0:1], None, ALU.mult, None,
               [bkcs, bkgc], [bkcT])
            TS("dve", vcT[:, :, n0:n0 + nm], pc2[0:64, 16:32].rearrange("p (g m) -> p g m", g=2)[:, :, m0:8], 0.5, None, ALU.mult, None,
               [bpc2], [bvcT])
            nv = 8 * i + 7
            chk(1.95)
            pvt, bpvt = bank("proj")
            for gg in range(2):
                TR(pvt[0:nv, gg * 64:(gg + 1) * 64], vcT[:, gg, 0:nv], identf[0:64, 0:64], [bvcT, bidf], [bpvt], inc=(gg == 1))
            CP("dve", vca[0:nv, :, 0:64], pvt[0:nv, 0:128].rearrange("p (g d) -> p g d", g=2), [bpvt], [bvca])
            dump(f"kcT_{T}", kcT[:], [bkcT], BF16)
            dump(f"vca_{T}", vca[:], [bvca])
            dump(f"qT2_{T}", qT2[:], [bqT2], BF16)
            chk(2)

            first_y = {0: True, 1: True}

            def finish(acc, bacc, br, gg):
                accv = acc[:, 0:260].rearrange("p (h e) -> p h e", h=4)
                c0 = br * 4
                TS("dve", coef[:, c0:c0 + 4], accv[:, :, 64], 1e-30, None, ALU.max, None, [bacc], [bcoef])
                RECIP(coef[:, c0:c0 + 4], coef[:, c0:c0 + 4], [bcoef], [bcoef])
                if br == 0:
                    CP("dve", coef[:, 12:16], coef[:, 0:4], [bcoef], [bcoef])
                gbr = {0: 0, 1: 1, 2: 2}[br]
                TT("dve", coef[:, c0:c0 + 4], coef[:, c0:c0 + 4], gsig[:, gbr, gg * 4:(gg + 1) * 4], ALU.mult, [bcoef, bgsig], [bcoef])
                yv = ynsa[:, gg * 256:(gg + 1) * 256].rearrange("p (h d) -> p h d", h=4)
                cb = bcast(coef[:, c0:c0 + 4], [128, 4, 64], 2)
                if first_y[gg]:
                    TT("dve", yv, accv[:, :, 0:64], cb, ALU.mult, [bacc, bcoef], [bynsa])
                    first_y[gg] = False
                else:
                    tv = ytmp[:].rearrange("p (h d) -> p h d", h=4)
                    TT("dve", tv, accv[:, :, 0:64], cb, ALU.mult, [bacc, bcoef], [bytmp])
                    TT("pool", yv, yv, tv, ALU.add, [bynsa, bytmp], [bynsa])

            def pv(acc, bacc, Pt_, bP, vrhs, bv, first, last, K=128):
                for h in range(4):
                    MM(acc[:, h * 65:(h + 1) * 65], Pt_[0:K, h * 128:(h + 1) * 128], vrhs, start=(first and h == 0), stop=(last and h == 3), r=[bP] + bv, w=[bacc],
                       inc=(h == 3), sgc=True)

            pti = [0]
            for gg in range(2):
                sc, bsc_ = bank("sc")
                MM(sc[0:nv, :], kcT[:, gg, 0:nv], qT2[0:64, gg * 4:(gg + 1) * 4, :].rearrange("p h q -> p (h q)"), start=True, stop=False,
                   r=[bkcT, bqT2], w=[bsc_], inc=False)
                off = 128 - 8 * i
                MM(sc[0:nv, :], zsh[:, off:off + nv], biasC[:, gg, :], start=False, stop=True, r=[bzsh], w=[bsc_])
                ACT(PcT[0:nv, :], sc[0:nv, :], AF.Exp, [bsc_], [bPcT], scale=0.125)
                acc, bacc = bank("acc")
                for h in range(4):
                    MM(acc[:, h * 65:(h + 1) * 65], PcT[0:nv, h * 128:(h + 1) * 128], vca[0:nv, gg, :], r=[bPcT, bvca], w=[bacc], inc=False)
                for h in range(4):
                    MM(acc[:, 320 + h * 32:320 + (h + 1) * 32], PcT[0:nv, h * 128:(h + 1) * 128], mimp[0:nv, :], r=[bPcT, bmimp], w=[bacc], inc=(h == 3))
                finish(acc, bacc, 0, gg)
                if i >= 8:
                    iv = impr[:, gg, :]
                    TS("dve", iv, acc[:, 320:352], coef[:, 12:13], None, ALU.mult, None, [bacc, bcoef], [bimpr])
                    for h in range(1, 4):
                        STT(iv, acc[:, 320 + h * 32:352 + h * 32], coef[:, 12 + h:13 + h], iv, ALU.mult, ALU.add, [bacc, bcoef, bimpr], [bimpr])
                    TT("dve", iv, iv, keepadd[:, i - 8, 0, :], ALU.mult, [bimpr, bka], [bimpr])
                    TT("dve", iv, iv, keepadd[:, i - 8, 1, :], ALU.add, [bimpr, bka], [bimpr])
                    MAX8(m8a[:], iv, [bimpr], [bm8a])
                    MREP(imp2[:], m8a[:], iv, [bimpr, bm8a], [bimp2])
                    MAX8(m8b[:], imp2[:], [bimp2], [bm8b])
                    TS("dve", nsel[:, gg, :], iv, m8b[:, 7:8], 1.0, ALU.is_ge, ALU.subtract, [bimpr, bm8b], [bnsel])
                    pn, bpn = bank("proj")
                    TR(pn[0:32, 0:128], nsel[:, gg, :], identf[:], [bnsel, bidf], [bpn])
                    CP("dve", nselT[0:32, gg, :], pn[0:32, 0:128], [bpn], [bnselT])
            dump(f"nsel_{T}", nsel[:], [bnsel])
            for br in (2, 1):
                for gg in range(2):
                    lo = 0 if br == 1 else 64
                    j0 = 0 if br == 1 else max(0, i - 4)
                    acc, bacc = bank("acc")
                    for j in range(j0, i + 1):
                        sc, bsc_ = bank("sc")
                        extra = []
                        if j == i:
                            extra.append((identb[:], biasDS[:, 0, gg, :], [bidb, bbias]))
                        if j == i - 1:
                            extra.append((identb[:], biasDS[:, 1, gg, :], [bidb, bbias]))
                        if br == 2 and j == i - 4:
                            extra.append((identb[:], bcast(masks[:, 0, :], [128, 4, 128], 1), [bidb, bmasks]))
                        if br == 1 and i >= 8:
                            extra.append((emat[:, j * 128:(j + 1) * 128], bcast(nselT[:, gg, :], [64, 4, 128], 1), [bemat, bnselT]))
                        MM(sc[:, :], kT[lo:lo + 64, gg, j * 128:(j + 1) * 128], qT2[lo:lo + 64, gg * 4:(gg + 1) * 4, :].rearrange("p h q -> p (h q)"),
                           start=True, stop=(len(extra) == 0), r=[bkT[j], bqT2], w=[bsc_], inc=(len(extra) == 0))
                        for ei, (l_, r_, bb_) in enumerate(extra):
                            lastx = ei == len(extra) - 1
                            MM(sc[:, :].rearrange("p (h q) -> p h q", h=4) if len(r_.shape) == 3 else sc[:, :], l_, r_, start=False, stop=lastx,
                               r=bb_, w=[bsc_], inc=lastx)
                        Pt_ = PT[pti[0] % 3]; bP = bPT[pti[0] % 3]; pti[0] += 1
                        ACT(Pt_[:, :], sc[:, :], AF.Exp, [bsc_], [bP], scale=0.125)
                        pv(acc, bacc, Pt_, bP, Vc[:, j, br - 1, gg, :], [bVc[j]], j == j0, j == i)
                    finish(acc, bacc, br, gg)
            dump(f"ynsa_{T}", ynsa[:], [bynsa])
            chk(3)

            for c3 in range(0, 13, 3):
                ps_, bps_ = bank("rw")
                ncs = min(3, 13 - c3)
                for cc in range(ncs):
                    c = c3 + cc
                    for kc in range(8):
                        MM(ps_[:, cc * 129:(cc + 1) * 129], Wres[:, kc, CS + c * 128:CS + (c + 1) * 128], hcur[:, kc, 0:129],
                           start=(kc == 0), stop=(kc == 7), r=[bh, bWres], w=[bps_], inc=(kc == 7))
                for cc in range(ncs):
                    c = c3 + cc
                    if c < 4:
                        dst, bd = rT[:, c, :], brT
                    elif c < 8:
                        dst, bd = kTr[:, c - 4, :], bkTr
                    elif c < 12:
                        dst, bd = vT[:, c - 8, :], bvT
                    else:
                        dst, bd = wdad[:, :], bwdad
                    ACT(dst, ps_[:, cc * 129 + 1:cc * 129 + 129], AF.Identity, [bps_, bmu], [bd], scale=mucol[:, 1, c:c + 1])
                    STT(dst, ps_[:, cc * 129:cc * 129 + 128], mucol[:, 0, c:c + 1], dst, ALU.mult, ALU.add, [bps_, bmu, bd], [bd])
            dump(f"rT_{T}", rT[:], [brT])
            dump(f"wdad_{T}", wdad[:], [bwdad])
            ACT(wdadb[0:64, :], wdad[0:64, :], AF.Tanh, [bwdad], [bwdadb])
            CP("pool", wdadb[64:128, :], wdad[64:128, :], [bwdad], [bwdadb])
            pz, bpz = bank("rw")
            pa, bpa = bank("rw")
            for c in range(4):
                MM(pz[:, c * 128:(c + 1) * 128], Lora[0:64, c * 128:(c + 1) * 128], wdadb[0:64, :], start=True, stop=False, r=[bLora, bwdadb], w=[bpz], inc=False)
                MM(pz[:, c * 128:(c + 1) * 128], w0a0[0:64, c * 128:(c + 1) * 128], ones_f[0:64, :], start=False, stop=True, r=[bw0a0, bones], w=[bpz], inc=(c == 3))
            for c in range(4):
                MM(pa[:, c * 128:(c + 1) * 128], Lora[64:128, c * 128:(c + 1) * 128], wdadb[64:128, :], start=True, stop=False, r=[bLora, bwdadb], w=[bpa], inc=False)
                MM(pa[:, c * 128:(c + 1) * 128], w0a0[64:128, c * 128:(c + 1) * 128], ones_f[64:128, :], start=False, stop=True, r=[bw0a0, bones], w=[bpa], inc=(c == 3))
            f4 = lambda t: t[:].rearrange("p c t -> p (c t)")
            ACT(f4(lwT), pz[:, :], AF.Tanh, [bpz], [blw], scale=0.5)
            cexp = math.exp(-0.5) * 0.5
            TS("dve", f4(lwT), f4(lwT), -cexp, -cexp, ALU.mult, ALU.add, [blw], [blw])
            ACT(f4(asg), pa[:, :], AF.Tanh, [bpa], [basg], scale=0.5)
            TS("pool", f4(asg), f4(asg), 0.5, 0.5, ALU.mult, ALU.add, [basg], [basg])
            for c in range(4):
                SCAN(LT[:, c, :], ones_f[:, :], lwT[:, c, :], [bones, blw], [bLT])
            TT("pool", f4(tmpA), f4(LT), f4(lwT), ALU.subtract, [bLT, blw], [btmpA])
            ACT(f4(e1), f4(tmpA), AF.Exp, [btmpA], [be1])
            ACT(f4(e2), f4(LT), AF.Exp, [bLT], [be2], scale=-1.0)
            ACT(f4(e3), f4(LT), AF.Exp, [bLT], [be3])
            ACT(eLC[:, :], LT[:, :, 127], AF.Exp, [bLT], [beLC])
            for c in range(4):
                TS("dve", kkn[:, c, :], kTr[:, c, :], vec4[:, 0, c:c + 1], None, ALU.mult, None, [bkTr, bvec4], [bkkn])
            TT("pool", f4(tmpA), f4(kkn), f4(kkn), ALU.mult, [bkkn], [btmpA])
            pk_, bpk_ = bank("rw")
            for c in range(4):
                MM(pk_[:, c * 2:c * 2 + 2], tmpA[:, c, :], ind2[:, :], r=[btmpA, bind2], w=[bpk_], inc=(c == 3))
            TS("dve", rn8[:, :], pk_[:, 0:8], 1e-24, None, ALU.max, None, [bpk_], [brn8])
            TT("pool", rn8[:, :], rn8[:, :], cneg[:, 0:8], ALU.pow, [brn8, bcneg], [brn8])
            TR(pk_[0:8, 128:256], rn8[:, :], identf[:], [brn8, bidf], [bpk_])
            CP("dve", rn8T[:, :], pk_[0:8, 128:256], [bpk_], [brn8T])
            pr_, bpr_ = bank("rw")
            for c in range(4):
                MM(pr_[:, c * 128:(c + 1) * 128], indT[:, c, :], rn8T[:, :], r=[bindT, brn8T], w=[bpr_], inc=(c == 3))
            TT("dve", f4(kkn), f4(kkn), pr_[:, :], ALU.mult, [bkkn, bpr_], [bkkn])
            dump(f"kkn_{T}", kkn[:], [bkkn])
            for c in range(4):
                TS("pool", tmpA[:, c, :], asg[:, c, :], -1.0, vec4[:, 1, c:c + 1], ALU.add, ALU.mult, [basg, bvec4], [btmpA])
            STT(f4(kmod), f4(tmpA), 1.0, f4(kTr), ALU.add, ALU.mult, [btmpA, bkTr], [bkmod])
            dump(f"kmod_{T}", kmod[:], [bkmod])
            for c in range(4):
                STT(tmpA[:, c, :], rT[:, c, :], vec4[:, 2, c:c + 1], kmod[:, c, :], ALU.mult, ALU.mult, [brT, bvec4, bkmod], [btmpA])
            for c in range(4):
                MM(pk_[:, 256 + c * 2:256 + c * 2 + 2], tmpA[:, c, :], ind2[:, :], r=[btmpA, bind2], w=[bpk_], inc=(c == 3))
            CP("dve", sbon[:, :], pk_[:, 256:264], [bpk_], [bsbon])
            STT(f4(At), f4(kkn), -1.0, f4(e1), ALU.mult, ALU.mult, [bkkn, be1], [bAt])
            TT("pool", f4(tmpA), f4(kkn), f4(asg), ALU.mult, [bkkn, basg], [btmpA])
            TT("dve", f4(tmpA), f4(tmpA), f4(e2), ALU.mult, [btmpA, be2], [btmpA])
            CP("pool", f4(Bt), f4(tmpA), [btmpA], [bBt])
            TT("pool", f4(e1), f4(kmod), f4(e2), ALU.mult, [bkmod, be2], [be1])
            CP("pool", f4(Kt), f4(e1), [be1], [bKt])
            TT("dve", f4(Rt), f4(rT), f4(e3), ALU.mult, [brT, be3], [bRt])
            for c in range(4):
                TS("dve", tmpA[:, c, :], tmpA[:, c, :], eLC[:, c:c + 1], None, ALU.mult, None, [btmpA, beLC], [btmpA])
                TS("pool", e1[:, c, :], e1[:, c, :], eLC[:, c:c + 1], None, ALU.mult, None, [be1, beLC], [be1])
            for (src, bsrc, dst, bdst) in ((tmpA, btmpA, Bh, bBh), (e1, be1, Kh, bKh), (vT, bvT, Vt, bVt)):
                pp, bp = bank("rw")
                for c in range(4):
                    TR(pp[:, c * 128:(c + 1) * 128], src[:, c, :], identf[:], [bsrc, bidf], [bp], inc=(c == 3))
                CP("act", dst[:, :], pp[:, :], [bp], [bdst])
            chk(4)
            def hs(t, h):
                return t[(h % 2) * 64:(h % 2) * 64 + 64, h // 2, :]

            def five(lt, blt, rt, brt, mask_i, dst, bdst):
                for par in range(2):
                    pp, bp = bank("rw")
                    for hh in range(4):
                        h = hh * 2 + par
                        MM(pp[:, hh * 128:(hh + 1) * 128], hs(lt, h), hs(rt, h), r=[blt, brt], w=[bp], inc=(hh == 3))
                    TT("dve", dst[:, par:8:2, :], pp[:, :].rearrange("p (h t) -> p h t", h=4), bcast(masks[:, mask_i, :], [128, 4, 128], 1), ALU.mult,
                       [bp, bmasks], [bdst])

            five(Bt, bBt, At, bAt, 1, Qm[0], bQm[0])
            five(At, bAt, Bt, bBt, 3, QmT[0], bQmT[0])
            five(Kt, bKt, At, bAt, 1, AakT, bAak)
            five(Bt, bBt, Rt, bRt, 2, MrbT, bMrb)
            five(Kt, bKt, Rt, bRt, 2, MrkT, bMrk)
            TT("pool", Xm[0][:], Qm[0][:], bcast(identb[:], [128, 8, 128], 1), ALU.add, [bQm[0], bidb], [bXm[0]])
            cur = 0
            for lvl in range(1, 7):
                nxt = cur ^ 1
                lastl = lvl == 6
                for half in range(2):
                    hsl = slice(half * 4, half * 4 + 4)
                    pT_, bpT_ = bank("rw")
                    for hh in range(4):
                        h = half * 4 + hh
                        MM(pT_[:, hh * 128:(hh + 1) * 128], Qm[cur][:, h, :], QmT[cur][:, h, :], r=[bQm[cur], bQmT[cur]], w=[bpT_], inc=(hh == 3))
                    CP("act", QmT[nxt][:, hsl, :], pT_[:, :].rearrange("p (h t) -> p h t", h=4), [bpT_], [bQmT[nxt]])
                    if not lastl:
                        pQ_, bpQ_ = bank("rw")
                        for hh in range(4):
                            h = half * 4 + hh
                            MM(pQ_[:, hh * 128:(hh + 1) * 128], QmT[cur][:, h, :], Qm[cur][:, h, :], r=[bQm[cur], bQmT[cur]], w=[bpQ_], inc=(hh == 3))
                        CP("dve", Qm[nxt][:, hsl, :], pQ_[:, :].rearrange("p (h t) -> p h t", h=4), [bpQ_], [bQm[nxt]])
                for half in range(2):
                    hsl = slice(half * 4, half * 4 + 4)
                    pX_, bpX_ = bank("rw")
                    for hh in range(4):
                        h = half * 4 + hh
                        MM(pX_[:, hh * 128:(hh + 1) * 128], QmT[nxt][:, h, :], Xm[cur][:, h, :], r=[bQmT[nxt], bXm[cur]], w=[bpX_], inc=(hh == 3))
                    TT("dve", Xm[nxt][:, hsl, :], pX_[:, :].rearrange("p (h t) -> p h t", h=4), Xm[cur][:, hsl, :], ALU.add, [bpX_, bXm[cur]], [bXm[nxt]])
                cur = nxt
            Xf = Xm[cur]; bXf = bXm[cur]
            chk(5)
            if i == 0:
                MSET("pool", Pst[:], 0.0, [bPst])
                MSET("pool", Pb[:], 0.0, [bPb])

            def ph_(t, h):
                return t[(h % 2) * 64:(h % 2) * 64 + 64, h // 2, :]

            def vh(t, h):
                return t[:, h * 64:(h + 1) * 64]

            for par in range(2):
                p1, bp1 = bank("rw")
                for hh in range(4):
                    h = hh * 2 + par
                    MM(p1[:, hh * 64:(hh + 1) * 64], hs(At, h), ph_(Pb, h), start=True, stop=False, r=[bAt, bPb], w=[bp1], inc=False)
                    MM(p1[:, hh * 64:(hh + 1) * 64], AakT[:, h, :], vh(Vt, h), start=False, stop=True, r=[bAak, bVt], w=[bp1], inc=(hh == 3))
                CP("act", rhs0[:, par:8:2, :], p1[:, 0:256].rearrange("p (h v) -> p h v", h=4), [bp1], [brhs0])
            chk(5.2)
            p2, bp2 = bank("rw")
            for h in range(8):
                MM(p2[:, h * 64:(h + 1) * 64], Xf[:, h, :], rhs0[:, h, :], r=[bXf, brhs0], w=[bp2], inc=(h == 7))
            CP("act", Ub[:].rearrange("p h v -> p (h v)"), p2[:, :], [bp2], [bUb])
            chk(5.4)
            p3s = []
            for par in range(2):
                p3, bp3 = bank("sc")
                p3s.append((p3, bp3))
                for hh in range(4):
                    h = hh * 2 + par
                    MM(p3[:, hh * 64:(hh + 1) * 64], hs(Rt, h), ph_(Pb, h), start=True, stop=False, r=[bRt, bPb], w=[bp3], inc=False)
                    MM(p3[:, hh * 64:(hh + 1) * 64], MrkT[:, h, :], vh(Vt, h), start=False, stop=False, r=[bMrk, bVt], w=[bp3], inc=False)
                    MM(p3[:, hh * 64:(hh + 1) * 64], MrbT[:, h, :], Ub[:, h, :], start=False, stop=True, r=[bMrb, bUb], w=[bp3], inc=(hh == 3))
            chk(5.6)
            p4, bp4 = bank("rw")
            for h in range(8):
                o_ = p4[(h % 2) * 64:(h % 2) * 64 + 64, (h // 2) * 64:(h // 2) * 64 + 64]
                MM(o_, vh(Bh, h), Ub[:, h, :], start=True, stop=False, r=[bBh, bUb], w=[bp4], inc=False)
                MM(o_, vh(Kh, h), vh(Vt, h), start=False, stop=True, r=[bKh, bVt], w=[bp4], inc=(h == 7))
            for c in range(4):
                STT(Pst[:, c, :], Pst[:, c, :], eLC[:, c:c + 1], p4[:, c * 64:(c + 1) * 64], ALU.mult, ALU.add, [bPst, beLC, bp4], [bPst])
            CP("pool", Pb[:], Pst[:], [bPst], [bPb])
            chk(5.8)
            yn3 = yn[:, :].rearrange("p (h d) -> p h d", h=8)
            for par in range(2):
                p3, bp3 = p3s[par]
                CP("act", yn3[:, par:8:2, :], p3[:, 0:256].rearrange("p (h d) -> p h d", h=4), [bp3], [byn])
            ACT(sq[:, 0:512], yn[:, :], AF.Square, [byn], [bsq])
            REDUCE(lnst[:, 0:8], yn3, [byn], [blnst])
            REDUCE(lnst[:, 8:16], sq[:, 0:512].rearrange("p (h d) -> p h d", h=8), [bsq], [blnst])
            chk(5.85)
            TS("dve", lnst[:, 0:16], lnst[:, 0:16], 1.0 / 64, None, ALU.mult, None, [blnst], [blnst])
            TT("dve", lnst[:, 16:24], lnst[:, 0:8], lnst[:, 0:8], ALU.mult, [blnst], [blnst])
            TT("dve", lnst[:, 16:24], lnst[:, 8:16], lnst[:, 16:24], ALU.subtract, [blnst], [blnst])
            TS("dve", lnst[:, 16:24], lnst[:, 16:24], 0.0, 64e-5, ALU.max, ALU.add, [blnst], [blnst])
            TT("pool", lnst[:, 24:32], lnst[:, 16:24], cneg[:, 0:8], ALU.pow, [blnst, bcneg], [blnst])
            STT(lnst[:, 16:24], lnst[:, 0:8], -1.0, lnst[:, 24:32], ALU.mult, ALU.mult, [blnst], [blnst])
            chk(5.9)
            for h in range(8):
                ACT(yn[:, h * 64:(h + 1) * 64], yn[:, h * 64:(h + 1) * 64], AF.Identity, [byn, blnst], [byn],
                    bias=lnst[:, 16 + h:17 + h], scale=lnst[:, 24 + h:25 + h])
            chk(5.95)
            TT("pool", yn[:, :], yn[:, :], ln_w_bc[:, :], ALU.mult, [byn, blnw], [byn])
            chk(5.97)
            for h in range(8):
                STT(yn[:, h * 64:(h + 1) * 64], Vt[:, h * 64:(h + 1) * 64], sbon[:, h:h + 1], yn[:, h * 64:(h + 1) * 64], ALU.mult, ALU.add,
                    [bVt, bsbon, byn], [byn])

            chk(6)
            def wsload(c):
                k = ws_i[0] % NWS
                ws_i[0] += 1
                DMA(WS[k][:].rearrange("p k n -> p (k n)"), wrest_s[c], sem_ws[k], w=[bWS[k]])
                return WS[k], bWS[k]

            def rest_chunk(c):
                W_, bW_ = wsload(c)
                pp, bp = bank("proj")
                for sub in range(2):
                    for kc in range(8):
                        MM(pp[:, sub * 128:(sub + 1) * 128], W_[:, kc, sub * 128:(sub + 1) * 128], hcur[:, kc, 1:129], start=(kc == 0), stop=(kc == 7),
                           r=[bW_, bh], w=[bp], inc=(kc == 7 and sub == 1))
                return pp, bp

            for c in range(12):
                pp, bp = rest_chunk(c)
                if c < 4:
                    dst, bd = (silA, bsilA) if c < 2 else (silB, bsilB)
                    dv = dst[:, (c % 2) * 2:(c % 2) * 2 + 2, :].rearrange("p a t -> p (a t)")
                    ACT(dv, pp[:, 0:256], AF.Tanh, [bp], [bd], scale=0.5)
                    STT(dv, dv, 1.0, pp[:, 0:256], ALU.add, ALU.mult, [bd, bp], [bd])
                else:
                    dst, bd = (thA, bthA) if c < 8 else (thB, bthB)
                    cc = (c - 4) % 4
                    ACT(dst[:, cc * 2:cc * 2 + 2, :].rearrange("p a t -> p (a t)"), pp[:, 0:256], AF.Tanh, [bp], [bd], scale=0.5)
            for (src, bsrc, sil, bsil, dst, bdst, lnb) in ((ynsa, bynsa, silA, bsilA, yaT, byaT, False), (yn, byn, silB, bsilB, ybT, bybT, True)):
                pp, bp = bank("proj")
                for c in range(4):
                    TR(pp[:, c * 128:(c + 1) * 128], src[:, c * 128:(c + 1) * 128], identf[:], [bsrc, bidf], [bp], inc=(c == 3))
                if not lnb:
                    STT(dst[:].rearrange("p c t -> p (c t)"), pp[:, :], 0.5, sil[:].rearrange("p c t -> p (c t)"), ALU.mult, ALU.mult, [bp, bsil], [bdst])
                else:
                    for c in range(4):
                        STT(tmpA[:, c, :], pp[:, c * 128:(c + 1) * 128], vec4[:, 3, c:c + 1], sil[:, c, :], ALU.add, ALU.mult, [bp, bvec4, bsil], [btmpA])
                    TS("pool", dst[:].rearrange("p c t -> p (c t)"), f4(tmpA), 0.5, None, ALU.mult, None, [btmpA], [bdst])
            dump(f"yaT_{T}", yaT[:], [byaT], BF16)
            dump(f"ybT_{T}", ybT[:], [bybT], BF16)
            for (yT_, byT_, W_, bW_, th, bth, mg, bmg) in ((yaT, byaT, Wouta, bWouta, thA, bthA, mg1, bmg1), (ybT, bybT, Woutb, bWoutb, thB, bthB, mg2, bmg2)):
                for half in range(2):
                    pp, bp = bank("proj")
                    for mm_ in range(4):
                        mc = half * 4 + mm_
                        for kc in range(4):
                            MM(pp[:, mm_ * 128:(mm_ + 1) * 128], W_[:, kc, mc * 128:(mc + 1) * 128], yT_[:, kc, :], start=(kc == 0), stop=(kc == 3),
                               r=[bW_, byT_], w=[bp], inc=(kc == 3 and mm_ == 3))
                    STT(mg[:, half * 4:(half + 1) * 4, :].rearrange("p a t -> p (a t)"), th[:, half * 4:(half + 1) * 4, :].rearrange("p a t -> p (a t)"), 1.0, pp[:, :],
                        ALU.add, ALU.mult, [bth, bp], [bmg])
            TT("pool", mgT[:].rearrange("p a t -> p (a t)"), mg1[:].rearrange("p a t -> p (a t)"), mg2[:].rearrange("p a t -> p (a t)"), ALU.add, [bmg1, bmg2], [bmgT])
            dump(f"mgT_{T}", mgT[:], [bmgT], BF16)
            if s == 1 and i == 0:
                DMA(Wog[:].rearrange("p k n -> p (k n)"), wog_s, sem_wog, w=[bWog])
            for half in range(2):
                pp, bp = bank("proj")
                for kc in range(8):
                    MM(pp[:, :], mgT[:, kc, :], Wog[:, kc, half * 512:(half + 1) * 512], start=(kc == 0), stop=(kc == 7), r=[bmgT, bWog], w=[bp], inc=(kc == 7))
                TT("dve", xs[:, half * 512:(half + 1) * 512], pp[:, :], x_t[:, half * 512:(half + 1) * 512], ALU.add, [bp, bx], [bxs])
            return DMA(out_d[tok0:tok0 + 128, :], xs[:, :], sem_out, r=[bxs], w=[])

        out_toks = []
        total = nseq * ntile
        seq_tiles = [(s, i) for s in range(nseq) for i in range(ntile)]
        DMA(xt[0][:], x_d[0:128, :], sem_x[0], w=[bxt[0]])
        for n_, (s, i) in enumerate(seq_tiles):
            T = s * 16 + i
            if n_ + 1 < total:
                s2, i2 = seq_tiles[n_ + 1]
                T2 = s2 * 16 + i2
                DMA(xt[T2 % 2][:], x_d[T2 * 128:(T2 + 1) * 128, :], sem_x[T2 % 2], w=[bxt[T2 % 2]])
            if i == 0:
                MSET("pool", kcT[:], 0.0, [bkcT])
                MSET("pool", vcT[:], 0.0, [bvcT])
                MSET("pool", vca[:, :, 0:64], 0.0, [bvca])
                MSET("pool", kvc[:], 0.0, [bkvc])
            out_toks.append(tile_body(s, i))
        S.wait_all("sp", out_toks[-4:] + dbg_outs + [(sem_out, S.dcnt[sem_out])])
        S.emit()
    return nc


_CACHE = {}


def kernel(**inputs):
    sh, per = host_prep(inputs)
    if "nc" not in _CACHE:
        _CACHE["nc"] = build()
    nc = _CACHE["nc"]
    in_maps = []
    for core in range(8):
        d = dict(sh)
        d.update(per[core])
        in_maps.append(d)
    res = run_bass_kernel_spmd(nc, in_maps, core_ids=list(range(8)))
    out = np.concatenate([np.asarray(r["out"]).reshape(2, 2048, 1024) for r in res.results], axis=0)
    return out.astype(np.float32)
```

```python
import math
import numpy as np
import concourse.bass as bass
import concourse.mybir as mybir
from concourse.bass_utils import run_bass_kernel_spmd
from contextlib import ExitStack

F32 = mybir.dt.float32
BF16 = mybir.dt.bfloat16
AF = mybir.ActivationFunctionType
ALU = mybir.AluOpType
AX = mybir.AxisListType

COMPUTE = ("pe", "act", "dve", "pool")
NEGM = -4096.0
NRES = 2968
CQ, CKV, CG, CC, CS = 0, 512, 1024, 1048, 1304


class Buf:
    __slots__ = ("w", "r")

    def __init__(self):
        self.w = None
        self.r = {}


class Sched:
    def __init__(self, nc, es):
        self.nc = nc
        self.es = es
        self.prog = {e: [] for e in COMPUTE + ("sp",)}
        self.cnt = {e: 0 for e in COMPUTE}
        self.sems = {}
        for e in COMPUTE:
            self.sems[e] = es.enter_context(nc.semaphore("sem_" + e))
        self.known = {e: {} for e in self.prog}
        self.snap = {}
        self.dcnt = {}
        self.pending = {e: False for e in COMPUTE}
        self.last = {}

    def dma_sem(self, name):
        self.sems[name] = self.es.enter_context(self.nc.semaphore("sem_" + name))
        self.dcnt[name] = 0
        return name

    @staticmethod
    def _flat(bs):
        out = []
        for b in bs:
            if isinstance(b, (list, tuple)):
                out.extend(Sched._flat(b))
            else:
                out.append(b)
        return out

    def op(self, eng, fn, reads=(), writes=(), inc=True, dsem=None):
        reads = self._flat(reads)
        writes = self._flat(writes)
        need = {}

        def req(tok, same_ok):
            if tok is None:
                return
            k, v = tok
            if same_ok and k == eng and eng == "pe":
                return
            if need.get(k, 0) < v:
                need[k] = v

        for b in reads:
            req(b.w, False)
        for b in writes:
            req(b.w, True)
            for k, v in b.r.items():
                req((k, v), True)
        kn = self.known[eng]
        waits = []
        for k, v in need.items():
            if kn.get(k, 0) < v:
                waits.append((k, v))
                kn[k] = v
                sn = self.snap.get((k, v))
                if sn is not None:
                    for k2, v2 in sn.items():
                        if kn.get(k2, 0) < v2:
                            kn[k2] = v2
        if dsem is not None:
            self.dcnt[dsem] += 16
            tok = (dsem, self.dcnt[dsem])
            incspec = (dsem, 16)
        elif inc:
            self.cnt[eng] += 1
            tok = (eng, self.cnt[eng])
            incspec = (eng, 1)
            self.pending[eng] = False
            self.snap[tok] = dict(kn)
        else:
            tok = (eng, self.cnt[eng] + 1)
            incspec = None
            self.pending[eng] = True
        self.last[tok[0]] = tok[1]
        for b in writes:
            b.w = tok
            b.r = {}
        for b in reads:
            if b.w is tok:
                continue
            if b.r.get(tok[0], 0) < tok[1]:
                b.r[tok[0]] = tok[1]
        self.prog[eng].append((waits, fn, incspec))
        return tok

    def wait_all(self, eng, toks):
        kn = self.known[eng]
        waits = []
        mx = {}
        for k, v in toks:
            if mx.get(k, 0) < v:
                mx[k] = v
        for k, v in mx.items():
            if kn.get(k, 0) < v:
                waits.append((k, v))
                kn[k] = v
        self.prog[eng].append((waits, None, None))

    def barrier(self):
        for e in COMPUTE:
            if self.pending[e]:
                self.op(e, lambda en: en.nop(), (), ())
        toks = list(self.last.items())
        for e in self.prog:
            self.wait_all(e, toks)

    def emit(self):
        nc = self.nc
        for e in COMPUTE:
            if self.pending[e]:
                self.op(e, lambda en: en.nop(), (), ())
        sems = self.sems
        prog = self.prog

        def run(engname):
            def f(e):
                for waits, fn, incspec in prog[engname]:
                    for k, v in waits:
                        e.wait_ge(sems[k], v)
                    if fn is None:
                        continue
                    ins = fn(e)
                    if incspec is not None:
                        ins.then_inc(sems[incspec[0]], incspec[1])
            return f

        with nc.Block() as block:
            block.sync(run("sp"))
            block.tensor(run("pe"))
            block.scalar(run("act"))
            block.vector(run("dve"))
            block.gpsimd(run("pool"))


def _t5_bucket(dist):
    n = np.maximum(dist, 0)
    nf = np.maximum(n, 16).astype(np.float32)
    large = 16 + (np.log(nf / np.float32(16)) / np.float32(math.log(128 / 16)) * np.float32(16)).astype(np.int32)
    return np.where(n < 16, n, np.minimum(large, 31))


def _perms():
    r = lambda a, b: list(range(a, b))
    res = (r(0, 512)
           + r(768, 832) + r(1024, 1088) + r(832, 896) + r(1088, 1152) + r(896, 1024) + r(1152, 1280)
           + r(1280, 1304)
           + r(512, 576) + r(640, 704) + r(576, 640) + r(704, 768)
           + r(1816, 3480))
    rest = r(1304, 1816) + r(3480, 3992) + r(3992, 5016) + r(5016, 6040)
    assert len(res) == NRES and len(rest) == 3072
    return np.array(res), np.array(rest)


def host_prep(inp):
    f = lambda k: np.ascontiguousarray(np.asarray(inp[k], dtype=np.float32))
    sh = {}
    pres, prest = _perms()
    w_in = f("w_in")[0]
    sh["w_res"] = np.ascontiguousarray(w_in[:, pres])
    sh["w_rest"] = np.ascontiguousarray(w_in[:, prest])
    sh["w_ada"] = f("w_ada")[0]
    sh["w_out_a"] = f("w_out_a")[0]
    sh["w_out_b"] = f("w_out_b")[0]
    sh["w_o"] = f("w_o")[0]
    sh["w1k"] = f("cmp_k_w1")[0]
    sh["w1v"] = f("cmp_v_w1")[0]
    col = lambda v, n: np.ascontiguousarray(v.reshape(n, 128).T)
    sh["b_ada"] = col(f("b_ada")[0], 24)
    sh["g_norm"] = col(f("norm_gain")[0], 8)
    sh["mu"] = col(f("shift_mu")[0], 13)
    vec4 = np.stack([col(f(k)[0].reshape(-1), 4) for k in ("k_k", "k_a", "r_k", "ln_x_b")], 1)
    sh["vec4"] = np.ascontiguousarray(vec4)
    rep = lambda v: np.ascontiguousarray(np.broadcast_to(v[None, :], (128, v.shape[0])))
    kng = f("k_norm_gain")[0]
    sh["bc_small"] = np.concatenate([rep(f("q_norm_gain")[0]), rep(kng[1]), rep(kng[2])], 1)
    sh["ln_w_bc"] = rep(f("ln_x_w")[0])
    sh["kgc"] = np.ascontiguousarray(kng[0].reshape(64, 1))
    sh["w0a0"] = np.ascontiguousarray(np.stack([f("w0")[0], f("a0")[0]], 0))
    sh["lora"] = np.ascontiguousarray(np.concatenate([f("w_lora_up")[0], f("a_lora_up")[0]], 0))
    w2 = lambda k: f(k)[0].reshape(2, 128, 64).transpose(1, 0, 2)
    sh["w2"] = np.ascontiguousarray(np.stack([w2("cmp_k_w2"), w2("cmp_v_w2")], 1))
    sh["peT"] = np.ascontiguousarray(np.concatenate([f("cmp_pos_k")[0].T, f("cmp_pos_v")[0].T], 0))
    tbl = f("rel_bias")
    k = np.arange(128)[:, None]
    q = np.arange(128)[None, :]
    tb = np.zeros((2, 2, 128, 4, 128), np.float32)
    for v, dist in enumerate((q - k, 128 + q - k)):
        bk = _t5_bucket(dist)
        for g in range(2):
            for h in range(4):
                tb[v, g, :, h, :] = tbl[bk, g * 4 + h]
    sh["tblDS"] = tb.reshape(2, 2, 128, 512)
    mk = np.zeros((128, 4, 128), np.float32)
    mk[np.broadcast_to(((q - k) < 0)[:, None, :], mk.shape)] = NEGM
    sh["maskD"] = mk.reshape(128, 512)
    c31 = np.zeros((2, 128, 4, 128), np.float32)
    for g in range(2):
        for h in range(4):
            c31[g, :, h, :] = tbl[31, g * 4 + h]
    sh["c31"] = c31.reshape(2, 128, 512)
    p = np.arange(16)[:, None]
    distc = q - 16 * p + 113
    bkc = _t5_bucket(distc)
    tc = np.zeros((2, 16, 4, 128), np.float32)
    for g in range(2):
        for h in range(4):
            tc[g, :, h, :] = tbl[bkc, g * 4 + h]
    sh["tblC"] = tc.reshape(2, 16, 512)
    mc = np.zeros((16, 4, 128), np.float32)
    mc[np.broadcast_to((distc < 0)[:, None, :], mc.shape)] = NEGM
    sh["maskC"] = mc.reshape(16, 512)
    sh["ident"] = np.eye(128, dtype=np.float32)
    far = np.where(k <= q, NEGM, 0.0).astype(np.float32)
    mus = (k < q).astype(np.float32)
    mui = (k <= q).astype(np.float32)
    mls = (k > q).astype(np.float32)
    sh["masks"] = np.ascontiguousarray(np.stack([far, mus, mui, mls], 1))
    z = np.zeros((16, 256), np.float32)
    z[np.arange(16), np.arange(16) + 119] = 1.0
    sh["zsh"] = z
    e = np.zeros((32, 2048), np.float32)
    e[np.arange(2048) // 64, np.arange(2048)] = -NEGM
    sh["emat"] = e
    mi = np.zeros((128, 32), np.float32)
    for j in range(32):
        for a in range(4):
            for b in range(2):
                n = 4 * j + a - b
                if 0 <= n < 127:
                    mi[n, j] += 1.0
    sh["mimp"] = mi
    ka = np.zeros((128, 8, 2, 32), np.float32)
    for i in range(8, 16):
        for qq in range(128):
            cur = (128 * i + qq) // 64
            for j in range(32):
                forced = (j == 0) or (j == cur) or (j == cur - 1)
                causal = j <= cur
                if forced:
                    ka[qq, i - 8, 0, j] = 0.0
                    ka[qq, i - 8, 1, j] = 1e30
                elif causal:
                    ka[qq, i - 8, 0, j] = 1.0
                else:
                    ka[qq, i - 8, 1, j] = -1e30
    sh["keepadd"] = ka.reshape(128, 512)
    ind2 = np.zeros((128, 2), np.float32)
    ind2[:64, 0] = 1.0
    ind2[64:, 1] = 1.0
    sh["ind2"] = ind2
    indT = np.zeros((8, 4, 128), np.float32)
    for h in range(8):
        indT[h, h // 2, (h % 2) * 64:(h % 2) * 64 + 64] = 1.0
    sh["indT"] = indT.reshape(8, 512)
    x = f("x")
    c = f("c")
    per = []
    for core in range(8):
        d = {"x": np.ascontiguousarray(x[2 * core:2 * core + 2].reshape(4096, 1024)),
             "cT": np.ascontiguousarray(c[2 * core:2 * core + 2].reshape(2, 8, 128).transpose(2, 1, 0))}
        per.append(d)
    return sh, per


class _Stop(Exception):
    pass


def build(nseq=2, ntile=16, dbg=None, stage=9):
    nc = bass.Bass("TRN2", target_bir_lowering=False)
    dbg = dbg or {}
    di = lambda name, shape: nc.dram_tensor(name, shape, F32, kind="ExternalInput").ap()
    x_d = di("x", [4096, 1024])
    cT_d = di("cT", [128, 8, 2])
    w_res_d = di("w_res", [1024, NRES])
    w_rest_d = di("w_rest", [1024, 3072])
    w_ada_d = di("w_ada", [1024, 3072])
    w_out_a_d = di("w_out_a", [512, 1024])
    w_out_b_d = di("w_out_b", [512, 1024])
    w_o_d = di("w_o", [1024, 1024])
    w1k_d = di("w1k", [2048, 256])
    w1v_d = di("w1v", [2048, 256])
    b_ada_d = di("b_ada", [128, 24])
    g_norm_d = di("g_norm", [128, 8])
    mu_d = di("mu", [128, 13])
    vec4_d = di("vec4", [128, 4, 4])
    bc_small_d = di("bc_small", [128, 192])
    ln_w_bc_d = di("ln_w_bc", [128, 512])
    kgc_d = di("kgc", [64, 1])
    w0a0_d = di("w0a0", [2, 512])
    lora_d = di("lora", [128, 512])
    w2_d = di("w2", [128, 2, 2, 64])
    peT_d = di("peT", [128, 32])
    tblDS_d = di("tblDS", [2, 2, 128, 512])
    maskD_d = di("maskD", [128, 512])
    c31_d = di("c31", [2, 128, 512])
    tblC_d = di("tblC", [2, 16, 512])
    maskC_d = di("maskC", [16, 512])
    ident_d = di("ident", [128, 128])
    masks_d = di("masks", [128, 4, 128])
    zsh_d = di("zsh", [16, 256])
    emat_d = di("emat", [32, 2048])
    mimp_d = di("mimp", [128, 32])
    keepadd_d = di("keepadd", [128, 512])
    ind2_d = di("ind2", [128, 2])
    indT_d = di("indT", [8, 512])
    out_d = nc.dram_tensor("out", [4096, 1024], F32, kind="ExternalOutput").ap()
    wrest_s = nc.dram_tensor("wrest_s", [12, 128, 2048], BF16, kind="Internal").ap()
    wog_s = nc.dram_tensor("wog_s", [128, 8192], BF16, kind="Internal").ap()

    with ExitStack() as es:
        S = Sched(nc, es)
        _n = [0]

        def sb(shape, dt, name=None):
            _n[0] += 1
            return es.enter_context(nc.sbuf_tensor("s_" + (name or f"sb{_n[0]}"), shape, dt))

        def psb(name):
            return es.enter_context(nc.psum_tensor(name, [128, 512], F32))

        dbg_outs = []

        def dump(name, ap, reads, dt=F32):
            if name not in dbg:
                return
            d = nc.dram_tensor("dbg_" + name, list(ap.shape), dt, kind="ExternalOutput").ap()
            dbg_outs.append(S.op("sp", lambda e: e.dma_start(out=d, in_=ap), reads, (), dsem=sem_dbg))

        def MM(out, lhsT, rhs, start=True, stop=True, r=(), w=(), inc=True, sgc=False):
            if sgc:
                return S.op("pe", lambda e: e.matmul(out, lhsT=lhsT, rhs=rhs, start=start, stop=stop, skip_group_check=True), r, w, inc=inc)
            return S.op("pe", lambda e: e.matmul(out, lhsT=lhsT, rhs=rhs, start=start, stop=stop), r, w, inc=inc)

        def TR(out, in_, ident, r=(), w=(), inc=True):
            return S.op("pe", lambda e: e.transpose(out=out, in_=in_, identity=ident), r, w, inc=inc)

        def ACT(out, in_, func, r=(), w=(), bias=None, scale=None, accum=None):
            kw = {}
            if bias is not None:
                kw["bias"] = bias
            if scale is not None:
                kw["scale"] = scale
            if accum is not None:
                kw["accum_out"] = accum
            return S.op("act", lambda e: e.activation(out=out, in_=in_, func=func, **kw), r, w)

        def TS(eng, out, in0, s1, s2, op0, op1=None, r=(), w=()):
            if op1 is None:
                return S.op(eng, lambda e: e.tensor_scalar(out=out, in0=in0, scalar1=s1, scalar2=None, op0=op0), r, w)
            return S.op(eng, lambda e: e.tensor_scalar(out=out, in0=in0, scalar1=s1, scalar2=s2, op0=op0, op1=op1), r, w)

        def TT(eng, out, in0, in1, op, r=(), w=()):
            return S.op(eng, lambda e: e.tensor_tensor(out=out, in0=in0, in1=in1, op=op), r, w)

        def STT(out, in0, scalar, in1, op0, op1, r=(), w=()):
            return S.op("dve", lambda e: e.scalar_tensor_tensor(out=out, in0=in0, scalar=scalar, in1=in1, op0=op0, op1=op1), r, w)

        def CP(eng, out, in_, r=(), w=()):
            if eng == "act":
                return S.op("act", lambda e: e.copy(out=out, in_=in_), r, w)
            return S.op(eng, lambda e: e.tensor_copy(out=out, in_=in_), r, w)

        def MSET(eng, ap, val, w=()):
            return S.op(eng, lambda e: e.memset(ap, val), (), w)

        def DMA(out, in_, sem, r=(), w=(), eng="sp"):
            return S.op(eng, lambda e: e.dma_start(out=out, in_=in_), r, w, dsem=sem)

        def bcast(ap, shape, axis):
            return ap.unsqueeze(axis).to_broadcast(shape)

        sem_dbg = S.dma_sem("dbg")
        sem_stg = [S.dma_sem("stg0"), S.dma_sem("stg1")]
        sem_scr = S.dma_sem("scr")
        sem_x = [S.dma_sem("x0"), S.dma_sem("x1")]
        sem_xr = S.dma_sem("xr")
        sem_ws = [S.dma_sem(f"ws{i}") for i in range(3)]
        sem_outs = [S.dma_sem("out0"), S.dma_sem("out1")]
        sem_wog = S.dma_sem("wog")

        PS = [psb(f"ps{i}") for i in range(8)]
        PSB = [Buf() for _ in range(8)]
        rot = {"proj": [0, 1], "sc": [2, 3], "acc": [4, 5], "rw": [6, 7]}
        rotc = {k: 0 for k in rot}

        def bank(cls):
            i = rot[cls][rotc[cls] % len(rot[cls])]
            rotc[cls] += 1
            return PS[i], PSB[i]

        NSLOT = 41
        AR = sb([128, NSLOT * 256], F32, "arena")
        SLB = [Buf() for _ in range(NSLOT)]

        def slot(start, shape, dt, P0=0):
            el = 4 if dt == F32 else 2
            n = int(np.prod(shape[1:]))
            nsl = (n * el + 1023) // 1024
            assert start + nsl <= NSLOT
            base = AR[:] if dt == F32 else AR[:].bitcast(BF16)
            o = start * 1024 // el
            ap = base[P0:P0 + shape[0], o:o + n]
            if len(shape) > 2:
                names = " ".join(f"d{i}" for i in range(len(shape) - 1))
                kw = {f"d{i}": shape[i + 1] for i in range(len(shape) - 1)}
                ap = ap.rearrange(f"p ({names}) -> p {names}", **kw)
            return ap, SLB[start:start + nsl]

        Wres = sb([128, 8, NRES], BF16, "Wres"); bWres = Buf()
        Wouta = sb([128, 4, 1024], BF16, "Wouta"); bWouta = Buf()
        Woutb = sb([128, 4, 1024], BF16, "Woutb"); bWoutb = Buf()
        Wog = sb([128, 8, 1024], BF16, "Wog"); bWog = Buf()
        W1c = sb([128, 32, 256], BF16, "W1c"); bW1c = Buf()
        W2c = sb([128, 2, 2, 64], BF16, "W2c"); bW2c = Buf()
        Lora = sb([128, 512], BF16, "Lora"); bLora = Buf()
        identf = sb([128, 128], F32, "identf"); bidf = Buf()
        identb = sb([128, 128], BF16, "identb"); bidb = Buf()
        masks = sb([128, 4, 128], BF16, "masks"); bmasks = Buf()
        biasDS = sb([128, 2, 2, 512], BF16, "biasDS"); bbias = Buf()
        emat = sb([64, 2048], BF16, "emat"); bemat = Buf()
        zsh = sb([128, 256], BF16, "zsh"); bzsh = Buf()
        biasC = sb([128, 2, 512], BF16, "biasC"); bbiasC = Buf()
        w0a0 = sb([128, 512], F32, "w0a0"); bw0a0 = Buf()
        bmisc = Buf()
        mimp = sb([128, 32], F32, "mimp"); bmimp = Buf()
        keepadd = sb([128, 8, 2, 32], F32, "keepadd"); bka = Buf()
        ind2 = sb([128, 2], F32, "ind2"); bind2 = Buf()
        indT = sb([8, 4, 128], F32, "indT"); bindT = Buf()
        ones_f = sb([128, 128], F32, "ones_f"); bones = Buf()
        bc_small = sb([128, 192], F32, "bc_small"); bbcs = Buf()
        ln_w_bc = sb([128, 512], F32, "ln_w_bc"); blnw = Buf()
        vec4 = sb([128, 4, 4], F32, "vec4"); bvec4 = Buf()
        mucol = sb([128, 2, 13], F32, "mucol"); bmu = Buf()
        kgc = sb([64, 1], F32, "kgc"); bkgc = Buf()
        gcol = sb([128, 8], F32, "gcol"); bgcol = Buf()
        badaT = sb([128, 24], F32, "badaT"); bbada = Buf()
        cTt = sb([128, 8, 2], F32, "cTt"); bcT = Buf()
        modT = sb([128, 24, 2], F32, "modT"); bmod = Buf()
        gsT = sb([128, 2, 8], F32, "gsT"); bgs = Buf()
        hb2 = sb([128, 2, 2], F32, "hb2"); bhb2 = Buf()
        cneg = sb([128, 16], F32, "cneg"); bcneg = Buf()
        peTb = sb([128, 32], BF16, "peTb"); bpeT = Buf()
        siluc = sb([128, 8, 2], F32, "siluc"); bsc = Buf()
        gtmp = sb([128, 16], F32, "gtmp"); bgtmp = Buf()

        stg = []; bstg = []
        for i_ in range(2):
            a_, b_ = slot(16 * i_, [128, 4096], F32)
            stg.append(a_); bstg.append(b_)
        kT = sb([128, 2, 2048], BF16, "kT"); bkT = [Buf() for _ in range(16)]
        Vcf = sb([128, 4160], BF16, "Vc"); bVc = [Buf() for _ in range(16)]
        Vc = Vcf[:].rearrange("p (a b c d) -> p a b c d", a=16, b=2, c=2)
        stgb = kT[:].rearrange("p a b -> p (a b)"); bstgb = bkT
        gate_bc = Vcf[:].bitcast(F32)[:, 0:2048].rearrange("p (s n) -> p s n", s=2); bgbc = bVc

        ldn = [0]
        sem_lds = [S.dma_sem(f"ld{i}") for i in range(8)]

        def ld(out, in_, w):
            sm = sem_lds[ldn[0] % 8]
            ldn[0] += 1
            if S.dcnt[sm] > 0:
                S.wait_all("sp", [(sm, S.dcnt[sm])])
            return DMA(out, in_, sm, w=w)

        ld(identf[:], ident_d, [bidf])
        CP("dve", identb[:], identf[:], [bidf], [bidb])
        ld(stg[0][:, 0:512].rearrange("p (a b) -> p a b", a=4), masks_d, [bstg[0]])
        CP("dve", masks[:], stg[0][:, 0:512].rearrange("p (a b) -> p a b", a=4), [bstg[0]], [bmasks])
        ld(mimp[:], mimp_d, [bmimp])
        ld(keepadd[:].rearrange("p a b c -> p (a b c)"), keepadd_d, [bka])
        ld(ind2[:], ind2_d, [bind2])
        ld(indT[:].rearrange("p a b -> p (a b)"), indT_d, [bindT])
        ld(bc_small[:], bc_small_d, [bbcs])
        ld(ln_w_bc[:], ln_w_bc_d, [blnw])
        ld(vec4[:], vec4_d, [bvec4])
        ld(mucol[:, 0, :], mu_d, [bmu])
        TS("dve", mucol[:, 1, :], mucol[:, 0, :], -1.0, 1.0, ALU.mult, ALU.add, [bmu], [bmu])
        ld(kgc[:], kgc_d, [bkgc])
        MSET("pool", w0a0[:], 0.0, [bw0a0])
        ld(w0a0[0:1, :], w0a0_d[0:1, :], [bw0a0])
        ld(w0a0[64:65, :], w0a0_d[1:2, :], [bw0a0])
        MSET("pool", emat[:], 0.0, [bemat])
        MSET("pool", zsh[:], 0.0, [bzsh])
        MSET("pool", biasC[:].rearrange("p a b -> p (a b)"), 0.0, [bbiasC])
        ld(gcol[:], g_norm_d, [bgcol])
        ld(badaT[:], b_ada_d, [bbada])
        ld(cTt[:], cT_d, [bcT])
        MSET("pool", ones_f[:], 1.0, [bones])
        MSET("pool", cneg[:], -0.5, [bcneg])
        ld(stg[1][0:16, 0:256], zsh_d, [bstg[1]])
        CP("dve", zsh[0:16, :], stg[1][0:16, 0:256], [bstg[1]], [bzsh])
        ld(stg[1][0:32, 0:2048], emat_d, [bstg[1]])
        CP("dve", emat[0:32, :], stg[1][0:32, 0:2048], [bstg[1]], [bemat])
        ld(stg[1][:, 2048:2560], lora_d, [bstg[1]])
        CP("dve", Lora[:], stg[1][:, 2048:2560], [bstg[1]], [bLora])
        ld(stg[1][:, 2560:2816].rearrange("p (a b c) -> p a b c", a=2, b=2), w2_d, [bstg[1]])
        CP("dve", W2c[:], stg[1][:, 2560:2816].rearrange("p (a b c) -> p a b c", a=2, b=2), [bstg[1]], [bW2c])
        ld(stg[1][:, 2816:2848], peT_d, [bstg[1]])
        CP("dve", peTb[:], stg[1][:, 2816:2848], [bstg[1]], [bpeT])
        for g in range(2):
            ld(stg[0][:, 0:512], c31_d[g], [bstg[0]])
            for v in range(2):
                ld(stg[1][:, 0:512], tblDS_d[v, g], [bstg[1]])
                TT("dve", stg[1][:, 0:512], stg[1][:, 0:512], stg[0][:, 0:512], ALU.subtract, [bstg[0], bstg[1]], [bstg[1]])
                if v == 0:
                    ld(stg[1][:, 512:1024], maskD_d, [bstg[1]])
                    STT(biasDS[:, v, g, :], stg[1][:, 0:512], 8.0, stg[1][:, 512:1024], ALU.mult, ALU.add, [bstg[1]], [bbias])
                else:
                    TS("dve", biasDS[:, v, g, :], stg[1][:, 0:512], 8.0, None, ALU.mult, None, [bstg[1]], [bbias])
            ld(stg[1][0:16, 0:512], tblC_d[g], [bstg[1]])
            ld(stg[1][0:16, 512:1024], maskC_d, [bstg[1]])
            TT("dve", stg[1][0:16, 0:512], stg[1][0:16, 0:512], stg[0][0:16, 0:512], ALU.subtract, [bstg[0], bstg[1]], [bstg[1]])
            STT(biasC[0:16, g, :], stg[1][0:16, 0:512], 8.0, stg[1][0:16, 512:1024], ALU.mult, ALU.add, [bstg[1]], [bbiasC])

        def stage_load(i, src_ap, ncols, nk=8):
            view = stg[i][:, 0:nk * ncols].rearrange("p (k n) -> p k n", k=nk)
            DMA(view, src_ap, sem_stg[i], w=[bstg[i]])
            return view

        si = 0
        for c0 in range(0, NRES, 512):
            n = min(512, NRES - c0)
            v = stage_load(si, w_res_d[:, c0:c0 + n].rearrange("(k p) n -> p k n", p=128), n)
            CP("dve" if si == 0 else "pool", Wres[:, :, c0:c0 + n], v, [bstg[si]], [bWres])
            si ^= 1
        for c in range(6):
            v = stage_load(si, w_rest_d[:, c * 512:(c + 1) * 512].rearrange("(k p) n -> p k n", p=128), 512)
            sv = stgb[:, 0:4096].rearrange("p (k n) -> p k n", k=8)
            CP("dve" if si == 0 else "pool", sv, v, [bstg[si]], [bstgb])
            for j_ in range(2):
                DMA(wrest_s[2 * c + j_].rearrange("p (k n) -> p k n", k=8),
                    stgb[:, 0:4096].rearrange("p (k j n) -> p k j n", k=8, j=2)[:, :, j_, :], sem_scr, r=[bstgb], w=[Buf()])
            si ^= 1
        for (wd_, Wt, bW) in ((w_out_a_d, Wouta, bWouta), (w_out_b_d, Woutb, bWoutb)):
            v = stage_load(si, wd_.rearrange("(k p) n -> p k n", p=128), 1024, nk=4)
            CP("dve" if si == 0 else "pool", Wt[:], v, [bstg[si]], [bW])
            si ^= 1
        for (wd_, lo) in ((w1k_d, 0), (w1v_d, 64)):
            for hh in range(2):
                view = stg[si][lo:lo + 64, 0:4096].rearrange("p (k n) -> p k n", k=16)
                DMA(view, wd_[hh * 1024:(hh + 1) * 1024, :].rearrange("(k p) n -> p k n", p=64), sem_stg[si], w=[bstg[si]])
                CP("dve" if si == 0 else "pool", W1c[lo:lo + 64, hh * 16:(hh + 1) * 16, :], view, [bstg[si]], [bW1c])
                si ^= 1
        ACT(siluc[:], cTt[:], AF.Tanh, [bcT], [bsc], scale=0.5)
        TS("dve", siluc[:], siluc[:], 0.5, 0.5, ALU.mult, ALU.add, [bsc], [bsc])
        TT("dve", siluc[:], siluc[:], cTt[:], ALU.mult, [bsc, bcT], [bsc])
        pm, bpm = bank("proj")
        for c in range(6):
            v = stage_load(si, w_ada_d[:, c * 512:(c + 1) * 512].rearrange("(k p) n -> p k n", p=128), 512)
            for jj in range(4):
                j = c * 4 + jj
                for kc in range(8):
                    MM(pm[:, j * 2:j * 2 + 2], v[:, kc, jj * 128:(jj + 1) * 128], siluc[:, kc, :], start=(kc == 0), stop=(kc == 7),
                       r=[bstg[si], bsc], w=[bpm], inc=(kc == 7))
            si ^= 1
        TT("dve", modT[:], pm[:, 0:48].rearrange("p (j b) -> p j b", b=2), bcast(badaT[:], [128, 24, 2], 2), ALU.add, [bpm, bbada], [bmod])
        for s in range(2):
            STT(gsT[:, s, :], modT[:, 8:16, s], 1.0, gcol[:], ALU.add, ALU.mult, [bmod, bgcol], [bgs])
        CP("dve", gtmp[:].rearrange("p (s j) -> p s j", s=2), modT[:, 16:24, :].rearrange("p j s -> p s j"), [bmod], [bgtmp])
        for q4 in range(4):
            pg, bpg = bank("proj")
            for jq in range(4):
                qq = q4 * 4 + jq
                MM(pg[0:1, jq * 128:(jq + 1) * 128], gtmp[:, qq:qq + 1], identf[:], r=[bgtmp, bidf], w=[bpg], inc=(jq == 3))
            CP("dve", stg[1][0:1, q4 * 512:(q4 + 1) * 512], pg[0:1, 0:512], [bpg], [bstg[1]])
        for s in range(2):
            for hh in range(2):
                pb_, bpb_ = bank("proj")
                MM(pb_[:, :], ones_f[0:1, :], stg[1][0:1, s * 1024 + hh * 512: s * 1024 + hh * 512 + 512], r=[bones, bstg[1]], w=[bpb_])
                TS("dve", gate_bc[:, s, hh * 512:(hh + 1) * 512], pb_[:, :], 0.5, None, ALU.mult, None, [bpb_], [bgbc])
        for s in (1, 0):
            for hh in range(2):
                v = stage_load(0, w_o_d[:, hh * 512:(hh + 1) * 512].rearrange("(k p) n -> p k n", p=128), 512)
                TT("dve", Wog[:, :, hh * 512:(hh + 1) * 512], v, bcast(gate_bc[:, s, hh * 512:(hh + 1) * 512], [128, 8, 512], 1), ALU.mult,
                   [bstg[0], bgbc], [bWog])
            if s == 1:
                DMA(wog_s, Wog[:].rearrange("p k n -> p (k n)"), sem_scr, r=[bWog], w=[Buf()])
        for kv in range(2):
            lo = kv * 64
            ph, bph = bank("proj")
            for jh in range(2):
                for pos in range(32):
                    MM(ph[:, jh:jh + 1], W1c[lo:lo + 64, pos, jh * 128:(jh + 1) * 128], peTb[lo:lo + 64, pos:pos + 1],
                       start=(pos == 0), stop=(pos == 31), r=[bW1c, bpeT], w=[bph], inc=(pos == 31))
            CP("dve", hb2[:, kv, :], ph[:, 0:2], [bph], [bhb2])
        S.barrier()
        print("SBUF remaining before main alloc:", nc.sbuf_bytes_remaining)

        xt = [sb([128, 1024], F32, f"xt{i}") for i in range(2)]; bxt = [Buf(), Buf()]
        hT = [sb([128, 8, 130], BF16, f"hT{i}") for i in range(2)]; bhT = [Buf(), Buf()]
        for i_ in range(2):
            MSET("pool", hT[i_][:].rearrange("p a b -> p (a b)"), 0.0, [bhT[i_]])
        ynsa = sb([128, 512], F32, "ynsa"); bynsa = Buf()
        yn = sb([128, 512], F32, "yn"); byn = Buf()
        st12 = sb([128, 16], F32, "st12"); bst12 = Buf()
        rs12 = sb([128, 16], F32, "rs12"); brs12 = Buf()
        MSET("pool", Vcf[:], 1.0, bVc)
        gsig = sb([128, 3, 8], F32, "gsig"); bgsig = Buf()
        kvc = sb([128, 2, 144], BF16, "kvc"); bkvc = Buf()
        kcT = sb([64, 2, 128], BF16, "kcT"); bkcT = Buf()
        vcT = sb([64, 2, 128], F32, "vcT"); bvcT = Buf()
        vca = sb([128, 2, 65], F32, "vca"); bvca = Buf()
        MSET("pool", vca[:].rearrange("p a b -> p (a b)"), 1.0, [bvca])
        hu = sb([128, 64], F32, "hu"); bhu = Buf()
        hw_ = sb([128, 64], F32, "hw_"); bhw = Buf()
        hid = sb([128, 64], BF16, "hid"); bhid = Buf()
        kcs = sb([64, 48], F32, "kcs"); bkcs = Buf()
        coef = sb([128, 16], F32, "coef"); bcoef = Buf()
        impr = sb([128, 2, 32], F32, "impr"); bimpr = Buf()
        imp2 = sb([128, 32], F32, "imp2"); bimp2 = Buf()
        m8a = sb([128, 8], F32, "m8a"); bm8a = Buf()
        m8b = sb([128, 8], F32, "m8b"); bm8b = Buf()
        nsel = sb([128, 2, 32], F32, "nsel"); bnsel = Buf()
        nselT = sb([64, 2, 128], BF16, "nselT"); bnselT = Buf()
        MSET("pool", nselT[:].rearrange("p a b -> p (a b)"), 0.0, [bnselT])
        wdad = sb([128, 128], F32, "wdad"); bwdad = Buf()
        wdadb = sb([128, 128], BF16, "wdadb"); bwdadb = Buf()
        eLC = sb([128, 4], F32, "eLC"); beLC = Buf()
        rn8 = sb([128, 8], F32, "rn8"); brn8 = Buf()
        rn8T = sb([8, 128], F32, "rn8T"); brn8T = Buf()
        sbon = sb([128, 8], F32, "sbon"); bsbon = Buf()
        Pst = sb([128, 4, 64], F32, "Pst"); bPst = Buf()
        Pb = sb([128, 4, 64], BF16, "Pb"); bPb = Buf()
        lnst = sb([128, 32], F32, "lnst"); blnst = Buf()
        NWS = 2
        WS = [sb([128, 8, 256], BF16, f"WS{i}") for i in range(NWS)]; bWS = [Buf() for _ in range(NWS)]
        sq, bsq = slot(0, [128, 1024], F32)
        xs, bxs = sq, bsq
        qn2, bqn2 = slot(4, [128, 8, 2, 64], BF16)
        qT2, bqT2 = slot(35, [128, 8, 128], BF16)
        kn2, bkn2 = slot(8, [128, 2, 2, 64], BF16)
        PT = []; bPT = []
        NPT = 2
        for i_ in range(NPT):
            a_, b_ = slot(37 + i_, [128, 512], BF16)
            PT.append(a_); bPT.append(b_)
        PcT, bPcT = slot(39, [128, 512], F32)
        silA, bsilA = slot(15, [128, 4, 128], F32)
        silB, bsilB = slot(17, [128, 4, 128], F32)
        thA, bthA = slot(19, [128, 8, 128], BF16)
        thB, bthB = slot(21, [128, 8, 128], BF16)
        mg1, bmg1 = slot(23, [128, 8, 128], F32)
        mg2, bmg2 = slot(27, [128, 8, 128], F32)
        mgT, bmgT = slot(31, [128, 8, 128], BF16)
        yaT, byaT = slot(33, [128, 4, 128], BF16)
        ybT, bybT = slot(34, [128, 4, 128], BF16)
        rT, brT = slot(4, [128, 4, 128], F32)
        kTr, bkTr = slot(6, [128, 4, 128], F32)
        vT, bvT = slot(8, [128, 4, 128], F32)
        lwT, blw = slot(10, [128, 4, 128], F32)
        LT, bLT = slot(12, [128, 4, 128], F32)
        asg, basg = slot(14, [128, 4, 128], F32)
        e1, be1 = slot(16, [128, 4, 128], F32)
        e2, be2 = slot(18, [128, 4, 128], F32)
        e3, be3 = slot(20, [128, 4, 128], F32)
        kkn, bkkn = slot(22, [128, 4, 128], F32)
        kmod, bkmod = slot(24, [128, 4, 128], F32)
        tmpA, btmpA = slot(26, [128, 4, 128], F32)
        At, bAt = slot(28, [128, 4, 128], BF16)
        Bt, bBt = slot(29, [128, 4, 128], BF16)
        Kt, bKt = slot(30, [128, 4, 128], BF16)
        Rt, bRt = slot(31, [128, 4, 128], BF16)
        Bh, bBh = slot(32, [128, 512], BF16)
        Kh, bKh = slot(33, [128, 512], BF16)
        Vt, bVt = slot(34, [128, 512], BF16)
        Qm = []; bQm = []; QmT = []; bQmT = []; Xm = []; bXm = []
        for st_ in (10, 12):
            a_, b_ = slot(st_, [128, 8, 128], BF16); Qm.append(a_); bQm.append(b_)
        for st_ in (14, 18):
            a_, b_ = slot(st_, [128, 8, 128], BF16); QmT.append(a_); bQmT.append(b_)
        for st_ in (20, 22):
            a_, b_ = slot(st_, [128, 8, 128], BF16); Xm.append(a_); bXm.append(b_)
        AakT, bAak = slot(24, [128, 8, 128], BF16)
        MrbT, bMrb = slot(4, [128, 8, 128], BF16)
        MrkT, bMrk = slot(6, [128, 8, 128], BF16)
        rhs0, brhs0 = slot(8, [128, 8, 64], BF16)
        Ub, bUb = slot(9, [128, 8, 64], BF16)

        def f4(t):
            return t.rearrange("p c t -> p (c t)")

        def REDUCE(out, in_, r, w):
            return S.op("dve", lambda e: e.tensor_reduce(out=out, in_=in_, axis=AX.X, op=ALU.add), r, w)

        def MAX8(out, in_, r, w):
            return S.op("dve", lambda e: e.max(out=out, in_=in_), r, w)

        def MREP(out, rep, vals, r, w):
            return S.op("dve", lambda e: e.match_replace(out=out, in_to_replace=rep, in_values=vals, imm_value=-3.0e38), r, w)

        def RECIP(out, in_, r, w):
            return S.op("dve", lambda e: e.reciprocal(out=out, in_=in_), r, w)

        def SCAN(out, d0, d1, r, w):
            return S.op("dve", lambda e: e.tensor_tensor_scan(out=out, data0=d0, data1=d1, initial=0.0, op0=ALU.mult, op1=ALU.add), r, w)

        print("SBUF remaining:", nc.sbuf_bytes_remaining)
        ws_i = [0]

        def chk(n):
            if stage <= n:
                raise _Stop()

        def tile_body(s, i):
            try:
                return tile_body2(s, i)
            except _Stop:
                T = s * 16 + i
                return DMA(out_d[T * 128:T * 128 + 128, :], xs[:, :], sem_outs[T % 2], r=[bxs], w=[])

        def tile_body2(s, i):
            T = s * 16 + i
            tok0 = T * 128
            xb_ = T % 2
            x_t = xt[xb_]; bx = bxt[xb_]
            hcur = hT[T % 2]; bh = bhT[T % 2]
            hprev = hT[(T + 1) % 2]; bhp = bhT[(T + 1) % 2]
            ACT(sq[:], x_t[:], AF.Square, [bx], [bsq, bst12], accum=st12[:, 0:1])
            TS("dve", st12[:, 0:1], st12[:, 0:1], 1.0 / 1024, 1e-6, ALU.mult, ALU.add, [bst12], [bst12])
            TT("pool", rs12[:, 0:1], st12[:, 0:1], cneg[:, 0:1], ALU.pow, [bst12, bcneg], [brs12])
            TS("dve", xs[:], x_t[:], rs12[:, 0:1], None, ALU.mult, None, [bx, brs12], [bxs])
            import os as _os
            _sk = _os.environ.get("SKIP", "")
            if i == 0:
                if "m" not in _sk:
                    MSET("pool", hcur[:, :, 0:1], 0.0, [bh])
            else:
                CP("pool", hcur[:, :, 0:1], hprev[:, :, 128:129], [bhp], [bh])
            for half in range(2):
                pp, bp = bank("proj")
                for j in range(4):
                    kc = half * 4 + j
                    TR(pp[:, j * 128:(j + 1) * 128], xs[:, kc * 128:(kc + 1) * 128], identf[:], [bxs, bidf], [bp], inc=(j == 3))
                for j in range(4):
                    kc = half * 4 + j
                    if "a" in _sk:
                        ACT(hcur[:, kc, 1:129], pp[:, j * 128:(j + 1) * 128], AF.Identity, [bp, bgs, bmod], [bh])
                    elif "b" in _sk:
                        ACT(hcur[:, kc, 2:130], pp[:, j * 128:(j + 1) * 128], AF.Identity, [bp, bgs, bmod], [bh],
                            bias=modT[:, kc, s:s + 1], scale=gsT[:, s, kc:kc + 1])
                    else:
                        ACT(hcur[:, kc, 1:129], pp[:, j * 128:(j + 1) * 128], AF.Identity, [bp, bgs, bmod], [bh],
                            bias=modT[:, kc, s:s + 1], scale=gsT[:, s, kc:kc + 1])
            dump(f"hT_{T}", hcur[:], [bh], BF16)
            chk(1)

            pq, bpq = bank("proj")
            for kc in range(8):
                MM(pq[:, :], hcur[:, kc, 1:129], Wres[:, kc, CQ:CQ + 512], start=(kc == 0), stop=(kc == 7), r=[bh, bWres], w=[bpq], inc=(kc == 7))
            ACT(sq[:, 0:512], pq[:, :], AF.Square, [bpq], [bsq])
            REDUCE(st12[:, 0:8], sq[:, 0:512].rearrange("p (h d) -> p h d", h=8), [bsq], [bst12])
            pkv, bpkv = bank("proj")
            for kc in range(8):
                MM(pkv[:, :], hcur[:, kc, 1:129], Wres[:, kc, CKV:CKV + 512], start=(kc == 0), stop=(kc == 7), r=[bh, bWres], w=[bpkv], inc=(kc == 7))
            ACT(sq[:, 512:768], pkv[:, 0:256], AF.Square, [bpkv], [bsq])
            REDUCE(st12[:, 8:12], sq[:, 512:768].rearrange("p (h d) -> p h d", h=4), [bsq], [bst12])
            TS("dve", st12[:, 0:12], st12[:, 0:12], 1.0 / 64, 1e-6, ALU.mult, ALU.add, [bst12], [bst12])
            TT("pool", rs12[:, 0:12], st12[:, 0:12], cneg[:, 0:12], ALU.pow, [bst12, bcneg], [brs12])
            chk(1.2)
            for h in range(8):
                STT(qn2[:, h, :, :], bcast(pq[:, h * 64:(h + 1) * 64], [128, 2, 64], 1), rs12[:, h:h + 1],
                    bcast(bc_small[:, 0:64], [128, 2, 64], 1), ALU.mult, ALU.mult, [bpq, brs12, bbcs], [bqn2])
            for gg in range(2):
                for br in range(2):
                    c0 = gg * 128 + br * 64
                    STT(kn2[:, gg, br, :], pkv[:, c0:c0 + 64], rs12[:, 8 + gg * 2 + br:9 + gg * 2 + br],
                        bc_small[:, 64 + br * 64:128 + br * 64], ALU.mult, ALU.mult, [bpkv, brs12, bbcs], [bkn2])
            CP("act", Vc[:, i, :, :, 0:64], pkv[:, 256:512].rearrange("p (b g d) -> p b g d", b=2, g=2), [bpkv], [bVc[i]])
            chk(1.4)
            pt, bpt = bank("proj")
            ptb = pt[:].bitcast(BF16)
            for h in range(8):
                TR(ptb[:, h * 128:(h + 1) * 128], qn2[:, h, :, :].rearrange("p c d -> p (c d)"), identb[:], [bqn2, bidb], [bpt], inc=(h == 7))
            CP("act", qT2[:].rearrange("p h q -> p (h q)"), ptb[:, 0:1024], [bpt], [bqT2])
            pt2, bpt2 = bank("proj")
            pt2b = pt2[:].bitcast(BF16)
            for gg in range(2):
                TR(pt2b[:, gg * 128:(gg + 1) * 128], kn2[:, gg, :, :].rearrange("p c d -> p (c d)"), identb[:], [bkn2, bidb], [bpt2], inc=(gg == 1))
            CP("dve", kT[:, :, i * 128:(i + 1) * 128], pt2b[:, 0:256].rearrange("p (g t) -> p g t", g=2), [bpt2], [bkT[i]])
            chk(1.6)
            pgt, bpgt = bank("proj")
            for kc in range(8):
                MM(pgt[:, 0:24], hcur[:, kc, 1:129], Wres[:, kc, CG:CG + 24], start=(kc == 0), stop=(kc == 7), r=[bh, bWres], w=[bpgt], inc=(kc == 7))
            ACT(gsig[:].rearrange("p a b -> p (a b)"), pgt[:, 0:24], AF.Tanh, [bpgt], [bgsig], scale=0.5)
            TS("dve", gsig[:].rearrange("p a b -> p (a b)"), gsig[:].rearrange("p a b -> p (a b)"), 0.5, 0.5, ALU.mult, ALU.add, [bgsig], [bgsig])
            pcm, bpcm = bank("proj")
            for gg in range(2):
                for kc in range(8):
                    MM(pcm[:, gg * 128:(gg + 1) * 128], Wres[:, kc, CC + gg * 128:CC + (gg + 1) * 128], hcur[:, kc, 1:129],
                       start=(kc == 0), stop=(kc == 7), r=[bh, bWres], w=[bpcm], inc=(kc == 7 and gg == 1))
            CP("pool", kvc[:, :, 0:16], kvc[:, :, 128:144], [bkvc], [bkvc])
            CP("act", kvc[:, :, 16:144], pcm[:, 0:256].rearrange("p (g t) -> p g t", g=2), [bpcm], [bkvc])
            chk(1.8)
            m0 = 1 if i == 0 else 0
            nm = 8 - m0
            for kv in range(2):
                lo = kv * 64
                phd, bphd = bank("proj")
                for jh in range(2):
                    for pos in range(32):
                        MM(phd[:, jh * 16:jh * 16 + 16].rearrange("p (g m) -> p g m", g=2), W1c[lo:lo + 64, pos, jh * 128:(jh + 1) * 128],
                           kvc[lo:lo + 64, :, pos:pos + 113:16], start=(pos == 0), stop=(pos == 31), r=[bW1c, bkvc], w=[bphd],
                           inc=(pos == 31))
                for jh in range(2):
                    reg = (kv * 2 + jh) * 16
                    ACT(hu[:, reg:reg + 16], phd[:, jh * 16:jh * 16 + 16], AF.Identity, [bphd, bhb2], [bhu], bias=hb2[:, kv, jh:jh + 1])
            chk(1.85)
            TT("dve", hw_[:], hu[:], hu[:], ALU.mult, [bhu], [bhw])
            TS("dve", hw_[:], hw_[:], 0.044715, 1.0, ALU.mult, ALU.add, [bhw], [bhw])
            TT("dve", hw_[:], hw_[:], hu[:], ALU.mult, [bhw, bhu], [bhw])
            ACT(hw_[:], hw_[:], AF.Tanh, [bhw], [bhw], scale=math.sqrt(2.0 / math.pi))
            STT(hid[:], hw_[:], 1.0, hu[:], ALU.add, ALU.mult, [bhw, bhu], [bhid])
            chk(1.9)
            pc2, bpc2 = bank("proj")
            for kv in range(2):
                for jh in range(2):
                    reg = (kv * 2 + jh) * 16
                    MM(pc2[0:64, kv * 16:(kv + 1) * 16], W2c[:, kv, jh, :], hid[:, reg:reg + 16], start=(jh == 0), stop=(jh == 1),
                       r=[bW2c, bhid], w=[bpc2], inc=(jh == 1))
            TS("dve", kcs[:, 0:16], pc2[0:64, 0:16], 0.5, None, ALU.mult, None, [bpc2], [bkcs])
            TT("dve", kcs[:, 16:32], kcs[:, 0:16], kcs[:, 0:16], ALU.mult, [bkcs], [bkcs])
            MM(pc2[0:64, 64:80], ones_f[0:64, 0:64], kcs[:, 16:32], r=[bones, bkcs], w=[bpc2])
            TS("dve", kcs[:, 32:48], pc2[0:64, 64:80], 1.0 / 64, 1e-6, ALU.mult, ALU.add, [bpc2], [bkcs])
            TT("pool", kcs[:, 16:32], kcs[:, 32:48], cneg[0:64, 0:16], ALU.pow, [bkcs, bcneg], [bkcs])
            TT("dve", kcs[:, 0:16], kcs[:, 0:16], kcs[:, 16:32], ALU.mult, [bkcs], [bkcs])
            n0 = 8 * i - 1 + m0
            TS("dve", kcT[:, :, n0:n0 + nm], kcs[:, 0:16].rearrange("p (g m) -> p g m", g=2)[:, :, m0:8], kgc[:, 0:1], None, ALU.mult, None,
               [bkcs, bkgc], [bkcT])
            TS("dve", vcT[:, :, n0:n0 + nm], pc2[0:64, 16:32].rearrange("p (g m) -> p g m", g=2)[:, :, m0:8], 0.5, None, ALU.mult, None,
               [bpc2], [bvcT])
            nv = 8 * i + 7
            chk(1.95)
            pvt, bpvt = bank("proj")
            for gg in range(2):
                TR(pvt[0:nv, gg * 64:(gg + 1) * 64], vcT[:, gg, 0:nv], identf[0:64, 0:64], [bvcT, bidf], [bpvt], inc=(gg == 1))
            CP("dve", vca[0:nv, :, 0:64], pvt[0:nv, 0:128].rearrange("p (g d) -> p g d", g=2), [bpvt], [bvca])
            dump(f"kcT_{T}", kcT[:], [bkcT], BF16)
            dump(f"vca_{T}", vca[:], [bvca])
            dump(f"qT2_{T}", qT2[:], [bqT2], BF16)
            chk(2)

            def attn_gen():
                first_y = {0: True, 1: True}

                def finish(acc, bacc, br, gg):
                    accv = acc[:, 0:260].rearrange("p (h e) -> p h e", h=4)
                    c0 = br * 4
                    TS("dve", coef[:, c0:c0 + 4], accv[:, :, 64], 1e-30, None, ALU.max, None, [bacc], [bcoef])
                    RECIP(coef[:, c0:c0 + 4], coef[:, c0:c0 + 4], [bcoef], [bcoef])
                    if br == 0:
                        CP("dve", coef[:, 12:16], coef[:, 0:4], [bcoef], [bcoef])
                    gbr = {0: 0, 1: 1, 2: 2}[br]
                    TT("dve", coef[:, c0:c0 + 4], coef[:, c0:c0 + 4], gsig[:, gbr, gg * 4:(gg + 1) * 4], ALU.mult, [bcoef, bgsig], [bcoef])
                    yv = ynsa[:, gg * 256:(gg + 1) * 256].rearrange("p (h d) -> p h d", h=4)
                    cb = bcast(coef[:, c0:c0 + 4], [128, 4, 64], 2)
                    if first_y[gg]:
                        TT("dve", yv, accv[:, :, 0:64], cb, ALU.mult, [bacc, bcoef], [bynsa])
                        first_y[gg] = False
                    else:
                        for h in range(4):
                            STT(yv[:, h, :], accv[:, h, 0:64], coef[:, c0 + h:c0 + h + 1], yv[:, h, :], ALU.mult, ALU.add, [bacc, bcoef, bynsa], [bynsa])

                def pv(acc, bacc, Pt_, bP, vrhs, bv, first, last, K=128):
                    for h in range(4):
                        MM(acc[:, h * 65:(h + 1) * 65], Pt_[0:K, h * 128:(h + 1) * 128], vrhs, start=(first and h == 0), stop=(last and h == 3), r=[bP] + bv, w=[bacc],
                           inc=(h == 3), sgc=True)

                pti = [0]
                for gg in range(2):
                    sc, bsc_ = bank("sc")
                    MM(sc[0:nv, :], kcT[:, gg, 0:nv], qT2[0:64, gg * 4:(gg + 1) * 4, :].rearrange("p h q -> p (h q)"), start=True, stop=False,
                       r=[bkcT, bqT2], w=[bsc_], inc=False)
                    off = 128 - 8 * i
                    MM(sc[0:nv, :], zsh[:, off:off + nv], biasC[:, gg, :], start=False, stop=True, r=[bzsh], w=[bsc_])
                    ACT(PcT[0:nv, :], sc[0:nv, :], AF.Exp, [bsc_], [bPcT], scale=0.125)
                    acc, bacc = bank("acc")
                    for h in range(4):
                        MM(acc[:, h * 65:(h + 1) * 65], PcT[0:nv, h * 128:(h + 1) * 128], vca[0:nv, gg, :], r=[bPcT, bvca], w=[bacc], inc=False)
                    for h in range(4):
                        MM(acc[:, 320 + h * 32:320 + (h + 1) * 32], PcT[0:nv, h * 128:(h + 1) * 128], mimp[0:nv, :], r=[bPcT, bmimp], w=[bacc], inc=(h == 3))
                    finish(acc, bacc, 0, gg)
                    yield
                    if i >= 8:
                        iv = impr[:, gg, :]
                        TS("dve", iv, acc[:, 320:352], coef[:, 12:13], None, ALU.mult, None, [bacc, bcoef], [bimpr])
                        for h in range(1, 4):
                            STT(iv, acc[:, 320 + h * 32:352 + h * 32], coef[:, 12 + h:13 + h], iv, ALU.mult, ALU.add, [bacc, bcoef, bimpr], [bimpr])
                        TT("dve", iv, iv, keepadd[:, i - 8, 0, :], ALU.mult, [bimpr, bka], [bimpr])
                        TT("dve", iv, iv, keepadd[:, i - 8, 1, :], ALU.add, [bimpr, bka], [bimpr])
                        MAX8(m8a[:], iv, [bimpr], [bm8a])
                        MREP(imp2[:], m8a[:], iv, [bimpr, bm8a], [bimp2])
                        MAX8(m8b[:], imp2[:], [bimp2], [bm8b])
                        TS("dve", nsel[:, gg, :], iv, m8b[:, 7:8], 1.0, ALU.is_ge, ALU.subtract, [bimpr, bm8b], [bnsel])
                        pn, bpn = bank("sc")
                        TR(pn[0:32, 0:128], nsel[:, gg, :], identf[:], [bnsel, bidf], [bpn])
                        CP("dve", nselT[0:32, gg, :], pn[0:32, 0:128], [bpn], [bnselT])
                dump(f"nsel_{T}", nsel[:], [bnsel])
                for br in (2, 1):
                    for gg in range(2):
                        lo = 0 if br == 1 else 64
                        j0 = 0 if br == 1 else max(0, i - 4)
                        acc, bacc = bank("acc")
                        for j in range(j0, i + 1):
                            sc, bsc_ = bank("sc")
                            extra = []
                            if j == i:
                                extra.append((identb[:], biasDS[:, 0, gg, :], [bidb, bbias]))
                            if j == i - 1:
                                extra.append((identb[:], biasDS[:, 1, gg, :], [bidb, bbias]))
                            if br == 2 and j == i - 4:
                                extra.append((identb[:], bcast(masks[:, 0, :], [128, 4, 128], 1), [bidb, bmasks]))
                            if br == 1 and i >= 8:
                                extra.append((emat[:, j * 128:(j + 1) * 128], bcast(nselT[:, gg, :], [64, 4, 128], 1), [bemat, bnselT]))
                            MM(sc[:, :], kT[lo:lo + 64, gg, j * 128:(j + 1) * 128], qT2[lo:lo + 64, gg * 4:(gg + 1) * 4, :].rearrange("p h q -> p (h q)"),
                               start=True, stop=(len(extra) == 0), r=[bkT[j], bqT2], w=[bsc_], inc=(len(extra) == 0))
                            for ei, (l_, r_, bb_) in enumerate(extra):
                                lastx = ei == len(extra) - 1
                                MM(sc[:, :].rearrange("p (h q) -> p h q", h=4) if len(r_.shape) == 3 else sc[:, :], l_, r_, start=False, stop=lastx,
                                   r=bb_, w=[bsc_], inc=lastx)
                            Pt_ = PT[pti[0] % NPT]; bP = bPT[pti[0] % NPT]; pti[0] += 1
                            ACT(Pt_[:, :], sc[:, :], AF.Exp, [bsc_], [bP], scale=0.125)
                            pv(acc, bacc, Pt_, bP, Vc[:, j, br - 1, gg, :], [bVc[j]], j == j0, j == i)
                            yield
                        finish(acc, bacc, br, gg)
                        yield
                dump(f"ynsa_{T}", ynsa[:], [bynsa])
                chk(3)

                yield
            def rwkv_gen():
                yield
                for c3 in range(0, 13, 3):
                    ps_, bps_ = bank("rw")
                    ncs = min(3, 13 - c3)
                    for cc in range(ncs):
                        c = c3 + cc
                        for kc in range(8):
                            MM(ps_[:, cc * 129:(cc + 1) * 129], Wres[:, kc, CS + c * 128:CS + (c + 1) * 128], hcur[:, kc, 0:129],
                               start=(kc == 0), stop=(kc == 7), r=[bh, bWres], w=[bps_], inc=(kc == 7))
                    for cc in range(ncs):
                        c = c3 + cc
                        if c < 4:
                            dst, bd = rT[:, c, :], brT
                        elif c < 8:
                            dst, bd = kTr[:, c - 4, :], bkTr
                        elif c < 12:
                            dst, bd = vT[:, c - 8, :], bvT
                        else:
                            dst, bd = wdad[:, :], bwdad
                        ACT(dst, ps_[:, cc * 129 + 1:cc * 129 + 129], AF.Identity, [bps_, bmu], [bd], scale=mucol[:, 1, c:c + 1])
                        STT(dst, ps_[:, cc * 129:cc * 129 + 128], mucol[:, 0, c:c + 1], dst, ALU.mult, ALU.add, [bps_, bmu, bd], [bd])
                yield
                dump(f"rT_{T}", rT[:], [brT])
                yield
                dump(f"wdad_{T}", wdad[:], [bwdad])
                yield
                ACT(wdadb[0:64, :], wdad[0:64, :], AF.Tanh, [bwdad], [bwdadb])
                yield
                CP("pool", wdadb[64:128, :], wdad[64:128, :], [bwdad], [bwdadb])
                yield
                pz, bpz = bank("rw")
                yield
                pa, bpa = bank("rw")
                yield
                for c in range(4):
                    MM(pz[:, c * 128:(c + 1) * 128], Lora[0:64, c * 128:(c + 1) * 128], wdadb[0:64, :], start=True, stop=False, r=[bLora, bwdadb], w=[bpz], inc=False)
                    MM(pz[:, c * 128:(c + 1) * 128], w0a0[0:64, c * 128:(c + 1) * 128], ones_f[0:64, :], start=False, stop=True, r=[bw0a0, bones], w=[bpz], inc=(c == 3))
                yield
                for c in range(4):
                    MM(pa[:, c * 128:(c + 1) * 128], Lora[64:128, c * 128:(c + 1) * 128], wdadb[64:128, :], start=True, stop=False, r=[bLora, bwdadb], w=[bpa], inc=False)
                    MM(pa[:, c * 128:(c + 1) * 128], w0a0[64:128, c * 128:(c + 1) * 128], ones_f[64:128, :], start=False, stop=True, r=[bw0a0, bones], w=[bpa], inc=(c == 3))
                yield
                f4 = lambda t: t[:].rearrange("p c t -> p (c t)")
                yield
                ACT(f4(lwT), pz[:, :], AF.Tanh, [bpz], [blw], scale=0.5)
                yield
                cexp = math.exp(-0.5) * 0.5
                yield
                TS("dve", f4(lwT), f4(lwT), -cexp, -cexp, ALU.mult, ALU.add, [blw], [blw])
                yield
                ACT(f4(asg), pa[:, :], AF.Tanh, [bpa], [basg], scale=0.5)
                yield
                TS("pool", f4(asg), f4(asg), 0.5, 0.5, ALU.mult, ALU.add, [basg], [basg])
                yield
                for c in range(4):
                    SCAN(LT[:, c, :], ones_f[:, :], lwT[:, c, :], [bones, blw], [bLT])
                yield
                TT("pool", f4(tmpA), f4(LT), f4(lwT), ALU.subtract, [bLT, blw], [btmpA])
                yield
                ACT(f4(e1), f4(tmpA), AF.Exp, [btmpA], [be1])
                yield
                ACT(f4(e2), f4(LT), AF.Exp, [bLT], [be2], scale=-1.0)
                yield
                ACT(f4(e3), f4(LT), AF.Exp, [bLT], [be3])
                yield
                ACT(eLC[:, :], LT[:, :, 127], AF.Exp, [bLT], [beLC])
                yield
                yield
                for c in range(4):
                    TS("dve", kkn[:, c, :], kTr[:, c, :], vec4[:, 0, c:c + 1], None, ALU.mult, None, [bkTr, bvec4], [bkkn])
                yield
                TT("pool", f4(tmpA), f4(kkn), f4(kkn), ALU.mult, [bkkn], [btmpA])
                yield
                pk_, bpk_ = bank("rw")
                yield
                for c in range(4):
                    MM(pk_[:, c * 2:c * 2 + 2], tmpA[:, c, :], ind2[:, :], r=[btmpA, bind2], w=[bpk_], inc=(c == 3))
                yield
                TS("dve", rn8[:, :], pk_[:, 0:8], 1e-24, None, ALU.max, None, [bpk_], [brn8])
                yield
                TT("pool", rn8[:, :], rn8[:, :], cneg[:, 0:8], ALU.pow, [brn8, bcneg], [brn8])
                yield
                TR(pk_[0:8, 128:256], rn8[:, :], identf[:], [brn8, bidf], [bpk_])
                yield
                CP("dve", rn8T[:, :], pk_[0:8, 128:256], [bpk_], [brn8T])
                yield
                pr_, bpr_ = bank("rw")
                yield
                for c in range(4):
                    MM(pr_[:, c * 128:(c + 1) * 128], indT[:, c, :], rn8T[:, :], r=[bindT, brn8T], w=[bpr_], inc=(c == 3))
                yield
                TT("dve", f4(kkn), f4(kkn), pr_[:, :], ALU.mult, [bkkn, bpr_], [bkkn])
                yield
                dump(f"kkn_{T}", kkn[:], [bkkn])
                yield
                yield
                for c in range(4):
                    TS("pool", tmpA[:, c, :], asg[:, c, :], -1.0, vec4[:, 1, c:c + 1], ALU.add, ALU.mult, [basg, bvec4], [btmpA])
                yield
                STT(f4(kmod), f4(tmpA), 1.0, f4(kTr), ALU.add, ALU.mult, [btmpA, bkTr], [bkmod])
                yield
                dump(f"kmod_{T}", kmod[:], [bkmod])
                yield
                yield
                for c in range(4):
                    STT(tmpA[:, c, :], rT[:, c, :], vec4[:, 2, c:c + 1], kmod[:, c, :], ALU.mult, ALU.mult, [brT, bvec4, bkmod], [btmpA])
                yield
                for c in range(4):
                    MM(pk_[:, 256 + c * 2:256 + c * 2 + 2], tmpA[:, c, :], ind2[:, :], r=[btmpA, bind2], w=[bpk_], inc=(c == 3))
                yield
                CP("dve", sbon[:, :], pk_[:, 256:264], [bpk_], [bsbon])
                yield
                yield
                STT(f4(At), f4(kkn), -1.0, f4(e1), ALU.mult, ALU.mult, [bkkn, be1], [bAt])
                yield
                TT("pool", f4(tmpA), f4(kkn), f4(asg), ALU.mult, [bkkn, basg], [btmpA])
                yield
                TT("dve", f4(tmpA), f4(tmpA), f4(e2), ALU.mult, [btmpA, be2], [btmpA])
                yield
                CP("pool", f4(Bt), f4(tmpA), [btmpA], [bBt])
                yield
                TT("pool", f4(e1), f4(kmod), f4(e2), ALU.mult, [bkmod, be2], [be1])
                yield
                CP("pool", f4(Kt), f4(e1), [be1], [bKt])
                yield
                TT("dve", f4(Rt), f4(rT), f4(e3), ALU.mult, [brT, be3], [bRt])
                yield
                yield
                for c in range(4):
                    TS("dve", tmpA[:, c, :], tmpA[:, c, :], eLC[:, c:c + 1], None, ALU.mult, None, [btmpA, beLC], [btmpA])
                    TS("pool", e1[:, c, :], e1[:, c, :], eLC[:, c:c + 1], None, ALU.mult, None, [be1, beLC], [be1])
                yield
                for (src, bsrc, dst, bdst) in ((tmpA, btmpA, Bh, bBh), (e1, be1, Kh, bKh), (vT, bvT, Vt, bVt)):
                    pp, bp = bank("rw")
                    for c in range(4):
                        TR(pp[:, c * 128:(c + 1) * 128], src[:, c, :], identf[:], [bsrc, bidf], [bp], inc=(c == 3))
                    CP("act", dst[:, :], pp[:, :], [bp], [bdst])
                yield
                chk(4)
                yield
                yield
                def hs(t, h):
                    return t[(h % 2) * 64:(h % 2) * 64 + 64, h // 2, :]
                yield

                def five(lt, blt, rt, brt, mask_i, dst, bdst):
                    for par in range(2):
                        pp, bp = bank("rw")
                        for hh in range(4):
                            h = hh * 2 + par
                            MM(pp[:, hh * 128:(hh + 1) * 128], hs(lt, h), hs(rt, h), r=[blt, brt], w=[bp], inc=(hh == 3))
                        TT("dve", dst[:, par:8:2, :], pp[:, :].rearrange("p (h t) -> p h t", h=4), bcast(masks[:, mask_i, :], [128, 4, 128], 1), ALU.mult,
                           [bp, bmasks], [bdst])
                yield

                five(Bt, bBt, At, bAt, 1, Qm[0], bQm[0])
                yield
                five(At, bAt, Bt, bBt, 3, QmT[0], bQmT[0])
                yield
                five(Kt, bKt, At, bAt, 1, AakT, bAak)
                yield
                five(Bt, bBt, Rt, bRt, 2, MrbT, bMrb)
                yield
                five(Kt, bKt, Rt, bRt, 2, MrkT, bMrk)
                yield
                yield
                TT("pool", Xm[0][:], Qm[0][:], bcast(identb[:], [128, 8, 128], 1), ALU.add, [bQm[0], bidb], [bXm[0]])
                yield
                cur = 0
                yield
                for lvl in range(1, 7):
                    nxt = cur ^ 1
                    lastl = lvl == 6
                    for half in range(2):
                        hsl = slice(half * 4, half * 4 + 4)
                        pT_, bpT_ = bank("rw")
                        for hh in range(4):
                            h = half * 4 + hh
                            MM(pT_[:, hh * 128:(hh + 1) * 128], Qm[cur][:, h, :], QmT[cur][:, h, :], r=[bQm[cur], bQmT[cur]], w=[bpT_], inc=(hh == 3))
                        CP("act", QmT[nxt][:, hsl, :], pT_[:, :].rearrange("p (h t) -> p h t", h=4), [bpT_], [bQmT[nxt]])
                        if not lastl:
                            pQ_, bpQ_ = bank("rw")
                            for hh in range(4):
                                h = half * 4 + hh
                                MM(pQ_[:, hh * 128:(hh + 1) * 128], QmT[cur][:, h, :], Qm[cur][:, h, :], r=[bQm[cur], bQmT[cur]], w=[bpQ_], inc=(hh == 3))
                            CP("dve", Qm[nxt][:, hsl, :], pQ_[:, :].rearrange("p (h t) -> p h t", h=4), [bpQ_], [bQm[nxt]])
                        yield
                    for half in range(2):
                        hsl = slice(half * 4, half * 4 + 4)
                        pX_, bpX_ = bank("rw")
                        for hh in range(4):
                            h = half * 4 + hh
                            MM(pX_[:, hh * 128:(hh + 1) * 128], QmT[nxt][:, h, :], Xm[cur][:, h, :], r=[bQmT[nxt], bXm[cur]], w=[bpX_], inc=(hh == 3))
                        TT("dve", Xm[nxt][:, hsl, :], pX_[:, :].rearrange("p (h t) -> p h t", h=4), Xm[cur][:, hsl, :], ALU.add, [bpX_, bXm[cur]], [bXm[nxt]])
                        yield
                    cur = nxt
                yield
                Xf = Xm[cur]; bXf = bXm[cur]
                yield
                chk(5)
                yield
                yield
                if i == 0:
                    MSET("pool", Pst[:], 0.0, [bPst])
                    MSET("pool", Pb[:], 0.0, [bPb])
                yield

                def ph_(t, h):
                    return t[(h % 2) * 64:(h % 2) * 64 + 64, h // 2, :]
                yield

                def vh(t, h):
                    return t[:, h * 64:(h + 1) * 64]
                yield

                for par in range(2):
                    p1, bp1 = bank("rw")
                    for hh in range(4):
                        h = hh * 2 + par
                        MM(p1[:, hh * 64:(hh + 1) * 64], hs(At, h), ph_(Pb, h), start=True, stop=False, r=[bAt, bPb], w=[bp1], inc=False)
                        MM(p1[:, hh * 64:(hh + 1) * 64], AakT[:, h, :], vh(Vt, h), start=False, stop=True, r=[bAak, bVt], w=[bp1], inc=(hh == 3))
                    CP("act", rhs0[:, par:8:2, :], p1[:, 0:256].rearrange("p (h v) -> p h v", h=4), [bp1], [brhs0])
                yield
                chk(5.2)
                yield
                p2, bp2 = bank("rw")
                yield
                for h in range(8):
                    MM(p2[:, h * 64:(h + 1) * 64], Xf[:, h, :], rhs0[:, h, :], r=[bXf, brhs0], w=[bp2], inc=(h == 7))
                yield
                CP("act", Ub[:].rearrange("p h v -> p (h v)"), p2[:, :], [bp2], [bUb])
                yield
                chk(5.4)
                yield
                p3s = []
                yield
                for par in range(2):
                    p3, bp3 = bank("proj")
                    p3s.append((p3, bp3))
                    for hh in range(4):
                        h = hh * 2 + par
                        MM(p3[:, hh * 64:(hh + 1) * 64], hs(Rt, h), ph_(Pb, h), start=True, stop=False, r=[bRt, bPb], w=[bp3], inc=False)
                        MM(p3[:, hh * 64:(hh + 1) * 64], MrkT[:, h, :], vh(Vt, h), start=False, stop=False, r=[bMrk, bVt], w=[bp3], inc=False)
                        MM(p3[:, hh * 64:(hh + 1) * 64], MrbT[:, h, :], Ub[:, h, :], start=False, stop=True, r=[bMrb, bUb], w=[bp3], inc=(hh == 3))
                yield
                chk(5.6)
                yield
                p4, bp4 = bank("rw")
                yield
                for h in range(8):
                    o_ = p4[(h % 2) * 64:(h % 2) * 64 + 64, (h // 2) * 64:(h // 2) * 64 + 64]
                    MM(o_, vh(Bh, h), Ub[:, h, :], start=True, stop=False, r=[bBh, bUb], w=[bp4], inc=False)
                    MM(o_, vh(Kh, h), vh(Vt, h), start=False, stop=True, r=[bKh, bVt], w=[bp4], inc=(h == 7))
                yield
                for c in range(4):
                    STT(Pst[:, c, :], Pst[:, c, :], eLC[:, c:c + 1], p4[:, c * 64:(c + 1) * 64], ALU.mult, ALU.add, [bPst, beLC, bp4], [bPst])
                yield
                CP("pool", Pb[:], Pst[:], [bPst], [bPb])
                yield
                chk(5.8)
                yield
                yield
                yn3 = yn[:, :].rearrange("p (h d) -> p h d", h=8)
                yield
                for par in range(2):
                    p3, bp3 = p3s[par]
                    CP("act", yn3[:, par:8:2, :], p3[:, 0:256].rearrange("p (h d) -> p h d", h=4), [bp3], [byn])
                yield
                ACT(sq[:, 0:512], yn[:, :], AF.Square, [byn], [bsq])
                yield
                REDUCE(lnst[:, 0:8], yn3, [byn], [blnst])
                yield
                REDUCE(lnst[:, 8:16], sq[:, 0:512].rearrange("p (h d) -> p h d", h=8), [bsq], [blnst])
                yield
                chk(5.85)
                yield
                TS("dve", lnst[:, 0:16], lnst[:, 0:16], 1.0 / 64, None, ALU.mult, None, [blnst], [blnst])
                yield
                TT("dve", lnst[:, 16:24], lnst[:, 0:8], lnst[:, 0:8], ALU.mult, [blnst], [blnst])
                yield
                TT("dve", lnst[:, 16:24], lnst[:, 8:16], lnst[:, 16:24], ALU.subtract, [blnst], [blnst])
                yield
                TS("dve", lnst[:, 16:24], lnst[:, 16:24], 0.0, 64e-5, ALU.max, ALU.add, [blnst], [blnst])
                yield
                TT("pool", lnst[:, 24:32], lnst[:, 16:24], cneg[:, 0:8], ALU.pow, [blnst, bcneg], [blnst])
                yield
                STT(lnst[:, 16:24], lnst[:, 0:8], -1.0, lnst[:, 24:32], ALU.mult, ALU.mult, [blnst], [blnst])
                yield
                chk(5.9)
                yield
                for h in range(8):
                    ACT(yn[:, h * 64:(h + 1) * 64], yn[:, h * 64:(h + 1) * 64], AF.Identity, [byn, blnst], [byn],
                        bias=lnst[:, 16 + h:17 + h], scale=lnst[:, 24 + h:25 + h])
                yield
                chk(5.95)
                yield
                TT("pool", yn[:, :], yn[:, :], ln_w_bc[:, :], ALU.mult, [byn, blnw], [byn])
                yield
                chk(5.97)
                yield
                for h in range(8):
                    STT(yn[:, h * 64:(h + 1) * 64], Vt[:, h * 64:(h + 1) * 64], sbon[:, h:h + 1], yn[:, h * 64:(h + 1) * 64], ALU.mult, ALU.add,
                        [bVt, bsbon, byn], [byn])
                yield

            na_ = 2 * (2 + (i + 1) + min(5, i + 1)) + 2
            nr_ = 150
            ga_, gr_ = attn_gen(), rwkv_gen()
            da_ = dr_ = 0
            alive_a = alive_r = True
            while alive_a or alive_r:
                pick_a = alive_a and (not alive_r or da_ * nr_ <= dr_ * na_)
                if pick_a:
                    try:
                        next(ga_); da_ += 1
                    except StopIteration:
                        alive_a = False
                else:
                    try:
                        next(gr_); dr_ += 1
                    except StopIteration:
                        alive_r = False
            chk(6)
            def wsload(c):
                k = ws_i[0] % NWS
                ws_i[0] += 1
                DMA(WS[k][:].rearrange("p k n -> p (k n)"), wrest_s[c], sem_ws[k], w=[bWS[k]])
                return WS[k], bWS[k]

            def rest_chunk(c):
                W_, bW_ = wsload(c)
                pp, bp = bank("proj")
                for sub in range(2):
                    for kc in range(8):
                        MM(pp[:, sub * 128:(sub + 1) * 128], W_[:, kc, sub * 128:(sub + 1) * 128], hcur[:, kc, 1:129], start=(kc == 0), stop=(kc == 7),
                           r=[bW_, bh], w=[bp], inc=(kc == 7 and sub == 1))
                return pp, bp

            for c in range(12):
                pp, bp = rest_chunk(c)
                if c < 4:
                    dst, bd = (silA, bsilA) if c < 2 else (silB, bsilB)
                    dv = dst[:, (c % 2) * 2:(c % 2) * 2 + 2, :].rearrange("p a t -> p (a t)")
                    ACT(dv, pp[:, 0:256], AF.Tanh, [bp], [bd], scale=0.5)
                    STT(dv, dv, 1.0, pp[:, 0:256], ALU.add, ALU.mult, [bd, bp], [bd])
                else:
                    dst, bd = (thA, bthA) if c < 8 else (thB, bthB)
                    cc = (c - 4) % 4
                    ACT(dst[:, cc * 2:cc * 2 + 2, :].rearrange("p a t -> p (a t)"), pp[:, 0:256], AF.Tanh, [bp], [bd], scale=0.5)
            for (src, bsrc, sil, bsil, dst, bdst, lnb) in ((ynsa, bynsa, silA, bsilA, yaT, byaT, False), (yn, byn, silB, bsilB, ybT, bybT, True)):
                pp, bp = bank("proj")
                for c in range(4):
                    TR(pp[:, c * 128:(c + 1) * 128], src[:, c * 128:(c + 1) * 128], identf[:], [bsrc, bidf], [bp], inc=(c == 3))
                if not lnb:
                    STT(dst[:].rearrange("p c t -> p (c t)"), pp[:, :], 0.5, sil[:].rearrange("p c t -> p (c t)"), ALU.mult, ALU.mult, [bp, bsil], [bdst])
                else:
                    for c in range(4):
                        STT(tmpA[:, c, :], pp[:, c * 128:(c + 1) * 128], vec4[:, 3, c:c + 1], sil[:, c, :], ALU.add, ALU.mult, [bp, bvec4, bsil], [btmpA])
                    TS("pool", dst[:].rearrange("p c t -> p (c t)"), f4(tmpA), 0.5, None, ALU.mult, None, [btmpA], [bdst])
            dump(f"yaT_{T}", yaT[:], [byaT], BF16)
            dump(f"ybT_{T}", ybT[:], [bybT], BF16)
            for (yT_, byT_, W_, bW_, th, bth, mg, bmg) in ((yaT, byaT, Wouta, bWouta, thA, bthA, mg1, bmg1), (ybT, bybT, Woutb, bWoutb, thB, bthB, mg2, bmg2)):
                for half in range(2):
                    pp, bp = bank("proj")
                    for mm_ in range(4):
                        mc = half * 4 + mm_
                        for kc in range(4):
                            MM(pp[:, mm_ * 128:(mm_ + 1) * 128], W_[:, kc, mc * 128:(mc + 1) * 128], yT_[:, kc, :], start=(kc == 0), stop=(kc == 3),
                               r=[bW_, byT_], w=[bp], inc=(kc == 3 and mm_ == 3))
                    STT(mg[:, half * 4:(half + 1) * 4, :].rearrange("p a t -> p (a t)"), th[:, half * 4:(half + 1) * 4, :].rearrange("p a t -> p (a t)"), 1.0, pp[:, :],
                        ALU.add, ALU.mult, [bth, bp], [bmg])
            TT("pool", mgT[:].rearrange("p a t -> p (a t)"), mg1[:].rearrange("p a t -> p (a t)"), mg2[:].rearrange("p a t -> p (a t)"), ALU.add, [bmg1, bmg2], [bmgT])
            dump(f"mgT_{T}", mgT[:], [bmgT], BF16)
            if s == 1 and i == 0:
                DMA(Wog[:].rearrange("p k n -> p (k n)"), wog_s, sem_wog, w=[bWog])
            for half in range(2):
                pp, bp = bank("proj")
                for kc in range(8):
                    MM(pp[:, :], mgT[:, kc, :], Wog[:, kc, half * 512:(half + 1) * 512], start=(kc == 0), stop=(kc == 7), r=[bmgT, bWog], w=[bp], inc=(kc == 7))
                TT("dve", x_t[:, half * 512:(half + 1) * 512], pp[:, :], x_t[:, half * 512:(half + 1) * 512], ALU.add, [bp, bx], [bx])
            return DMA(out_d[tok0:tok0 + 128, :], x_t[:, :], sem_outs[T % 2], r=[bx], w=[])

        out_toks = []
        total = nseq * ntile
        seq_tiles = [(s, i) for s in range(nseq) for i in range(ntile)]
        DMA(xt[0][:], x_d[0:128, :], sem_x[0], w=[bxt[0]])
        for n_, (s, i) in enumerate(seq_tiles):
            T = s * 16 + i
            if n_ + 1 < total:
                s2, i2 = seq_tiles[n_ + 1]
                T2 = s2 * 16 + i2
                DMA(xt[T2 % 2][:], x_d[T2 * 128:(T2 + 1) * 128, :], sem_x[T2 % 2], w=[bxt[T2 % 2]])
            if i == 0:
                MSET("pool", kcT[:], 0.0, [bkcT])
                MSET("pool", vcT[:], 0.0, [bvcT])
                MSET("pool", vca[:, :, 0:64], 0.0, [bvca])
                MSET("pool", kvc[:], 0.0, [bkvc])
            out_toks.append(tile_body(s, i))
        S.wait_all("sp", out_toks[-4:] + dbg_outs + [(sm_, S.dcnt[sm_]) for sm_ in sem_outs])
        S.emit()
    return nc


_CACHE = {}


def kernel(**inputs):
    sh, per = host_prep(inputs)
    if "nc" not in _CACHE:
        _CACHE["nc"] = build()
    nc = _CACHE["nc"]
    in_maps = []
    for core in range(8):
        d = dict(sh)
        d.update(per[core])
        in_maps.append(d)
    res = run_bass_kernel_spmd(nc, in_maps, core_ids=list(range(8)))
    out = np.concatenate([np.asarray(r["out"]).reshape(2, 2048, 1024) for r in res.results], axis=0)
    return out.astype(np.float32)
```

```python
import math
import numpy as np
import concourse.bass as bass
import concourse.mybir as mybir
from concourse.bass_utils import run_bass_kernel_spmd
from contextlib import ExitStack

F32 = mybir.dt.float32
BF16 = mybir.dt.bfloat16
AF = mybir.ActivationFunctionType
ALU = mybir.AluOpType
AX = mybir.AxisListType

COMPUTE = ("pe", "act", "dve", "pool")
NEGM = -4096.0
NRES = 2968
CQ, CKV, CG, CC, CS = 0, 512, 1024, 1048, 1304


class Buf:
    __slots__ = ("w", "r")

    def __init__(self):
        self.w = None
        self.r = {}


class Sched:
    ANNOTATE = False

    def __init__(self, nc, es):
        self.nc = nc
        self.es = es
        self.prog = {e: [] for e in COMPUTE + ("sp",)}
        self.cnt = {e: 0 for e in COMPUTE}
        self.sems = {}
        for e in COMPUTE:
            self.sems[e] = es.enter_context(nc.semaphore("sem_" + e))
        self.known = {e: {} for e in self.prog}
        self.snap = {}
        self.dcnt = {}
        self.pending = {e: False for e in COMPUTE}
        self.last = {}

    def dma_sem(self, name):
        self.sems[name] = self.es.enter_context(self.nc.semaphore("sem_" + name))
        self.dcnt[name] = 0
        return name

    @staticmethod
    def _flat(bs):
        out = []
        for b in bs:
            if isinstance(b, (list, tuple)):
                out.extend(Sched._flat(b))
            else:
                out.append(b)
        return out

    def op(self, eng, fn, reads=(), writes=(), inc=True, dsem=None):
        reads = self._flat(reads)
        writes = self._flat(writes)
        need = {}

        def req(tok, same_ok):
            if tok is None:
                return
            k, v = tok
            if same_ok and k == eng and eng == "pe":
                return
            if need.get(k, 0) < v:
                need[k] = v

        for b in reads:
            req(b.w, False)
        for b in writes:
            req(b.w, True)
            for k, v in b.r.items():
                req((k, v), True)
        kn = self.known[eng]
        waits = []
        for k, v in need.items():
            if kn.get(k, 0) < v:
                waits.append((k, v))
                kn[k] = v
                sn = self.snap.get((k, v))
                if sn is not None:
                    for k2, v2 in sn.items():
                        if kn.get(k2, 0) < v2:
                            kn[k2] = v2
        if dsem is not None:
            self.dcnt[dsem] += 16
            tok = (dsem, self.dcnt[dsem])
            incspec = (dsem, 16)
        elif inc:
            self.cnt[eng] += 1
            tok = (eng, self.cnt[eng])
            incspec = (eng, 1)
            self.pending[eng] = False
            self.snap[tok] = dict(kn)
        else:
            tok = (eng, self.cnt[eng] + 1)
            incspec = None
            self.pending[eng] = True
        self.last[tok[0]] = tok[1]
        for b in writes:
            b.w = tok
            b.r = {}
        for b in reads:
            if b.w is tok:
                continue
            if b.r.get(tok[0], 0) < tok[1]:
                b.r[tok[0]] = tok[1]
        note = None
        if Sched.ANNOTATE:
            import sys as _sys
            f_ = _sys._getframe(1)
            while f_ is not None and f_.f_code.co_name not in ("tile_body2", "attn_gen", "rwkv_gen", "build", "finish", "pv", "five", "rest_chunk", "wsload"):
                f_ = f_.f_back
            note = f"L{f_.f_lineno}" if f_ is not None else None
        self.prog[eng].append((waits, fn, incspec, note))
        return tok

    def wait_all(self, eng, toks):
        kn = self.known[eng]
        waits = []
        mx = {}
        for k, v in toks:
            if mx.get(k, 0) < v:
                mx[k] = v
        for k, v in mx.items():
            if kn.get(k, 0) < v:
                waits.append((k, v))
                kn[k] = v
        self.prog[eng].append((waits, None, None, None))

    def barrier(self):
        for e in COMPUTE:
            if self.pending[e]:
                self.op(e, lambda en: en.nop(), (), ())
        toks = list(self.last.items())
        for e in self.prog:
            self.wait_all(e, toks)

    def emit(self):
        nc = self.nc
        for e in COMPUTE:
            if self.pending[e]:
                self.op(e, lambda en: en.nop(), (), ())
        sems = self.sems
        prog = self.prog

        def run(engname):
            def f(e):
                for waits, fn, incspec, note in prog[engname]:
                    for k, v in waits:
                        e.wait_ge(sems[k], v)
                    if fn is None:
                        continue
                    ins = fn(e)
                    if note is not None:
                        ins.annotate(note)
                    if incspec is not None:
                        ins.then_inc(sems[incspec[0]], incspec[1])
            return f

        with nc.Block() as block:
            block.sync(run("sp"))
            block.tensor(run("pe"))
            block.scalar(run("act"))
            block.vector(run("dve"))
            block.gpsimd(run("pool"))


def _t5_bucket(dist):
    n = np.maximum(dist, 0)
    nf = np.maximum(n, 16).astype(np.float32)
    large = 16 + (np.log(nf / np.float32(16)) / np.float32(math.log(128 / 16)) * np.float32(16)).astype(np.int32)
    return np.where(n < 16, n, np.minimum(large, 31))


def _perms():
    r = lambda a, b: list(range(a, b))
    res = (r(0, 512)
           + r(768, 832) + r(1024, 1088) + r(832, 896) + r(1088, 1152) + r(896, 1024) + r(1152, 1280)
           + r(1280, 1304)
           + r(512, 576) + r(640, 704) + r(576, 640) + r(704, 768)
           + r(1816, 3480))
    rest = r(1304, 1816) + r(3480, 3992) + r(3992, 5016) + r(5016, 6040)
    assert len(res) == NRES and len(rest) == 3072
    return np.array(res), np.array(rest)


def host_prep(inp):
    f = lambda k: np.ascontiguousarray(np.asarray(inp[k], dtype=np.float32))
    sh = {}
    pres, prest = _perms()
    w_in = f("w_in")[0]
    sh["w_res"] = np.ascontiguousarray(w_in[:, pres])
    sh["w_rest"] = np.ascontiguousarray(w_in[:, prest])
    sh["w_ada"] = f("w_ada")[0]
    sh["w_out_a"] = f("w_out_a")[0]
    sh["w_out_b"] = f("w_out_b")[0]
    sh["w_o"] = f("w_o")[0]
    sh["w1k"] = f("cmp_k_w1")[0]
    sh["w1v"] = f("cmp_v_w1")[0]
    col = lambda v, n: np.ascontiguousarray(v.reshape(n, 128).T)
    sh["b_ada"] = col(f("b_ada")[0], 24)
    sh["g_norm"] = col(f("norm_gain")[0], 8)
    sh["mu"] = col(f("shift_mu")[0], 13)
    vec4 = np.stack([col(f(k)[0].reshape(-1), 4) for k in ("k_k", "k_a", "r_k", "ln_x_b")], 1)
    sh["vec4"] = np.ascontiguousarray(vec4)
    rep = lambda v: np.ascontiguousarray(np.broadcast_to(v[None, :], (128, v.shape[0])))
    kng = f("k_norm_gain")[0]
    sh["bc_small"] = np.concatenate([rep(f("q_norm_gain")[0]), rep(kng[1]), rep(kng[2])], 1)
    sh["ln_w_bc"] = rep(f("ln_x_w")[0])
    sh["kgc"] = np.ascontiguousarray(kng[0].reshape(64, 1))
    sh["w0a0"] = np.ascontiguousarray(np.stack([f("w0")[0], f("a0")[0]], 0))
    sh["lora"] = np.ascontiguousarray(np.concatenate([f("w_lora_up")[0], f("a_lora_up")[0]], 0))
    w2 = lambda k: f(k)[0].reshape(2, 128, 64).transpose(1, 0, 2)
    sh["w2"] = np.ascontiguousarray(np.stack([w2("cmp_k_w2"), w2("cmp_v_w2")], 1))
    sh["peT"] = np.ascontiguousarray(np.concatenate([f("cmp_pos_k")[0].T, f("cmp_pos_v")[0].T], 0))
    tbl = f("rel_bias")
    k = np.arange(128)[:, None]
    q = np.arange(128)[None, :]
    tb = np.zeros((2, 2, 128, 4, 128), np.float32)
    for v, dist in enumerate((q - k, 128 + q - k)):
        bk = _t5_bucket(dist)
        for g in range(2):
            for h in range(4):
                tb[v, g, :, h, :] = tbl[bk, g * 4 + h]
    sh["tblDS"] = tb.reshape(2, 2, 128, 512)
    mk = np.zeros((128, 4, 128), np.float32)
    mk[np.broadcast_to(((q - k) < 0)[:, None, :], mk.shape)] = NEGM
    sh["maskD"] = mk.reshape(128, 512)
    c31 = np.zeros((2, 128, 4, 128), np.float32)
    for g in range(2):
        for h in range(4):
            c31[g, :, h, :] = tbl[31, g * 4 + h]
    sh["c31"] = c31.reshape(2, 128, 512)
    p = np.arange(16)[:, None]
    distc = q - 16 * p + 113
    bkc = _t5_bucket(distc)
    tc = np.zeros((2, 16, 4, 128), np.float32)
    for g in range(2):
        for h in range(4):
            tc[g, :, h, :] = tbl[bkc, g * 4 + h]
    sh["tblC"] = tc.reshape(2, 16, 512)
    mc = np.zeros((16, 4, 128), np.float32)
    mc[np.broadcast_to((distc < 0)[:, None, :], mc.shape)] = NEGM
    sh["maskC"] = mc.reshape(16, 512)
    sh["ident"] = np.eye(128, dtype=np.float32)
    far = np.where(k <= q, NEGM, 0.0).astype(np.float32)
    mus = (k < q).astype(np.float32)
    mui = (k <= q).astype(np.float32)
    mls = (k > q).astype(np.float32)
    sh["masks"] = np.ascontiguousarray(np.stack([far, mus, mui, mls], 1))
    z = np.zeros((16, 256), np.float32)
    z[np.arange(16), np.arange(16) + 119] = 1.0
    sh["zsh"] = z
    e = np.zeros((32, 2048), np.float32)
    e[np.arange(2048) // 64, np.arange(2048)] = -NEGM
    sh["emat"] = e
    mi = np.zeros((128, 32), np.float32)
    for j in range(32):
        for a in range(4):
            for b in range(2):
                n = 4 * j + a - b
                if 0 <= n < 127:
                    mi[n, j] += 1.0
    sh["mimp"] = mi
    ka = np.zeros((128, 8, 2, 32), np.float32)
    for i in range(8, 16):
        for qq in range(128):
            cur = (128 * i + qq) // 64
            for j in range(32):
                forced = (j == 0) or (j == cur) or (j == cur - 1)
                causal = j <= cur
                if forced:
                    ka[qq, i - 8, 0, j] = 0.0
                    ka[qq, i - 8, 1, j] = 1e30
                elif causal:
                    ka[qq, i - 8, 0, j] = 1.0
                else:
                    ka[qq, i - 8, 1, j] = -1e30
    sh["keepadd"] = ka.reshape(128, 512)
    ind2 = np.zeros((128, 2), np.float32)
    ind2[:64, 0] = 1.0
    ind2[64:, 1] = 1.0
    sh["ind2"] = ind2
    indT = np.zeros((8, 4, 128), np.float32)
    for h in range(8):
        indT[h, h // 2, (h % 2) * 64:(h % 2) * 64 + 64] = 1.0
    sh["indT"] = indT.reshape(8, 512)
    x = f("x")
    c = f("c")
    per = []
    for core in range(8):
        d = {"x": np.ascontiguousarray(x[2 * core:2 * core + 2].reshape(4096, 1024)),
             "cT": np.ascontiguousarray(c[2 * core:2 * core + 2].reshape(2, 8, 128).transpose(2, 1, 0))}
        per.append(d)
    return sh, per


class _Stop(Exception):
    pass


def build(nseq=2, ntile=16, dbg=None, stage=9):
    nc = bass.Bass("TRN2", target_bir_lowering=False)
    dbg = dbg or {}
    di = lambda name, shape: nc.dram_tensor(name, shape, F32, kind="ExternalInput").ap()
    x_d = di("x", [4096, 1024])
    cT_d = di("cT", [128, 8, 2])
    w_res_d = di("w_res", [1024, NRES])
    w_rest_d = di("w_rest", [1024, 3072])
    w_ada_d = di("w_ada", [1024, 3072])
    w_out_a_d = di("w_out_a", [512, 1024])
    w_out_b_d = di("w_out_b", [512, 1024])
    w_o_d = di("w_o", [1024, 1024])
    w1k_d = di("w1k", [2048, 256])
    w1v_d = di("w1v", [2048, 256])
    b_ada_d = di("b_ada", [128, 24])
    g_norm_d = di("g_norm", [128, 8])
    mu_d = di("mu", [128, 13])
    vec4_d = di("vec4", [128, 4, 4])
    bc_small_d = di("bc_small", [128, 192])
    ln_w_bc_d = di("ln_w_bc", [128, 512])
    kgc_d = di("kgc", [64, 1])
    w0a0_d = di("w0a0", [2, 512])
    lora_d = di("lora", [128, 512])
    w2_d = di("w2", [128, 2, 2, 64])
    peT_d = di("peT", [128, 32])
    tblDS_d = di("tblDS", [2, 2, 128, 512])
    maskD_d = di("maskD", [128, 512])
    c31_d = di("c31", [2, 128, 512])
    tblC_d = di("tblC", [2, 16, 512])
    maskC_d = di("maskC", [16, 512])
    ident_d = di("ident", [128, 128])
    masks_d = di("masks", [128, 4, 128])
    zsh_d = di("zsh", [16, 256])
    emat_d = di("emat", [32, 2048])
    mimp_d = di("mimp", [128, 32])
    keepadd_d = di("keepadd", [128, 512])
    ind2_d = di("ind2", [128, 2])
    indT_d = di("indT", [8, 512])
    out_d = nc.dram_tensor("out", [4096, 1024], F32, kind="ExternalOutput").ap()
    wrest_s = nc.dram_tensor("wrest_s", [12, 128, 2048], BF16, kind="Internal").ap()
    wog_s = nc.dram_tensor("wog_s", [128, 8192], BF16, kind="Internal").ap()

    with ExitStack() as es:
        S = Sched(nc, es)
        _n = [0]

        def sb(shape, dt, name=None):
            _n[0] += 1
            return es.enter_context(nc.sbuf_tensor("s_" + (name or f"sb{_n[0]}"), shape, dt))

        def psb(name):
            return es.enter_context(nc.psum_tensor(name, [128, 512], F32))

        dbg_outs = []

        def dump(name, ap, reads, dt=F32):
            if name not in dbg:
                return
            d = nc.dram_tensor("dbg_" + name, list(ap.shape), dt, kind="ExternalOutput").ap()
            dbg_outs.append(S.op("sp", lambda e: e.dma_start(out=d, in_=ap), reads, (), dsem=sem_dbg))

        def MM(out, lhsT, rhs, start=True, stop=True, r=(), w=(), inc=True, sgc=False):
            if sgc:
                return S.op("pe", lambda e: e.matmul(out, lhsT=lhsT, rhs=rhs, start=start, stop=stop, skip_group_check=True), r, w, inc=inc)
            return S.op("pe", lambda e: e.matmul(out, lhsT=lhsT, rhs=rhs, start=start, stop=stop), r, w, inc=inc)

        def TR(out, in_, ident, r=(), w=(), inc=True):
            return S.op("pe", lambda e: e.transpose(out=out, in_=in_, identity=ident), r, w, inc=inc)

        def ACT(out, in_, func, r=(), w=(), bias=None, scale=None, accum=None):
            kw = {}
            if bias is not None:
                kw["bias"] = bias
            if scale is not None:
                kw["scale"] = scale
            if accum is not None:
                kw["accum_out"] = accum
            return S.op("act", lambda e: e.activation(out=out, in_=in_, func=func, **kw), r, w)

        def TS(eng, out, in0, s1, s2, op0, op1=None, r=(), w=()):
            if op1 is None:
                return S.op(eng, lambda e: e.tensor_scalar(out=out, in0=in0, scalar1=s1, scalar2=None, op0=op0), r, w)
            return S.op(eng, lambda e: e.tensor_scalar(out=out, in0=in0, scalar1=s1, scalar2=s2, op0=op0, op1=op1), r, w)

        def TT(eng, out, in0, in1, op, r=(), w=()):
            return S.op(eng, lambda e: e.tensor_tensor(out=out, in0=in0, in1=in1, op=op), r, w)

        def STT(out, in0, scalar, in1, op0, op1, r=(), w=()):
            return S.op("dve", lambda e: e.scalar_tensor_tensor(out=out, in0=in0, scalar=scalar, in1=in1, op0=op0, op1=op1), r, w)

        def CP(eng, out, in_, r=(), w=()):
            if eng == "act":
                return S.op("act", lambda e: e.copy(out=out, in_=in_), r, w)
            return S.op(eng, lambda e: e.tensor_copy(out=out, in_=in_), r, w)

        def MSET(eng, ap, val, w=()):
            return S.op(eng, lambda e: e.memset(ap, val), (), w)

        def DMA(out, in_, sem, r=(), w=(), eng="sp"):
            return S.op(eng, lambda e: e.dma_start(out=out, in_=in_), r, w, dsem=sem)

        def bcast(ap, shape, axis):
            return ap.unsqueeze(axis).to_broadcast(shape)

        sem_dbg = S.dma_sem("dbg")
        sem_stg = [S.dma_sem("stg0"), S.dma_sem("stg1")]
        sem_scr = S.dma_sem("scr")
        sem_x = [S.dma_sem("x0"), S.dma_sem("x1")]
        sem_xr = S.dma_sem("xr")
        sem_ws = [S.dma_sem(f"ws{i}") for i in range(3)]
        sem_outs = [S.dma_sem("out0"), S.dma_sem("out1")]
        sem_wog = S.dma_sem("wog")

        PS = [psb(f"ps{i}") for i in range(8)]
        PSB = [Buf() for _ in range(8)]
        rot = {"proj": [0, 1], "sc": [2, 3], "acc": [4, 5], "rw": [6, 7], "tl": [4, 5, 6, 7]}
        rotc = {k: 0 for k in rot}

        def bank(cls):
            i = rot[cls][rotc[cls] % len(rot[cls])]
            rotc[cls] += 1
            return PS[i], PSB[i]

        NSLOT = 41
        AR = sb([128, NSLOT * 256], F32, "arena")
        SLB = [Buf() for _ in range(NSLOT)]

        def slot(start, shape, dt, P0=0):
            el = 4 if dt == F32 else 2
            n = int(np.prod(shape[1:]))
            nsl = (n * el + 1023) // 1024
            assert start + nsl <= NSLOT
            base = AR[:] if dt == F32 else AR[:].bitcast(BF16)
            o = start * 1024 // el
            ap = base[P0:P0 + shape[0], o:o + n]
            if len(shape) > 2:
                names = " ".join(f"d{i}" for i in range(len(shape) - 1))
                kw = {f"d{i}": shape[i + 1] for i in range(len(shape) - 1)}
                ap = ap.rearrange(f"p ({names}) -> p {names}", **kw)
            return ap, SLB[start:start + nsl]

        Wres = sb([128, 8, NRES], BF16, "Wres"); bWres = Buf()
        Wouta = sb([128, 4, 1024], BF16, "Wouta"); bWouta = Buf()
        Woutb = sb([128, 4, 1024], BF16, "Woutb"); bWoutb = Buf()
        Wog = sb([128, 8, 1024], BF16, "Wog"); bWog = Buf()
        W1c = sb([128, 32, 256], BF16, "W1c"); bW1c = Buf()
        W2c = sb([128, 2, 2, 64], BF16, "W2c"); bW2c = Buf()
        Lora = sb([128, 512], BF16, "Lora"); bLora = Buf()
        identf = sb([128, 128], F32, "identf"); bidf = Buf()
        identb = sb([128, 128], BF16, "identb"); bidb = Buf()
        masks = sb([128, 4, 128], BF16, "masks"); bmasks = Buf()
        biasDS = sb([128, 2, 2, 512], BF16, "biasDS"); bbias = Buf()
        emat = sb([64, 2048], BF16, "emat"); bemat = Buf()
        zsh = sb([128, 256], BF16, "zsh"); bzsh = Buf()
        biasC = sb([128, 2, 512], BF16, "biasC"); bbiasC = Buf()
        w0a0 = sb([128, 512], F32, "w0a0"); bw0a0 = Buf()
        bmisc = Buf()
        mimp = sb([128, 32], F32, "mimp"); bmimp = Buf()
        keepadd = sb([128, 8, 2, 32], F32, "keepadd"); bka = Buf()
        ind2 = sb([128, 2], F32, "ind2"); bind2 = Buf()
        indT = sb([8, 4, 128], F32, "indT"); bindT = Buf()
        ones_f = sb([128, 128], F32, "ones_f"); bones = Buf()
        bc_small = sb([128, 192], F32, "bc_small"); bbcs = Buf()
        ln_w_bc = sb([128, 512], F32, "ln_w_bc"); blnw = Buf()
        vec4 = sb([128, 4, 4], F32, "vec4"); bvec4 = Buf()
        mucol = sb([128, 2, 13], F32, "mucol"); bmu = Buf()
        kgc = sb([64, 1], F32, "kgc"); bkgc = Buf()
        gcol = sb([128, 8], F32, "gcol"); bgcol = Buf()
        badaT = sb([128, 24], F32, "badaT"); bbada = Buf()
        cTt = sb([128, 8, 2], F32, "cTt"); bcT = Buf()
        modT = sb([128, 24, 2], F32, "modT"); bmod = Buf()
        gsT = sb([128, 2, 8], F32, "gsT"); bgs = Buf()
        hb2 = sb([128, 2, 2], F32, "hb2"); bhb2 = Buf()
        cneg = sb([128, 16], F32, "cneg"); bcneg = Buf()
        peTb = sb([128, 32], BF16, "peTb"); bpeT = Buf()
        siluc = sb([128, 8, 2], F32, "siluc"); bsc = Buf()
        gtmp = sb([128, 16], F32, "gtmp"); bgtmp = Buf()

        stg = []; bstg = []
        for i_ in range(2):
            a_, b_ = slot(16 * i_, [128, 4096], F32)
            stg.append(a_); bstg.append(b_)
        kT = sb([128, 2, 2048], BF16, "kT"); bkT = [Buf() for _ in range(16)]
        Vcf = sb([128, 4160], BF16, "Vc"); bVc = [Buf() for _ in range(16)]
        Vc = Vcf[:].rearrange("p (a b c d) -> p a b c d", a=16, b=2, c=2)
        stgb = kT[:].rearrange("p a b -> p (a b)"); bstgb = bkT
        gate_bc = Vcf[:].bitcast(F32)[:, 0:2048].rearrange("p (s n) -> p s n", s=2); bgbc = bVc

        ldn = [0]
        sem_lds = [S.dma_sem(f"ld{i}") for i in range(8)]

        def ld(out, in_, w):
            sm = sem_lds[ldn[0] % 8]
            ldn[0] += 1
            if S.dcnt[sm] > 0:
                S.wait_all("sp", [(sm, S.dcnt[sm])])
            return DMA(out, in_, sm, w=w)

        ld(identf[:], ident_d, [bidf])
        CP("dve", identb[:], identf[:], [bidf], [bidb])
        ld(stg[0][:, 0:512].rearrange("p (a b) -> p a b", a=4), masks_d, [bstg[0]])
        CP("dve", masks[:], stg[0][:, 0:512].rearrange("p (a b) -> p a b", a=4), [bstg[0]], [bmasks])
        ld(mimp[:], mimp_d, [bmimp])
        ld(keepadd[:].rearrange("p a b c -> p (a b c)"), keepadd_d, [bka])
        ld(ind2[:], ind2_d, [bind2])
        ld(indT[:].rearrange("p a b -> p (a b)"), indT_d, [bindT])
        ld(bc_small[:], bc_small_d, [bbcs])
        ld(ln_w_bc[:], ln_w_bc_d, [blnw])
        ld(vec4[:], vec4_d, [bvec4])
        ld(mucol[:, 0, :], mu_d, [bmu])
        TS("dve", mucol[:, 1, :], mucol[:, 0, :], -1.0, 1.0, ALU.mult, ALU.add, [bmu], [bmu])
        ld(kgc[:], kgc_d, [bkgc])
        MSET("pool", w0a0[:], 0.0, [bw0a0])
        ld(w0a0[0:1, :], w0a0_d[0:1, :], [bw0a0])
        ld(w0a0[64:65, :], w0a0_d[1:2, :], [bw0a0])
        MSET("pool", emat[:], 0.0, [bemat])
        MSET("pool", zsh[:], 0.0, [bzsh])
        MSET("pool", biasC[:].rearrange("p a b -> p (a b)"), 0.0, [bbiasC])
        ld(gcol[:], g_norm_d, [bgcol])
        ld(badaT[:], b_ada_d, [bbada])
        ld(cTt[:], cT_d, [bcT])
        MSET("pool", ones_f[:], 1.0, [bones])
        MSET("pool", cneg[:], -0.5, [bcneg])
        ld(stg[1][0:16, 0:256], zsh_d, [bstg[1]])
        CP("dve", zsh[0:16, :], stg[1][0:16, 0:256], [bstg[1]], [bzsh])
        ld(stg[1][0:32, 0:2048], emat_d, [bstg[1]])
        CP("dve", emat[0:32, :], stg[1][0:32, 0:2048], [bstg[1]], [bemat])
        ld(stg[1][:, 2048:2560], lora_d, [bstg[1]])
        CP("dve", Lora[:], stg[1][:, 2048:2560], [bstg[1]], [bLora])
        ld(stg[1][:, 2560:2816].rearrange("p (a b c) -> p a b c", a=2, b=2), w2_d, [bstg[1]])
        CP("dve", W2c[:], stg[1][:, 2560:2816].rearrange("p (a b c) -> p a b c", a=2, b=2), [bstg[1]], [bW2c])
        ld(stg[1][:, 2816:2848], peT_d, [bstg[1]])
        CP("dve", peTb[:], stg[1][:, 2816:2848], [bstg[1]], [bpeT])
        for g in range(2):
            ld(stg[0][:, 0:512], c31_d[g], [bstg[0]])
            for v in range(2):
                ld(stg[1][:, 0:512], tblDS_d[v, g], [bstg[1]])
                TT("dve", stg[1][:, 0:512], stg[1][:, 0:512], stg[0][:, 0:512], ALU.subtract, [bstg[0], bstg[1]], [bstg[1]])
                if v == 0:
                    ld(stg[1][:, 512:1024], maskD_d, [bstg[1]])
                    STT(biasDS[:, v, g, :], stg[1][:, 0:512], 8.0, stg[1][:, 512:1024], ALU.mult, ALU.add, [bstg[1]], [bbias])
                else:
                    TS("dve", biasDS[:, v, g, :], stg[1][:, 0:512], 8.0, None, ALU.mult, None, [bstg[1]], [bbias])
            ld(stg[1][0:16, 0:512], tblC_d[g], [bstg[1]])
            ld(stg[1][0:16, 512:1024], maskC_d, [bstg[1]])
            TT("dve", stg[1][0:16, 0:512], stg[1][0:16, 0:512], stg[0][0:16, 0:512], ALU.subtract, [bstg[0], bstg[1]], [bstg[1]])
            STT(biasC[0:16, g, :], stg[1][0:16, 0:512], 8.0, stg[1][0:16, 512:1024], ALU.mult, ALU.add, [bstg[1]], [bbiasC])

        def stage_load(i, src_ap, ncols, nk=8):
            view = stg[i][:, 0:nk * ncols].rearrange("p (k n) -> p k n", k=nk)
            DMA(view, src_ap, sem_stg[i], w=[bstg[i]])
            return view

        si = 0
        for c0 in range(0, NRES, 512):
            n = min(512, NRES - c0)
            v = stage_load(si, w_res_d[:, c0:c0 + n].rearrange("(k p) n -> p k n", p=128), n)
            CP("dve" if si == 0 else "act", Wres[:, :, c0:c0 + n], v, [bstg[si]], [bWres])
            si ^= 1
        for c in range(6):
            v = stage_load(si, w_rest_d[:, c * 512:(c + 1) * 512].rearrange("(k p) n -> p k n", p=128), 512)
            sv = stgb[:, 0:4096].rearrange("p (k n) -> p k n", k=8)
            CP("dve" if si == 0 else "act", sv, v, [bstg[si]], [bstgb])
            for j_ in range(2):
                DMA(wrest_s[2 * c + j_].rearrange("p (k n) -> p k n", k=8),
                    stgb[:, 0:4096].rearrange("p (k j n) -> p k j n", k=8, j=2)[:, :, j_, :], sem_scr, r=[bstgb], w=[Buf()])
            si ^= 1
        for (wd_, Wt, bW) in ((w_out_a_d, Wouta, bWouta), (w_out_b_d, Woutb, bWoutb)):
            v = stage_load(si, wd_.rearrange("(k p) n -> p k n", p=128), 1024, nk=4)
            CP("dve" if si == 0 else "act", Wt[:], v, [bstg[si]], [bW])
            si ^= 1
        for (wd_, lo) in ((w1k_d, 0), (w1v_d, 64)):
            for hh in range(2):
                view = stg[si][lo:lo + 64, 0:4096].rearrange("p (k n) -> p k n", k=16)
                DMA(view, wd_[hh * 1024:(hh + 1) * 1024, :].rearrange("(k p) n -> p k n", p=64), sem_stg[si], w=[bstg[si]])
                CP("dve" if si == 0 else "act", W1c[lo:lo + 64, hh * 16:(hh + 1) * 16, :], view, [bstg[si]], [bW1c])
                si ^= 1
        ACT(siluc[:], cTt[:], AF.Tanh, [bcT], [bsc], scale=0.5)
        TS("dve", siluc[:], siluc[:], 0.5, 0.5, ALU.mult, ALU.add, [bsc], [bsc])
        TT("dve", siluc[:], siluc[:], cTt[:], ALU.mult, [bsc, bcT], [bsc])
        pm, bpm = bank("proj")
        silucb = sb([128, 8, 2], BF16, "silucb"); bscb = Buf()
        CP("dve", silucb[:], siluc[:], [bsc], [bscb])
        for c in range(6):
            v = stage_load(si, w_ada_d[:, c * 512:(c + 1) * 512].rearrange("(k p) n -> p k n", p=128), 512)
            vb = stgb[:, 0:4096].rearrange("p (k n) -> p k n", k=8)
            CP("dve" if si == 0 else "act", vb, v, [bstg[si]], [bstgb])
            for jj in range(4):
                j = c * 4 + jj
                for kc in range(8):
                    MM(pm[:, j * 2:j * 2 + 2], vb[:, kc, jj * 128:(jj + 1) * 128], silucb[:, kc, :], start=(kc == 0), stop=(kc == 7),
                       r=[bstgb, bscb], w=[bpm], inc=(kc == 7))
            si ^= 1
        TT("dve", modT[:], pm[:, 0:48].rearrange("p (j b) -> p j b", b=2), bcast(badaT[:], [128, 24, 2], 2), ALU.add, [bpm, bbada], [bmod])
        for s in range(2):
            STT(gsT[:, s, :], modT[:, 8:16, s], 1.0, gcol[:], ALU.add, ALU.mult, [bmod, bgcol], [bgs])
        CP("dve", gtmp[:].rearrange("p (s j) -> p s j", s=2), modT[:, 16:24, :].rearrange("p j s -> p s j"), [bmod], [bgtmp])
        for q4 in range(4):
            pg, bpg = bank("proj")
            for jq in range(4):
                qq = q4 * 4 + jq
                MM(pg[0:1, jq * 128:(jq + 1) * 128], gtmp[:, qq:qq + 1], identf[:], r=[bgtmp, bidf], w=[bpg], inc=(jq == 3))
            CP("dve", stg[1][0:1, q4 * 512:(q4 + 1) * 512], pg[0:1, 0:512], [bpg], [bstg[1]])
        for s in range(2):
            for hh in range(2):
                pb_, bpb_ = bank("proj")
                MM(pb_[:, :], ones_f[0:1, :], stg[1][0:1, s * 1024 + hh * 512: s * 1024 + hh * 512 + 512], r=[bones, bstg[1]], w=[bpb_])
                TS("dve", gate_bc[:, s, hh * 512:(hh + 1) * 512], pb_[:, :], 0.5, None, ALU.mult, None, [bpb_], [bgbc])
        for s in (1, 0):
            for hh in range(2):
                v = stage_load(0, w_o_d[:, hh * 512:(hh + 1) * 512].rearrange("(k p) n -> p k n", p=128), 512)
                TT("dve", Wog[:, :, hh * 512:(hh + 1) * 512], v, bcast(gate_bc[:, s, hh * 512:(hh + 1) * 512], [128, 8, 512], 1), ALU.mult,
                   [bstg[0], bgbc], [bWog])
            if s == 1:
                DMA(wog_s, Wog[:].rearrange("p k n -> p (k n)"), sem_scr, r=[bWog], w=[Buf()])
        for kv in range(2):
            lo = kv * 64
            ph, bph = bank("proj")
            for jh in range(2):
                for pos in range(32):
                    MM(ph[:, jh:jh + 1], W1c[lo:lo + 64, pos, jh * 128:(jh + 1) * 128], peTb[lo:lo + 64, pos:pos + 1],
                       start=(pos == 0), stop=(pos == 31), r=[bW1c, bpeT], w=[bph], inc=(pos == 31))
            CP("dve", hb2[:, kv, :], ph[:, 0:2], [bph], [bhb2])
        S.barrier()
        print("SBUF remaining before main alloc:", nc.sbuf_bytes_remaining)

        xt = [sb([128, 1024], F32, f"xt{i}") for i in range(2)]; bxt = [Buf(), Buf()]
        hT = [sb([128, 8, 130], BF16, f"hT{i}") for i in range(2)]; bhT = [Buf(), Buf()]
        for i_ in range(2):
            MSET("pool", hT[i_][:].rearrange("p a b -> p (a b)"), 0.0, [bhT[i_]])
        ynsa = sb([128, 512], F32, "ynsa"); bynsa = Buf()
        yn = sb([128, 512], F32, "yn"); byn = Buf()
        st12 = sb([128, 16], F32, "st12"); bst12 = Buf()
        rs12 = sb([128, 16], F32, "rs12"); brs12 = Buf()
        MSET("pool", Vcf[:], 1.0, bVc)
        gsig = sb([128, 3, 8], F32, "gsig"); bgsig = Buf()
        kvc = sb([128, 2, 144], BF16, "kvc"); bkvc = Buf()
        kcT = sb([64, 2, 128], BF16, "kcT"); bkcT = Buf()
        vcT = sb([64, 2, 128], F32, "vcT"); bvcT = Buf()
        vca = sb([128, 2, 65], F32, "vca"); bvca = Buf()
        MSET("pool", vca[:].rearrange("p a b -> p (a b)"), 1.0, [bvca])
        hu = sb([128, 64], F32, "hu"); bhu = Buf()
        hw_ = sb([128, 64], F32, "hw_"); bhw = Buf()
        hid = sb([128, 64], BF16, "hid"); bhid = Buf()
        kcs = sb([64, 48], F32, "kcs"); bkcs = Buf()
        coef = sb([128, 16], F32, "coef"); bcoef = Buf()
        impr = sb([128, 2, 32], F32, "impr"); bimpr = Buf()
        imp2 = sb([128, 32], F32, "imp2"); bimp2 = Buf()
        m8a = sb([128, 8], F32, "m8a"); bm8a = Buf()
        m8b = sb([128, 8], F32, "m8b"); bm8b = Buf()
        nsel = sb([128, 2, 32], F32, "nsel"); bnsel = Buf()
        nselT = sb([64, 2, 128], BF16, "nselT"); bnselT = Buf()
        MSET("pool", nselT[:].rearrange("p a b -> p (a b)"), 0.0, [bnselT])
        wdad = sb([128, 128], F32, "wdad"); bwdad = Buf()
        wdadb = sb([128, 128], BF16, "wdadb"); bwdadb = Buf()
        eLC = sb([128, 4], F32, "eLC"); beLC = Buf()
        rn8 = sb([128, 8], F32, "rn8"); brn8 = Buf()
        rn8T = sb([8, 128], F32, "rn8T"); brn8T = Buf()
        sbon = sb([128, 8], F32, "sbon"); bsbon = Buf()
        Pst = sb([128, 4, 64], F32, "Pst"); bPst = Buf()
        Pb = sb([128, 4, 64], BF16, "Pb"); bPb = Buf()
        lnst = sb([128, 32], F32, "lnst"); blnst = Buf()
        NWS = 2
        WS = [sb([128, 8, 256], BF16, f"WS{i}") for i in range(NWS)]; bWS = [Buf() for _ in range(NWS)]
        sq, bsq = slot(0, [128, 1024], F32)
        xs, bxs = sq, bsq
        qn2, bqn2 = slot(4, [128, 8, 2, 64], BF16)
        qT2, bqT2 = slot(35, [128, 8, 128], BF16)
        kn2, bkn2 = slot(8, [128, 2, 2, 64], BF16)
        PT = []; bPT = []
        NPT = 2
        for i_ in range(NPT):
            a_, b_ = slot(37 + i_, [128, 512], BF16)
            PT.append(a_); bPT.append(b_)
        PcT, bPcT = slot(39, [128, 512], F32)
        silA, bsilA = slot(15, [128, 4, 128], F32)
        silB, bsilB = slot(17, [128, 4, 128], F32)
        thA, bthA = slot(19, [128, 8, 128], BF16)
        thB, bthB = slot(21, [128, 8, 128], BF16)
        mg1, bmg1 = slot(23, [128, 8, 128], F32)
        mg2, bmg2 = slot(27, [128, 8, 128], F32)
        mgT, bmgT = slot(31, [128, 8, 128], BF16)
        yaT, byaT = slot(33, [128, 4, 128], BF16)
        ybT, bybT = slot(34, [128, 4, 128], BF16)
        rT, brT = slot(4, [128, 4, 128], F32)
        kTr, bkTr = slot(6, [128, 4, 128], F32)
        vT, bvT = slot(8, [128, 4, 128], F32)
        lwT, blw = slot(10, [128, 4, 128], F32)
        LT, bLT = slot(12, [128, 4, 128], F32)
        asg, basg = slot(14, [128, 4, 128], F32)
        e1, be1 = slot(16, [128, 4, 128], F32)
        e2, be2 = slot(18, [128, 4, 128], F32)
        e3, be3 = slot(20, [128, 4, 128], F32)
        kkn, bkkn = slot(22, [128, 4, 128], F32)
        kmod, bkmod = slot(24, [128, 4, 128], F32)
        tmpA, btmpA = slot(26, [128, 4, 128], F32)
        At, bAt = slot(28, [128, 4, 128], BF16)
        Bt, bBt = slot(29, [128, 4, 128], BF16)
        Kt, bKt = slot(30, [128, 4, 128], BF16)
        Rt, bRt = slot(31, [128, 4, 128], BF16)
        Bh, bBh = slot(32, [128, 512], BF16)
        Kh, bKh = slot(33, [128, 512], BF16)
        Vt, bVt = slot(34, [128, 512], BF16)
        Qm = []; bQm = []; QmT = []; bQmT = []; Xm = []; bXm = []
        for st_ in (10, 12):
            a_, b_ = slot(st_, [128, 8, 128], BF16); Qm.append(a_); bQm.append(b_)
        for st_ in (14, 18):
            a_, b_ = slot(st_, [128, 8, 128], BF16); QmT.append(a_); bQmT.append(b_)
        for st_ in (20, 22):
            a_, b_ = slot(st_, [128, 8, 128], BF16); Xm.append(a_); bXm.append(b_)
        AakT, bAak = slot(24, [128, 8, 128], BF16)
        MrbT, bMrb = slot(4, [128, 8, 128], BF16)
        MrkT, bMrk = slot(6, [128, 8, 128], BF16)
        rhs0, brhs0 = slot(8, [128, 8, 64], BF16)
        Ub, bUb = slot(9, [128, 8, 64], BF16)

        def f4(t):
            return t.rearrange("p c t -> p (c t)")

        def REDUCE(out, in_, r, w):
            return S.op("dve", lambda e: e.tensor_reduce(out=out, in_=in_, axis=AX.X, op=ALU.add), r, w)

        def MAX8(out, in_, r, w):
            return S.op("dve", lambda e: e.max(out=out, in_=in_), r, w)

        def MREP(out, rep, vals, r, w):
            return S.op("dve", lambda e: e.match_replace(out=out, in_to_replace=rep, in_values=vals, imm_value=-3.0e38), r, w)

        def RECIP(out, in_, r, w):
            return S.op("dve", lambda e: e.reciprocal(out=out, in_=in_), r, w)

        def SCAN(out, d0, d1, r, w):
            return S.op("dve", lambda e: e.tensor_tensor_scan(out=out, data0=d0, data1=d1, initial=0.0, op0=ALU.mult, op1=ALU.add), r, w)

        print("SBUF remaining:", nc.sbuf_bytes_remaining)
        ws_i = [0]

        def chk(n):
            if stage <= n:
                raise _Stop()

        def tile_body2(s, i):
            T = s * 16 + i
            yield "f"
            tok0 = T * 128
            yield "f"
            xb_ = T % 2
            yield "f"
            x_t = xt[xb_]; bx = bxt[xb_]
            yield "f"
            hcur = hT[T % 2]; bh = bhT[T % 2]
            yield "f"
            hprev = hT[(T + 1) % 2]; bhp = bhT[(T + 1) % 2]
            yield "f"
            ACT(sq[:], x_t[:], AF.Square, [bx], [bsq, bst12], accum=st12[:, 0:1])
            yield "f"
            TS("dve", st12[:, 0:1], st12[:, 0:1], 1.0 / 1024, 1e-6, ALU.mult, ALU.add, [bst12], [bst12])
            yield "f"
            TT("pool", rs12[:, 0:1], st12[:, 0:1], cneg[:, 0:1], ALU.pow, [bst12, bcneg], [brs12])
            yield "f"
            TS("dve", xs[:], x_t[:], rs12[:, 0:1], None, ALU.mult, None, [bx, brs12], [bxs])
            yield "f"
            import os as _os
            yield "f"
            _sk = _os.environ.get("SKIP", "")
            yield "f"
            if i == 0:
                if "m" not in _sk:
                    MSET("pool", hcur[:, :, 0:1], 0.0, [bh])
            else:
                CP("pool", hcur[:, :, 0:1], hprev[:, :, 128:129], [bhp], [bh])
            yield "f"
            for half in range(2):
                pp, bp = bank("proj")
                for j in range(4):
                    kc = half * 4 + j
                    TR(pp[:, j * 128:(j + 1) * 128], xs[:, kc * 128:(kc + 1) * 128], identf[:], [bxs, bidf], [bp], inc=(j == 3))
                for j in range(4):
                    kc = half * 4 + j
                    if "a" in _sk:
                        ACT(hcur[:, kc, 1:129], pp[:, j * 128:(j + 1) * 128], AF.Identity, [bp, bgs, bmod], [bh])
                    elif "b" in _sk:
                        ACT(hcur[:, kc, 2:130], pp[:, j * 128:(j + 1) * 128], AF.Identity, [bp, bgs, bmod], [bh],
                            bias=modT[:, kc, s:s + 1], scale=gsT[:, s, kc:kc + 1])
                    else:
                        ACT(hcur[:, kc, 1:129], pp[:, j * 128:(j + 1) * 128], AF.Identity, [bp, bgs, bmod], [bh],
                            bias=modT[:, kc, s:s + 1], scale=gsT[:, s, kc:kc + 1])
            yield "f"
            dump(f"hT_{T}", hcur[:], [bh], BF16)
            yield "f"
            chk(1)
            yield "f"

            pq, bpq = bank("proj")
            yield "f"
            for kc in range(8):
                MM(pq[:, :], hcur[:, kc, 1:129], Wres[:, kc, CQ:CQ + 512], start=(kc == 0), stop=(kc == 7), r=[bh, bWres], w=[bpq], inc=(kc == 7))
            yield "f"
            ACT(sq[:, 0:512], pq[:, :], AF.Square, [bpq], [bsq])
            yield "f"
            REDUCE(st12[:, 0:8], sq[:, 0:512].rearrange("p (h d) -> p h d", h=8), [bsq], [bst12])
            yield "f"
            pkv, bpkv = bank("proj")
            yield "f"
            for kc in range(8):
                MM(pkv[:, :], hcur[:, kc, 1:129], Wres[:, kc, CKV:CKV + 512], start=(kc == 0), stop=(kc == 7), r=[bh, bWres], w=[bpkv], inc=(kc == 7))
            yield "f"
            ACT(sq[:, 512:768], pkv[:, 0:256], AF.Square, [bpkv], [bsq])
            yield "f"
            REDUCE(st12[:, 8:12], sq[:, 512:768].rearrange("p (h d) -> p h d", h=4), [bsq], [bst12])
            yield "f"
            TS("dve", st12[:, 0:12], st12[:, 0:12], 1.0 / 64, 1e-6, ALU.mult, ALU.add, [bst12], [bst12])
            yield "f"
            TT("pool", rs12[:, 0:12], st12[:, 0:12], cneg[:, 0:12], ALU.pow, [bst12, bcneg], [brs12])
            yield "f"
            chk(1.2)
            yield "f"
            for h in range(8):
                STT(qn2[:, h, :, :], bcast(pq[:, h * 64:(h + 1) * 64], [128, 2, 64], 1), rs12[:, h:h + 1],
                    bcast(bc_small[:, 0:64], [128, 2, 64], 1), ALU.mult, ALU.mult, [bpq, brs12, bbcs], [bqn2])
            yield "f"
            for gg in range(2):
                for br in range(2):
                    c0 = gg * 128 + br * 64
                    STT(kn2[:, gg, br, :], pkv[:, c0:c0 + 64], rs12[:, 8 + gg * 2 + br:9 + gg * 2 + br],
                        bc_small[:, 64 + br * 64:128 + br * 64], ALU.mult, ALU.mult, [bpkv, brs12, bbcs], [bkn2])
            yield "f"
            CP("act", Vc[:, i, :, :, 0:64], pkv[:, 256:512].rearrange("p (b g d) -> p b g d", b=2, g=2), [bpkv], [bVc[i]])
            yield "f"
            chk(1.4)
            yield "f"
            pt, bpt = bank("proj")
            yield "f"
            ptb = pt[:].bitcast(BF16)
            yield "f"
            for h in range(8):
                TR(ptb[:, h * 128:(h + 1) * 128], qn2[:, h, :, :].rearrange("p c d -> p (c d)"), identb[:], [bqn2, bidb], [bpt], inc=(h == 7))
            yield "f"
            CP("act", qT2[:].rearrange("p h q -> p (h q)"), ptb[:, 0:1024], [bpt], [bqT2])
            yield "f"
            pt2, bpt2 = bank("proj")
            yield "f"
            pt2b = pt2[:].bitcast(BF16)
            yield "f"
            for gg in range(2):
                TR(pt2b[:, gg * 128:(gg + 1) * 128], kn2[:, gg, :, :].rearrange("p c d -> p (c d)"), identb[:], [bkn2, bidb], [bpt2], inc=(gg == 1))
            yield "f"
            CP("dve", kT[:, :, i * 128:(i + 1) * 128], pt2b[:, 0:256].rearrange("p (g t) -> p g t", g=2), [bpt2], [bkT[i]])
            yield "f"
            chk(1.6)
            yield "f"
            pgt, bpgt = bank("proj")
            yield "f"
            for kc in range(8):
                MM(pgt[:, 0:24], hcur[:, kc, 1:129], Wres[:, kc, CG:CG + 24], start=(kc == 0), stop=(kc == 7), r=[bh, bWres], w=[bpgt], inc=(kc == 7))
            yield "f"
            ACT(gsig[:].rearrange("p a b -> p (a b)"), pgt[:, 0:24], AF.Tanh, [bpgt], [bgsig], scale=0.5)
            yield "f"
            TS("dve", gsig[:].rearrange("p a b -> p (a b)"), gsig[:].rearrange("p a b -> p (a b)"), 0.5, 0.5, ALU.mult, ALU.add, [bgsig], [bgsig])
            yield "f"
            pcm, bpcm = bank("proj")
            yield "f"
            for gg in range(2):
                for kc in range(8):
                    MM(pcm[:, gg * 128:(gg + 1) * 128], Wres[:, kc, CC + gg * 128:CC + (gg + 1) * 128], hcur[:, kc, 1:129],
                       start=(kc == 0), stop=(kc == 7), r=[bh, bWres], w=[bpcm], inc=(kc == 7 and gg == 1))
            yield "f"
            CP("pool", kvc[:, :, 0:16], kvc[:, :, 128:144], [bkvc], [bkvc])
            yield "f"
            CP("act", kvc[:, :, 16:144], pcm[:, 0:256].rearrange("p (g t) -> p g t", g=2), [bpcm], [bkvc])
            yield "f"
            chk(1.8)
            yield "f"
            m0 = 1 if i == 0 else 0
            yield "f"
            nm = 8 - m0
            yield "f"
            for kv in range(2):
                lo = kv * 64
                phd, bphd = bank("proj")
                for jh in range(2):
                    for pos in range(32):
                        MM(phd[:, jh * 16:jh * 16 + 16].rearrange("p (g m) -> p g m", g=2), W1c[lo:lo + 64, pos, jh * 128:(jh + 1) * 128],
                           kvc[lo:lo + 64, :, pos:pos + 113:16], start=(pos == 0), stop=(pos == 31), r=[bW1c, bkvc], w=[bphd],
                           inc=(pos == 31))
                for jh in range(2):
                    reg = (kv * 2 + jh) * 16
                    ACT(hu[:, reg:reg + 16], phd[:, jh * 16:jh * 16 + 16], AF.Identity, [bphd, bhb2], [bhu], bias=hb2[:, kv, jh:jh + 1])
            yield "f"
            chk(1.85)
            yield "f"
            TT("dve", hw_[:], hu[:], hu[:], ALU.mult, [bhu], [bhw])
            yield "f"
            TS("dve", hw_[:], hw_[:], 0.044715, 1.0, ALU.mult, ALU.add, [bhw], [bhw])
            yield "f"
            TT("dve", hw_[:], hw_[:], hu[:], ALU.mult, [bhw, bhu], [bhw])
            yield "f"
            ACT(hw_[:], hw_[:], AF.Tanh, [bhw], [bhw], scale=math.sqrt(2.0 / math.pi))
            yield "f"
            STT(hid[:], hw_[:], 1.0, hu[:], ALU.add, ALU.mult, [bhw, bhu], [bhid])
            yield "f"
            chk(1.9)
            yield "f"
            pc2, bpc2 = bank("proj")
            yield "f"
            for kv in range(2):
                for jh in range(2):
                    reg = (kv * 2 + jh) * 16
                    MM(pc2[0:64, kv * 16:(kv + 1) * 16], W2c[:, kv, jh, :], hid[:, reg:reg + 16], start=(jh == 0), stop=(jh == 1),
                       r=[bW2c, bhid], w=[bpc2], inc=(jh == 1))
            yield "f"
            TS("dve", kcs[:, 0:16], pc2[0:64, 0:16], 0.5, None, ALU.mult, None, [bpc2], [bkcs])
            yield "f"
            TT("dve", kcs[:, 16:32], kcs[:, 0:16], kcs[:, 0:16], ALU.mult, [bkcs], [bkcs])
            yield "f"
            MM(pc2[0:64, 64:80], ones_f[0:64, 0:64], kcs[:, 16:32], r=[bones, bkcs], w=[bpc2])
            yield "f"
            TS("dve", kcs[:, 32:48], pc2[0:64, 64:80], 1.0 / 64, 1e-6, ALU.mult, ALU.add, [bpc2], [bkcs])
            yield "f"
            TT("pool", kcs[:, 16:32], kcs[:, 32:48], cneg[0:64, 0:16], ALU.pow, [bkcs, bcneg], [bkcs])
            yield "f"
            TT("dve", kcs[:, 0:16], kcs[:, 0:16], kcs[:, 16:32], ALU.mult, [bkcs], [bkcs])
            yield "f"
            n0 = 8 * i - 1 + m0
            yield "f"
            TS("dve", kcT[:, :, n0:n0 + nm], kcs[:, 0:16].rearrange("p (g m) -> p g m", g=2)[:, :, m0:8], kgc[:, 0:1], None, ALU.mult, None,
               [bkcs, bkgc], [bkcT])
            yield "f"
            TS("dve", vcT[:, :, n0:n0 + nm], pc2[0:64, 16:32].rearrange("p (g m) -> p g m", g=2)[:, :, m0:8], 0.5, None, ALU.mult, None,
               [bpc2], [bvcT])
            yield "f"
            nv = 8 * i + 7
            yield "f"
            chk(1.95)
            yield "f"
            pvt, bpvt = bank("proj")
            yield "f"
            for gg in range(2):
                TR(pvt[0:nv, gg * 64:(gg + 1) * 64], vcT[:, gg, 0:nv], identf[0:64, 0:64], [bvcT, bidf], [bpvt], inc=(gg == 1))
            yield "f"
            CP("dve", vca[0:nv, :, 0:64], pvt[0:nv, 0:128].rearrange("p (g d) -> p g d", g=2), [bpvt], [bvca])
            yield "f"
            dump(f"kcT_{T}", kcT[:], [bkcT], BF16)
            yield "f"
            dump(f"vca_{T}", vca[:], [bvca])
            yield "f"
            dump(f"qT2_{T}", qT2[:], [bqT2], BF16)
            yield "f"
            chk(2)
            yield "f"

            yield "front_done"
            def attn_gen():
                first_y = {0: True, 1: True}

                def finish(acc, bacc, br, gg):
                    accv = acc[:, 0:260].rearrange("p (h e) -> p h e", h=4)
                    c0 = br * 4
                    TS("dve", coef[:, c0:c0 + 4], accv[:, :, 64], 1e-30, None, ALU.max, None, [bacc], [bcoef])
                    RECIP(coef[:, c0:c0 + 4], coef[:, c0:c0 + 4], [bcoef], [bcoef])
                    if br == 0:
                        CP("dve", coef[:, 12:16], coef[:, 0:4], [bcoef], [bcoef])
                    gbr = {0: 0, 1: 1, 2: 2}[br]
                    TT("dve", coef[:, c0:c0 + 4], coef[:, c0:c0 + 4], gsig[:, gbr, gg * 4:(gg + 1) * 4], ALU.mult, [bcoef, bgsig], [bcoef])
                    yv = ynsa[:, gg * 256:(gg + 1) * 256].rearrange("p (h d) -> p h d", h=4)
                    cb = bcast(coef[:, c0:c0 + 4], [128, 4, 64], 2)
                    if first_y[gg]:
                        TT("dve", yv, accv[:, :, 0:64], cb, ALU.mult, [bacc, bcoef], [bynsa])
                        first_y[gg] = False
                    else:
                        for h in range(4):
                            STT(yv[:, h, :], accv[:, h, 0:64], coef[:, c0 + h:c0 + h + 1], yv[:, h, :], ALU.mult, ALU.add, [bacc, bcoef, bynsa], [bynsa])

                def pv(acc, bacc, Pt_, bP, vrhs, bv, first, last, K=128):
                    for h in range(4):
                        MM(acc[:, h * 65:(h + 1) * 65], Pt_[0:K, h * 128:(h + 1) * 128], vrhs, start=(first and h == 0), stop=(last and h == 3), r=[bP] + bv, w=[bacc],
                           inc=(h == 3), sgc=True)

                pti = [0]
                for gg in range(2):
                    sc, bsc_ = bank("sc")
                    MM(sc[0:nv, :], kcT[:, gg, 0:nv], qT2[0:64, gg * 4:(gg + 1) * 4, :].rearrange("p h q -> p (h q)"), start=True, stop=False,
                       r=[bkcT, bqT2], w=[bsc_], inc=False)
                    off = 128 - 8 * i
                    MM(sc[0:nv, :], zsh[:, off:off + nv], biasC[:, gg, :], start=False, stop=True, r=[bzsh], w=[bsc_])
                    ACT(PcT[0:nv, :], sc[0:nv, :], AF.Exp, [bsc_], [bPcT], scale=0.125)
                    acc, bacc = bank("acc")
                    for h in range(4):
                        MM(acc[:, h * 65:(h + 1) * 65], PcT[0:nv, h * 128:(h + 1) * 128], vca[0:nv, gg, :], r=[bPcT, bvca], w=[bacc], inc=False)
                    for h in range(4):
                        MM(acc[:, 320 + h * 32:320 + (h + 1) * 32], PcT[0:nv, h * 128:(h + 1) * 128], mimp[0:nv, :], r=[bPcT, bmimp], w=[bacc], inc=(h == 3))
                    finish(acc, bacc, 0, gg)
                    yield
                    if i >= 8:
                        iv = impr[:, gg, :]
                        TS("dve", iv, acc[:, 320:352], coef[:, 12:13], None, ALU.mult, None, [bacc, bcoef], [bimpr])
                        for h in range(1, 4):
                            STT(iv, acc[:, 320 + h * 32:352 + h * 32], coef[:, 12 + h:13 + h], iv, ALU.mult, ALU.add, [bacc, bcoef, bimpr], [bimpr])
                        TT("dve", iv, iv, keepadd[:, i - 8, 0, :], ALU.mult, [bimpr, bka], [bimpr])
                        TT("dve", iv, iv, keepadd[:, i - 8, 1, :], ALU.add, [bimpr, bka], [bimpr])
                        MAX8(m8a[:], iv, [bimpr], [bm8a])
                        MREP(imp2[:], m8a[:], iv, [bimpr, bm8a], [bimp2])
                        MAX8(m8b[:], imp2[:], [bimp2], [bm8b])
                        TS("dve", nsel[:, gg, :], iv, m8b[:, 7:8], 1.0, ALU.is_ge, ALU.subtract, [bimpr, bm8b], [bnsel])
                        pn, bpn = bank("sc")
                        TR(pn[0:32, 0:128], nsel[:, gg, :], identf[:], [bnsel, bidf], [bpn])
                        CP("dve", nselT[0:32, gg, :], pn[0:32, 0:128], [bpn], [bnselT])
                dump(f"nsel_{T}", nsel[:], [bnsel])
                for br in (2, 1):
                    for gg in range(2):
                        lo = 0 if br == 1 else 64
                        j0 = 0 if br == 1 else max(0, i - 4)
                        acc, bacc = bank("acc")
                        for j in range(j0, i + 1):
                            sc, bsc_ = bank("sc")
                            extra = []
                            if j == i:
                                extra.append((identb[:], biasDS[:, 0, gg, :], [bidb, bbias]))
                            if j == i - 1:
                                extra.append((identb[:], biasDS[:, 1, gg, :], [bidb, bbias]))
                            if br == 2 and j == i - 4:
                                extra.append((identb[:], bcast(masks[:, 0, :], [128, 4, 128], 1), [bidb, bmasks]))
                            if br == 1 and i >= 8:
                                extra.append((emat[:, j * 128:(j + 1) * 128], bcast(nselT[:, gg, :], [64, 4, 128], 1), [bemat, bnselT]))
                            MM(sc[:, :], kT[lo:lo + 64, gg, j * 128:(j + 1) * 128], qT2[lo:lo + 64, gg * 4:(gg + 1) * 4, :].rearrange("p h q -> p (h q)"),
                               start=True, stop=(len(extra) == 0), r=[bkT[j], bqT2], w=[bsc_], inc=(len(extra) == 0))
                            for ei, (l_, r_, bb_) in enumerate(extra):
                                lastx = ei == len(extra) - 1
                                MM(sc[:, :].rearrange("p (h q) -> p h q", h=4) if len(r_.shape) == 3 else sc[:, :], l_, r_, start=False, stop=lastx,
                                   r=bb_, w=[bsc_], inc=lastx)
                            Pt_ = PT[pti[0] % NPT]; bP = bPT[pti[0] % NPT]; pti[0] += 1
                            ACT(Pt_[:, :], sc[:, :], AF.Exp, [bsc_], [bP], scale=0.125)
                            pv(acc, bacc, Pt_, bP, Vc[:, j, br - 1, gg, :], [bVc[j]], j == j0, j == i)
                            yield
                        finish(acc, bacc, br, gg)
                        yield
                dump(f"ynsa_{T}", ynsa[:], [bynsa])
                chk(3)

                yield
            def rwkv_gen():
                yield
                for c3 in range(0, 13, 3):
                    ps_, bps_ = bank("rw")
                    ncs = min(3, 13 - c3)
                    for cc in range(ncs):
                        c = c3 + cc
                        for kc in range(8):
                            MM(ps_[:, cc * 129:(cc + 1) * 129], Wres[:, kc, CS + c * 128:CS + (c + 1) * 128], hcur[:, kc, 0:129],
                               start=(kc == 0), stop=(kc == 7), r=[bh, bWres], w=[bps_], inc=(kc == 7))
                    for cc in range(ncs):
                        c = c3 + cc
                        if c < 4:
                            dst, bd = rT[:, c, :], brT
                        elif c < 8:
                            dst, bd = kTr[:, c - 4, :], bkTr
                        elif c < 12:
                            dst, bd = vT[:, c - 8, :], bvT
                        else:
                            dst, bd = wdad[:, :], bwdad
                        ACT(dst, ps_[:, cc * 129 + 1:cc * 129 + 129], AF.Identity, [bps_, bmu], [bd], scale=mucol[:, 1, c:c + 1])
                        STT(dst, ps_[:, cc * 129:cc * 129 + 128], mucol[:, 0, c:c + 1], dst, ALU.mult, ALU.add, [bps_, bmu, bd], [bd])
                yield
                dump(f"rT_{T}", rT[:], [brT])
                yield
                dump(f"wdad_{T}", wdad[:], [bwdad])
                yield
                ACT(wdadb[0:64, :], wdad[0:64, :], AF.Tanh, [bwdad], [bwdadb])
                yield
                CP("pool", wdadb[64:128, :], wdad[64:128, :], [bwdad], [bwdadb])
                yield
                pz, bpz = bank("rw")
                yield
                pa, bpa = bank("rw")
                yield
                for c in range(4):
                    MM(pz[:, c * 128:(c + 1) * 128], Lora[0:64, c * 128:(c + 1) * 128], wdadb[0:64, :], start=True, stop=False, r=[bLora, bwdadb], w=[bpz], inc=False)
                    MM(pz[:, c * 128:(c + 1) * 128], w0a0[0:64, c * 128:(c + 1) * 128], ones_f[0:64, :], start=False, stop=True, r=[bw0a0, bones], w=[bpz], inc=(c == 3))
                yield
                for c in range(4):
                    MM(pa[:, c * 128:(c + 1) * 128], Lora[64:128, c * 128:(c + 1) * 128], wdadb[64:128, :], start=True, stop=False, r=[bLora, bwdadb], w=[bpa], inc=False)
                    MM(pa[:, c * 128:(c + 1) * 128], w0a0[64:128, c * 128:(c + 1) * 128], ones_f[64:128, :], start=False, stop=True, r=[bw0a0, bones], w=[bpa], inc=(c == 3))
                yield
                f4 = lambda t: t[:].rearrange("p c t -> p (c t)")
                yield
                ACT(f4(lwT), pz[:, :], AF.Tanh, [bpz], [blw], scale=0.5)
                yield
                cexp = math.exp(-0.5) * 0.5
                yield
                TS("dve", f4(lwT), f4(lwT), -cexp, -cexp, ALU.mult, ALU.add, [blw], [blw])
                yield
                ACT(f4(asg), pa[:, :], AF.Tanh, [bpa], [basg], scale=0.5)
                yield
                TS("pool", f4(asg), f4(asg), 0.5, 0.5, ALU.mult, ALU.add, [basg], [basg])
                yield
                for c in range(4):
                    SCAN(LT[:, c, :], ones_f[:, :], lwT[:, c, :], [bones, blw], [bLT])
                yield
                TT("pool", f4(tmpA), f4(LT), f4(lwT), ALU.subtract, [bLT, blw], [btmpA])
                yield
                ACT(f4(e1), f4(tmpA), AF.Exp, [btmpA], [be1])
                yield
                ACT(f4(e2), f4(LT), AF.Exp, [bLT], [be2], scale=-1.0)
                yield
                ACT(f4(e3), f4(LT), AF.Exp, [bLT], [be3])
                yield
                ACT(eLC[:, :], LT[:, :, 127], AF.Exp, [bLT], [beLC])
                yield
                yield
                for c in range(4):
                    TS("dve", kkn[:, c, :], kTr[:, c, :], vec4[:, 0, c:c + 1], None, ALU.mult, None, [bkTr, bvec4], [bkkn])
                yield
                TT("pool", f4(tmpA), f4(kkn), f4(kkn), ALU.mult, [bkkn], [btmpA])
                yield
                pk_, bpk_ = bank("rw")
                yield
                for c in range(4):
                    MM(pk_[:, c * 2:c * 2 + 2], tmpA[:, c, :], ind2[:, :], r=[btmpA, bind2], w=[bpk_], inc=(c == 3))
                yield
                TS("dve", rn8[:, :], pk_[:, 0:8], 1e-24, None, ALU.max, None, [bpk_], [brn8])
                yield
                TT("pool", rn8[:, :], rn8[:, :], cneg[:, 0:8], ALU.pow, [brn8, bcneg], [brn8])
                yield
                TR(pk_[0:8, 128:256], rn8[:, :], identf[:], [brn8, bidf], [bpk_])
                yield
                CP("dve", rn8T[:, :], pk_[0:8, 128:256], [bpk_], [brn8T])
                yield
                pr_, bpr_ = bank("rw")
                yield
                for c in range(4):
                    MM(pr_[:, c * 128:(c + 1) * 128], indT[:, c, :], rn8T[:, :], r=[bindT, brn8T], w=[bpr_], inc=(c == 3))
                yield
                TT("dve", f4(kkn), f4(kkn), pr_[:, :], ALU.mult, [bkkn, bpr_], [bkkn])
                yield
                dump(f"kkn_{T}", kkn[:], [bkkn])
                yield
                yield
                for c in range(4):
                    TS("pool", tmpA[:, c, :], asg[:, c, :], -1.0, vec4[:, 1, c:c + 1], ALU.add, ALU.mult, [basg, bvec4], [btmpA])
                yield
                STT(f4(kmod), f4(tmpA), 1.0, f4(kTr), ALU.add, ALU.mult, [btmpA, bkTr], [bkmod])
                yield
                dump(f"kmod_{T}", kmod[:], [bkmod])
                yield
                yield
                for c in range(4):
                    STT(tmpA[:, c, :], rT[:, c, :], vec4[:, 2, c:c + 1], kmod[:, c, :], ALU.mult, ALU.mult, [brT, bvec4, bkmod], [btmpA])
                yield
                for c in range(4):
                    MM(pk_[:, 256 + c * 2:256 + c * 2 + 2], tmpA[:, c, :], ind2[:, :], r=[btmpA, bind2], w=[bpk_], inc=(c == 3))
                yield
                CP("dve", sbon[:, :], pk_[:, 256:264], [bpk_], [bsbon])
                yield
                yield
                STT(f4(At), f4(kkn), -1.0, f4(e1), ALU.mult, ALU.mult, [bkkn, be1], [bAt])
                yield
                TT("pool", f4(tmpA), f4(kkn), f4(asg), ALU.mult, [bkkn, basg], [btmpA])
                yield
                TT("dve", f4(tmpA), f4(tmpA), f4(e2), ALU.mult, [btmpA, be2], [btmpA])
                yield
                CP("pool", f4(Bt), f4(tmpA), [btmpA], [bBt])
                yield
                TT("pool", f4(e1), f4(kmod), f4(e2), ALU.mult, [bkmod, be2], [be1])
                yield
                CP("pool", f4(Kt), f4(e1), [be1], [bKt])
                yield
                TT("dve", f4(Rt), f4(rT), f4(e3), ALU.mult, [brT, be3], [bRt])
                yield
                yield
                for c in range(4):
                    TS("dve", tmpA[:, c, :], tmpA[:, c, :], eLC[:, c:c + 1], None, ALU.mult, None, [btmpA, beLC], [btmpA])
                    TS("pool", e1[:, c, :], e1[:, c, :], eLC[:, c:c + 1], None, ALU.mult, None, [be1, beLC], [be1])
                yield
                for (src, bsrc, dst, bdst) in ((tmpA, btmpA, Bh, bBh), (e1, be1, Kh, bKh), (vT, bvT, Vt, bVt)):
                    pp, bp = bank("rw")
                    for c in range(4):
                        TR(pp[:, c * 128:(c + 1) * 128], src[:, c, :], identf[:], [bsrc, bidf], [bp], inc=(c == 3))
                    CP("act", dst[:, :], pp[:, :], [bp], [bdst])
                yield
                chk(4)
                yield
                yield
                def hs(t, h):
                    return t[(h % 2) * 64:(h % 2) * 64 + 64, h // 2, :]
                yield

                def five(lt, blt, rt, brt, mask_i, dst, bdst):
                    for par in range(2):
                        pp, bp = bank("rw")
                        for hh in range(4):
                            h = hh * 2 + par
                            MM(pp[:, hh * 128:(hh + 1) * 128], hs(lt, h), hs(rt, h), r=[blt, brt], w=[bp], inc=(hh == 3))
                        TT("dve", dst[:, par:8:2, :], pp[:, :].rearrange("p (h t) -> p h t", h=4), bcast(masks[:, mask_i, :], [128, 4, 128], 1), ALU.mult,
                           [bp, bmasks], [bdst])
                yield

                five(Bt, bBt, At, bAt, 1, Qm[0], bQm[0])
                yield
                five(At, bAt, Bt, bBt, 3, QmT[0], bQmT[0])
                yield
                five(Kt, bKt, At, bAt, 1, AakT, bAak)
                yield
                five(Bt, bBt, Rt, bRt, 2, MrbT, bMrb)
                yield
                five(Kt, bKt, Rt, bRt, 2, MrkT, bMrk)
                yield
                yield
                TT("pool", Xm[0][:], Qm[0][:], bcast(identb[:], [128, 8, 128], 1), ALU.add, [bQm[0], bidb], [bXm[0]])
                yield
                cur = 0
                yield
                for lvl in range(1, 7):
                    nxt = cur ^ 1
                    lastl = lvl == 6
                    for half in range(2):
                        hsl = slice(half * 4, half * 4 + 4)
                        pT_, bpT_ = bank("rw")
                        for hh in range(4):
                            h = half * 4 + hh
                            MM(pT_[:, hh * 128:(hh + 1) * 128], Qm[cur][:, h, :], QmT[cur][:, h, :], r=[bQm[cur], bQmT[cur]], w=[bpT_], inc=(hh == 3))
                        CP("act", QmT[nxt][:, hsl, :], pT_[:, :].rearrange("p (h t) -> p h t", h=4), [bpT_], [bQmT[nxt]])
                        if not lastl:
                            pQ_, bpQ_ = bank("rw")
                            for hh in range(4):
                                h = half * 4 + hh
                                MM(pQ_[:, hh * 128:(hh + 1) * 128], QmT[cur][:, h, :], Qm[cur][:, h, :], r=[bQm[cur], bQmT[cur]], w=[bpQ_], inc=(hh == 3))
                            CP("dve", Qm[nxt][:, hsl, :], pQ_[:, :].rearrange("p (h t) -> p h t", h=4), [bpQ_], [bQm[nxt]])
                        yield
                    for half in range(2):
                        hsl = slice(half * 4, half * 4 + 4)
                        pX_, bpX_ = bank("rw")
                        for hh in range(4):
                            h = half * 4 + hh
                            MM(pX_[:, hh * 128:(hh + 1) * 128], QmT[nxt][:, h, :], Xm[cur][:, h, :], r=[bQmT[nxt], bXm[cur]], w=[bpX_], inc=(hh == 3))
                        TT("dve", Xm[nxt][:, hsl, :], pX_[:, :].rearrange("p (h t) -> p h t", h=4), Xm[cur][:, hsl, :], ALU.add, [bpX_, bXm[cur]], [bXm[nxt]])
                        yield
                    cur = nxt
                yield
                Xf = Xm[cur]; bXf = bXm[cur]
                yield
                chk(5)
                yield
                yield
                if i == 0:
                    MSET("pool", Pst[:], 0.0, [bPst])
                    MSET("pool", Pb[:], 0.0, [bPb])
                yield

                def ph_(t, h):
                    return t[(h % 2) * 64:(h % 2) * 64 + 64, h // 2, :]
                yield

                def vh(t, h):
                    return t[:, h * 64:(h + 1) * 64]
                yield

                for par in range(2):
                    p1, bp1 = bank("rw")
                    for hh in range(4):
                        h = hh * 2 + par
                        MM(p1[:, hh * 64:(hh + 1) * 64], hs(At, h), ph_(Pb, h), start=True, stop=False, r=[bAt, bPb], w=[bp1], inc=False)
                        MM(p1[:, hh * 64:(hh + 1) * 64], AakT[:, h, :], vh(Vt, h), start=False, stop=True, r=[bAak, bVt], w=[bp1], inc=(hh == 3))
                    CP("act", rhs0[:, par:8:2, :], p1[:, 0:256].rearrange("p (h v) -> p h v", h=4), [bp1], [brhs0])
                yield
                chk(5.2)
                yield
                p2, bp2 = bank("rw")
                yield
                for h in range(8):
                    MM(p2[:, h * 64:(h + 1) * 64], Xf[:, h, :], rhs0[:, h, :], r=[bXf, brhs0], w=[bp2], inc=(h == 7))
                yield
                CP("act", Ub[:].rearrange("p h v -> p (h v)"), p2[:, :], [bp2], [bUb])
                yield
                chk(5.4)
                yield
                p3s = []
                yield
                for par in range(2):
                    p3, bp3 = bank("proj")
                    p3s.append((p3, bp3))
                    for hh in range(4):
                        h = hh * 2 + par
                        MM(p3[:, hh * 64:(hh + 1) * 64], hs(Rt, h), ph_(Pb, h), start=True, stop=False, r=[bRt, bPb], w=[bp3], inc=False)
                        MM(p3[:, hh * 64:(hh + 1) * 64], MrkT[:, h, :], vh(Vt, h), start=False, stop=False, r=[bMrk, bVt], w=[bp3], inc=False)
                        MM(p3[:, hh * 64:(hh + 1) * 64], MrbT[:, h, :], Ub[:, h, :], start=False, stop=True, r=[bMrb, bUb], w=[bp3], inc=(hh == 3))
                yield
                chk(5.6)
                yield
                p4, bp4 = bank("rw")
                yield
                for h in range(8):
                    o_ = p4[(h % 2) * 64:(h % 2) * 64 + 64, (h // 2) * 64:(h // 2) * 64 + 64]
                    MM(o_, vh(Bh, h), Ub[:, h, :], start=True, stop=False, r=[bBh, bUb], w=[bp4], inc=False)
                    MM(o_, vh(Kh, h), vh(Vt, h), start=False, stop=True, r=[bKh, bVt], w=[bp4], inc=(h == 7))
                yield
                for c in range(4):
                    STT(Pst[:, c, :], Pst[:, c, :], eLC[:, c:c + 1], p4[:, c * 64:(c + 1) * 64], ALU.mult, ALU.add, [bPst, beLC, bp4], [bPst])
                yield
                CP("pool", Pb[:], Pst[:], [bPst], [bPb])
                yield
                chk(5.8)
                yield
                yield
                yn3 = yn[:, :].rearrange("p (h d) -> p h d", h=8)
                yield
                for par in range(2):
                    p3, bp3 = p3s[par]
                    CP("act", yn3[:, par:8:2, :], p3[:, 0:256].rearrange("p (h d) -> p h d", h=4), [bp3], [byn])
                yield
                ACT(sq[:, 0:512], yn[:, :], AF.Square, [byn], [bsq])
                yield
                REDUCE(lnst[:, 0:8], yn3, [byn], [blnst])
                yield
                REDUCE(lnst[:, 8:16], sq[:, 0:512].rearrange("p (h d) -> p h d", h=8), [bsq], [blnst])
                yield
                chk(5.85)
                yield
                TS("dve", lnst[:, 0:16], lnst[:, 0:16], 1.0 / 64, None, ALU.mult, None, [blnst], [blnst])
                yield
                TT("dve", lnst[:, 16:24], lnst[:, 0:8], lnst[:, 0:8], ALU.mult, [blnst], [blnst])
                yield
                TT("dve", lnst[:, 16:24], lnst[:, 8:16], lnst[:, 16:24], ALU.subtract, [blnst], [blnst])
                yield
                TS("dve", lnst[:, 16:24], lnst[:, 16:24], 0.0, 64e-5, ALU.max, ALU.add, [blnst], [blnst])
                yield
                TT("pool", lnst[:, 24:32], lnst[:, 16:24], cneg[:, 0:8], ALU.pow, [blnst, bcneg], [blnst])
                yield
                STT(lnst[:, 16:24], lnst[:, 0:8], -1.0, lnst[:, 24:32], ALU.mult, ALU.mult, [blnst], [blnst])
                yield
                chk(5.9)
                yield
                for h in range(8):
                    ACT(yn[:, h * 64:(h + 1) * 64], yn[:, h * 64:(h + 1) * 64], AF.Identity, [byn, blnst], [byn],
                        bias=lnst[:, 16 + h:17 + h], scale=lnst[:, 24 + h:25 + h])
                yield
                chk(5.95)
                yield
                TT("pool", yn[:, :], yn[:, :], ln_w_bc[:, :], ALU.mult, [byn, blnw], [byn])
                yield
                chk(5.97)
                yield
                for h in range(8):
                    STT(yn[:, h * 64:(h + 1) * 64], Vt[:, h * 64:(h + 1) * 64], sbon[:, h:h + 1], yn[:, h * 64:(h + 1) * 64], ALU.mult, ALU.add,
                        [bVt, bsbon, byn], [byn])
                yield

            na_ = 2 * (2 + (i + 1) + min(5, i + 1)) + 2
            nr_ = 150
            ga_, gr_ = attn_gen(), rwkv_gen()
            da_ = dr_ = 0
            alive_a = alive_r = True
            while alive_a or alive_r:
                pick_a = alive_a and (not alive_r or da_ * nr_ <= dr_ * na_)
                if pick_a:
                    try:
                        next(ga_); da_ += 1
                    except StopIteration:
                        alive_a = False
                else:
                    try:
                        next(gr_); dr_ += 1
                    except StopIteration:
                        alive_r = False
            yield "mid_done"
            chk(6)
            yield "t"
            def wsload(c):
                k = ws_i[0] % NWS
                ws_i[0] += 1
                DMA(WS[k][:].rearrange("p k n -> p (k n)"), wrest_s[c], sem_ws[k], w=[bWS[k]])
                return WS[k], bWS[k]

            def rest_chunk(c):
                W_, bW_ = wsload(c)
                pp, bp = bank("tl")
                for sub in range(2):
                    for kc in range(8):
                        MM(pp[:, sub * 128:(sub + 1) * 128], W_[:, kc, sub * 128:(sub + 1) * 128], hcur[:, kc, 1:129], start=(kc == 0), stop=(kc == 7),
                           r=[bW_, bh], w=[bp], inc=(kc == 7 and sub == 1))
                return pp, bp

            for c in range(12):
                pp, bp = rest_chunk(c)
                if c < 4:
                    dst, bd = (silA, bsilA) if c < 2 else (silB, bsilB)
                    dv = dst[:, (c % 2) * 2:(c % 2) * 2 + 2, :].rearrange("p a t -> p (a t)")
                    ACT(dv, pp[:, 0:256], AF.Tanh, [bp], [bd], scale=0.5)
                    STT(dv, dv, 1.0, pp[:, 0:256], ALU.add, ALU.mult, [bd, bp], [bd])
                else:
                    dst, bd = (thA, bthA) if c < 8 else (thB, bthB)
                    cc = (c - 4) % 4
                    ACT(dst[:, cc * 2:cc * 2 + 2, :].rearrange("p a t -> p (a t)"), pp[:, 0:256], AF.Tanh, [bp], [bd], scale=0.5)
            yield "t"
            for (src, bsrc, sil, bsil, dst, bdst, lnb) in ((ynsa, bynsa, silA, bsilA, yaT, byaT, False), (yn, byn, silB, bsilB, ybT, bybT, True)):
                pp, bp = bank("tl")
                for c in range(4):
                    TR(pp[:, c * 128:(c + 1) * 128], src[:, c * 128:(c + 1) * 128], identf[:], [bsrc, bidf], [bp], inc=(c == 3))
                if not lnb:
                    STT(dst[:].rearrange("p c t -> p (c t)"), pp[:, :], 0.5, sil[:].rearrange("p c t -> p (c t)"), ALU.mult, ALU.mult, [bp, bsil], [bdst])
                else:
                    for c in range(4):
                        STT(tmpA[:, c, :], pp[:, c * 128:(c + 1) * 128], vec4[:, 3, c:c + 1], sil[:, c, :], ALU.add, ALU.mult, [bp, bvec4, bsil], [btmpA])
                    TS("pool", dst[:].rearrange("p c t -> p (c t)"), f4(tmpA), 0.5, None, ALU.mult, None, [btmpA], [bdst])
            yield "t"
            dump(f"yaT_{T}", yaT[:], [byaT], BF16)
            yield "t"
            dump(f"ybT_{T}", ybT[:], [bybT], BF16)
            yield "t"
            for (yT_, byT_, W_, bW_, th, bth, mg, bmg) in ((yaT, byaT, Wouta, bWouta, thA, bthA, mg1, bmg1), (ybT, bybT, Woutb, bWoutb, thB, bthB, mg2, bmg2)):
                for half in range(2):
                    pp, bp = bank("tl")
                    for mm_ in range(4):
                        mc = half * 4 + mm_
                        for kc in range(4):
                            MM(pp[:, mm_ * 128:(mm_ + 1) * 128], W_[:, kc, mc * 128:(mc + 1) * 128], yT_[:, kc, :], start=(kc == 0), stop=(kc == 3),
                               r=[bW_, byT_], w=[bp], inc=(kc == 3 and mm_ == 3))
                    STT(mg[:, half * 4:(half + 1) * 4, :].rearrange("p a t -> p (a t)"), th[:, half * 4:(half + 1) * 4, :].rearrange("p a t -> p (a t)"), 1.0, pp[:, :],
                        ALU.add, ALU.mult, [bth, bp], [bmg])
            yield "t"
            TT("pool", mgT[:].rearrange("p a t -> p (a t)"), mg1[:].rearrange("p a t -> p (a t)"), mg2[:].rearrange("p a t -> p (a t)"), ALU.add, [bmg1, bmg2], [bmgT])
            yield "t"
            dump(f"mgT_{T}", mgT[:], [bmgT], BF16)
            yield "t"
            if s == 1 and i == 0:
                DMA(Wog[:].rearrange("p k n -> p (k n)"), wog_s, sem_wog, w=[bWog])
            yield "t"
            for half in range(2):
                pp, bp = bank("tl")
                for kc in range(8):
                    MM(pp[:, :], mgT[:, kc, :], Wog[:, kc, half * 512:(half + 1) * 512], start=(kc == 0), stop=(kc == 7), r=[bmgT, bWog], w=[bp], inc=(kc == 7))
                TT("dve", x_t[:, half * 512:(half + 1) * 512], pp[:, :], x_t[:, half * 512:(half + 1) * 512], ALU.add, [bp, bx], [bx])
            yield "t"
            return DMA(out_d[tok0:tok0 + 128, :], x_t[:, :], sem_outs[T % 2], r=[bx], w=[])

        out_toks = []
        total = nseq * ntile
        seq_tiles = [(s, i) for s in range(nseq) for i in range(ntile)]

        def make_gen(n_):
            s_, i_ = seq_tiles[n_]
            T_ = s_ * 16 + i_
            if i_ == 0:
                MSET("pool", kcT[:].rearrange("p a b -> p (a b)"), 0.0, [bkcT])
                MSET("pool", vcT[:].rearrange("p a b -> p (a b)"), 0.0, [bvcT])
                MSET("pool", vca[:, :, 0:64], 0.0, [bvca])
                MSET("pool", kvc[:].rearrange("p a b -> p (a b)"), 0.0, [bkvc])
            return tile_body2(s_, i_)

        def xload(n_):
            if n_ < total:
                s2, i2 = seq_tiles[n_]
                T2 = s2 * 16 + i2
                DMA(xt[T2 % 2][:], x_d[T2 * 128:(T2 + 1) * 128, :], sem_x[T2 % 2], w=[bxt[T2 % 2]])

        def step(g, until):
            while True:
                try:
                    m_ = next(g)
                except StopIteration as e_:
                    return None, True, e_.value
                if m_ in until:
                    return m_, False, None

        def early(n_):
            s_, i_ = seq_tiles[n_]
            T_ = s_ * 16 + i_
            return DMA(out_d[T_ * 128:T_ * 128 + 128, :], xs[:, :], sem_outs[T_ % 2], r=[bxs], w=[])

        if total > 0:
            xload(0)
            xload(1)
            if stage < 9:
                for n_ in range(total):
                    g = make_gen(n_)
                    try:
                        _, _, val = step(g, ())
                        out_toks.append(val)
                    except _Stop:
                        out_toks.append(early(n_))
                    xload(n_ + 2)
            else:
                cur = make_gen(0)
                step(cur, ("front_done",))
                for n_ in range(total):
                    step(cur, ("mid_done",))
                    nxt = make_gen(n_ + 1) if n_ + 1 < total else None
                    cur_done = False
                    nxt_done = nxt is None
                    while not (cur_done and nxt_done):
                        if not cur_done:
                            m_, fin, val = step(cur, ("t",))
                            if fin:
                                cur_done = True
                                out_toks.append(val)
                        if not nxt_done:
                            m_, fin, val = step(nxt, ("f", "front_done"))
                            if m_ == "front_done":
                                nxt_done = True
                    xload(n_ + 2)
                    cur = nxt
        S.wait_all("sp", out_toks[-4:] + dbg_outs + [(sm_, S.dcnt[sm_]) for sm_ in sem_outs])
        S.emit()
    return nc


_CACHE = {}


def kernel(**inputs):
    sh, per = host_prep(inputs)
    if "nc" not in _CACHE:
        _CACHE["nc"] = build()
    nc = _CACHE["nc"]
    in_maps = []
    for core in range(8):
        d = dict(sh)
        d.update(per[core])
        in_maps.append(d)
    res = run_bass_kernel_spmd(nc, in_maps, core_ids=list(range(8)))
    out = np.concatenate([np.asarray(r["out"]).reshape(2, 2048, 1024) for r in res.results], axis=0)
    return out.astype(np.float32)
```

```python
import math
import numpy as np
import concourse.bass as bass
import concourse.mybir as mybir
from concourse.bass_utils import run_bass_kernel_spmd
from contextlib import ExitStack

F32 = mybir.dt.float32
BF16 = mybir.dt.bfloat16
AF = mybir.ActivationFunctionType
ALU = mybir.AluOpType
AX = mybir.AxisListType

COMPUTE = ("pe", "act", "dve", "pool")
NEGM = -4096.0
NRES = 2968
CQ, CKV, CG, CC, CS = 0, 512, 1024, 1048, 1304


class Buf:
    __slots__ = ("w", "r")

    def __init__(self):
        self.w = None
        self.r = {}


class Sched:
    ANNOTATE = False

    def __init__(self, nc, es):
        self.nc = nc
        self.es = es
        self.prog = {e: [] for e in COMPUTE + ("sp",)}
        self.cnt = {e: 0 for e in COMPUTE}
        self.sems = {}
        for e in COMPUTE:
            self.sems[e] = es.enter_context(nc.semaphore("sem_" + e))
        self.known = {e: {} for e in self.prog}
        self.snap = {}
        self.dcnt = {}
        self.pending = {e: False for e in COMPUTE}
        self.last = {}

    def dma_sem(self, name):
        self.sems[name] = self.es.enter_context(self.nc.semaphore("sem_" + name))
        self.dcnt[name] = 0
        return name

    @staticmethod
    def _flat(bs):
        out = []
        for b in bs:
            if isinstance(b, (list, tuple)):
                out.extend(Sched._flat(b))
            else:
                out.append(b)
        return out

    def op(self, eng, fn, reads=(), writes=(), inc=True, dsem=None):
        reads = self._flat(reads)
        writes = self._flat(writes)
        need = {}

        def req(tok, same_ok):
            if tok is None:
                return
            k, v = tok
            if same_ok and k == eng and eng == "pe":
                return
            if need.get(k, 0) < v:
                need[k] = v

        for b in reads:
            req(b.w, False)
        for b in writes:
            req(b.w, True)
            for k, v in b.r.items():
                req((k, v), True)
        kn = self.known[eng]
        waits = []
        for k, v in need.items():
            if kn.get(k, 0) < v:
                waits.append((k, v))
                kn[k] = v
                sn = self.snap.get((k, v))
                if sn is not None:
                    for k2, v2 in sn.items():
                        if kn.get(k2, 0) < v2:
                            kn[k2] = v2
        if dsem is not None:
            self.dcnt[dsem] += 16
            tok = (dsem, self.dcnt[dsem])
            incspec = (dsem, 16)
        elif inc:
            self.cnt[eng] += 1
            tok = (eng, self.cnt[eng])
            incspec = (eng, 1)
            self.pending[eng] = False
            self.snap[tok] = dict(kn)
        else:
            tok = (eng, self.cnt[eng] + 1)
            incspec = None
            self.pending[eng] = True
        self.last[tok[0]] = tok[1]
        for b in writes:
            b.w = tok
            b.r = {}
        for b in reads:
            if b.w is tok:
                continue
            if b.r.get(tok[0], 0) < tok[1]:
                b.r[tok[0]] = tok[1]
        note = None
        if Sched.ANNOTATE:
            import sys as _sys
            f_ = _sys._getframe(1)
            while f_ is not None and f_.f_code.co_name not in ("tile_body2", "attn_gen", "rwkv_gen", "build", "finish", "pv", "five", "rest_chunk", "wsload"):
                f_ = f_.f_back
            note = f"L{f_.f_lineno}" if f_ is not None else None
        self.prog[eng].append((waits, fn, incspec, note))
        return tok

    def wait_all(self, eng, toks):
        kn = self.known[eng]
        waits = []
        mx = {}
        for k, v in toks:
            if mx.get(k, 0) < v:
                mx[k] = v
        for k, v in mx.items():
            if kn.get(k, 0) < v:
                waits.append((k, v))
                kn[k] = v
        self.prog[eng].append((waits, None, None, None))

    def barrier(self):
        for e in COMPUTE:
            if self.pending[e]:
                self.op(e, lambda en: en.nop(), (), ())
        toks = list(self.last.items())
        for e in self.prog:
            self.wait_all(e, toks)

    def emit(self):
        nc = self.nc
        for e in COMPUTE:
            if self.pending[e]:
                self.op(e, lambda en: en.nop(), (), ())
        sems = self.sems
        prog = self.prog

        def run(engname):
            def f(e):
                for waits, fn, incspec, note in prog[engname]:
                    for k, v in waits:
                        e.wait_ge(sems[k], v)
                    if fn is None:
                        continue
                    ins = fn(e)
                    if note is not None:
                        ins.annotate(note)
                    if incspec is not None:
                        ins.then_inc(sems[incspec[0]], incspec[1])
            return f

        with nc.Block() as block:
            block.sync(run("sp"))
            block.tensor(run("pe"))
            block.scalar(run("act"))
            block.vector(run("dve"))
            block.gpsimd(run("pool"))


def _t5_bucket(dist):
    n = np.maximum(dist, 0)
    nf = np.maximum(n, 16).astype(np.float32)
    large = 16 + (np.log(nf / np.float32(16)) / np.float32(math.log(128 / 16)) * np.float32(16)).astype(np.int32)
    return np.where(n < 16, n, np.minimum(large, 31))


def _perms():
    r = lambda a, b: list(range(a, b))
    res = (r(0, 512)
           + r(768, 832) + r(1024, 1088) + r(832, 896) + r(1088, 1152) + r(896, 1024) + r(1152, 1280)
           + r(1280, 1304)
           + r(512, 576) + r(640, 704) + r(576, 640) + r(704, 768)
           + r(1816, 3480))
    rest = r(1304, 1816) + r(3480, 3992) + r(3992, 5016) + r(5016, 6040)
    assert len(res) == NRES and len(rest) == 3072
    return np.array(res), np.array(rest)


def host_prep(inp):
    f = lambda k: np.ascontiguousarray(np.asarray(inp[k], dtype=np.float32))
    sh = {}
    pres, prest = _perms()
    w_in = f("w_in")[0]
    sh["w_res"] = np.ascontiguousarray(w_in[:, pres])
    sh["w_rest"] = np.ascontiguousarray(w_in[:, prest])
    sh["w_ada"] = f("w_ada")[0]
    sh["w_out_a"] = f("w_out_a")[0]
    sh["w_out_b"] = f("w_out_b")[0]
    sh["w_o"] = f("w_o")[0]
    sh["w1k"] = f("cmp_k_w1")[0]
    sh["w1v"] = f("cmp_v_w1")[0]
    col = lambda v, n: np.ascontiguousarray(v.reshape(n, 128).T)
    sh["b_ada"] = col(f("b_ada")[0], 24)
    sh["g_norm"] = col(f("norm_gain")[0], 8)
    sh["mu"] = col(f("shift_mu")[0], 13)
    vec4 = np.stack([col(f(k)[0].reshape(-1), 4) for k in ("k_k", "k_a", "r_k", "ln_x_b")], 1)
    sh["vec4"] = np.ascontiguousarray(vec4)
    rep = lambda v: np.ascontiguousarray(np.broadcast_to(v[None, :], (128, v.shape[0])))
    kng = f("k_norm_gain")[0]
    sh["bc_small"] = np.concatenate([rep(f("q_norm_gain")[0]), rep(kng[1]), rep(kng[2])], 1)
    sh["ln_w_bc"] = rep(f("ln_x_w")[0])
    sh["kgc"] = np.ascontiguousarray(kng[0].reshape(64, 1))
    sh["w0a0"] = np.ascontiguousarray(np.stack([f("w0")[0], f("a0")[0]], 0))
    sh["lora"] = np.ascontiguousarray(np.concatenate([f("w_lora_up")[0], f("a_lora_up")[0]], 0))
    w2 = lambda k: f(k)[0].reshape(2, 128, 64).transpose(1, 0, 2)
    sh["w2"] = np.ascontiguousarray(np.stack([w2("cmp_k_w2"), w2("cmp_v_w2")], 1))
    sh["peT"] = np.ascontiguousarray(np.concatenate([f("cmp_pos_k")[0].T, f("cmp_pos_v")[0].T], 0))
    tbl = f("rel_bias")
    k = np.arange(128)[:, None]
    q = np.arange(128)[None, :]
    tb = np.zeros((2, 2, 128, 4, 128), np.float32)
    for v, dist in enumerate((q - k, 128 + q - k)):
        bk = _t5_bucket(dist)
        for g in range(2):
            for h in range(4):
                tb[v, g, :, h, :] = tbl[bk, g * 4 + h]
    sh["tblDS"] = tb.reshape(2, 2, 128, 512)
    mk = np.zeros((128, 4, 128), np.float32)
    mk[np.broadcast_to(((q - k) < 0)[:, None, :], mk.shape)] = NEGM
    sh["maskD"] = mk.reshape(128, 512)
    c31 = np.zeros((2, 128, 4, 128), np.float32)
    for g in range(2):
        for h in range(4):
            c31[g, :, h, :] = tbl[31, g * 4 + h]
    sh["c31"] = c31.reshape(2, 128, 512)
    p = np.arange(16)[:, None]
    distc = q - 16 * p + 113
    bkc = _t5_bucket(distc)
    tc = np.zeros((2, 16, 4, 128), np.float32)
    for g in range(2):
        for h in range(4):
            tc[g, :, h, :] = tbl[bkc, g * 4 + h]
    sh["tblC"] = tc.reshape(2, 16, 512)
    mc = np.zeros((16, 4, 128), np.float32)
    mc[np.broadcast_to((distc < 0)[:, None, :], mc.shape)] = NEGM
    sh["maskC"] = mc.reshape(16, 512)
    sh["ident"] = np.eye(128, dtype=np.float32)
    far = np.where(k <= q, NEGM, 0.0).astype(np.float32)
    mus = (k < q).astype(np.float32)
    mui = (k <= q).astype(np.float32)
    mls = (k > q).astype(np.float32)
    sh["masks"] = np.ascontiguousarray(np.stack([far, mus, mui, mls], 1))
    z = np.zeros((16, 256), np.float32)
    z[np.arange(16), np.arange(16) + 119] = 1.0
    sh["zsh"] = z
    e = np.zeros((32, 2048), np.float32)
    e[np.arange(2048) // 64, np.arange(2048)] = -NEGM
    sh["emat"] = e
    mi = np.zeros((128, 32), np.float32)
    for j in range(32):
        for a in range(4):
            for b in range(2):
                n = 4 * j + a - b
                if 0 <= n < 127:
                    mi[n, j] += 1.0
    sh["mimp"] = mi
    ka = np.zeros((128, 8, 2, 32), np.float32)
    for i in range(8, 16):
        for qq in range(128):
            cur = (128 * i + qq) // 64
            for j in range(32):
                forced = (j == 0) or (j == cur) or (j == cur - 1)
                causal = j <= cur
                if forced:
                    ka[qq, i - 8, 0, j] = 0.0
                    ka[qq, i - 8, 1, j] = 1e30
                elif causal:
                    ka[qq, i - 8, 0, j] = 1.0
                else:
                    ka[qq, i - 8, 1, j] = -1e30
    sh["keepadd"] = ka.reshape(128, 512)
    ind2 = np.zeros((128, 2), np.float32)
    ind2[:64, 0] = 1.0
    ind2[64:, 1] = 1.0
    sh["ind2"] = ind2
    indT = np.zeros((8, 4, 128), np.float32)
    for h in range(8):
        indT[h, h // 2, (h % 2) * 64:(h % 2) * 64 + 64] = 1.0
    sh["indT"] = indT.reshape(8, 512)
    x = f("x")
    c = f("c")
    per = []
    for core in range(8):
        d = {"x": np.ascontiguousarray(x[2 * core:2 * core + 2].reshape(4096, 1024)),
             "cT": np.ascontiguousarray(c[2 * core:2 * core + 2].reshape(2, 8, 128).transpose(2, 1, 0))}
        per.append(d)
    return sh, per


class _Stop(Exception):
    pass


def build(nseq=2, ntile=16, dbg=None, stage=9):
    nc = bass.Bass("TRN2", target_bir_lowering=False)
    dbg = dbg or {}
    di = lambda name, shape: nc.dram_tensor(name, shape, F32, kind="ExternalInput").ap()
    x_d = di("x", [4096, 1024])
    cT_d = di("cT", [128, 8, 2])
    w_res_d = di("w_res", [1024, NRES])
    w_rest_d = di("w_rest", [1024, 3072])
    w_ada_d = di("w_ada", [1024, 3072])
    w_out_a_d = di("w_out_a", [512, 1024])
    w_out_b_d = di("w_out_b", [512, 1024])
    w_o_d = di("w_o", [1024, 1024])
    w1k_d = di("w1k", [2048, 256])
    w1v_d = di("w1v", [2048, 256])
    b_ada_d = di("b_ada", [128, 24])
    g_norm_d = di("g_norm", [128, 8])
    mu_d = di("mu", [128, 13])
    vec4_d = di("vec4", [128, 4, 4])
    bc_small_d = di("bc_small", [128, 192])
    ln_w_bc_d = di("ln_w_bc", [128, 512])
    kgc_d = di("kgc", [64, 1])
    w0a0_d = di("w0a0", [2, 512])
    lora_d = di("lora", [128, 512])
    w2_d = di("w2", [128, 2, 2, 64])
    peT_d = di("peT", [128, 32])
    tblDS_d = di("tblDS", [2, 2, 128, 512])
    maskD_d = di("maskD", [128, 512])
    c31_d = di("c31", [2, 128, 512])
    tblC_d = di("tblC", [2, 16, 512])
    maskC_d = di("maskC", [16, 512])
    ident_d = di("ident", [128, 128])
    masks_d = di("masks", [128, 4, 128])
    zsh_d = di("zsh", [16, 256])
    emat_d = di("emat", [32, 2048])
    mimp_d = di("mimp", [128, 32])
    keepadd_d = di("keepadd", [128, 512])
    ind2_d = di("ind2", [128, 2])
    indT_d = di("indT", [8, 512])
    out_d = nc.dram_tensor("out", [4096, 1024], F32, kind="ExternalOutput").ap()
    wrest_s = nc.dram_tensor("wrest_s", [12, 128, 2048], BF16, kind="Internal").ap()
    wog_s = nc.dram_tensor("wog_s", [128, 8192], BF16, kind="Internal").ap()

    with ExitStack() as es:
        S = Sched(nc, es)
        _n = [0]

        def sb(shape, dt, name=None):
            _n[0] += 1
            return es.enter_context(nc.sbuf_tensor("s_" + (name or f"sb{_n[0]}"), shape, dt))

        def psb(name):
            return es.enter_context(nc.psum_tensor(name, [128, 512], F32))

        dbg_outs = []

        def dump(name, ap, reads, dt=F32):
            if name not in dbg:
                return
            d = nc.dram_tensor("dbg_" + name, list(ap.shape), dt, kind="ExternalOutput").ap()
            dbg_outs.append(S.op("sp", lambda e: e.dma_start(out=d, in_=ap), reads, (), dsem=sem_dbg))

        def MM(out, lhsT, rhs, start=True, stop=True, r=(), w=(), inc=True, sgc=False):
            if sgc:
                return S.op("pe", lambda e: e.matmul(out, lhsT=lhsT, rhs=rhs, start=start, stop=stop, skip_group_check=True), r, w, inc=inc)
            return S.op("pe", lambda e: e.matmul(out, lhsT=lhsT, rhs=rhs, start=start, stop=stop), r, w, inc=inc)

        def TR(out, in_, ident, r=(), w=(), inc=True):
            return S.op("pe", lambda e: e.transpose(out=out, in_=in_, identity=ident), r, w, inc=inc)

        def ACT(out, in_, func, r=(), w=(), bias=None, scale=None, accum=None):
            kw = {}
            if bias is not None:
                kw["bias"] = bias
            if scale is not None:
                kw["scale"] = scale
            if accum is not None:
                kw["accum_out"] = accum
            return S.op("act", lambda e: e.activation(out=out, in_=in_, func=func, **kw), r, w)

        def TS(eng, out, in0, s1, s2, op0, op1=None, r=(), w=()):
            if op1 is None:
                return S.op(eng, lambda e: e.tensor_scalar(out=out, in0=in0, scalar1=s1, scalar2=None, op0=op0), r, w)
            return S.op(eng, lambda e: e.tensor_scalar(out=out, in0=in0, scalar1=s1, scalar2=s2, op0=op0, op1=op1), r, w)

        def TT(eng, out, in0, in1, op, r=(), w=()):
            return S.op(eng, lambda e: e.tensor_tensor(out=out, in0=in0, in1=in1, op=op), r, w)

        def STT(out, in0, scalar, in1, op0, op1, r=(), w=()):
            return S.op("dve", lambda e: e.scalar_tensor_tensor(out=out, in0=in0, scalar=scalar, in1=in1, op0=op0, op1=op1), r, w)

        def CP(eng, out, in_, r=(), w=()):
            if eng == "act":
                return S.op("act", lambda e: e.copy(out=out, in_=in_), r, w)
            return S.op(eng, lambda e: e.tensor_copy(out=out, in_=in_), r, w)

        def MSET(eng, ap, val, w=()):
            return S.op(eng, lambda e: e.memset(ap, val), (), w)

        def DMA(out, in_, sem, r=(), w=(), eng="sp"):
            return S.op(eng, lambda e: e.dma_start(out=out, in_=in_), r, w, dsem=sem)

        def bcast(ap, shape, axis):
            return ap.unsqueeze(axis).to_broadcast(shape)

        sem_dbg = S.dma_sem("dbg")
        sem_stg = [S.dma_sem("stg0"), S.dma_sem("stg1")]
        sem_scr = S.dma_sem("scr")
        sem_x = [S.dma_sem("x0"), S.dma_sem("x1")]
        sem_xr = S.dma_sem("xr")
        sem_ws = [S.dma_sem(f"ws{i}") for i in range(3)]
        sem_outs = [S.dma_sem("out0"), S.dma_sem("out1")]
        sem_wog = S.dma_sem("wog")

        PS = [psb(f"ps{i}") for i in range(8)]
        PSB = [Buf() for _ in range(8)]
        rot = {"proj": [0, 1], "sc": [2, 3], "acc": [4, 5], "rw": [6, 7], "tl": [4, 5, 6, 7]}
        rotc = {k: 0 for k in rot}

        def bank(cls):
            i = rot[cls][rotc[cls] % len(rot[cls])]
            rotc[cls] += 1
            return PS[i], PSB[i]

        NSLOT = 41
        AR = sb([128, NSLOT * 256], F32, "arena")
        SLB = [Buf() for _ in range(NSLOT)]

        def slot(start, shape, dt, P0=0):
            el = 4 if dt == F32 else 2
            n = int(np.prod(shape[1:]))
            nsl = (n * el + 1023) // 1024
            assert start + nsl <= NSLOT
            base = AR[:] if dt == F32 else AR[:].bitcast(BF16)
            o = start * 1024 // el
            ap = base[P0:P0 + shape[0], o:o + n]
            if len(shape) > 2:
                names = " ".join(f"d{i}" for i in range(len(shape) - 1))
                kw = {f"d{i}": shape[i + 1] for i in range(len(shape) - 1)}
                ap = ap.rearrange(f"p ({names}) -> p {names}", **kw)
            return ap, SLB[start:start + nsl]

        Wres = sb([128, 8, NRES], BF16, "Wres"); bWres = Buf()
        Wouta = sb([128, 4, 1024], BF16, "Wouta"); bWouta = Buf()
        Woutb = sb([128, 4, 1024], BF16, "Woutb"); bWoutb = Buf()
        Wog = sb([128, 8, 1024], BF16, "Wog"); bWog = Buf()
        W1c = sb([128, 32, 256], BF16, "W1c"); bW1c = Buf()
        W2c = sb([128, 2, 2, 64], BF16, "W2c"); bW2c = Buf()
        Lora = sb([128, 512], BF16, "Lora"); bLora = Buf()
        identf = sb([128, 128], F32, "identf"); bidf = Buf()
        identb = sb([128, 128], BF16, "identb"); bidb = Buf()
        masks = sb([128, 4, 128], BF16, "masks"); bmasks = Buf()
        biasDS = sb([128, 2, 2, 512], BF16, "biasDS"); bbias = Buf()
        emat = sb([64, 2048], BF16, "emat"); bemat = Buf()
        zsh = sb([128, 256], BF16, "zsh"); bzsh = Buf()
        biasC = sb([128, 2, 512], BF16, "biasC"); bbiasC = Buf()
        w0a0 = sb([128, 512], F32, "w0a0"); bw0a0 = Buf()
        bmisc = Buf()
        mimp = sb([128, 32], F32, "mimp"); bmimp = Buf()
        keepadd = sb([128, 8, 2, 32], F32, "keepadd"); bka = Buf()
        ind2 = sb([128, 2], F32, "ind2"); bind2 = Buf()
        indT = sb([8, 4, 128], F32, "indT"); bindT = Buf()
        ones_f = sb([128, 128], F32, "ones_f"); bones = Buf()
        bc_small = sb([128, 192], F32, "bc_small"); bbcs = Buf()
        ln_w_bc = sb([128, 512], F32, "ln_w_bc"); blnw = Buf()
        vec4 = sb([128, 4, 4], F32, "vec4"); bvec4 = Buf()
        mucol = sb([128, 2, 13], F32, "mucol"); bmu = Buf()
        kgc = sb([64, 1], F32, "kgc"); bkgc = Buf()
        gcol = sb([128, 8], F32, "gcol"); bgcol = Buf()
        badaT = sb([128, 24], F32, "badaT"); bbada = Buf()
        cTt = sb([128, 8, 2], F32, "cTt"); bcT = Buf()
        modT = sb([128, 24, 2], F32, "modT"); bmod = Buf()
        gsT = sb([128, 2, 8], F32, "gsT"); bgs = Buf()
        hb2 = sb([128, 2, 2], F32, "hb2"); bhb2 = Buf()
        cneg = sb([128, 16], F32, "cneg"); bcneg = Buf()
        peTb = sb([128, 32], BF16, "peTb"); bpeT = Buf()
        siluc = sb([128, 8, 2], F32, "siluc"); bsc = Buf()
        gtmp = sb([128, 16], F32, "gtmp"); bgtmp = Buf()

        stg = []; bstg = []
        for i_ in range(2):
            a_, b_ = slot(16 * i_, [128, 4096], F32)
            stg.append(a_); bstg.append(b_)
        kT = sb([128, 2, 2048], BF16, "kT"); bkT = [Buf() for _ in range(16)]
        Vcf = sb([128, 4160], BF16, "Vc"); bVc = [Buf() for _ in range(16)]
        Vc = Vcf[:].rearrange("p (a b c d) -> p a b c d", a=16, b=2, c=2)
        stgb = kT[:].rearrange("p a b -> p (a b)"); bstgb = bkT
        gate_bc = Vcf[:].bitcast(F32)[:, 0:2048].rearrange("p (s n) -> p s n", s=2); bgbc = bVc

        ldn = [0]
        sem_lds = [S.dma_sem(f"ld{i}") for i in range(8)]

        def ld(out, in_, w):
            sm = sem_lds[ldn[0] % 8]
            ldn[0] += 1
            if S.dcnt[sm] > 0:
                S.wait_all("sp", [(sm, S.dcnt[sm])])
            return DMA(out, in_, sm, w=w)

        ld(identf[:], ident_d, [bidf])
        CP("dve", identb[:], identf[:], [bidf], [bidb])
        ld(stg[0][:, 0:512].rearrange("p (a b) -> p a b", a=4), masks_d, [bstg[0]])
        CP("dve", masks[:], stg[0][:, 0:512].rearrange("p (a b) -> p a b", a=4), [bstg[0]], [bmasks])
        ld(mimp[:], mimp_d, [bmimp])
        ld(keepadd[:].rearrange("p a b c -> p (a b c)"), keepadd_d, [bka])
        ld(ind2[:], ind2_d, [bind2])
        ld(indT[:].rearrange("p a b -> p (a b)"), indT_d, [bindT])
        ld(bc_small[:], bc_small_d, [bbcs])
        ld(ln_w_bc[:], ln_w_bc_d, [blnw])
        ld(vec4[:], vec4_d, [bvec4])
        ld(mucol[:, 0, :], mu_d, [bmu])
        TS("dve", mucol[:, 1, :], mucol[:, 0, :], -1.0, 1.0, ALU.mult, ALU.add, [bmu], [bmu])
        ld(kgc[:], kgc_d, [bkgc])
        MSET("pool", w0a0[:], 0.0, [bw0a0])
        ld(w0a0[0:1, :], w0a0_d[0:1, :], [bw0a0])
        ld(w0a0[64:65, :], w0a0_d[1:2, :], [bw0a0])
        MSET("pool", emat[:], 0.0, [bemat])
        MSET("pool", zsh[:], 0.0, [bzsh])
        MSET("pool", biasC[:].rearrange("p a b -> p (a b)"), 0.0, [bbiasC])
        ld(gcol[:], g_norm_d, [bgcol])
        ld(badaT[:], b_ada_d, [bbada])
        ld(cTt[:], cT_d, [bcT])
        MSET("pool", ones_f[:], 1.0, [bones])
        MSET("pool", cneg[:], -0.5, [bcneg])
        ld(stg[1][0:16, 0:256], zsh_d, [bstg[1]])
        CP("dve", zsh[0:16, :], stg[1][0:16, 0:256], [bstg[1]], [bzsh])
        ld(stg[1][0:32, 0:2048], emat_d, [bstg[1]])
        CP("dve", emat[0:32, :], stg[1][0:32, 0:2048], [bstg[1]], [bemat])
        ld(stg[1][:, 2048:2560], lora_d, [bstg[1]])
        CP("dve", Lora[:], stg[1][:, 2048:2560], [bstg[1]], [bLora])
        ld(stg[1][:, 2560:2816].rearrange("p (a b c) -> p a b c", a=2, b=2), w2_d, [bstg[1]])
        CP("dve", W2c[:], stg[1][:, 2560:2816].rearrange("p (a b c) -> p a b c", a=2, b=2), [bstg[1]], [bW2c])
        ld(stg[1][:, 2816:2848], peT_d, [bstg[1]])
        CP("dve", peTb[:], stg[1][:, 2816:2848], [bstg[1]], [bpeT])
        for g in range(2):
            ld(stg[0][:, 0:512], c31_d[g], [bstg[0]])
            for v in range(2):
                ld(stg[1][:, 0:512], tblDS_d[v, g], [bstg[1]])
                TT("dve", stg[1][:, 0:512], stg[1][:, 0:512], stg[0][:, 0:512], ALU.subtract, [bstg[0], bstg[1]], [bstg[1]])
                if v == 0:
                    ld(stg[1][:, 512:1024], maskD_d, [bstg[1]])
                    STT(biasDS[:, v, g, :], stg[1][:, 0:512], 8.0, stg[1][:, 512:1024], ALU.mult, ALU.add, [bstg[1]], [bbias])
                else:
                    TS("dve", biasDS[:, v, g, :], stg[1][:, 0:512], 8.0, None, ALU.mult, None, [bstg[1]], [bbias])
            ld(stg[1][0:16, 0:512], tblC_d[g], [bstg[1]])
            ld(stg[1][0:16, 512:1024], maskC_d, [bstg[1]])
            TT("dve", stg[1][0:16, 0:512], stg[1][0:16, 0:512], stg[0][0:16, 0:512], ALU.subtract, [bstg[0], bstg[1]], [bstg[1]])
            STT(biasC[0:16, g, :], stg[1][0:16, 0:512], 8.0, stg[1][0:16, 512:1024], ALU.mult, ALU.add, [bstg[1]], [bbiasC])

        def stage_load(i, src_ap, ncols, nk=8):
            view = stg[i][:, 0:nk * ncols].rearrange("p (k n) -> p k n", k=nk)
            DMA(view, src_ap, sem_stg[i], w=[bstg[i]])
            return view

        si = 0
        for c0 in range(0, NRES, 512):
            n = min(512, NRES - c0)
            v = stage_load(si, w_res_d[:, c0:c0 + n].rearrange("(k p) n -> p k n", p=128), n)
            CP("dve" if si == 0 else "act", Wres[:, :, c0:c0 + n], v, [bstg[si]], [bWres])
            si ^= 1
        for c in range(6):
            v = stage_load(si, w_rest_d[:, c * 512:(c + 1) * 512].rearrange("(k p) n -> p k n", p=128), 512)
            sv = stgb[:, 0:4096].rearrange("p (k n) -> p k n", k=8)
            CP("dve" if si == 0 else "act", sv, v, [bstg[si]], [bstgb])
            for j_ in range(2):
                DMA(wrest_s[2 * c + j_].rearrange("p (k n) -> p k n", k=8),
                    stgb[:, 0:4096].rearrange("p (k j n) -> p k j n", k=8, j=2)[:, :, j_, :], sem_scr, r=[bstgb], w=[Buf()])
            si ^= 1
        for (wd_, Wt, bW) in ((w_out_a_d, Wouta, bWouta), (w_out_b_d, Woutb, bWoutb)):
            v = stage_load(si, wd_.rearrange("(k p) n -> p k n", p=128), 1024, nk=4)
            CP("dve" if si == 0 else "act", Wt[:], v, [bstg[si]], [bW])
            si ^= 1
        for (wd_, lo) in ((w1k_d, 0), (w1v_d, 64)):
            for hh in range(2):
                view = stg[si][lo:lo + 64, 0:4096].rearrange("p (k n) -> p k n", k=16)
                DMA(view, wd_[hh * 1024:(hh + 1) * 1024, :].rearrange("(k p) n -> p k n", p=64), sem_stg[si], w=[bstg[si]])
                CP("dve" if si == 0 else "act", W1c[lo:lo + 64, hh * 16:(hh + 1) * 16, :], view, [bstg[si]], [bW1c])
                si ^= 1
        ACT(siluc[:], cTt[:], AF.Tanh, [bcT], [bsc], scale=0.5)
        TS("dve", siluc[:], siluc[:], 0.5, 0.5, ALU.mult, ALU.add, [bsc], [bsc])
        TT("dve", siluc[:], siluc[:], cTt[:], ALU.mult, [bsc, bcT], [bsc])
        pm, bpm = bank("proj")
        silucb = sb([128, 8, 2], BF16, "silucb"); bscb = Buf()
        CP("dve", silucb[:], siluc[:], [bsc], [bscb])
        for c in range(6):
            v = stage_load(si, w_ada_d[:, c * 512:(c + 1) * 512].rearrange("(k p) n -> p k n", p=128), 512)
            vb = stgb[:, 0:4096].rearrange("p (k n) -> p k n", k=8)
            CP("dve" if si == 0 else "act", vb, v, [bstg[si]], [bstgb])
            for jj in range(4):
                j = c * 4 + jj
                for kc in range(8):
                    MM(pm[:, j * 2:j * 2 + 2], vb[:, kc, jj * 128:(jj + 1) * 128], silucb[:, kc, :], start=(kc == 0), stop=(kc == 7),
                       r=[bstgb, bscb], w=[bpm], inc=(kc == 7))
            si ^= 1
        TT("dve", modT[:], pm[:, 0:48].rearrange("p (j b) -> p j b", b=2), bcast(badaT[:], [128, 24, 2], 2), ALU.add, [bpm, bbada], [bmod])
        for s in range(2):
            STT(gsT[:, s, :], modT[:, 8:16, s], 1.0, gcol[:], ALU.add, ALU.mult, [bmod, bgcol], [bgs])
        CP("dve", gtmp[:].rearrange("p (s j) -> p s j", s=2), modT[:, 16:24, :].rearrange("p j s -> p s j"), [bmod], [bgtmp])
        for q4 in range(4):
            pg, bpg = bank("proj")
            for jq in range(4):
                qq = q4 * 4 + jq
                MM(pg[0:1, jq * 128:(jq + 1) * 128], gtmp[:, qq:qq + 1], identf[:], r=[bgtmp, bidf], w=[bpg], inc=(jq == 3))
            CP("dve", stg[1][0:1, q4 * 512:(q4 + 1) * 512], pg[0:1, 0:512], [bpg], [bstg[1]])
        for s in range(2):
            for hh in range(2):
                pb_, bpb_ = bank("proj")
                MM(pb_[:, :], ones_f[0:1, :], stg[1][0:1, s * 1024 + hh * 512: s * 1024 + hh * 512 + 512], r=[bones, bstg[1]], w=[bpb_])
                TS("dve", gate_bc[:, s, hh * 512:(hh + 1) * 512], pb_[:, :], 0.5, None, ALU.mult, None, [bpb_], [bgbc])
        for s in (1, 0):
            for hh in range(2):
                v = stage_load(0, w_o_d[:, hh * 512:(hh + 1) * 512].rearrange("(k p) n -> p k n", p=128), 512)
                TT("dve", Wog[:, :, hh * 512:(hh + 1) * 512], v, bcast(gate_bc[:, s, hh * 512:(hh + 1) * 512], [128, 8, 512], 1), ALU.mult,
                   [bstg[0], bgbc], [bWog])
            if s == 1:
                DMA(wog_s, Wog[:].rearrange("p k n -> p (k n)"), sem_scr, r=[bWog], w=[Buf()])
        for kv in range(2):
            lo = kv * 64
            ph, bph = bank("proj")
            for jh in range(2):
                for pos in range(32):
                    MM(ph[:, jh:jh + 1], W1c[lo:lo + 64, pos, jh * 128:(jh + 1) * 128], peTb[lo:lo + 64, pos:pos + 1],
                       start=(pos == 0), stop=(pos == 31), r=[bW1c, bpeT], w=[bph], inc=(pos == 31))
            CP("dve", hb2[:, kv, :], ph[:, 0:2], [bph], [bhb2])
        S.barrier()
        print("SBUF remaining before main alloc:", nc.sbuf_bytes_remaining)

        xt = [sb([128, 1024], F32, f"xt{i}") for i in range(2)]; bxt = [Buf(), Buf()]
        hT = [sb([128, 8, 130], BF16, f"hT{i}") for i in range(2)]; bhT = [Buf(), Buf()]
        for i_ in range(2):
            MSET("pool", hT[i_][:].rearrange("p a b -> p (a b)"), 0.0, [bhT[i_]])
        ynsa = sb([128, 512], F32, "ynsa"); bynsa = Buf()
        yn = sb([128, 512], F32, "yn"); byn = Buf()
        st12 = sb([128, 16], F32, "st12"); bst12 = Buf()
        rs12 = sb([128, 16], F32, "rs12"); brs12 = Buf()
        MSET("pool", Vcf[:], 1.0, bVc)
        gsig = sb([128, 3, 8], F32, "gsig"); bgsig = Buf()
        kvc = sb([128, 2, 144], BF16, "kvc"); bkvc = Buf()
        kcT = sb([64, 2, 128], BF16, "kcT"); bkcT = Buf()
        vcT = sb([64, 2, 128], F32, "vcT"); bvcT = Buf()
        vca = sb([128, 2, 65], F32, "vca"); bvca = Buf()
        MSET("pool", vca[:].rearrange("p a b -> p (a b)"), 1.0, [bvca])
        hu = sb([128, 64], F32, "hu"); bhu = Buf()
        hw_ = sb([128, 64], F32, "hw_"); bhw = Buf()
        hid = sb([128, 64], BF16, "hid"); bhid = Buf()
        kcs = sb([64, 48], F32, "kcs"); bkcs = Buf()
        coef = sb([128, 16], F32, "coef"); bcoef = Buf()
        impr = sb([128, 2, 32], F32, "impr"); bimpr = Buf()
        imp2 = sb([128, 32], F32, "imp2"); bimp2 = Buf()
        m8a = sb([128, 8], F32, "m8a"); bm8a = Buf()
        m8b = sb([128, 8], F32, "m8b"); bm8b = Buf()
        nsel = sb([128, 2, 32], F32, "nsel"); bnsel = Buf()
        nselT = sb([64, 2, 128], BF16, "nselT"); bnselT = Buf()
        MSET("pool", nselT[:].rearrange("p a b -> p (a b)"), 0.0, [bnselT])
        wdad = sb([128, 128], F32, "wdad"); bwdad = Buf()
        wdadb = sb([128, 128], BF16, "wdadb"); bwdadb = Buf()
        eLC = sb([128, 4], F32, "eLC"); beLC = Buf()
        rn8 = sb([128, 8], F32, "rn8"); brn8 = Buf()
        rn8T = sb([8, 128], F32, "rn8T"); brn8T = Buf()
        sbon = sb([128, 8], F32, "sbon"); bsbon = Buf()
        Pst = sb([128, 4, 64], F32, "Pst"); bPst = Buf()
        Pb = sb([128, 4, 64], BF16, "Pb"); bPb = Buf()
        lnst = sb([128, 32], F32, "lnst"); blnst = Buf()
        NWS = 2
        WS = [sb([128, 8, 256], BF16, f"WS{i}") for i in range(NWS)]; bWS = [Buf() for _ in range(NWS)]
        sq, bsq = slot(0, [128, 1024], F32)
        xs, bxs = sq, bsq
        qn2, bqn2 = slot(4, [128, 8, 2, 64], BF16)
        qT2, bqT2 = slot(35, [128, 8, 128], BF16)
        kn2, bkn2 = slot(8, [128, 2, 2, 64], BF16)
        PT = []; bPT = []
        NPT = 2
        for i_ in range(NPT):
            a_, b_ = slot(37 + i_, [128, 512], BF16)
            PT.append(a_); bPT.append(b_)
        PcT, bPcT = slot(39, [128, 512], F32)
        silA, bsilA = slot(15, [128, 4, 128], F32)
        silB, bsilB = slot(17, [128, 4, 128], F32)
        thA, bthA = slot(19, [128, 8, 128], BF16)
        thB, bthB = slot(21, [128, 8, 128], BF16)
        mg1, bmg1 = slot(23, [128, 8, 128], F32)
        mg2, bmg2 = slot(27, [128, 8, 128], F32)
        mgT, bmgT = slot(31, [128, 8, 128], BF16)
        yaT, byaT = slot(33, [128, 4, 128], BF16)
        ybT, bybT = slot(34, [128, 4, 128], BF16)
        rT, brT = slot(4, [128, 4, 128], F32)
        kTr, bkTr = slot(6, [128, 4, 128], F32)
        vT, bvT = slot(8, [128, 4, 128], F32)
        lwT, blw = slot(10, [128, 4, 128], F32)
        LT, bLT = slot(12, [128, 4, 128], F32)
        asg, basg = slot(14, [128, 4, 128], F32)
        e1, be1 = slot(16, [128, 4, 128], F32)
        e2, be2 = slot(18, [128, 4, 128], F32)
        e3, be3 = slot(20, [128, 4, 128], F32)
        kkn, bkkn = slot(22, [128, 4, 128], F32)
        kmod, bkmod = slot(24, [128, 4, 128], F32)
        tmpA, btmpA = slot(26, [128, 4, 128], F32)
        At, bAt = slot(28, [128, 4, 128], BF16)
        Bt, bBt = slot(29, [128, 4, 128], BF16)
        Kt, bKt = slot(30, [128, 4, 128], BF16)
        Rt, bRt = slot(31, [128, 4, 128], BF16)
        Bh, bBh = slot(32, [128, 512], BF16)
        Kh, bKh = slot(33, [128, 512], BF16)
        Vt, bVt = slot(34, [128, 512], BF16)
        Qm = []; bQm = []; QmT = []; bQmT = []; Xm = []; bXm = []
        for st_ in (10, 12):
            a_, b_ = slot(st_, [128, 8, 128], BF16); Qm.append(a_); bQm.append(b_)
        for st_ in (14, 18):
            a_, b_ = slot(st_, [128, 8, 128], BF16); QmT.append(a_); bQmT.append(b_)
        for st_ in (20, 22):
            a_, b_ = slot(st_, [128, 8, 128], BF16); Xm.append(a_); bXm.append(b_)
        AakT, bAak = slot(24, [128, 8, 128], BF16)
        MrbT, bMrb = slot(4, [128, 8, 128], BF16)
        MrkT, bMrk = slot(6, [128, 8, 128], BF16)
        rhs0, brhs0 = slot(8, [128, 8, 64], BF16)
        Ub, bUb = slot(9, [128, 8, 64], BF16)

        def f4(t):
            return t.rearrange("p c t -> p (c t)")

        def REDUCE(out, in_, r, w):
            return S.op("dve", lambda e: e.tensor_reduce(out=out, in_=in_, axis=AX.X, op=ALU.add), r, w)

        def MAX8(out, in_, r, w):
            return S.op("dve", lambda e: e.max(out=out, in_=in_), r, w)

        def MREP(out, rep, vals, r, w):
            return S.op("dve", lambda e: e.match_replace(out=out, in_to_replace=rep, in_values=vals, imm_value=-3.0e38), r, w)

        def RECIP(out, in_, r, w):
            return S.op("dve", lambda e: e.reciprocal(out=out, in_=in_), r, w)

        def SCAN(out, d0, d1, r, w):
            return S.op("dve", lambda e: e.tensor_tensor_scan(out=out, data0=d0, data1=d1, initial=0.0, op0=ALU.mult, op1=ALU.add), r, w)

        print("SBUF remaining:", nc.sbuf_bytes_remaining)
        ws_i = [0]

        def chk(n):
            if stage <= n:
                raise _Stop()

        def tile_body2(s, i):
            T = s * 16 + i
            yield "f"
            tok0 = T * 128
            yield "f"
            xb_ = T % 2
            yield "f"
            x_t = xt[xb_]; bx = bxt[xb_]
            yield "f"
            hcur = hT[T % 2]; bh = bhT[T % 2]
            yield "f"
            hprev = hT[(T + 1) % 2]; bhp = bhT[(T + 1) % 2]
            yield "f"
            ACT(sq[:], x_t[:], AF.Square, [bx], [bsq, bst12], accum=st12[:, 0:1])
            yield "f"
            TS("dve", st12[:, 0:1], st12[:, 0:1], 1.0 / 1024, 1e-6, ALU.mult, ALU.add, [bst12], [bst12])
            yield "f"
            TT("pool", rs12[:, 0:1], st12[:, 0:1], cneg[:, 0:1], ALU.pow, [bst12, bcneg], [brs12])
            yield "f"
            TS("dve", xs[:], x_t[:], rs12[:, 0:1], None, ALU.mult, None, [bx, brs12], [bxs])
            yield "f"
            import os as _os
            yield "f"
            _sk = _os.environ.get("SKIP", "")
            yield "f"
            if i == 0:
                if "m" not in _sk:
                    MSET("pool", hcur[:, :, 0:1], 0.0, [bh])
            else:
                CP("pool", hcur[:, :, 0:1], hprev[:, :, 128:129], [bhp], [bh])
            yield "f"
            for half in range(2):
                pp, bp = bank("proj")
                for j in range(4):
                    kc = half * 4 + j
                    TR(pp[:, j * 128:(j + 1) * 128], xs[:, kc * 128:(kc + 1) * 128], identf[:], [bxs, bidf], [bp], inc=(j == 3))
                for j in range(4):
                    kc = half * 4 + j
                    if "a" in _sk:
                        ACT(hcur[:, kc, 1:129], pp[:, j * 128:(j + 1) * 128], AF.Identity, [bp, bgs, bmod], [bh])
                    elif "b" in _sk:
                        ACT(hcur[:, kc, 2:130], pp[:, j * 128:(j + 1) * 128], AF.Identity, [bp, bgs, bmod], [bh],
                            bias=modT[:, kc, s:s + 1], scale=gsT[:, s, kc:kc + 1])
                    else:
                        ACT(hcur[:, kc, 1:129], pp[:, j * 128:(j + 1) * 128], AF.Identity, [bp, bgs, bmod], [bh],
                            bias=modT[:, kc, s:s + 1], scale=gsT[:, s, kc:kc + 1])
            yield "f"
            dump(f"hT_{T}", hcur[:], [bh], BF16)
            yield "f"
            chk(1)
            yield "f"

            pq, bpq = bank("proj")
            yield "f"
            for kc in range(8):
                MM(pq[:, :], hcur[:, kc, 1:129], Wres[:, kc, CQ:CQ + 512], start=(kc == 0), stop=(kc == 7), r=[bh, bWres], w=[bpq], inc=(kc == 7))
            yield "f"
            ACT(sq[:, 0:512], pq[:, :], AF.Square, [bpq], [bsq])
            yield "f"
            REDUCE(st12[:, 0:8], sq[:, 0:512].rearrange("p (h d) -> p h d", h=8), [bsq], [bst12])
            yield "f"
            pkv, bpkv = bank("proj")
            yield "f"
            for kc in range(8):
                MM(pkv[:, :], hcur[:, kc, 1:129], Wres[:, kc, CKV:CKV + 512], start=(kc == 0), stop=(kc == 7), r=[bh, bWres], w=[bpkv], inc=(kc == 7))
            yield "f"
            ACT(sq[:, 512:768], pkv[:, 0:256], AF.Square, [bpkv], [bsq])
            yield "f"
            REDUCE(st12[:, 8:12], sq[:, 512:768].rearrange("p (h d) -> p h d", h=4), [bsq], [bst12])
            yield "f"
            TS("dve", st12[:, 0:12], st12[:, 0:12], 1.0 / 64, 1e-6, ALU.mult, ALU.add, [bst12], [bst12])
            yield "f"
            TT("pool", rs12[:, 0:12], st12[:, 0:12], cneg[:, 0:12], ALU.pow, [bst12, bcneg], [brs12])
            yield "f"
            chk(1.2)
            yield "f"
            for h in range(8):
                STT(qn2[:, h, :, :], bcast(pq[:, h * 64:(h + 1) * 64], [128, 2, 64], 1), rs12[:, h:h + 1],
                    bcast(bc_small[:, 0:64], [128, 2, 64], 1), ALU.mult, ALU.mult, [bpq, brs12, bbcs], [bqn2])
            yield "f"
            for gg in range(2):
                for br in range(2):
                    c0 = gg * 128 + br * 64
                    STT(kn2[:, gg, br, :], pkv[:, c0:c0 + 64], rs12[:, 8 + gg * 2 + br:9 + gg * 2 + br],
                        bc_small[:, 64 + br * 64:128 + br * 64], ALU.mult, ALU.mult, [bpkv, brs12, bbcs], [bkn2])
            yield "f"
            CP("act", Vc[:, i, :, :, 0:64], pkv[:, 256:512].rearrange("p (b g d) -> p b g d", b=2, g=2), [bpkv], [bVc[i]])
            yield "f"
            chk(1.4)
            yield "f"
            pt, bpt = bank("proj")
            yield "f"
            ptb = pt[:].bitcast(BF16)
            yield "f"
            for h in range(8):
                TR(ptb[:, h * 128:(h + 1) * 128], qn2[:, h, :, :].rearrange("p c d -> p (c d)"), identb[:], [bqn2, bidb], [bpt], inc=(h == 7))
            yield "f"
            CP("act", qT2[:].rearrange("p h q -> p (h q)"), ptb[:, 0:1024], [bpt], [bqT2])
            yield "f"
            pt2, bpt2 = bank("proj")
            yield "f"
            pt2b = pt2[:].bitcast(BF16)
            yield "f"
            for gg in range(2):
                TR(pt2b[:, gg * 128:(gg + 1) * 128], kn2[:, gg, :, :].rearrange("p c d -> p (c d)"), identb[:], [bkn2, bidb], [bpt2], inc=(gg == 1))
            yield "f"
            CP("dve", kT[:, :, i * 128:(i + 1) * 128], pt2b[:, 0:256].rearrange("p (g t) -> p g t", g=2), [bpt2], [bkT[i]])
            yield "f"
            chk(1.6)
            yield "f"
            pgt, bpgt = bank("proj")
            yield "f"
            for kc in range(8):
                MM(pgt[:, 0:24], hcur[:, kc, 1:129], Wres[:, kc, CG:CG + 24], start=(kc == 0), stop=(kc == 7), r=[bh, bWres], w=[bpgt], inc=(kc == 7))
            yield "f"
            ACT(gsig[:].rearrange("p a b -> p (a b)"), pgt[:, 0:24], AF.Tanh, [bpgt], [bgsig], scale=0.5)
            yield "f"
            TS("dve", gsig[:].rearrange("p a b -> p (a b)"), gsig[:].rearrange("p a b -> p (a b)"), 0.5, 0.5, ALU.mult, ALU.add, [bgsig], [bgsig])
            yield "f"
            pcm, bpcm = bank("proj")
            yield "f"
            for gg in range(2):
                for kc in range(8):
                    MM(pcm[:, gg * 128:(gg + 1) * 128], Wres[:, kc, CC + gg * 128:CC + (gg + 1) * 128], hcur[:, kc, 1:129],
                       start=(kc == 0), stop=(kc == 7), r=[bh, bWres], w=[bpcm], inc=(kc == 7 and gg == 1))
            yield "f"
            CP("pool", kvc[:, :, 0:16], kvc[:, :, 128:144], [bkvc], [bkvc])
            yield "f"
            CP("act", kvc[:, :, 16:144], pcm[:, 0:256].rearrange("p (g t) -> p g t", g=2), [bpcm], [bkvc])
            yield "f"
            chk(1.8)
            yield "f"
            m0 = 1 if i == 0 else 0
            yield "f"
            nm = 8 - m0
            yield "f"
            for kv in range(2):
                lo = kv * 64
                phd, bphd = bank("proj")
                for jh in range(2):
                    for pos in range(32):
                        MM(phd[:, jh * 16:jh * 16 + 16].rearrange("p (g m) -> p g m", g=2), W1c[lo:lo + 64, pos, jh * 128:(jh + 1) * 128],
                           kvc[lo:lo + 64, :, pos:pos + 113:16], start=(pos == 0), stop=(pos == 31), r=[bW1c, bkvc], w=[bphd],
                           inc=(pos == 31))
                for jh in range(2):
                    reg = (kv * 2 + jh) * 16
                    ACT(hu[:, reg:reg + 16], phd[:, jh * 16:jh * 16 + 16], AF.Identity, [bphd, bhb2], [bhu], bias=hb2[:, kv, jh:jh + 1])
            yield "f"
            chk(1.85)
            yield "f"
            TT("dve", hw_[:], hu[:], hu[:], ALU.mult, [bhu], [bhw])
            yield "f"
            TS("dve", hw_[:], hw_[:], 0.044715, 1.0, ALU.mult, ALU.add, [bhw], [bhw])
            yield "f"
            TT("dve", hw_[:], hw_[:], hu[:], ALU.mult, [bhw, bhu], [bhw])
            yield "f"
            ACT(hw_[:], hw_[:], AF.Tanh, [bhw], [bhw], scale=math.sqrt(2.0 / math.pi))
            yield "f"
            STT(hid[:], hw_[:], 1.0, hu[:], ALU.add, ALU.mult, [bhw, bhu], [bhid])
            yield "f"
            chk(1.9)
            yield "f"
            pc2, bpc2 = bank("proj")
            yield "f"
            for kv in range(2):
                for jh in range(2):
                    reg = (kv * 2 + jh) * 16
                    MM(pc2[0:64, kv * 16:(kv + 1) * 16], W2c[:, kv, jh, :], hid[:, reg:reg + 16], start=(jh == 0), stop=(jh == 1),
                       r=[bW2c, bhid], w=[bpc2], inc=(jh == 1))
            yield "f"
            TS("dve", kcs[:, 0:16], pc2[0:64, 0:16], 0.5, None, ALU.mult, None, [bpc2], [bkcs])
            yield "f"
            TT("dve", kcs[:, 16:32], kcs[:, 0:16], kcs[:, 0:16], ALU.mult, [bkcs], [bkcs])
            yield "f"
            MM(pc2[0:64, 64:80], ones_f[0:64, 0:64], kcs[:, 16:32], r=[bones, bkcs], w=[bpc2])
            yield "f"
            TS("dve", kcs[:, 32:48], pc2[0:64, 64:80], 1.0 / 64, 1e-6, ALU.mult, ALU.add, [bpc2], [bkcs])
            yield "f"
            TT("pool", kcs[:, 16:32], kcs[:, 32:48], cneg[0:64, 0:16], ALU.pow, [bkcs, bcneg], [bkcs])
            yield "f"
            TT("dve", kcs[:, 0:16], kcs[:, 0:16], kcs[:, 16:32], ALU.mult, [bkcs], [bkcs])
            yield "f"
            n0 = 8 * i - 1 + m0
            yield "f"
            TS("dve", kcT[:, :, n0:n0 + nm], kcs[:, 0:16].rearrange("p (g m) -> p g m", g=2)[:, :, m0:8], kgc[:, 0:1], None, ALU.mult, None,
               [bkcs, bkgc], [bkcT])
            yield "f"
            TS("dve", vcT[:, :, n0:n0 + nm], pc2[0:64, 16:32].rearrange("p (g m) -> p g m", g=2)[:, :, m0:8], 0.5, None, ALU.mult, None,
               [bpc2], [bvcT])
            yield "f"
            nv = 8 * i + 7
            yield "f"
            chk(1.95)
            yield "f"
            pvt, bpvt = bank("proj")
            yield "f"
            for gg in range(2):
                TR(pvt[0:nv, gg * 64:(gg + 1) * 64], vcT[:, gg, 0:nv], identf[0:64, 0:64], [bvcT, bidf], [bpvt], inc=(gg == 1))
            yield "f"
            CP("dve", vca[0:nv, :, 0:64], pvt[0:nv, 0:128].rearrange("p (g d) -> p g d", g=2), [bpvt], [bvca])
            yield "f"
            dump(f"kcT_{T}", kcT[:], [bkcT], BF16)
            yield "f"
            dump(f"vca_{T}", vca[:], [bvca])
            yield "f"
            dump(f"qT2_{T}", qT2[:], [bqT2], BF16)
            yield "f"
            chk(2)
            yield "f"

            yield "front_done"
            def attn_gen():
                first_y = {0: True, 1: True}

                def finish(acc, bacc, br, gg):
                    accv = acc[:, 0:260].rearrange("p (h e) -> p h e", h=4)
                    c0 = br * 4
                    TS("dve", coef[:, c0:c0 + 4], accv[:, :, 64], 1e-30, None, ALU.max, None, [bacc], [bcoef])
                    RECIP(coef[:, c0:c0 + 4], coef[:, c0:c0 + 4], [bcoef], [bcoef])
                    if br == 0:
                        CP("dve", coef[:, 12:16], coef[:, 0:4], [bcoef], [bcoef])
                    gbr = {0: 0, 1: 1, 2: 2}[br]
                    TT("dve", coef[:, c0:c0 + 4], coef[:, c0:c0 + 4], gsig[:, gbr, gg * 4:(gg + 1) * 4], ALU.mult, [bcoef, bgsig], [bcoef])
                    yv = ynsa[:, gg * 256:(gg + 1) * 256].rearrange("p (h d) -> p h d", h=4)
                    cb = bcast(coef[:, c0:c0 + 4], [128, 4, 64], 2)
                    if first_y[gg]:
                        TT("dve", yv, accv[:, :, 0:64], cb, ALU.mult, [bacc, bcoef], [bynsa])
                        first_y[gg] = False
                    else:
                        for h in range(4):
                            STT(yv[:, h, :], accv[:, h, 0:64], coef[:, c0 + h:c0 + h + 1], yv[:, h, :], ALU.mult, ALU.add, [bacc, bcoef, bynsa], [bynsa])

                def pv(acc, bacc, Pt_, bP, vrhs, bv, first, last, K=128):
                    for h in range(4):
                        MM(acc[:, h * 65:(h + 1) * 65], Pt_[0:K, h * 128:(h + 1) * 128], vrhs, start=(first and h == 0), stop=(last and h == 3), r=[bP] + bv, w=[bacc],
                           inc=(h == 3), sgc=True)

                pti = [0]
                for gg in range(2):
                    sc, bsc_ = bank("sc")
                    MM(sc[0:nv, :], kcT[:, gg, 0:nv], qT2[0:64, gg * 4:(gg + 1) * 4, :].rearrange("p h q -> p (h q)"), start=True, stop=False,
                       r=[bkcT, bqT2], w=[bsc_], inc=False)
                    off = 128 - 8 * i
                    MM(sc[0:nv, :], zsh[:, off:off + nv], biasC[:, gg, :], start=False, stop=True, r=[bzsh], w=[bsc_])
                    ACT(PcT[0:nv, :], sc[0:nv, :], AF.Exp, [bsc_], [bPcT], scale=0.125)
                    acc, bacc = bank("acc")
                    for h in range(4):
                        MM(acc[:, h * 65:(h + 1) * 65], PcT[0:nv, h * 128:(h + 1) * 128], vca[0:nv, gg, :], r=[bPcT, bvca], w=[bacc], inc=False)
                    for h in range(4):
                        MM(acc[:, 320 + h * 32:320 + (h + 1) * 32], PcT[0:nv, h * 128:(h + 1) * 128], mimp[0:nv, :], r=[bPcT, bmimp], w=[bacc], inc=(h == 3))
                    finish(acc, bacc, 0, gg)
                    yield
                    if i >= 8:
                        iv = impr[:, gg, :]
                        TS("dve", iv, acc[:, 320:352], coef[:, 12:13], None, ALU.mult, None, [bacc, bcoef], [bimpr])
                        for h in range(1, 4):
                            STT(iv, acc[:, 320 + h * 32:352 + h * 32], coef[:, 12 + h:13 + h], iv, ALU.mult, ALU.add, [bacc, bcoef, bimpr], [bimpr])
                        TT("dve", iv, iv, keepadd[:, i - 8, 0, :], ALU.mult, [bimpr, bka], [bimpr])
                        TT("dve", iv, iv, keepadd[:, i - 8, 1, :], ALU.add, [bimpr, bka], [bimpr])
                        MAX8(m8a[:], iv, [bimpr], [bm8a])
                        MREP(imp2[:], m8a[:], iv, [bimpr, bm8a], [bimp2])
                        MAX8(m8b[:], imp2[:], [bimp2], [bm8b])
                        TS("dve", nsel[:, gg, :], iv, m8b[:, 7:8], 1.0, ALU.is_ge, ALU.subtract, [bimpr, bm8b], [bnsel])
                        pn, bpn = bank("sc")
                        TR(pn[0:32, 0:128], nsel[:, gg, :], identf[:], [bnsel, bidf], [bpn])
                        CP("dve", nselT[0:32, gg, :], pn[0:32, 0:128], [bpn], [bnselT])
                dump(f"nsel_{T}", nsel[:], [bnsel])
                for br in (2, 1):
                    for gg in range(2):
                        lo = 0 if br == 1 else 64
                        j0 = 0 if br == 1 else max(0, i - 4)
                        acc, bacc = bank("acc")
                        prev_ = None
                        for j in range(j0, i + 1):
                            sc, bsc_ = bank("sc")
                            extra = []
                            if j == i:
                                extra.append((identb[:], biasDS[:, 0, gg, :], [bidb, bbias]))
                            if j == i - 1:
                                extra.append((identb[:], biasDS[:, 1, gg, :], [bidb, bbias]))
                            if br == 2 and j == i - 4:
                                extra.append((identb[:], bcast(masks[:, 0, :], [128, 4, 128], 1), [bidb, bmasks]))
                            if br == 1 and i >= 8:
                                extra.append((emat[:, j * 128:(j + 1) * 128], bcast(nselT[:, gg, :], [64, 4, 128], 1), [bemat, bnselT]))
                            MM(sc[:, :], kT[lo:lo + 64, gg, j * 128:(j + 1) * 128], qT2[lo:lo + 64, gg * 4:(gg + 1) * 4, :].rearrange("p h q -> p (h q)"),
                               start=True, stop=(len(extra) == 0), r=[bkT[j], bqT2], w=[bsc_], inc=(len(extra) == 0))
                            for ei, (l_, r_, bb_) in enumerate(extra):
                                lastx = ei == len(extra) - 1
                                MM(sc[:, :].rearrange("p (h q) -> p h q", h=4) if len(r_.shape) == 3 else sc[:, :], l_, r_, start=False, stop=lastx,
                                   r=bb_, w=[bsc_], inc=lastx)
                            Pt_ = PT[pti[0] % NPT]; bP = bPT[pti[0] % NPT]; pti[0] += 1
                            ACT(Pt_[:, :], sc[:, :], AF.Exp, [bsc_], [bP], scale=0.125)
                            if prev_ is not None:
                                pv(acc, bacc, prev_[0], prev_[1], Vc[:, prev_[2], br - 1, gg, :], [bVc[prev_[2]]], prev_[2] == j0, False)
                            prev_ = (Pt_, bP, j)
                            yield
                        pv(acc, bacc, prev_[0], prev_[1], Vc[:, prev_[2], br - 1, gg, :], [bVc[prev_[2]]], prev_[2] == j0, True)
                        finish(acc, bacc, br, gg)
                        yield
                dump(f"ynsa_{T}", ynsa[:], [bynsa])
                chk(3)

                yield
            def rwkv_gen():
                yield
                for c3 in range(0, 13, 3):
                    ps_, bps_ = bank("rw")
                    ncs = min(3, 13 - c3)
                    for cc in range(ncs):
                        c = c3 + cc
                        for kc in range(8):
                            MM(ps_[:, cc * 129:(cc + 1) * 129], Wres[:, kc, CS + c * 128:CS + (c + 1) * 128], hcur[:, kc, 0:129],
                               start=(kc == 0), stop=(kc == 7), r=[bh, bWres], w=[bps_], inc=(kc == 7))
                    for cc in range(ncs):
                        c = c3 + cc
                        if c < 4:
                            dst, bd = rT[:, c, :], brT
                        elif c < 8:
                            dst, bd = kTr[:, c - 4, :], bkTr
                        elif c < 12:
                            dst, bd = vT[:, c - 8, :], bvT
                        else:
                            dst, bd = wdad[:, :], bwdad
                        ACT(dst, ps_[:, cc * 129 + 1:cc * 129 + 129], AF.Identity, [bps_, bmu], [bd], scale=mucol[:, 1, c:c + 1])
                        STT(dst, ps_[:, cc * 129:cc * 129 + 128], mucol[:, 0, c:c + 1], dst, ALU.mult, ALU.add, [bps_, bmu, bd], [bd])
                yield
                dump(f"rT_{T}", rT[:], [brT])
                yield
                dump(f"wdad_{T}", wdad[:], [bwdad])
                yield
                ACT(wdadb[0:64, :], wdad[0:64, :], AF.Tanh, [bwdad], [bwdadb])
                yield
                CP("act", wdadb[64:128, :], wdad[64:128, :], [bwdad], [bwdadb])
                yield
                pz, bpz = bank("rw")
                yield
                pa, bpa = bank("rw")
                yield
                for c in range(4):
                    MM(pz[:, c * 128:(c + 1) * 128], Lora[0:64, c * 128:(c + 1) * 128], wdadb[0:64, :], start=True, stop=False, r=[bLora, bwdadb], w=[bpz], inc=False)
                    MM(pz[:, c * 128:(c + 1) * 128], w0a0[0:64, c * 128:(c + 1) * 128], ones_f[0:64, :], start=False, stop=True, r=[bw0a0, bones], w=[bpz], inc=(c == 3))
                yield
                for c in range(4):
                    MM(pa[:, c * 128:(c + 1) * 128], Lora[64:128, c * 128:(c + 1) * 128], wdadb[64:128, :], start=True, stop=False, r=[bLora, bwdadb], w=[bpa], inc=False)
                    MM(pa[:, c * 128:(c + 1) * 128], w0a0[64:128, c * 128:(c + 1) * 128], ones_f[64:128, :], start=False, stop=True, r=[bw0a0, bones], w=[bpa], inc=(c == 3))
                yield
                f4 = lambda t: t[:].rearrange("p c t -> p (c t)")
                yield
                ACT(f4(lwT), pz[:, :], AF.Tanh, [bpz], [blw], scale=0.5)
                yield
                cexp = math.exp(-0.5) * 0.5
                yield
                TS("dve", f4(lwT), f4(lwT), -cexp, -cexp, ALU.mult, ALU.add, [blw], [blw])
                yield
                ACT(f4(asg), pa[:, :], AF.Tanh, [bpa], [basg], scale=0.5)
                yield
                TS("dve", f4(asg), f4(asg), 0.5, 0.5, ALU.mult, ALU.add, [basg], [basg])
                yield
                for c in range(4):
                    SCAN(LT[:, c, :], ones_f[:, :], lwT[:, c, :], [bones, blw], [bLT])
                yield
                TT("dve", f4(tmpA), f4(LT), f4(lwT), ALU.subtract, [bLT, blw], [btmpA])
                yield
                ACT(f4(e1), f4(tmpA), AF.Exp, [btmpA], [be1])
                yield
                ACT(f4(e2), f4(LT), AF.Exp, [bLT], [be2], scale=-1.0)
                yield
                ACT(f4(e3), f4(LT), AF.Exp, [bLT], [be3])
                yield
                ACT(eLC[:, :], LT[:, :, 127], AF.Exp, [bLT], [beLC])
                yield
                yield
                for c in range(4):
                    TS("dve", kkn[:, c, :], kTr[:, c, :], vec4[:, 0, c:c + 1], None, ALU.mult, None, [bkTr, bvec4], [bkkn])
                yield
                ACT(f4(tmpA), f4(kkn), AF.Square, [bkkn], [btmpA])
                yield
                pk_, bpk_ = bank("rw")
                yield
                for c in range(4):
                    MM(pk_[:, c * 2:c * 2 + 2], tmpA[:, c, :], ind2[:, :], r=[btmpA, bind2], w=[bpk_], inc=(c == 3))
                yield
                TS("dve", rn8[:, :], pk_[:, 0:8], 1e-24, None, ALU.max, None, [bpk_], [brn8])
                yield
                TT("pool", rn8[:, :], rn8[:, :], cneg[:, 0:8], ALU.pow, [brn8, bcneg], [brn8])
                yield
                TR(pk_[0:8, 128:256], rn8[:, :], identf[:], [brn8, bidf], [bpk_])
                yield
                CP("dve", rn8T[:, :], pk_[0:8, 128:256], [bpk_], [brn8T])
                yield
                pr_, bpr_ = bank("rw")
                yield
                for c in range(4):
                    MM(pr_[:, c * 128:(c + 1) * 128], indT[:, c, :], rn8T[:, :], r=[bindT, brn8T], w=[bpr_], inc=(c == 3))
                yield
                TT("dve", f4(kkn), f4(kkn), pr_[:, :], ALU.mult, [bkkn, bpr_], [bkkn])
                yield
                dump(f"kkn_{T}", kkn[:], [bkkn])
                yield
                yield
                for c in range(4):
                    TS("dve", tmpA[:, c, :], asg[:, c, :], -1.0, vec4[:, 1, c:c + 1], ALU.add, ALU.mult, [basg, bvec4], [btmpA])
                yield
                STT(f4(kmod), f4(tmpA), 1.0, f4(kTr), ALU.add, ALU.mult, [btmpA, bkTr], [bkmod])
                yield
                dump(f"kmod_{T}", kmod[:], [bkmod])
                yield
                yield
                for c in range(4):
                    STT(tmpA[:, c, :], rT[:, c, :], vec4[:, 2, c:c + 1], kmod[:, c, :], ALU.mult, ALU.mult, [brT, bvec4, bkmod], [btmpA])
                yield
                for c in range(4):
                    MM(pk_[:, 256 + c * 2:256 + c * 2 + 2], tmpA[:, c, :], ind2[:, :], r=[btmpA, bind2], w=[bpk_], inc=(c == 3))
                yield
                CP("dve", sbon[:, :], pk_[:, 256:264], [bpk_], [bsbon])
                yield
                yield
                STT(f4(At), f4(kkn), -1.0, f4(e1), ALU.mult, ALU.mult, [bkkn, be1], [bAt])
                yield
                TT("dve", f4(tmpA), f4(kkn), f4(asg), ALU.mult, [bkkn, basg], [btmpA])
                yield
                TT("dve", f4(tmpA), f4(tmpA), f4(e2), ALU.mult, [btmpA, be2], [btmpA])
                yield
                CP("act", f4(Bt), f4(tmpA), [btmpA], [bBt])
                yield
                TT("dve", f4(e1), f4(kmod), f4(e2), ALU.mult, [bkmod, be2], [be1])
                yield
                CP("act", f4(Kt), f4(e1), [be1], [bKt])
                yield
                TT("dve", f4(Rt), f4(rT), f4(e3), ALU.mult, [brT, be3], [bRt])
                yield
                yield
                for c in range(4):
                    TS("dve", tmpA[:, c, :], tmpA[:, c, :], eLC[:, c:c + 1], None, ALU.mult, None, [btmpA, beLC], [btmpA])
                    ACT(e1[:, c, :], e1[:, c, :], AF.Identity, [be1, beLC], [be1], scale=eLC[:, c:c + 1])
                yield
                for (src, bsrc, dst, bdst) in ((tmpA, btmpA, Bh, bBh), (e1, be1, Kh, bKh), (vT, bvT, Vt, bVt)):
                    pp, bp = bank("rw")
                    for c in range(4):
                        TR(pp[:, c * 128:(c + 1) * 128], src[:, c, :], identf[:], [bsrc, bidf], [bp], inc=(c == 3))
                    CP("act", dst[:, :], pp[:, :], [bp], [bdst])
                yield
                chk(4)
                yield
                yield
                def hs(t, h):
                    return t[(h % 2) * 64:(h % 2) * 64 + 64, h // 2, :]
                yield

                def five(lt, blt, rt, brt, mask_i, dst, bdst):
                    for par in range(2):
                        pp, bp = bank("rw")
                        for hh in range(4):
                            h = hh * 2 + par
                            MM(pp[:, hh * 128:(hh + 1) * 128], hs(lt, h), hs(rt, h), r=[blt, brt], w=[bp], inc=(hh == 3))
                        TT("dve", dst[:, par:8:2, :], pp[:, :].rearrange("p (h t) -> p h t", h=4), bcast(masks[:, mask_i, :], [128, 4, 128], 1), ALU.mult,
                           [bp, bmasks], [bdst])
                yield

                five(Bt, bBt, At, bAt, 1, Qm[0], bQm[0])
                yield
                five(At, bAt, Bt, bBt, 3, QmT[0], bQmT[0])
                yield
                five(Kt, bKt, At, bAt, 1, AakT, bAak)
                yield
                five(Bt, bBt, Rt, bRt, 2, MrbT, bMrb)
                yield
                five(Kt, bKt, Rt, bRt, 2, MrkT, bMrk)
                yield
                yield
                TT("dve", Xm[0][:], Qm[0][:], bcast(identb[:], [128, 8, 128], 1), ALU.add, [bQm[0], bidb], [bXm[0]])
                yield
                cur = 0
                yield
                for lvl in range(1, 7):
                    nxt = cur ^ 1
                    lastl = lvl == 6
                    for half in range(2):
                        hsl = slice(half * 4, half * 4 + 4)
                        pT_, bpT_ = bank("rw")
                        for hh in range(4):
                            h = half * 4 + hh
                            MM(pT_[:, hh * 128:(hh + 1) * 128], Qm[cur][:, h, :], QmT[cur][:, h, :], r=[bQm[cur], bQmT[cur]], w=[bpT_], inc=(hh == 3))
                        CP("act", QmT[nxt][:, hsl, :], pT_[:, :].rearrange("p (h t) -> p h t", h=4), [bpT_], [bQmT[nxt]])
                        if not lastl:
                            pQ_, bpQ_ = bank("rw")
                            for hh in range(4):
                                h = half * 4 + hh
                                MM(pQ_[:, hh * 128:(hh + 1) * 128], QmT[cur][:, h, :], Qm[cur][:, h, :], r=[bQm[cur], bQmT[cur]], w=[bpQ_], inc=(hh == 3))
                            CP("dve", Qm[nxt][:, hsl, :], pQ_[:, :].rearrange("p (h t) -> p h t", h=4), [bpQ_], [bQm[nxt]])
                        yield
                    for half in range(2):
                        hsl = slice(half * 4, half * 4 + 4)
                        pX_, bpX_ = bank("rw")
                        for hh in range(4):
                            h = half * 4 + hh
                            MM(pX_[:, hh * 128:(hh + 1) * 128], QmT[nxt][:, h, :], Xm[cur][:, h, :], r=[bQmT[nxt], bXm[cur]], w=[bpX_], inc=(hh == 3))
                        TT("dve", Xm[nxt][:, hsl, :], pX_[:, :].rearrange("p (h t) -> p h t", h=4), Xm[cur][:, hsl, :], ALU.add, [bpX_, bXm[cur]], [bXm[nxt]])
                        yield
                    cur = nxt
                yield
                Xf = Xm[cur]; bXf = bXm[cur]
                yield
                chk(5)
                yield
                yield
                if i == 0:
                    MSET("pool", Pst[:], 0.0, [bPst])
                    MSET("pool", Pb[:], 0.0, [bPb])
                yield

                def ph_(t, h):
                    return t[(h % 2) * 64:(h % 2) * 64 + 64, h // 2, :]
                yield

                def vh(t, h):
                    return t[:, h * 64:(h + 1) * 64]
                yield

                for par in range(2):
                    p1, bp1 = bank("rw")
                    for hh in range(4):
                        h = hh * 2 + par
                        MM(p1[:, hh * 64:(hh + 1) * 64], hs(At, h), ph_(Pb, h), start=True, stop=False, r=[bAt, bPb], w=[bp1], inc=False)
                        MM(p1[:, hh * 64:(hh + 1) * 64], AakT[:, h, :], vh(Vt, h), start=False, stop=True, r=[bAak, bVt], w=[bp1], inc=(hh == 3))
                    CP("act", rhs0[:, par:8:2, :], p1[:, 0:256].rearrange("p (h v) -> p h v", h=4), [bp1], [brhs0])
                yield
                chk(5.2)
                yield
                p2, bp2 = bank("rw")
                yield
                for h in range(8):
                    MM(p2[:, h * 64:(h + 1) * 64], Xf[:, h, :], rhs0[:, h, :], r=[bXf, brhs0], w=[bp2], inc=(h == 7))
                yield
                CP("act", Ub[:].rearrange("p h v -> p (h v)"), p2[:, :], [bp2], [bUb])
                yield
                chk(5.4)
                yield
                p3s = []
                yield
                for par in range(2):
                    p3, bp3 = bank("proj")
                    p3s.append((p3, bp3))
                    for hh in range(4):
                        h = hh * 2 + par
                        MM(p3[:, hh * 64:(hh + 1) * 64], hs(Rt, h), ph_(Pb, h), start=True, stop=False, r=[bRt, bPb], w=[bp3], inc=False)
                        MM(p3[:, hh * 64:(hh + 1) * 64], MrkT[:, h, :], vh(Vt, h), start=False, stop=False, r=[bMrk, bVt], w=[bp3], inc=False)
                        MM(p3[:, hh * 64:(hh + 1) * 64], MrbT[:, h, :], Ub[:, h, :], start=False, stop=True, r=[bMrb, bUb], w=[bp3], inc=(hh == 3))
                yield
                chk(5.6)
                yield
                p4, bp4 = bank("rw")
                yield
                for h in range(8):
                    o_ = p4[(h % 2) * 64:(h % 2) * 64 + 64, (h // 2) * 64:(h // 2) * 64 + 64]
                    MM(o_, vh(Bh, h), Ub[:, h, :], start=True, stop=False, r=[bBh, bUb], w=[bp4], inc=False)
                    MM(o_, vh(Kh, h), vh(Vt, h), start=False, stop=True, r=[bKh, bVt], w=[bp4], inc=(h == 7))
                yield
                for c in range(4):
                    STT(Pst[:, c, :], Pst[:, c, :], eLC[:, c:c + 1], p4[:, c * 64:(c + 1) * 64], ALU.mult, ALU.add, [bPst, beLC, bp4], [bPst])
                yield
                CP("act", Pb[:].rearrange("p a b -> p (a b)"), Pst[:].rearrange("p a b -> p (a b)"), [bPst], [bPb])
                yield
                chk(5.8)
                yield
                yield
                yn3 = yn[:, :].rearrange("p (h d) -> p h d", h=8)
                yield
                for par in range(2):
                    p3, bp3 = p3s[par]
                    CP("act", yn3[:, par:8:2, :], p3[:, 0:256].rearrange("p (h d) -> p h d", h=4), [bp3], [byn])
                yield
                ACT(sq[:, 0:512], yn[:, :], AF.Square, [byn], [bsq])
                yield
                REDUCE(lnst[:, 0:8], yn3, [byn], [blnst])
                yield
                REDUCE(lnst[:, 8:16], sq[:, 0:512].rearrange("p (h d) -> p h d", h=8), [bsq], [blnst])
                yield
                chk(5.85)
                yield
                TS("dve", lnst[:, 0:16], lnst[:, 0:16], 1.0 / 64, None, ALU.mult, None, [blnst], [blnst])
                yield
                TT("dve", lnst[:, 16:24], lnst[:, 0:8], lnst[:, 0:8], ALU.mult, [blnst], [blnst])
                yield
                TT("dve", lnst[:, 16:24], lnst[:, 8:16], lnst[:, 16:24], ALU.subtract, [blnst], [blnst])
                yield
                TS("dve", lnst[:, 16:24], lnst[:, 16:24], 0.0, 64e-5, ALU.max, ALU.add, [blnst], [blnst])
                yield
                TT("pool", lnst[:, 24:32], lnst[:, 16:24], cneg[:, 0:8], ALU.pow, [blnst, bcneg], [blnst])
                yield
                STT(lnst[:, 16:24], lnst[:, 0:8], -1.0, lnst[:, 24:32], ALU.mult, ALU.mult, [blnst], [blnst])
                yield
                chk(5.9)
                yield
                for h in range(8):
                    ACT(yn[:, h * 64:(h + 1) * 64], yn[:, h * 64:(h + 1) * 64], AF.Identity, [byn, blnst], [byn],
                        bias=lnst[:, 16 + h:17 + h], scale=lnst[:, 24 + h:25 + h])
                yield
                chk(5.95)
                yield
                TT("dve", yn[:, :], yn[:, :], ln_w_bc[:, :], ALU.mult, [byn, blnw], [byn])
                yield
                chk(5.97)
                yield
                for h in range(8):
                    STT(yn[:, h * 64:(h + 1) * 64], Vt[:, h * 64:(h + 1) * 64], sbon[:, h:h + 1], yn[:, h * 64:(h + 1) * 64], ALU.mult, ALU.add,
                        [bVt, bsbon, byn], [byn])
                yield

            na_ = 2 * (2 + (i + 1) + min(5, i + 1)) + 2
            nr_ = 150
            ga_, gr_ = attn_gen(), rwkv_gen()
            da_ = dr_ = 0
            alive_a = alive_r = True
            while alive_a or alive_r:
                pick_a = alive_a and (not alive_r or da_ * nr_ <= dr_ * na_)
                if pick_a:
                    try:
                        next(ga_); da_ += 1
                    except StopIteration:
                        alive_a = False
                else:
                    try:
                        next(gr_); dr_ += 1
                    except StopIteration:
                        alive_r = False
            yield "mid_done"
            chk(6)
            yield "t"
            def wsload(c):
                k = ws_i[0] % NWS
                ws_i[0] += 1
                DMA(WS[k][:].rearrange("p k n -> p (k n)"), wrest_s[c], sem_ws[k], w=[bWS[k]])
                return WS[k], bWS[k]

            def rest_chunk(c):
                W_, bW_ = wsload(c)
                pp, bp = bank("tl")
                for sub in range(2):
                    for kc in range(8):
                        MM(pp[:, sub * 128:(sub + 1) * 128], W_[:, kc, sub * 128:(sub + 1) * 128], hcur[:, kc, 1:129], start=(kc == 0), stop=(kc == 7),
                           r=[bW_, bh], w=[bp], inc=(kc == 7 and sub == 1))
                return pp, bp

            for c in range(12):
                pp, bp = rest_chunk(c)
                if c < 4:
                    dst, bd = (silA, bsilA) if c < 2 else (silB, bsilB)
                    dv = dst[:, (c % 2) * 2:(c % 2) * 2 + 2, :].rearrange("p a t -> p (a t)")
                    ACT(dv, pp[:, 0:256], AF.Tanh, [bp], [bd], scale=0.5)
                    STT(dv, dv, 1.0, pp[:, 0:256], ALU.add, ALU.mult, [bd, bp], [bd])
                else:
                    dst, bd = (thA, bthA) if c < 8 else (thB, bthB)
                    cc = (c - 4) % 4
                    ACT(dst[:, cc * 2:cc * 2 + 2, :].rearrange("p a t -> p (a t)"), pp[:, 0:256], AF.Tanh, [bp], [bd], scale=0.5)
            yield "t"
            for (src, bsrc, sil, bsil, dst, bdst, lnb) in ((ynsa, bynsa, silA, bsilA, yaT, byaT, False), (yn, byn, silB, bsilB, ybT, bybT, True)):
                pp, bp = bank("tl")
                for c in range(4):
                    TR(pp[:, c * 128:(c + 1) * 128], src[:, c * 128:(c + 1) * 128], identf[:], [bsrc, bidf], [bp], inc=(c == 3))
                if not lnb:
                    STT(dst[:].rearrange("p c t -> p (c t)"), pp[:, :], 0.5, sil[:].rearrange("p c t -> p (c t)"), ALU.mult, ALU.mult, [bp, bsil], [bdst])
                else:
                    for c in range(4):
                        STT(tmpA[:, c, :], pp[:, c * 128:(c + 1) * 128], vec4[:, 3, c:c + 1], sil[:, c, :], ALU.add, ALU.mult, [bp, bvec4, bsil], [btmpA])
                    ACT(dst[:].rearrange("p c t -> p (c t)"), f4(tmpA), AF.Copy, [btmpA], [bdst], scale=0.5)
            yield "t"
            dump(f"yaT_{T}", yaT[:], [byaT], BF16)
            yield "t"
            dump(f"ybT_{T}", ybT[:], [bybT], BF16)
            yield "t"
            for (yT_, byT_, W_, bW_, th, bth, mg, bmg) in ((yaT, byaT, Wouta, bWouta, thA, bthA, mg1, bmg1), (ybT, bybT, Woutb, bWoutb, thB, bthB, mg2, bmg2)):
                for half in range(2):
                    pp, bp = bank("tl")
                    for mm_ in range(4):
                        mc = half * 4 + mm_
                        for kc in range(4):
                            MM(pp[:, mm_ * 128:(mm_ + 1) * 128], W_[:, kc, mc * 128:(mc + 1) * 128], yT_[:, kc, :], start=(kc == 0), stop=(kc == 3),
                               r=[bW_, byT_], w=[bp], inc=(kc == 3 and mm_ == 3))
                    STT(mg[:, half * 4:(half + 1) * 4, :].rearrange("p a t -> p (a t)"), th[:, half * 4:(half + 1) * 4, :].rearrange("p a t -> p (a t)"), 1.0, pp[:, :],
                        ALU.add, ALU.mult, [bth, bp], [bmg])
            yield "t"
            TT("dve", mgT[:].rearrange("p a t -> p (a t)"), mg1[:].rearrange("p a t -> p (a t)"), mg2[:].rearrange("p a t -> p (a t)"), ALU.add, [bmg1, bmg2], [bmgT])
            yield "t"
            dump(f"mgT_{T}", mgT[:], [bmgT], BF16)
            yield "t"
            if s == 1 and i == 0:
                DMA(Wog[:].rearrange("p k n -> p (k n)"), wog_s, sem_wog, w=[bWog])
            yield "t"
            for half in range(2):
                pp, bp = bank("tl")
                for kc in range(8):
                    MM(pp[:, :], mgT[:, kc, :], Wog[:, kc, half * 512:(half + 1) * 512], start=(kc == 0), stop=(kc == 7), r=[bmgT, bWog], w=[bp], inc=(kc == 7))
                TT("dve", x_t[:, half * 512:(half + 1) * 512], pp[:, :], x_t[:, half * 512:(half + 1) * 512], ALU.add, [bp, bx], [bx])
            yield "t"
            return DMA(out_d[tok0:tok0 + 128, :], x_t[:, :], sem_outs[T % 2], r=[bx], w=[])

        out_toks = []
        total = nseq * ntile
        seq_tiles = [(s, i) for s in range(nseq) for i in range(ntile)]

        def make_gen(n_):
            s_, i_ = seq_tiles[n_]
            T_ = s_ * 16 + i_
            if i_ == 0:
                MSET("pool", kcT[:].rearrange("p a b -> p (a b)"), 0.0, [bkcT])
                MSET("pool", vcT[:].rearrange("p a b -> p (a b)"), 0.0, [bvcT])
                MSET("pool", vca[:, :, 0:64], 0.0, [bvca])
                MSET("pool", kvc[:].rearrange("p a b -> p (a b)"), 0.0, [bkvc])
            return tile_body2(s_, i_)

        def xload(n_):
            if n_ < total:
                s2, i2 = seq_tiles[n_]
                T2 = s2 * 16 + i2
                DMA(xt[T2 % 2][:], x_d[T2 * 128:(T2 + 1) * 128, :], sem_x[T2 % 2], w=[bxt[T2 % 2]])

        def step(g, until):
            while True:
                try:
                    m_ = next(g)
                except StopIteration as e_:
                    return None, True, e_.value
                if m_ in until:
                    return m_, False, None

        def early(n_):
            s_, i_ = seq_tiles[n_]
            T_ = s_ * 16 + i_
            return DMA(out_d[T_ * 128:T_ * 128 + 128, :], xs[:, :], sem_outs[T_ % 2], r=[bxs], w=[])

        if total > 0:
            xload(0)
            xload(1)
            if stage < 9:
                for n_ in range(total):
                    g = make_gen(n_)
                    try:
                        _, _, val = step(g, ())
                        out_toks.append(val)
                    except _Stop:
                        out_toks.append(early(n_))
                    xload(n_ + 2)
            else:
                cur = make_gen(0)
                step(cur, ("front_done",))
                for n_ in range(total):
                    step(cur, ("mid_done",))
                    nxt = make_gen(n_ + 1) if n_ + 1 < total else None
                    cur_done = False
                    nxt_done = nxt is None
                    while not (cur_done and nxt_done):
                        if not cur_done:
                            m_, fin, val = step(cur, ("t",))
                            if fin:
                                cur_done = True
                                out_toks.append(val)
                        if not nxt_done:
                            m_, fin, val = step(nxt, ("f", "front_done"))
                            if m_ == "front_done":
                                nxt_done = True
                    xload(n_ + 2)
                    cur = nxt
        S.wait_all("sp", out_toks[-4:] + dbg_outs + [(sm_, S.dcnt[sm_]) for sm_ in sem_outs])
        S.emit()
    return nc


_CACHE = {}


def kernel(**inputs):
    sh, per = host_prep(inputs)
    if "nc" not in _CACHE:
        _CACHE["nc"] = build()
    nc = _CACHE["nc"]
    in_maps = []
    for core in range(8):
        d = dict(sh)
        d.update(per[core])
        in_maps.append(d)
    res = run_bass_kernel_spmd(nc, in_maps, core_ids=list(range(8)))
    out = np.concatenate([np.asarray(r["out"]).reshape(2, 2048, 1024) for r in res.results], axis=0)
    return out.astype(np.float32)
```

```python
import math
import numpy as np
import concourse.bass as bass
import concourse.mybir as mybir
from concourse.bass_utils import run_bass_kernel_spmd
from contextlib import ExitStack

F32 = mybir.dt.float32
BF16 = mybir.dt.bfloat16
AF = mybir.ActivationFunctionType
ALU = mybir.AluOpType
AX = mybir.AxisListType

COMPUTE = ("pe", "act", "dve", "pool")
NEGM = -4096.0
NRES = 2968
CQ, CKV, CG, CC, CS = 0, 512, 1024, 1048, 1304


class Buf:
    __slots__ = ("w", "r")

    def __init__(self):
        self.w = None
        self.r = {}


class Sched:
    ANNOTATE = False

    def __init__(self, nc, es):
        self.nc = nc
        self.es = es
        self.prog = {e: [] for e in COMPUTE + ("sp",)}
        self.cnt = {e: 0 for e in COMPUTE}
        self.sems = {}
        for e in COMPUTE:
            self.sems[e] = es.enter_context(nc.semaphore("sem_" + e))
        self.known = {e: {} for e in self.prog}
        self.snap = {}
        self.dcnt = {}
        self.pending = {e: False for e in COMPUTE}
        self.last = {}

    def dma_sem(self, name):
        self.sems[name] = self.es.enter_context(self.nc.semaphore("sem_" + name))
        self.dcnt[name] = 0
        return name

    @staticmethod
    def _flat(bs):
        out = []
        for b in bs:
            if isinstance(b, (list, tuple)):
                out.extend(Sched._flat(b))
            else:
                out.append(b)
        return out

    def op(self, eng, fn, reads=(), writes=(), inc=True, dsem=None):
        reads = self._flat(reads)
        writes = self._flat(writes)
        need = {}

        def req(tok, same_ok):
            if tok is None:
                return
            k, v = tok
            if same_ok and k == eng and eng == "pe":
                return
            if need.get(k, 0) < v:
                need[k] = v

        for b in reads:
            req(b.w, False)
        for b in writes:
            req(b.w, True)
            for k, v in b.r.items():
                req((k, v), True)
        kn = self.known[eng]
        waits = []
        for k, v in need.items():
            if kn.get(k, 0) < v:
                waits.append((k, v))
                kn[k] = v
                sn = self.snap.get((k, v))
                if sn is not None:
                    for k2, v2 in sn.items():
                        if kn.get(k2, 0) < v2:
                            kn[k2] = v2
        if dsem is not None:
            self.dcnt[dsem] += 16
            tok = (dsem, self.dcnt[dsem])
            incspec = (dsem, 16)
        elif inc:
            self.cnt[eng] += 1
            tok = (eng, self.cnt[eng])
            incspec = (eng, 1)
            self.pending[eng] = False
            self.snap[tok] = dict(kn)
        else:
            tok = (eng, self.cnt[eng] + 1)
            incspec = None
            self.pending[eng] = True
        self.last[tok[0]] = tok[1]
        for b in writes:
            b.w = tok
            b.r = {}
        for b in reads:
            if b.w is tok:
                continue
            if b.r.get(tok[0], 0) < tok[1]:
                b.r[tok[0]] = tok[1]
        note = None
        if Sched.ANNOTATE:
            import sys as _sys
            f_ = _sys._getframe(1)
            while f_ is not None and f_.f_code.co_name not in ("tile_body2", "attn_gen", "rwkv_gen", "build", "finish", "pv", "five", "rest_chunk", "wsload"):
                f_ = f_.f_back
            note = f"L{f_.f_lineno}" if f_ is not None else None
        self.prog[eng].append((waits, fn, incspec, note))
        return tok

    def wait_all(self, eng, toks):
        kn = self.known[eng]
        waits = []
        mx = {}
        for k, v in toks:
            if mx.get(k, 0) < v:
                mx[k] = v
        for k, v in mx.items():
            if kn.get(k, 0) < v:
                waits.append((k, v))
                kn[k] = v
        self.prog[eng].append((waits, None, None, None))

    def barrier(self):
        for e in COMPUTE:
            if self.pending[e]:
                self.op(e, lambda en: en.nop(), (), ())
        toks = list(self.last.items())
        for e in self.prog:
            self.wait_all(e, toks)

    def emit(self):
        nc = self.nc
        for e in COMPUTE:
            if self.pending[e]:
                self.op(e, lambda en: en.nop(), (), ())
        sems = self.sems
        prog = self.prog

        def run(engname):
            def f(e):
                for waits, fn, incspec, note in prog[engname]:
                    for k, v in waits:
                        e.wait_ge(sems[k], v)
                    if fn is None:
                        continue
                    ins = fn(e)
                    if note is not None:
                        ins.annotate(note)
                    if incspec is not None:
                        ins.then_inc(sems[incspec[0]], incspec[1])
            return f

        with nc.Block() as block:
            block.sync(run("sp"))
            block.tensor(run("pe"))
            block.scalar(run("act"))
            block.vector(run("dve"))
            block.gpsimd(run("pool"))


def _t5_bucket(dist):
    n = np.maximum(dist, 0)
    nf = np.maximum(n, 16).astype(np.float32)
    large = 16 + (np.log(nf / np.float32(16)) / np.float32(math.log(128 / 16)) * np.float32(16)).astype(np.int32)
    return np.where(n < 16, n, np.minimum(large, 31))


def _perms():
    r = lambda a, b: list(range(a, b))
    res = (r(0, 512)
           + r(768, 832) + r(1024, 1088) + r(832, 896) + r(1088, 1152) + r(896, 1024) + r(1152, 1280)
           + r(1280, 1304)
           + r(512, 576) + r(640, 704) + r(576, 640) + r(704, 768)
           + r(1816, 3480))
    rest = r(1304, 1816) + r(3480, 3992) + r(3992, 5016) + r(5016, 6040)
    assert len(res) == NRES and len(rest) == 3072
    return np.array(res), np.array(rest)


def host_prep(inp):
    f = lambda k: np.ascontiguousarray(np.asarray(inp[k], dtype=np.float32))
    sh = {}
    pres, prest = _perms()
    w_in = f("w_in")[0]
    sh["w_res"] = np.ascontiguousarray(w_in[:, pres])
    sh["w_rest"] = np.ascontiguousarray(w_in[:, prest])
    sh["w_ada"] = f("w_ada")[0]
    sh["w_out_a"] = f("w_out_a")[0]
    sh["w_out_b"] = f("w_out_b")[0]
    sh["w_o"] = f("w_o")[0]
    sh["w1k"] = f("cmp_k_w1")[0]
    sh["w1v"] = f("cmp_v_w1")[0]
    col = lambda v, n: np.ascontiguousarray(v.reshape(n, 128).T)
    sh["b_ada"] = col(f("b_ada")[0], 24)
    sh["g_norm"] = col(f("norm_gain")[0], 8)
    sh["mu"] = col(f("shift_mu")[0], 13)
    vec4 = np.stack([col(f(k)[0].reshape(-1), 4) for k in ("k_k", "k_a", "r_k", "ln_x_b")], 1)
    sh["vec4"] = np.ascontiguousarray(vec4)
    rep = lambda v: np.ascontiguousarray(np.broadcast_to(v[None, :], (128, v.shape[0])))
    kng = f("k_norm_gain")[0]
    sh["bc_small"] = np.concatenate([rep(f("q_norm_gain")[0]), rep(kng[1]), rep(kng[2])], 1)
    sh["ln_w_bc"] = rep(f("ln_x_w")[0])
    sh["kgc"] = np.ascontiguousarray(kng[0].reshape(64, 1))
    sh["w0a0"] = np.ascontiguousarray(np.stack([f("w0")[0], f("a0")[0]], 0))
    sh["lora"] = np.ascontiguousarray(np.concatenate([f("w_lora_up")[0], f("a_lora_up")[0]], 0))
    w2 = lambda k: f(k)[0].reshape(2, 128, 64).transpose(1, 0, 2)
    sh["w2"] = np.ascontiguousarray(np.stack([w2("cmp_k_w2"), w2("cmp_v_w2")], 1))
    sh["peT"] = np.ascontiguousarray(np.concatenate([f("cmp_pos_k")[0].T, f("cmp_pos_v")[0].T], 0))
    tbl = f("rel_bias")
    k = np.arange(128)[:, None]
    q = np.arange(128)[None, :]
    tb = np.zeros((2, 2, 128, 4, 128), np.float32)
    for v, dist in enumerate((q - k, 128 + q - k)):
        bk = _t5_bucket(dist)
        for g in range(2):
            for h in range(4):
                tb[v, g, :, h, :] = tbl[bk, g * 4 + h]
    sh["tblDS"] = tb.reshape(2, 2, 128, 512)
    mk = np.zeros((128, 4, 128), np.float32)
    mk[np.broadcast_to(((q - k) < 0)[:, None, :], mk.shape)] = NEGM
    sh["maskD"] = mk.reshape(128, 512)
    c31 = np.zeros((2, 128, 4, 128), np.float32)
    for g in range(2):
        for h in range(4):
            c31[g, :, h, :] = tbl[31, g * 4 + h]
    sh["c31"] = c31.reshape(2, 128, 512)
    p = np.arange(16)[:, None]
    distc = q - 16 * p + 113
    bkc = _t5_bucket(distc)
    tc = np.zeros((2, 16, 4, 128), np.float32)
    for g in range(2):
        for h in range(4):
            tc[g, :, h, :] = tbl[bkc, g * 4 + h]
    sh["tblC"] = tc.reshape(2, 16, 512)
    mc = np.zeros((16, 4, 128), np.float32)
    mc[np.broadcast_to((distc < 0)[:, None, :], mc.shape)] = NEGM
    sh["maskC"] = mc.reshape(16, 512)
    sh["ident"] = np.eye(128, dtype=np.float32)
    far = np.where(k <= q, NEGM, 0.0).astype(np.float32)
    mus = (k < q).astype(np.float32)
    mui = (k <= q).astype(np.float32)
    mls = (k > q).astype(np.float32)
    sh["masks"] = np.ascontiguousarray(np.stack([far, mus, mui, mls], 1))
    z = np.zeros((16, 256), np.float32)
    z[np.arange(16), np.arange(16) + 119] = 1.0
    sh["zsh"] = z
    e = np.zeros((32, 2048), np.float32)
    e[np.arange(2048) // 64, np.arange(2048)] = -NEGM
    sh["emat"] = e
    mi = np.zeros((128, 32), np.float32)
    for j in range(32):
        for a in range(4):
            for b in range(2):
                n = 4 * j + a - b
                if 0 <= n < 127:
                    mi[n, j] += 1.0
    sh["mimp"] = mi
    ka = np.zeros((128, 8, 2, 32), np.float32)
    for i in range(8, 16):
        for qq in range(128):
            cur = (128 * i + qq) // 64
            for j in range(32):
                forced = (j == 0) or (j == cur) or (j == cur - 1)
                causal = j <= cur
                if forced:
                    ka[qq, i - 8, 0, j] = 0.0
                    ka[qq, i - 8, 1, j] = 1e30
                elif causal:
                    ka[qq, i - 8, 0, j] = 1.0
                else:
                    ka[qq, i - 8, 1, j] = -1e30
    sh["keepadd"] = ka.reshape(128, 512)
    ind2 = np.zeros((128, 2), np.float32)
    ind2[:64, 0] = 1.0
    ind2[64:, 1] = 1.0
    sh["ind2"] = ind2
    indT = np.zeros((8, 4, 128), np.float32)
    for h in range(8):
        indT[h, h // 2, (h % 2) * 64:(h % 2) * 64 + 64] = 1.0
    sh["indT"] = indT.reshape(8, 512)
    x = f("x")
    c = f("c")
    per = []
    for core in range(8):
        d = {"x": np.ascontiguousarray(x[2 * core:2 * core + 2].reshape(4096, 1024)),
             "cT": np.ascontiguousarray(c[2 * core:2 * core + 2].reshape(2, 8, 128).transpose(2, 1, 0))}
        per.append(d)
    return sh, per


class _Stop(Exception):
    pass


def build(nseq=2, ntile=16, dbg=None, stage=9):
    nc = bass.Bass("TRN2", target_bir_lowering=False)
    dbg = dbg or {}
    di = lambda name, shape: nc.dram_tensor(name, shape, F32, kind="ExternalInput").ap()
    x_d = di("x", [4096, 1024])
    cT_d = di("cT", [128, 8, 2])
    w_res_d = di("w_res", [1024, NRES])
    w_rest_d = di("w_rest", [1024, 3072])
    w_ada_d = di("w_ada", [1024, 3072])
    w_out_a_d = di("w_out_a", [512, 1024])
    w_out_b_d = di("w_out_b", [512, 1024])
    w_o_d = di("w_o", [1024, 1024])
    w1k_d = di("w1k", [2048, 256])
    w1v_d = di("w1v", [2048, 256])
    b_ada_d = di("b_ada", [128, 24])
    g_norm_d = di("g_norm", [128, 8])
    mu_d = di("mu", [128, 13])
    vec4_d = di("vec4", [128, 4, 4])
    bc_small_d = di("bc_small", [128, 192])
    ln_w_bc_d = di("ln_w_bc", [128, 512])
    kgc_d = di("kgc", [64, 1])
    w0a0_d = di("w0a0", [2, 512])
    lora_d = di("lora", [128, 512])
    w2_d = di("w2", [128, 2, 2, 64])
    peT_d = di("peT", [128, 32])
    tblDS_d = di("tblDS", [2, 2, 128, 512])
    maskD_d = di("maskD", [128, 512])
    c31_d = di("c31", [2, 128, 512])
    tblC_d = di("tblC", [2, 16, 512])
    maskC_d = di("maskC", [16, 512])
    ident_d = di("ident", [128, 128])
    masks_d = di("masks", [128, 4, 128])
    zsh_d = di("zsh", [16, 256])
    emat_d = di("emat", [32, 2048])
    mimp_d = di("mimp", [128, 32])
    keepadd_d = di("keepadd", [128, 512])
    ind2_d = di("ind2", [128, 2])
    indT_d = di("indT", [8, 512])
    out_d = nc.dram_tensor("out", [4096, 1024], F32, kind="ExternalOutput").ap()
    wrest_s = nc.dram_tensor("wrest_s", [12, 128, 2048], BF16, kind="Internal").ap()
    wog_s = nc.dram_tensor("wog_s", [128, 8192], BF16, kind="Internal").ap()

    with ExitStack() as es:
        S = Sched(nc, es)
        _n = [0]

        def sb(shape, dt, name=None):
            _n[0] += 1
            return es.enter_context(nc.sbuf_tensor("s_" + (name or f"sb{_n[0]}"), shape, dt))

        def psb(name):
            return es.enter_context(nc.psum_tensor(name, [128, 512], F32))

        dbg_outs = []

        def dump(name, ap, reads, dt=F32):
            if name not in dbg:
                return
            d = nc.dram_tensor("dbg_" + name, list(ap.shape), dt, kind="ExternalOutput").ap()
            dbg_outs.append(S.op("sp", lambda e: e.dma_start(out=d, in_=ap), reads, (), dsem=sem_dbg))

        def MM(out, lhsT, rhs, start=True, stop=True, r=(), w=(), inc=True, sgc=False):
            if sgc:
                return S.op("pe", lambda e: e.matmul(out, lhsT=lhsT, rhs=rhs, start=start, stop=stop, skip_group_check=True), r, w, inc=inc)
            return S.op("pe", lambda e: e.matmul(out, lhsT=lhsT, rhs=rhs, start=start, stop=stop), r, w, inc=inc)

        def TR(out, in_, ident, r=(), w=(), inc=True):
            return S.op("pe", lambda e: e.transpose(out=out, in_=in_, identity=ident), r, w, inc=inc)

        def ACT(out, in_, func, r=(), w=(), bias=None, scale=None, accum=None):
            kw = {}
            if bias is not None:
                kw["bias"] = bias
            if scale is not None:
                kw["scale"] = scale
            if accum is not None:
                kw["accum_out"] = accum
            return S.op("act", lambda e: e.activation(out=out, in_=in_, func=func, **kw), r, w)

        def TS(eng, out, in0, s1, s2, op0, op1=None, r=(), w=()):
            if op1 is None:
                return S.op(eng, lambda e: e.tensor_scalar(out=out, in0=in0, scalar1=s1, scalar2=None, op0=op0), r, w)
            return S.op(eng, lambda e: e.tensor_scalar(out=out, in0=in0, scalar1=s1, scalar2=s2, op0=op0, op1=op1), r, w)

        def TT(eng, out, in0, in1, op, r=(), w=()):
            return S.op(eng, lambda e: e.tensor_tensor(out=out, in0=in0, in1=in1, op=op), r, w)

        def STT(out, in0, scalar, in1, op0, op1, r=(), w=()):
            return S.op("dve", lambda e: e.scalar_tensor_tensor(out=out, in0=in0, scalar=scalar, in1=in1, op0=op0, op1=op1), r, w)

        def CP(eng, out, in_, r=(), w=()):
            if eng == "act":
                return S.op("act", lambda e: e.copy(out=out, in_=in_), r, w)
            return S.op(eng, lambda e: e.tensor_copy(out=out, in_=in_), r, w)

        def MSET(eng, ap, val, w=()):
            return S.op(eng, lambda e: e.memset(ap, val), (), w)

        def DMA(out, in_, sem, r=(), w=(), eng="sp"):
            return S.op(eng, lambda e: e.dma_start(out=out, in_=in_), r, w, dsem=sem)

        def bcast(ap, shape, axis):
            return ap.unsqueeze(axis).to_broadcast(shape)

        sem_dbg = S.dma_sem("dbg")
        sem_stg = [S.dma_sem("stg0"), S.dma_sem("stg1")]
        sem_scr = S.dma_sem("scr")
        sem_x = [S.dma_sem("x0"), S.dma_sem("x1")]
        sem_xr = S.dma_sem("xr")
        sem_ws = [S.dma_sem(f"ws{i}") for i in range(3)]
        sem_outs = [S.dma_sem("out0"), S.dma_sem("out1")]
        sem_wog = S.dma_sem("wog")

        PS = [psb(f"ps{i}") for i in range(8)]
        PSB = [Buf() for _ in range(8)]
        rot = {"proj": [0, 1], "sc": [2, 3], "acc": [4, 5], "rw": [6, 7], "tl": [4, 5, 6, 7]}
        rotc = {k: 0 for k in rot}

        def bank(cls):
            i = rot[cls][rotc[cls] % len(rot[cls])]
            rotc[cls] += 1
            return PS[i], PSB[i]

        NSLOT = 41
        AR = sb([128, NSLOT * 256], F32, "arena")
        SLB = [Buf() for _ in range(NSLOT)]

        def slot(start, shape, dt, P0=0):
            el = 4 if dt == F32 else 2
            n = int(np.prod(shape[1:]))
            nsl = (n * el + 1023) // 1024
            assert start + nsl <= NSLOT
            base = AR[:] if dt == F32 else AR[:].bitcast(BF16)
            o = start * 1024 // el
            ap = base[P0:P0 + shape[0], o:o + n]
            if len(shape) > 2:
                names = " ".join(f"d{i}" for i in range(len(shape) - 1))
                kw = {f"d{i}": shape[i + 1] for i in range(len(shape) - 1)}
                ap = ap.rearrange(f"p ({names}) -> p {names}", **kw)
            return ap, SLB[start:start + nsl]

        Wres = sb([128, 8, NRES], BF16, "Wres"); bWres = Buf()
        Wouta = sb([128, 4, 1024], BF16, "Wouta"); bWouta = Buf()
        Woutb = sb([128, 4, 1024], BF16, "Woutb"); bWoutb = Buf()
        Wog = sb([128, 8, 1024], BF16, "Wog"); bWog = Buf()
        W1c = sb([128, 32, 256], BF16, "W1c"); bW1c = Buf()
        W2c = sb([128, 2, 2, 64], BF16, "W2c"); bW2c = Buf()
        Lora = sb([128, 512], BF16, "Lora"); bLora = Buf()
        identf = sb([128, 128], F32, "identf"); bidf = Buf()
        identb = sb([128, 128], BF16, "identb"); bidb = Buf()
        masks = sb([128, 4, 128], BF16, "masks"); bmasks = Buf()
        biasDS = sb([128, 2, 2, 512], BF16, "biasDS"); bbias = Buf()
        emat = sb([64, 2048], BF16, "emat"); bemat = Buf()
        zsh = sb([128, 256], BF16, "zsh"); bzsh = Buf()
        biasC = sb([128, 2, 512], BF16, "biasC"); bbiasC = Buf()
        w0a0 = sb([128, 512], F32, "w0a0"); bw0a0 = Buf()
        bmisc = Buf()
        mimp = sb([128, 32], F32, "mimp"); bmimp = Buf()
        keepadd = sb([128, 8, 2, 32], F32, "keepadd"); bka = Buf()
        ind2 = sb([128, 2], F32, "ind2"); bind2 = Buf()
        indT = sb([8, 4, 128], F32, "indT"); bindT = Buf()
        ones_f = sb([128, 128], F32, "ones_f"); bones = Buf()
        bc_small = sb([128, 192], F32, "bc_small"); bbcs = Buf()
        ln_w_bc = sb([128, 512], F32, "ln_w_bc"); blnw = Buf()
        vec4 = sb([128, 4, 4], F32, "vec4"); bvec4 = Buf()
        mucol = sb([128, 2, 13], F32, "mucol"); bmu = Buf()
        kgc = sb([64, 1], F32, "kgc"); bkgc = Buf()
        gcol = sb([128, 8], F32, "gcol"); bgcol = Buf()
        badaT = sb([128, 24], F32, "badaT"); bbada = Buf()
        cTt = sb([128, 8, 2], F32, "cTt"); bcT = Buf()
        modT = sb([128, 24, 2], F32, "modT"); bmod = Buf()
        gsT = sb([128, 2, 8], F32, "gsT"); bgs = Buf()
        hb2 = sb([128, 2, 2], F32, "hb2"); bhb2 = Buf()
        cneg = sb([128, 16], F32, "cneg"); bcneg = Buf()
        peTb = sb([128, 32], BF16, "peTb"); bpeT = Buf()
        siluc = sb([128, 8, 2], F32, "siluc"); bsc = Buf()
        gtmp = sb([128, 16], F32, "gtmp"); bgtmp = Buf()

        stg = []; bstg = []
        for i_ in range(2):
            a_, b_ = slot(16 * i_, [128, 4096], F32)
            stg.append(a_); bstg.append(b_)
        kT = sb([128, 2, 2048], BF16, "kT"); bkT = [Buf() for _ in range(16)]
        Vcf = sb([128, 4160], BF16, "Vc"); bVc = [Buf() for _ in range(16)]
        Vc = Vcf[:].rearrange("p (a b c d) -> p a b c d", a=16, b=2, c=2)
        stgb = kT[:].rearrange("p a b -> p (a b)"); bstgb = bkT
        gate_bc = Vcf[:].bitcast(F32)[:, 0:2048].rearrange("p (s n) -> p s n", s=2); bgbc = bVc

        ldn = [0]
        sem_lds = [S.dma_sem(f"ld{i}") for i in range(8)]

        def ld(out, in_, w):
            sm = sem_lds[ldn[0] % 8]
            ldn[0] += 1
            if S.dcnt[sm] > 0:
                S.wait_all("sp", [(sm, S.dcnt[sm])])
            return DMA(out, in_, sm, w=w)

        ld(identf[:], ident_d, [bidf])
        CP("dve", identb[:], identf[:], [bidf], [bidb])
        ld(stg[0][:, 0:512].rearrange("p (a b) -> p a b", a=4), masks_d, [bstg[0]])
        CP("dve", masks[:], stg[0][:, 0:512].rearrange("p (a b) -> p a b", a=4), [bstg[0]], [bmasks])
        ld(mimp[:], mimp_d, [bmimp])
        ld(keepadd[:].rearrange("p a b c -> p (a b c)"), keepadd_d, [bka])
        ld(ind2[:], ind2_d, [bind2])
        ld(indT[:].rearrange("p a b -> p (a b)"), indT_d, [bindT])
        ld(bc_small[:], bc_small_d, [bbcs])
        ld(ln_w_bc[:], ln_w_bc_d, [blnw])
        ld(vec4[:], vec4_d, [bvec4])
        ld(mucol[:, 0, :], mu_d, [bmu])
        TS("dve", mucol[:, 1, :], mucol[:, 0, :], -1.0, 1.0, ALU.mult, ALU.add, [bmu], [bmu])
        ld(kgc[:], kgc_d, [bkgc])
        MSET("pool", w0a0[:], 0.0, [bw0a0])
        ld(w0a0[0:1, :], w0a0_d[0:1, :], [bw0a0])
        ld(w0a0[64:65, :], w0a0_d[1:2, :], [bw0a0])
        MSET("pool", emat[:], 0.0, [bemat])
        MSET("pool", zsh[:], 0.0, [bzsh])
        MSET("pool", biasC[:].rearrange("p a b -> p (a b)"), 0.0, [bbiasC])
        ld(gcol[:], g_norm_d, [bgcol])
        ld(badaT[:], b_ada_d, [bbada])
        ld(cTt[:], cT_d, [bcT])
        MSET("pool", ones_f[:], 1.0, [bones])
        MSET("pool", cneg[:], -0.5, [bcneg])
        ld(stg[1][0:16, 0:256], zsh_d, [bstg[1]])
        CP("dve", zsh[0:16, :], stg[1][0:16, 0:256], [bstg[1]], [bzsh])
        ld(stg[1][0:32, 0:2048], emat_d, [bstg[1]])
        CP("dve", emat[0:32, :], stg[1][0:32, 0:2048], [bstg[1]], [bemat])
        ld(stg[1][:, 2048:2560], lora_d, [bstg[1]])
        CP("dve", Lora[:], stg[1][:, 2048:2560], [bstg[1]], [bLora])
        ld(stg[1][:, 2560:2816].rearrange("p (a b c) -> p a b c", a=2, b=2), w2_d, [bstg[1]])
        CP("dve", W2c[:], stg[1][:, 2560:2816].rearrange("p (a b c) -> p a b c", a=2, b=2), [bstg[1]], [bW2c])
        ld(stg[1][:, 2816:2848], peT_d, [bstg[1]])
        CP("dve", peTb[:], stg[1][:, 2816:2848], [bstg[1]], [bpeT])
        for g in range(2):
            ld(stg[0][:, 0:512], c31_d[g], [bstg[0]])
            for v in range(2):
                ld(stg[1][:, 0:512], tblDS_d[v, g], [bstg[1]])
                TT("dve", stg[1][:, 0:512], stg[1][:, 0:512], stg[0][:, 0:512], ALU.subtract, [bstg[0], bstg[1]], [bstg[1]])
                if v == 0:
                    ld(stg[1][:, 512:1024], maskD_d, [bstg[1]])
                    STT(biasDS[:, v, g, :], stg[1][:, 0:512], 8.0, stg[1][:, 512:1024], ALU.mult, ALU.add, [bstg[1]], [bbias])
                else:
                    TS("dve", biasDS[:, v, g, :], stg[1][:, 0:512], 8.0, None, ALU.mult, None, [bstg[1]], [bbias])
            ld(stg[1][0:16, 0:512], tblC_d[g], [bstg[1]])
            ld(stg[1][0:16, 512:1024], maskC_d, [bstg[1]])
            TT("dve", stg[1][0:16, 0:512], stg[1][0:16, 0:512], stg[0][0:16, 0:512], ALU.subtract, [bstg[0], bstg[1]], [bstg[1]])
            STT(biasC[0:16, g, :], stg[1][0:16, 0:512], 8.0, stg[1][0:16, 512:1024], ALU.mult, ALU.add, [bstg[1]], [bbiasC])

        def stage_load(i, src_ap, ncols, nk=8):
            view = stg[i][:, 0:nk * ncols].rearrange("p (k n) -> p k n", k=nk)
            DMA(view, src_ap, sem_stg[i], w=[bstg[i]])
            return view

        si = 0
        for c0 in range(0, NRES, 512):
            n = min(512, NRES - c0)
            v = stage_load(si, w_res_d[:, c0:c0 + n].rearrange("(k p) n -> p k n", p=128), n)
            CP("dve" if si == 0 else "act", Wres[:, :, c0:c0 + n], v, [bstg[si]], [bWres])
            si ^= 1
        for c in range(6):
            v = stage_load(si, w_rest_d[:, c * 512:(c + 1) * 512].rearrange("(k p) n -> p k n", p=128), 512)
            sv = stgb[:, 0:4096].rearrange("p (k n) -> p k n", k=8)
            CP("dve" if si == 0 else "act", sv, v, [bstg[si]], [bstgb])
            for j_ in range(2):
                DMA(wrest_s[2 * c + j_].rearrange("p (k n) -> p k n", k=8),
                    stgb[:, 0:4096].rearrange("p (k j n) -> p k j n", k=8, j=2)[:, :, j_, :], sem_scr, r=[bstgb], w=[Buf()])
            si ^= 1
        for (wd_, Wt, bW) in ((w_out_a_d, Wouta, bWouta), (w_out_b_d, Woutb, bWoutb)):
            v = stage_load(si, wd_.rearrange("(k p) n -> p k n", p=128), 1024, nk=4)
            CP("dve" if si == 0 else "act", Wt[:], v, [bstg[si]], [bW])
            si ^= 1
        for (wd_, lo) in ((w1k_d, 0), (w1v_d, 64)):
            for hh in range(2):
                view = stg[si][lo:lo + 64, 0:4096].rearrange("p (k n) -> p k n", k=16)
                DMA(view, wd_[hh * 1024:(hh + 1) * 1024, :].rearrange("(k p) n -> p k n", p=64), sem_stg[si], w=[bstg[si]])
                CP("dve" if si == 0 else "act", W1c[lo:lo + 64, hh * 16:(hh + 1) * 16, :], view, [bstg[si]], [bW1c])
                si ^= 1
        ACT(siluc[:], cTt[:], AF.Tanh, [bcT], [bsc], scale=0.5)
        TS("dve", siluc[:], siluc[:], 0.5, 0.5, ALU.mult, ALU.add, [bsc], [bsc])
        TT("dve", siluc[:], siluc[:], cTt[:], ALU.mult, [bsc, bcT], [bsc])
        pm, bpm = bank("proj")
        silucb = sb([128, 8, 2], BF16, "silucb"); bscb = Buf()
        CP("dve", silucb[:], siluc[:], [bsc], [bscb])
        for c in range(6):
            v = stage_load(si, w_ada_d[:, c * 512:(c + 1) * 512].rearrange("(k p) n -> p k n", p=128), 512)
            vb = stgb[:, 0:4096].rearrange("p (k n) -> p k n", k=8)
            CP("dve" if si == 0 else "act", vb, v, [bstg[si]], [bstgb])
            for jj in range(4):
                j = c * 4 + jj
                for kc in range(8):
                    MM(pm[:, j * 2:j * 2 + 2], vb[:, kc, jj * 128:(jj + 1) * 128], silucb[:, kc, :], start=(kc == 0), stop=(kc == 7),
                       r=[bstgb, bscb], w=[bpm], inc=(kc == 7))
            si ^= 1
        TT("dve", modT[:], pm[:, 0:48].rearrange("p (j b) -> p j b", b=2), bcast(badaT[:], [128, 24, 2], 2), ALU.add, [bpm, bbada], [bmod])
        for s in range(2):
            STT(gsT[:, s, :], modT[:, 8:16, s], 1.0, gcol[:], ALU.add, ALU.mult, [bmod, bgcol], [bgs])
        CP("dve", gtmp[:].rearrange("p (s j) -> p s j", s=2), modT[:, 16:24, :].rearrange("p j s -> p s j"), [bmod], [bgtmp])
        for q4 in range(4):
            pg, bpg = bank("proj")
            for jq in range(4):
                qq = q4 * 4 + jq
                MM(pg[0:1, jq * 128:(jq + 1) * 128], gtmp[:, qq:qq + 1], identf[:], r=[bgtmp, bidf], w=[bpg], inc=(jq == 3))
            CP("dve", stg[1][0:1, q4 * 512:(q4 + 1) * 512], pg[0:1, 0:512], [bpg], [bstg[1]])
        for s in range(2):
            for hh in range(2):
                pb_, bpb_ = bank("proj")
                MM(pb_[:, :], ones_f[0:1, :], stg[1][0:1, s * 1024 + hh * 512: s * 1024 + hh * 512 + 512], r=[bones, bstg[1]], w=[bpb_])
                TS("dve", gate_bc[:, s, hh * 512:(hh + 1) * 512], pb_[:, :], 0.5, None, ALU.mult, None, [bpb_], [bgbc])
        for s in (1, 0):
            for hh in range(2):
                v = stage_load(0, w_o_d[:, hh * 512:(hh + 1) * 512].rearrange("(k p) n -> p k n", p=128), 512)
                TT("dve", Wog[:, :, hh * 512:(hh + 1) * 512], v, bcast(gate_bc[:, s, hh * 512:(hh + 1) * 512], [128, 8, 512], 1), ALU.mult,
                   [bstg[0], bgbc], [bWog])
            if s == 1:
                DMA(wog_s, Wog[:].rearrange("p k n -> p (k n)"), sem_scr, r=[bWog], w=[Buf()])
        for kv in range(2):
            lo = kv * 64
            ph, bph = bank("proj")
            for jh in range(2):
                for pos in range(32):
                    MM(ph[:, jh:jh + 1], W1c[lo:lo + 64, pos, jh * 128:(jh + 1) * 128], peTb[lo:lo + 64, pos:pos + 1],
                       start=(pos == 0), stop=(pos == 31), r=[bW1c, bpeT], w=[bph], inc=(pos == 31))
            CP("dve", hb2[:, kv, :], ph[:, 0:2], [bph], [bhb2])
        S.barrier()
        print("SBUF remaining before main alloc:", nc.sbuf_bytes_remaining)

        xt = [sb([128, 1024], F32, f"xt{i}") for i in range(2)]; bxt = [Buf(), Buf()]
        hT = [sb([128, 8, 130], BF16, f"hT{i}") for i in range(2)]; bhT = [Buf(), Buf()]
        for i_ in range(2):
            MSET("pool", hT[i_][:].rearrange("p a b -> p (a b)"), 0.0, [bhT[i_]])
        ynsa = sb([128, 512], F32, "ynsa"); bynsa = Buf()
        yn = sb([128, 512], F32, "yn"); byn = Buf()
        st12 = sb([128, 16], F32, "st12"); bst12 = Buf()
        rs12 = sb([128, 16], F32, "rs12"); brs12 = Buf()
        MSET("pool", Vcf[:], 1.0, bVc)
        gsig = sb([128, 3, 8], F32, "gsig"); bgsig = Buf()
        kvc = sb([128, 2, 144], BF16, "kvc"); bkvc = Buf()
        kcT = sb([64, 2, 128], BF16, "kcT"); bkcT = Buf()
        vcT = sb([64, 2, 128], F32, "vcT"); bvcT = Buf()
        vca = sb([128, 2, 65], F32, "vca"); bvca = Buf()
        MSET("pool", vca[:].rearrange("p a b -> p (a b)"), 1.0, [bvca])
        hu = sb([128, 64], F32, "hu"); bhu = Buf()
        hw_ = sb([128, 64], F32, "hw_"); bhw = Buf()
        hid = sb([128, 64], BF16, "hid"); bhid = Buf()
        kcs = sb([64, 48], F32, "kcs"); bkcs = Buf()
        coef = sb([128, 16], F32, "coef"); bcoef = Buf()
        impr = sb([128, 2, 32], F32, "impr"); bimpr = Buf()
        imp2 = sb([128, 32], F32, "imp2"); bimp2 = Buf()
        m8a = sb([128, 8], F32, "m8a"); bm8a = Buf()
        m8b = sb([128, 8], F32, "m8b"); bm8b = Buf()
        nsel = sb([128, 2, 32], F32, "nsel"); bnsel = Buf()
        nselT = sb([64, 2, 128], BF16, "nselT"); bnselT = Buf()
        MSET("pool", nselT[:].rearrange("p a b -> p (a b)"), 0.0, [bnselT])
        wdad = sb([128, 128], F32, "wdad"); bwdad = Buf()
        wdadb = sb([128, 128], BF16, "wdadb"); bwdadb = Buf()
        eLC = sb([128, 4], F32, "eLC"); beLC = Buf()
        rn8 = sb([128, 8], F32, "rn8"); brn8 = Buf()
        rn8T = sb([8, 128], F32, "rn8T"); brn8T = Buf()
        sbon = sb([128, 8], F32, "sbon"); bsbon = Buf()
        Pst = sb([128, 4, 64], F32, "Pst"); bPst = Buf()
        Pb = sb([128, 4, 64], BF16, "Pb"); bPb = Buf()
        lnst = sb([128, 32], F32, "lnst"); blnst = Buf()
        NWS = 2
        WS = [sb([128, 8, 256], BF16, f"WS{i}") for i in range(NWS)]; bWS = [Buf() for _ in range(NWS)]
        sq, bsq = slot(0, [128, 1024], F32)
        xs, bxs = sq, bsq
        qn2, bqn2 = slot(4, [128, 8, 2, 64], BF16)
        qT2, bqT2 = slot(35, [128, 8, 128], BF16)
        kn2, bkn2 = slot(8, [128, 2, 2, 64], BF16)
        PT = []; bPT = []
        NPT = 2
        for i_ in range(NPT):
            a_, b_ = slot(37 + i_, [128, 512], BF16)
            PT.append(a_); bPT.append(b_)
        PcT, bPcT = slot(39, [128, 512], F32)
        silA, bsilA = slot(15, [128, 4, 128], F32)
        silB, bsilB = slot(17, [128, 4, 128], F32)
        thA, bthA = slot(19, [128, 8, 128], BF16)
        thB, bthB = slot(21, [128, 8, 128], BF16)
        mg1, bmg1 = slot(23, [128, 8, 128], F32)
        mg2, bmg2 = slot(27, [128, 8, 128], F32)
        mgT, bmgT = slot(31, [128, 8, 128], BF16)
        yaT, byaT = slot(33, [128, 4, 128], BF16)
        ybT, bybT = slot(34, [128, 4, 128], BF16)
        rT, brT = slot(4, [128, 4, 128], F32)
        kTr, bkTr = slot(6, [128, 4, 128], F32)
        vT, bvT = slot(8, [128, 4, 128], F32)
        lwT, blw = slot(10, [128, 4, 128], F32)
        LT, bLT = slot(12, [128, 4, 128], F32)
        asg, basg = slot(14, [128, 4, 128], F32)
        e1, be1 = slot(16, [128, 4, 128], F32)
        e2, be2 = slot(18, [128, 4, 128], F32)
        e3, be3 = slot(20, [128, 4, 128], F32)
        kkn, bkkn = slot(22, [128, 4, 128], F32)
        kmod, bkmod = slot(24, [128, 4, 128], F32)
        tmpA, btmpA = slot(26, [128, 4, 128], F32)
        At, bAt = slot(28, [128, 4, 128], BF16)
        Bt, bBt = slot(29, [128, 4, 128], BF16)
        Kt, bKt = slot(30, [128, 4, 128], BF16)
        Rt, bRt = slot(31, [128, 4, 128], BF16)
        Bh, bBh = slot(32, [128, 512], BF16)
        Kh, bKh = slot(33, [128, 512], BF16)
        Vt, bVt = slot(34, [128, 512], BF16)
        Qm = []; bQm = []; QmT = []; bQmT = []; Xm = []; bXm = []
        for st_ in (10, 12):
            a_, b_ = slot(st_, [128, 8, 128], BF16); Qm.append(a_); bQm.append(b_)
        for st_ in (14, 18):
            a_, b_ = slot(st_, [128, 8, 128], BF16); QmT.append(a_); bQmT.append(b_)
        for st_ in (20, 22):
            a_, b_ = slot(st_, [128, 8, 128], BF16); Xm.append(a_); bXm.append(b_)
        AakT, bAak = slot(24, [128, 8, 128], BF16)
        MrbT, bMrb = slot(4, [128, 8, 128], BF16)
        MrkT, bMrk = slot(6, [128, 8, 128], BF16)
        rhs0, brhs0 = slot(8, [128, 8, 64], BF16)
        Ub, bUb = slot(9, [128, 8, 64], BF16)

        def f4(t):
            return t.rearrange("p c t -> p (c t)")

        def REDUCE(out, in_, r, w):
            return S.op("dve", lambda e: e.tensor_reduce(out=out, in_=in_, axis=AX.X, op=ALU.add), r, w)

        def MAX8(out, in_, r, w):
            return S.op("dve", lambda e: e.max(out=out, in_=in_), r, w)

        def MREP(out, rep, vals, r, w):
            return S.op("dve", lambda e: e.match_replace(out=out, in_to_replace=rep, in_values=vals, imm_value=-3.0e38), r, w)

        def RECIP(out, in_, r, w):
            return S.op("dve", lambda e: e.reciprocal(out=out, in_=in_), r, w)

        def SCAN(out, d0, d1, r, w):
            return S.op("dve", lambda e: e.tensor_tensor_scan(out=out, data0=d0, data1=d1, initial=0.0, op0=ALU.mult, op1=ALU.add), r, w)

        print("SBUF remaining:", nc.sbuf_bytes_remaining)
        ws_i = [0]

        def chk(n):
            if stage <= n:
                raise _Stop()

        def tile_body2(s, i):
            T = s * 16 + i
            yield "f"
            tok0 = T * 128
            yield "f"
            xb_ = T % 2
            yield "f"
            x_t = xt[xb_]; bx = bxt[xb_]
            yield "f"
            hcur = hT[T % 2]; bh = bhT[T % 2]
            yield "f"
            hprev = hT[(T + 1) % 2]; bhp = bhT[(T + 1) % 2]
            yield "f"
            ACT(sq[:], x_t[:], AF.Square, [bx], [bsq, bst12], accum=st12[:, 0:1])
            yield "f"
            TS("dve", st12[:, 0:1], st12[:, 0:1], 1.0 / 1024, 1e-6, ALU.mult, ALU.add, [bst12], [bst12])
            yield "f"
            TT("pool", rs12[:, 0:1], st12[:, 0:1], cneg[:, 0:1], ALU.pow, [bst12, bcneg], [brs12])
            yield "f"
            TS("dve", xs[:], x_t[:], rs12[:, 0:1], None, ALU.mult, None, [bx, brs12], [bxs])
            yield "f"
            import os as _os
            yield "f"
            _sk = _os.environ.get("SKIP", "")
            yield "f"
            if i == 0:
                if "m" not in _sk:
                    MSET("pool", hcur[:, :, 0:1], 0.0, [bh])
            else:
                CP("pool", hcur[:, :, 0:1], hprev[:, :, 128:129], [bhp], [bh])
            yield "f"
            for half in range(2):
                pp, bp = bank("proj")
                for j in range(4):
                    kc = half * 4 + j
                    TR(pp[:, j * 128:(j + 1) * 128], xs[:, kc * 128:(kc + 1) * 128], identf[:], [bxs, bidf], [bp], inc=(j == 3))
                for j in range(4):
                    kc = half * 4 + j
                    if "a" in _sk:
                        ACT(hcur[:, kc, 1:129], pp[:, j * 128:(j + 1) * 128], AF.Identity, [bp, bgs, bmod], [bh])
                    elif "b" in _sk:
                        ACT(hcur[:, kc, 2:130], pp[:, j * 128:(j + 1) * 128], AF.Identity, [bp, bgs, bmod], [bh],
                            bias=modT[:, kc, s:s + 1], scale=gsT[:, s, kc:kc + 1])
                    else:
                        ACT(hcur[:, kc, 1:129], pp[:, j * 128:(j + 1) * 128], AF.Identity, [bp, bgs, bmod], [bh],
                            bias=modT[:, kc, s:s + 1], scale=gsT[:, s, kc:kc + 1])
            yield "f"
            dump(f"hT_{T}", hcur[:], [bh], BF16)
            yield "f"
            chk(1)
            yield "f"

            pq, bpq = bank("proj")
            yield "f"
            for kc in range(8):
                MM(pq[:, :], hcur[:, kc, 1:129], Wres[:, kc, CQ:CQ + 512], start=(kc == 0), stop=(kc == 7), r=[bh, bWres], w=[bpq], inc=(kc == 7))
            yield "f"
            ACT(sq[:, 0:512], pq[:, :], AF.Square, [bpq], [bsq])
            yield "f"
            REDUCE(st12[:, 0:8], sq[:, 0:512].rearrange("p (h d) -> p h d", h=8), [bsq], [bst12])
            yield "f"
            pkv, bpkv = bank("proj")
            yield "f"
            for kc in range(8):
                MM(pkv[:, :], hcur[:, kc, 1:129], Wres[:, kc, CKV:CKV + 512], start=(kc == 0), stop=(kc == 7), r=[bh, bWres], w=[bpkv], inc=(kc == 7))
            yield "f"
            ACT(sq[:, 512:768], pkv[:, 0:256], AF.Square, [bpkv], [bsq])
            yield "f"
            REDUCE(st12[:, 8:12], sq[:, 512:768].rearrange("p (h d) -> p h d", h=4), [bsq], [bst12])
            yield "f"
            TS("dve", st12[:, 0:12], st12[:, 0:12], 1.0 / 64, 1e-6, ALU.mult, ALU.add, [bst12], [bst12])
            yield "f"
            TT("pool", rs12[:, 0:12], st12[:, 0:12], cneg[:, 0:12], ALU.pow, [bst12, bcneg], [brs12])
            yield "f"
            chk(1.2)
            yield "f"
            for h in range(8):
                STT(qn2[:, h, :, :], bcast(pq[:, h * 64:(h + 1) * 64], [128, 2, 64], 1), rs12[:, h:h + 1],
                    bcast(bc_small[:, 0:64], [128, 2, 64], 1), ALU.mult, ALU.mult, [bpq, brs12, bbcs], [bqn2])
            yield "f"
            for gg in range(2):
                for br in range(2):
                    c0 = gg * 128 + br * 64
                    STT(kn2[:, gg, br, :], pkv[:, c0:c0 + 64], rs12[:, 8 + gg * 2 + br:9 + gg * 2 + br],
                        bc_small[:, 64 + br * 64:128 + br * 64], ALU.mult, ALU.mult, [bpkv, brs12, bbcs], [bkn2])
            yield "f"
            CP("act", Vc[:, i, :, :, 0:64], pkv[:, 256:512].rearrange("p (b g d) -> p b g d", b=2, g=2), [bpkv], [bVc[i]])
            yield "f"
            chk(1.4)
            yield "f"
            pt, bpt = bank("proj")
            yield "f"
            ptb = pt[:].bitcast(BF16)
            yield "f"
            for h in range(8):
                TR(ptb[:, h * 128:(h + 1) * 128], qn2[:, h, :, :].rearrange("p c d -> p (c d)"), identb[:], [bqn2, bidb], [bpt], inc=(h == 7))
            yield "f"
            CP("act", qT2[:].rearrange("p h q -> p (h q)"), ptb[:, 0:1024], [bpt], [bqT2])
            yield "f"
            pt2, bpt2 = bank("proj")
            yield "f"
            pt2b = pt2[:].bitcast(BF16)
            yield "f"
            for gg in range(2):
                TR(pt2b[:, gg * 128:(gg + 1) * 128], kn2[:, gg, :, :].rearrange("p c d -> p (c d)"), identb[:], [bkn2, bidb], [bpt2], inc=(gg == 1))
            yield "f"
            CP("dve", kT[:, :, i * 128:(i + 1) * 128], pt2b[:, 0:256].rearrange("p (g t) -> p g t", g=2), [bpt2], [bkT[i]])
            yield "f"
            chk(1.6)
            yield "f"
            pgt, bpgt = bank("proj")
            yield "f"
            for kc in range(8):
                MM(pgt[:, 0:24], hcur[:, kc, 1:129], Wres[:, kc, CG:CG + 24], start=(kc == 0), stop=(kc == 7), r=[bh, bWres], w=[bpgt], inc=(kc == 7))
            yield "f"
            ACT(gsig[:].rearrange("p a b -> p (a b)"), pgt[:, 0:24], AF.Tanh, [bpgt], [bgsig], scale=0.5)
            yield "f"
            TS("dve", gsig[:].rearrange("p a b -> p (a b)"), gsig[:].rearrange("p a b -> p (a b)"), 0.5, 0.5, ALU.mult, ALU.add, [bgsig], [bgsig])
            yield "f"
            pcm, bpcm = bank("proj")
            yield "f"
            for gg in range(2):
                for kc in range(8):
                    MM(pcm[:, gg * 128:(gg + 1) * 128], Wres[:, kc, CC + gg * 128:CC + (gg + 1) * 128], hcur[:, kc, 1:129],
                       start=(kc == 0), stop=(kc == 7), r=[bh, bWres], w=[bpcm], inc=(kc == 7 and gg == 1))
            yield "f"
            CP("pool", kvc[:, :, 0:16], kvc[:, :, 128:144], [bkvc], [bkvc])
            yield "f"
            CP("act", kvc[:, :, 16:144], pcm[:, 0:256].rearrange("p (g t) -> p g t", g=2), [bpcm], [bkvc])
            yield "f"
            chk(1.8)
            yield "f"
            m0 = 1 if i == 0 else 0
            yield "f"
            nm = 8 - m0
            yield "f"
            for kv in range(2):
                lo = kv * 64
                phd, bphd = bank("proj")
                for jh in range(2):
                    for pos in range(32):
                        MM(phd[:, jh * 16:jh * 16 + 16].rearrange("p (g m) -> p g m", g=2), W1c[lo:lo + 64, pos, jh * 128:(jh + 1) * 128],
                           kvc[lo:lo + 64, :, pos:pos + 113:16], start=(pos == 0), stop=(pos == 31), r=[bW1c, bkvc], w=[bphd],
                           inc=(pos == 31))
                for jh in range(2):
                    reg = (kv * 2 + jh) * 16
                    ACT(hu[:, reg:reg + 16], phd[:, jh * 16:jh * 16 + 16], AF.Identity, [bphd, bhb2], [bhu], bias=hb2[:, kv, jh:jh + 1])
            yield "f"
            chk(1.85)
            yield "f"
            TT("dve", hw_[:], hu[:], hu[:], ALU.mult, [bhu], [bhw])
            yield "f"
            TS("dve", hw_[:], hw_[:], 0.044715, 1.0, ALU.mult, ALU.add, [bhw], [bhw])
            yield "f"
            TT("dve", hw_[:], hw_[:], hu[:], ALU.mult, [bhw, bhu], [bhw])
            yield "f"
            ACT(hw_[:], hw_[:], AF.Tanh, [bhw], [bhw], scale=math.sqrt(2.0 / math.pi))
            yield "f"
            STT(hid[:], hw_[:], 1.0, hu[:], ALU.add, ALU.mult, [bhw, bhu], [bhid])
            yield "f"
            chk(1.9)
            yield "f"
            pc2, bpc2 = bank("proj")
            yield "f"
            for kv in range(2):
                for jh in range(2):
                    reg = (kv * 2 + jh) * 16
                    MM(pc2[0:64, kv * 16:(kv + 1) * 16], W2c[:, kv, jh, :], hid[:, reg:reg + 16], start=(jh == 0), stop=(jh == 1),
                       r=[bW2c, bhid], w=[bpc2], inc=(jh == 1))
            yield "f"
            TS("dve", kcs[:, 0:16], pc2[0:64, 0:16], 0.5, None, ALU.mult, None, [bpc2], [bkcs])
            yield "f"
            TT("dve", kcs[:, 16:32], kcs[:, 0:16], kcs[:, 0:16], ALU.mult, [bkcs], [bkcs])
            yield "f"
            MM(pc2[0:64, 64:80], ones_f[0:64, 0:64], kcs[:, 16:32], r=[bones, bkcs], w=[bpc2])
            yield "f"
            TS("dve", kcs[:, 32:48], pc2[0:64, 64:80], 1.0 / 64, 1e-6, ALU.mult, ALU.add, [bpc2], [bkcs])
            yield "f"
            TT("pool", kcs[:, 16:32], kcs[:, 32:48], cneg[0:64, 0:16], ALU.pow, [bkcs, bcneg], [bkcs])
            yield "f"
            TT("dve", kcs[:, 0:16], kcs[:, 0:16], kcs[:, 16:32], ALU.mult, [bkcs], [bkcs])
            yield "f"
            n0 = 8 * i - 1 + m0
            yield "f"
            TS("dve", kcT[:, :, n0:n0 + nm], kcs[:, 0:16].rearrange("p (g m) -> p g m", g=2)[:, :, m0:8], kgc[:, 0:1], None, ALU.mult, None,
               [bkcs, bkgc], [bkcT])
            yield "f"
            TS("dve", vcT[:, :, n0:n0 + nm], pc2[0:64, 16:32].rearrange("p (g m) -> p g m", g=2)[:, :, m0:8], 0.5, None, ALU.mult, None,
               [bpc2], [bvcT])
            yield "f"
            nv = 8 * i + 7
            yield "f"
            chk(1.95)
            yield "f"
            pvt, bpvt = bank("proj")
            yield "f"
            for gg in range(2):
                TR(pvt[0:nv, gg * 64:(gg + 1) * 64], vcT[:, gg, 0:nv], identf[0:64, 0:64], [bvcT, bidf], [bpvt], inc=(gg == 1))
            yield "f"
            CP("dve", vca[0:nv, :, 0:64], pvt[0:nv, 0:128].rearrange("p (g d) -> p g d", g=2), [bpvt], [bvca])
            yield "f"
            dump(f"kcT_{T}", kcT[:], [bkcT], BF16)
            yield "f"
            dump(f"vca_{T}", vca[:], [bvca])
            yield "f"
            dump(f"qT2_{T}", qT2[:], [bqT2], BF16)
            yield "f"
            chk(2)
            yield "f"

            yield "front_done"
            def attn_gen():
                first_y = {0: True, 1: True}

                def finish(acc, bacc, br, gg):
                    accv = acc[:, 0:260].rearrange("p (h e) -> p h e", h=4)
                    c0 = br * 4
                    TS("dve", coef[:, c0:c0 + 4], accv[:, :, 64], 1e-30, None, ALU.max, None, [bacc], [bcoef])
                    RECIP(coef[:, c0:c0 + 4], coef[:, c0:c0 + 4], [bcoef], [bcoef])
                    if br == 0:
                        CP("dve", coef[:, 12:16], coef[:, 0:4], [bcoef], [bcoef])
                    gbr = {0: 0, 1: 1, 2: 2}[br]
                    TT("dve", coef[:, c0:c0 + 4], coef[:, c0:c0 + 4], gsig[:, gbr, gg * 4:(gg + 1) * 4], ALU.mult, [bcoef, bgsig], [bcoef])
                    yv = ynsa[:, gg * 256:(gg + 1) * 256].rearrange("p (h d) -> p h d", h=4)
                    cb = bcast(coef[:, c0:c0 + 4], [128, 4, 64], 2)
                    if first_y[gg]:
                        TT("dve", yv, accv[:, :, 0:64], cb, ALU.mult, [bacc, bcoef], [bynsa])
                        first_y[gg] = False
                    else:
                        for h in range(4):
                            STT(yv[:, h, :], accv[:, h, 0:64], coef[:, c0 + h:c0 + h + 1], yv[:, h, :], ALU.mult, ALU.add, [bacc, bcoef, bynsa], [bynsa])

                def pv(acc, bacc, Pt_, bP, vrhs, bv, first, last, K=128):
                    for h in range(4):
                        MM(acc[:, h * 65:(h + 1) * 65], Pt_[0:K, h * 128:(h + 1) * 128], vrhs, start=(first and h == 0), stop=(last and h == 3), r=[bP] + bv, w=[bacc],
                           inc=(h == 3), sgc=True)

                pti = [0]
                for gg in range(2):
                    sc, bsc_ = bank("sc")
                    MM(sc[0:nv, :], kcT[:, gg, 0:nv], qT2[0:64, gg * 4:(gg + 1) * 4, :].rearrange("p h q -> p (h q)"), start=True, stop=False,
                       r=[bkcT, bqT2], w=[bsc_], inc=False)
                    off = 128 - 8 * i
                    MM(sc[0:nv, :], zsh[:, off:off + nv], biasC[:, gg, :], start=False, stop=True, r=[bzsh], w=[bsc_])
                    ACT(PcT[0:nv, :], sc[0:nv, :], AF.Exp, [bsc_], [bPcT], scale=0.125)
                    acc, bacc = bank("acc")
                    for h in range(4):
                        MM(acc[:, h * 65:(h + 1) * 65], PcT[0:nv, h * 128:(h + 1) * 128], vca[0:nv, gg, :], r=[bPcT, bvca], w=[bacc], inc=False)
                    for h in range(4):
                        MM(acc[:, 320 + h * 32:320 + (h + 1) * 32], PcT[0:nv, h * 128:(h + 1) * 128], mimp[0:nv, :], r=[bPcT, bmimp], w=[bacc], inc=(h == 3))
                    finish(acc, bacc, 0, gg)
                    yield
                    if i >= 8:
                        iv = impr[:, gg, :]
                        TS("dve", iv, acc[:, 320:352], coef[:, 12:13], None, ALU.mult, None, [bacc, bcoef], [bimpr])
                        for h in range(1, 4):
                            STT(iv, acc[:, 320 + h * 32:352 + h * 32], coef[:, 12 + h:13 + h], iv, ALU.mult, ALU.add, [bacc, bcoef, bimpr], [bimpr])
                        TT("dve", iv, iv, keepadd[:, i - 8, 0, :], ALU.mult, [bimpr, bka], [bimpr])
                        TT("dve", iv, iv, keepadd[:, i - 8, 1, :], ALU.add, [bimpr, bka], [bimpr])
                        MAX8(m8a[:], iv, [bimpr], [bm8a])
                        MREP(imp2[:], m8a[:], iv, [bimpr, bm8a], [bimp2])
                        MAX8(m8b[:], imp2[:], [bimp2], [bm8b])
                        TS("dve", nsel[:, gg, :], iv, m8b[:, 7:8], 1.0, ALU.is_ge, ALU.subtract, [bimpr, bm8b], [bnsel])
                        pn, bpn = bank("sc")
                        TR(pn[0:32, 0:128], nsel[:, gg, :], identf[:], [bnsel, bidf], [bpn])
                        CP("dve", nselT[0:32, gg, :], pn[0:32, 0:128], [bpn], [bnselT])
                dump(f"nsel_{T}", nsel[:], [bnsel])
                for br in (2, 1):
                    for gg in range(2):
                        lo = 0 if br == 1 else 64
                        j0 = 0 if br == 1 else max(0, i - 4)
                        acc, bacc = bank("acc")
                        prev_ = None
                        for j in range(j0, i + 1):
                            sc, bsc_ = bank("sc")
                            extra = []
                            if j == i:
                                extra.append((identb[:], biasDS[:, 0, gg, :], [bidb, bbias]))
                            if j == i - 1:
                                extra.append((identb[:], biasDS[:, 1, gg, :], [bidb, bbias]))
                            if br == 2 and j == i - 4:
                                extra.append((identb[:], bcast(masks[:, 0, :], [128, 4, 128], 1), [bidb, bmasks]))
                            if br == 1 and i >= 8:
                                extra.append((emat[:, j * 128:(j + 1) * 128], bcast(nselT[:, gg, :], [64, 4, 128], 1), [bemat, bnselT]))
                            MM(sc[:, :], kT[lo:lo + 64, gg, j * 128:(j + 1) * 128], qT2[lo:lo + 64, gg * 4:(gg + 1) * 4, :].rearrange("p h q -> p (h q)"),
                               start=True, stop=(len(extra) == 0), r=[bkT[j], bqT2], w=[bsc_], inc=(len(extra) == 0))
                            for ei, (l_, r_, bb_) in enumerate(extra):
                                lastx = ei == len(extra) - 1
                                MM(sc[:, :].rearrange("p (h q) -> p h q", h=4) if len(r_.shape) == 3 else sc[:, :], l_, r_, start=False, stop=lastx,
                                   r=bb_, w=[bsc_], inc=lastx)
                            Pt_ = PT[pti[0] % NPT]; bP = bPT[pti[0] % NPT]; pti[0] += 1
                            ACT(Pt_[:, :], sc[:, :], AF.Exp, [bsc_], [bP], scale=0.125)
                            if prev_ is not None:
                                pv(acc, bacc, prev_[0], prev_[1], Vc[:, prev_[2], br - 1, gg, :], [bVc[prev_[2]]], prev_[2] == j0, False)
                            prev_ = (Pt_, bP, j)
                            yield
                        pv(acc, bacc, prev_[0], prev_[1], Vc[:, prev_[2], br - 1, gg, :], [bVc[prev_[2]]], prev_[2] == j0, True)
                        finish(acc, bacc, br, gg)
                        yield
                dump(f"ynsa_{T}", ynsa[:], [bynsa])
                chk(3)

                yield
            def rwkv_gen():
                yield
                for c3 in range(0, 13, 3):
                    ps_, bps_ = bank("rw")
                    ncs = min(3, 13 - c3)
                    for cc in range(ncs):
                        c = c3 + cc
                        for kc in range(8):
                            MM(ps_[:, cc * 129:(cc + 1) * 129], Wres[:, kc, CS + c * 128:CS + (c + 1) * 128], hcur[:, kc, 0:129],
                               start=(kc == 0), stop=(kc == 7), r=[bh, bWres], w=[bps_], inc=(kc == 7))
                    for cc in range(ncs):
                        c = c3 + cc
                        if c < 4:
                            dst, bd = rT[:, c, :], brT
                        elif c < 8:
                            dst, bd = kTr[:, c - 4, :], bkTr
                        elif c < 12:
                            dst, bd = vT[:, c - 8, :], bvT
                        else:
                            dst, bd = wdad[:, :], bwdad
                        ACT(dst, ps_[:, cc * 129 + 1:cc * 129 + 129], AF.Identity, [bps_, bmu], [bd], scale=mucol[:, 1, c:c + 1])
                        STT(dst, ps_[:, cc * 129:cc * 129 + 128], mucol[:, 0, c:c + 1], dst, ALU.mult, ALU.add, [bps_, bmu, bd], [bd])
                yield
                dump(f"rT_{T}", rT[:], [brT])
                yield
                dump(f"wdad_{T}", wdad[:], [bwdad])
                yield
                ACT(wdadb[0:64, :], wdad[0:64, :], AF.Tanh, [bwdad], [bwdadb])
                yield
                CP("dve", wdadb[64:128, :], wdad[64:128, :], [bwdad], [bwdadb])
                yield
                pz, bpz = bank("rw")
                yield
                pa, bpa = bank("rw")
                yield
                for c in range(4):
                    MM(pz[:, c * 128:(c + 1) * 128], Lora[0:64, c * 128:(c + 1) * 128], wdadb[0:64, :], start=True, stop=False, r=[bLora, bwdadb], w=[bpz], inc=False)
                    MM(pz[:, c * 128:(c + 1) * 128], w0a0[0:64, c * 128:(c + 1) * 128], ones_f[0:64, :], start=False, stop=True, r=[bw0a0, bones], w=[bpz], inc=(c == 3))
                yield
                for c in range(4):
                    MM(pa[:, c * 128:(c + 1) * 128], Lora[64:128, c * 128:(c + 1) * 128], wdadb[64:128, :], start=True, stop=False, r=[bLora, bwdadb], w=[bpa], inc=False)
                    MM(pa[:, c * 128:(c + 1) * 128], w0a0[64:128, c * 128:(c + 1) * 128], ones_f[64:128, :], start=False, stop=True, r=[bw0a0, bones], w=[bpa], inc=(c == 3))
                yield
                f4 = lambda t: t[:].rearrange("p c t -> p (c t)")
                yield
                ACT(f4(lwT), pz[:, :], AF.Tanh, [bpz], [blw], scale=0.5)
                yield
                cexp = math.exp(-0.5) * 0.5
                yield
                TS("dve", f4(lwT), f4(lwT), -cexp, -cexp, ALU.mult, ALU.add, [blw], [blw])
                yield
                ACT(f4(asg), pa[:, :], AF.Tanh, [bpa], [basg], scale=0.5)
                yield
                TS("dve", f4(asg), f4(asg), 0.5, 0.5, ALU.mult, ALU.add, [basg], [basg])
                yield
                for c in range(4):
                    SCAN(LT[:, c, :], ones_f[:, :], lwT[:, c, :], [bones, blw], [bLT])
                yield
                TT("dve", f4(tmpA), f4(LT), f4(lwT), ALU.subtract, [bLT, blw], [btmpA])
                yield
                ACT(f4(e1), f4(tmpA), AF.Exp, [btmpA], [be1])
                yield
                ACT(f4(e2), f4(LT), AF.Exp, [bLT], [be2], scale=-1.0)
                yield
                ACT(f4(e3), f4(LT), AF.Exp, [bLT], [be3])
                yield
                ACT(eLC[:, :], LT[:, :, 127], AF.Exp, [bLT], [beLC])
                yield
                yield
                for c in range(4):
                    TS("dve", kkn[:, c, :], kTr[:, c, :], vec4[:, 0, c:c + 1], None, ALU.mult, None, [bkTr, bvec4], [bkkn])
                yield
                ACT(f4(tmpA), f4(kkn), AF.Square, [bkkn], [btmpA])
                yield
                pk_, bpk_ = bank("rw")
                yield
                for c in range(4):
                    MM(pk_[:, c * 2:c * 2 + 2], tmpA[:, c, :], ind2[:, :], r=[btmpA, bind2], w=[bpk_], inc=(c == 3))
                yield
                TS("dve", rn8[:, :], pk_[:, 0:8], 1e-24, None, ALU.max, None, [bpk_], [brn8])
                yield
                TT("pool", rn8[:, :], rn8[:, :], cneg[:, 0:8], ALU.pow, [brn8, bcneg], [brn8])
                yield
                TR(pk_[0:8, 128:256], rn8[:, :], identf[:], [brn8, bidf], [bpk_])
                yield
                CP("dve", rn8T[:, :], pk_[0:8, 128:256], [bpk_], [brn8T])
                yield
                pr_, bpr_ = bank("rw")
                yield
                for c in range(4):
                    MM(pr_[:, c * 128:(c + 1) * 128], indT[:, c, :], rn8T[:, :], r=[bindT, brn8T], w=[bpr_], inc=(c == 3))
                yield
                TT("dve", f4(kkn), f4(kkn), pr_[:, :], ALU.mult, [bkkn, bpr_], [bkkn])
                yield
                dump(f"kkn_{T}", kkn[:], [bkkn])
                yield
                yield
                for c in range(4):
                    TS("dve", tmpA[:, c, :], asg[:, c, :], -1.0, vec4[:, 1, c:c + 1], ALU.add, ALU.mult, [basg, bvec4], [btmpA])
                yield
                STT(f4(kmod), f4(tmpA), 1.0, f4(kTr), ALU.add, ALU.mult, [btmpA, bkTr], [bkmod])
                yield
                dump(f"kmod_{T}", kmod[:], [bkmod])
                yield
                yield
                for c in range(4):
                    STT(tmpA[:, c, :], rT[:, c, :], vec4[:, 2, c:c + 1], kmod[:, c, :], ALU.mult, ALU.mult, [brT, bvec4, bkmod], [btmpA])
                yield
                for c in range(4):
                    MM(pk_[:, 256 + c * 2:256 + c * 2 + 2], tmpA[:, c, :], ind2[:, :], r=[btmpA, bind2], w=[bpk_], inc=(c == 3))
                yield
                CP("dve", sbon[:, :], pk_[:, 256:264], [bpk_], [bsbon])
                yield
                yield
                STT(f4(At), f4(kkn), -1.0, f4(e1), ALU.mult, ALU.mult, [bkkn, be1], [bAt])
                yield
                TT("dve", f4(tmpA), f4(kkn), f4(asg), ALU.mult, [bkkn, basg], [btmpA])
                yield
                TT("dve", f4(tmpA), f4(tmpA), f4(e2), ALU.mult, [btmpA, be2], [btmpA])
                yield
                CP("dve", f4(Bt), f4(tmpA), [btmpA], [bBt])
                yield
                TT("dve", f4(e1), f4(kmod), f4(e2), ALU.mult, [bkmod, be2], [be1])
                yield
                CP("dve", f4(Kt), f4(e1), [be1], [bKt])
                yield
                TT("dve", f4(Rt), f4(rT), f4(e3), ALU.mult, [brT, be3], [bRt])
                yield
                yield
                for c in range(4):
                    TS("dve", tmpA[:, c, :], tmpA[:, c, :], eLC[:, c:c + 1], None, ALU.mult, None, [btmpA, beLC], [btmpA])
                    TS("dve", e1[:, c, :], e1[:, c, :], eLC[:, c:c + 1], None, ALU.mult, None, [be1, beLC], [be1])
                yield
                for (src, bsrc, dst, bdst) in ((tmpA, btmpA, Bh, bBh), (e1, be1, Kh, bKh), (vT, bvT, Vt, bVt)):
                    pp, bp = bank("rw")
                    for c in range(4):
                        TR(pp[:, c * 128:(c + 1) * 128], src[:, c, :], identf[:], [bsrc, bidf], [bp], inc=(c == 3))
                    CP("dve", dst[:, :], pp[:, :], [bp], [bdst])
                yield
                chk(4)
                yield
                yield
                def hs(t, h):
                    return t[(h % 2) * 64:(h % 2) * 64 + 64, h // 2, :]
                yield

                def five(lt, blt, rt, brt, mask_i, dst, bdst):
                    for par in range(2):
                        pp, bp = bank("rw")
                        for hh in range(4):
                            h = hh * 2 + par
                            MM(pp[:, hh * 128:(hh + 1) * 128], hs(lt, h), hs(rt, h), r=[blt, brt], w=[bp], inc=(hh == 3))
                        TT("dve", dst[:, par:8:2, :], pp[:, :].rearrange("p (h t) -> p h t", h=4), bcast(masks[:, mask_i, :], [128, 4, 128], 1), ALU.mult,
                           [bp, bmasks], [bdst])
                yield

                five(Bt, bBt, At, bAt, 1, Qm[0], bQm[0])
                yield
                five(At, bAt, Bt, bBt, 3, QmT[0], bQmT[0])
                yield
                five(Kt, bKt, At, bAt, 1, AakT, bAak)
                yield
                five(Bt, bBt, Rt, bRt, 2, MrbT, bMrb)
                yield
                five(Kt, bKt, Rt, bRt, 2, MrkT, bMrk)
                yield
                yield
                TT("dve", Xm[0][:], Qm[0][:], bcast(identb[:], [128, 8, 128], 1), ALU.add, [bQm[0], bidb], [bXm[0]])
                yield
                cur = 0
                yield
                for lvl in range(1, 7):
                    nxt = cur ^ 1
                    lastl = lvl == 6
                    for half in range(2):
                        hsl = slice(half * 4, half * 4 + 4)
                        pT_, bpT_ = bank("rw")
                        for hh in range(4):
                            h = half * 4 + hh
                            MM(pT_[:, hh * 128:(hh + 1) * 128], Qm[cur][:, h, :], QmT[cur][:, h, :], r=[bQm[cur], bQmT[cur]], w=[bpT_], inc=(hh == 3))
                        CP("dve", QmT[nxt][:, hsl, :], pT_[:, :].rearrange("p (h t) -> p h t", h=4), [bpT_], [bQmT[nxt]])
                        if not lastl:
                            pQ_, bpQ_ = bank("rw")
                            for hh in range(4):
                                h = half * 4 + hh
                                MM(pQ_[:, hh * 128:(hh + 1) * 128], QmT[cur][:, h, :], Qm[cur][:, h, :], r=[bQm[cur], bQmT[cur]], w=[bpQ_], inc=(hh == 3))
                            CP("dve", Qm[nxt][:, hsl, :], pQ_[:, :].rearrange("p (h t) -> p h t", h=4), [bpQ_], [bQm[nxt]])
                        yield
                    for half in range(2):
                        hsl = slice(half * 4, half * 4 + 4)
                        pX_, bpX_ = bank("rw")
                        for hh in range(4):
                            h = half * 4 + hh
                            MM(pX_[:, hh * 128:(hh + 1) * 128], QmT[nxt][:, h, :], Xm[cur][:, h, :], r=[bQmT[nxt], bXm[cur]], w=[bpX_], inc=(hh == 3))
                        TT("dve", Xm[nxt][:, hsl, :], pX_[:, :].rearrange("p (h t) -> p h t", h=4), Xm[cur][:, hsl, :], ALU.add, [bpX_, bXm[cur]], [bXm[nxt]])
                        yield
                    cur = nxt
                yield
                Xf = Xm[cur]; bXf = bXm[cur]
                yield
                chk(5)
                yield
                yield
                if i == 0:
                    MSET("pool", Pst[:], 0.0, [bPst])
                    MSET("pool", Pb[:], 0.0, [bPb])
                yield

                def ph_(t, h):
                    return t[(h % 2) * 64:(h % 2) * 64 + 64, h // 2, :]
                yield

                def vh(t, h):
                    return t[:, h * 64:(h + 1) * 64]
                yield

                for par in range(2):
                    p1, bp1 = bank("rw")
                    for hh in range(4):
                        h = hh * 2 + par
                        MM(p1[:, hh * 64:(hh + 1) * 64], hs(At, h), ph_(Pb, h), start=True, stop=False, r=[bAt, bPb], w=[bp1], inc=False)
                        MM(p1[:, hh * 64:(hh + 1) * 64], AakT[:, h, :], vh(Vt, h), start=False, stop=True, r=[bAak, bVt], w=[bp1], inc=(hh == 3))
                    CP("dve", rhs0[:, par:8:2, :], p1[:, 0:256].rearrange("p (h v) -> p h v", h=4), [bp1], [brhs0])
                yield
                chk(5.2)
                yield
                p2, bp2 = bank("rw")
                yield
                for h in range(8):
                    MM(p2[:, h * 64:(h + 1) * 64], Xf[:, h, :], rhs0[:, h, :], r=[bXf, brhs0], w=[bp2], inc=(h == 7))
                yield
                CP("dve", Ub[:].rearrange("p h v -> p (h v)"), p2[:, :], [bp2], [bUb])
                yield
                chk(5.4)
                yield
                p3s = []
                yield
                for par in range(2):
                    p3, bp3 = bank("proj")
                    p3s.append((p3, bp3))
                    for hh in range(4):
                        h = hh * 2 + par
                        MM(p3[:, hh * 64:(hh + 1) * 64], hs(Rt, h), ph_(Pb, h), start=True, stop=False, r=[bRt, bPb], w=[bp3], inc=False)
                        MM(p3[:, hh * 64:(hh + 1) * 64], MrkT[:, h, :], vh(Vt, h), start=False, stop=False, r=[bMrk, bVt], w=[bp3], inc=False)
                        MM(p3[:, hh * 64:(hh + 1) * 64], MrbT[:, h, :], Ub[:, h, :], start=False, stop=True, r=[bMrb, bUb], w=[bp3], inc=(hh == 3))
                yield
                chk(5.6)
                yield
                p4, bp4 = bank("rw")
                yield
                for h in range(8):
                    o_ = p4[(h % 2) * 64:(h % 2) * 64 + 64, (h // 2) * 64:(h // 2) * 64 + 64]
                    MM(o_, vh(Bh, h), Ub[:, h, :], start=True, stop=False, r=[bBh, bUb], w=[bp4], inc=False)
                    MM(o_, vh(Kh, h), vh(Vt, h), start=False, stop=True, r=[bKh, bVt], w=[bp4], inc=(h == 7))
                yield
                for c in range(4):
                    STT(Pst[:, c, :], Pst[:, c, :], eLC[:, c:c + 1], p4[:, c * 64:(c + 1) * 64], ALU.mult, ALU.add, [bPst, beLC, bp4], [bPst])
                yield
                CP("dve", Pb[:].rearrange("p a b -> p (a b)"), Pst[:].rearrange("p a b -> p (a b)"), [bPst], [bPb])
                yield
                chk(5.8)
                yield
                yield
                yn3 = yn[:, :].rearrange("p (h d) -> p h d", h=8)
                yield
                for par in range(2):
                    p3, bp3 = p3s[par]
                    CP("dve", yn3[:, par:8:2, :], p3[:, 0:256].rearrange("p (h d) -> p h d", h=4), [bp3], [byn])
                yield
                ACT(sq[:, 0:512], yn[:, :], AF.Square, [byn], [bsq])
                yield
                REDUCE(lnst[:, 0:8], yn3, [byn], [blnst])
                yield
                REDUCE(lnst[:, 8:16], sq[:, 0:512].rearrange("p (h d) -> p h d", h=8), [bsq], [blnst])
                yield
                chk(5.85)
                yield
                TS("dve", lnst[:, 0:16], lnst[:, 0:16], 1.0 / 64, None, ALU.mult, None, [blnst], [blnst])
                yield
                TT("dve", lnst[:, 16:24], lnst[:, 0:8], lnst[:, 0:8], ALU.mult, [blnst], [blnst])
                yield
                TT("dve", lnst[:, 16:24], lnst[:, 8:16], lnst[:, 16:24], ALU.subtract, [blnst], [blnst])
                yield
                TS("dve", lnst[:, 16:24], lnst[:, 16:24], 0.0, 64e-5, ALU.max, ALU.add, [blnst], [blnst])
                yield
                TT("pool", lnst[:, 24:32], lnst[:, 16:24], cneg[:, 0:8], ALU.pow, [blnst, bcneg], [blnst])
                yield
                STT(lnst[:, 16:24], lnst[:, 0:8], -1.0, lnst[:, 24:32], ALU.mult, ALU.mult, [blnst], [blnst])
                yield
                chk(5.9)
                yield
                for h in range(8):
                    ACT(yn[:, h * 64:(h + 1) * 64], yn[:, h * 64:(h + 1) * 64], AF.Identity, [byn, blnst], [byn],
                        bias=lnst[:, 16 + h:17 + h], scale=lnst[:, 24 + h:25 + h])
                yield
                chk(5.95)
                yield
                TT("dve", yn[:, :], yn[:, :], ln_w_bc[:, :], ALU.mult, [byn, blnw], [byn])
                yield
                chk(5.97)
                yield
                for h in range(8):
                    STT(yn[:, h * 64:(h + 1) * 64], Vt[:, h * 64:(h + 1) * 64], sbon[:, h:h + 1], yn[:, h * 64:(h + 1) * 64], ALU.mult, ALU.add,
                        [bVt, bsbon, byn], [byn])
                yield

            na_ = 2 * (2 + (i + 1) + min(5, i + 1)) + 2
            nr_ = 150
            ga_, gr_ = attn_gen(), rwkv_gen()
            da_ = dr_ = 0
            alive_a = alive_r = True
            while alive_a or alive_r:
                pick_a = alive_a and (not alive_r or da_ * nr_ <= dr_ * na_)
                if pick_a:
                    try:
                        next(ga_); da_ += 1
                    except StopIteration:
                        alive_a = False
                else:
                    try:
                        next(gr_); dr_ += 1
                    except StopIteration:
                        alive_r = False
            yield "mid_done"
            chk(6)
            yield "t"
            def wsload(c):
                k = ws_i[0] % NWS
                ws_i[0] += 1
                DMA(WS[k][:].rearrange("p k n -> p (k n)"), wrest_s[c], sem_ws[k], w=[bWS[k]])
                return WS[k], bWS[k]

            def rest_chunk(c):
                W_, bW_ = wsload(c)
                pp, bp = bank("tl")
                for sub in range(2):
                    for kc in range(8):
                        MM(pp[:, sub * 128:(sub + 1) * 128], W_[:, kc, sub * 128:(sub + 1) * 128], hcur[:, kc, 1:129], start=(kc == 0), stop=(kc == 7),
                           r=[bW_, bh], w=[bp], inc=(kc == 7 and sub == 1))
                return pp, bp

            for c in range(12):
                pp, bp = rest_chunk(c)
                if c < 4:
                    dst, bd = (silA, bsilA) if c < 2 else (silB, bsilB)
                    dv = dst[:, (c % 2) * 2:(c % 2) * 2 + 2, :].rearrange("p a t -> p (a t)")
                    ACT(dv, pp[:, 0:256], AF.Tanh, [bp], [bd], scale=0.5)
                    STT(dv, dv, 1.0, pp[:, 0:256], ALU.add, ALU.mult, [bd, bp], [bd])
                else:
                    dst, bd = (thA, bthA) if c < 8 else (thB, bthB)
                    cc = (c - 4) % 4
                    ACT(dst[:, cc * 2:cc * 2 + 2, :].rearrange("p a t -> p (a t)"), pp[:, 0:256], AF.Tanh, [bp], [bd], scale=0.5)
            yield "t"
            for (src, bsrc, sil, bsil, dst, bdst, lnb) in ((ynsa, bynsa, silA, bsilA, yaT, byaT, False), (yn, byn, silB, bsilB, ybT, bybT, True)):
                pp, bp = bank("tl")
                for c in range(4):
                    TR(pp[:, c * 128:(c + 1) * 128], src[:, c * 128:(c + 1) * 128], identf[:], [bsrc, bidf], [bp], inc=(c == 3))
                if not lnb:
                    STT(dst[:].rearrange("p c t -> p (c t)"), pp[:, :], 0.5, sil[:].rearrange("p c t -> p (c t)"), ALU.mult, ALU.mult, [bp, bsil], [bdst])
                else:
                    for c in range(4):
                        STT(tmpA[:, c, :], pp[:, c * 128:(c + 1) * 128], vec4[:, 3, c:c + 1], sil[:, c, :], ALU.add, ALU.mult, [bp, bvec4, bsil], [btmpA])
                    ACT(dst[:].rearrange("p c t -> p (c t)"), f4(tmpA), AF.Copy, [btmpA], [bdst], scale=0.5)
            yield "t"
            dump(f"yaT_{T}", yaT[:], [byaT], BF16)
            yield "t"
            dump(f"ybT_{T}", ybT[:], [bybT], BF16)
            yield "t"
            for (yT_, byT_, W_, bW_, th, bth, mg, bmg) in ((yaT, byaT, Wouta, bWouta, thA, bthA, mg1, bmg1), (ybT, bybT, Woutb, bWoutb, thB, bthB, mg2, bmg2)):
                for half in range(2):
                    pp, bp = bank("tl")
                    for mm_ in range(4):
                        mc = half * 4 + mm_
                        for kc in range(4):
                            MM(pp[:, mm_ * 128:(mm_ + 1) * 128], W_[:, kc, mc * 128:(mc + 1) * 128], yT_[:, kc, :], start=(kc == 0), stop=(kc == 3),
                               r=[bW_, byT_], w=[bp], inc=(kc == 3 and mm_ == 3))
                    STT(mg[:, half * 4:(half + 1) * 4, :].rearrange("p a t -> p (a t)"), th[:, half * 4:(half + 1) * 4, :].rearrange("p a t -> p (a t)"), 1.0, pp[:, :],
                        ALU.add, ALU.mult, [bth, bp], [bmg])
            yield "t"
            TT("dve", mgT[:].rearrange("p a t -> p (a t)"), mg1[:].rearrange("p a t -> p (a t)"), mg2[:].rearrange("p a t -> p (a t)"), ALU.add, [bmg1, bmg2], [bmgT])
            yield "t"
            dump(f"mgT_{T}", mgT[:], [bmgT], BF16)
            yield "t"
            if s == 1 and i == 0:
                DMA(Wog[:].rearrange("p k n -> p (k n)"), wog_s, sem_wog, w=[bWog])
            yield "t"
            for half in range(2):
                pp, bp = bank("tl")
                for kc in range(8):
                    MM(pp[:, :], mgT[:, kc, :], Wog[:, kc, half * 512:(half + 1) * 512], start=(kc == 0), stop=(kc == 7), r=[bmgT, bWog], w=[bp], inc=(kc == 7))
                TT("dve", x_t[:, half * 512:(half + 1) * 512], pp[:, :], x_t[:, half * 512:(half + 1) * 512], ALU.add, [bp, bx], [bx])
            yield "t"
            return DMA(out_d[tok0:tok0 + 128, :], x_t[:, :], sem_outs[T % 2], r=[bx], w=[])

        out_toks = []
        total = nseq * ntile
        seq_tiles = [(s, i) for s in range(nseq) for i in range(ntile)]

        def make_gen(n_):
            s_, i_ = seq_tiles[n_]
            T_ = s_ * 16 + i_
            if i_ == 0:
                MSET("pool", kcT[:].rearrange("p a b -> p (a b)"), 0.0, [bkcT])
                MSET("pool", vcT[:].rearrange("p a b -> p (a b)"), 0.0, [bvcT])
                MSET("pool", vca[:, :, 0:64], 0.0, [bvca])
                MSET("pool", kvc[:].rearrange("p a b -> p (a b)"), 0.0, [bkvc])
            return tile_body2(s_, i_)

        def xload(n_):
            if n_ < total:
                s2, i2 = seq_tiles[n_]
                T2 = s2 * 16 + i2
                DMA(xt[T2 % 2][:], x_d[T2 * 128:(T2 + 1) * 128, :], sem_x[T2 % 2], w=[bxt[T2 % 2]])

        def step(g, until):
            while True:
                try:
                    m_ = next(g)
                except StopIteration as e_:
                    return None, True, e_.value
                if m_ in until:
                    return m_, False, None

        def early(n_):
            s_, i_ = seq_tiles[n_]
            T_ = s_ * 16 + i_
            return DMA(out_d[T_ * 128:T_ * 128 + 128, :], xs[:, :], sem_outs[T_ % 2], r=[bxs], w=[])

        if total > 0:
            xload(0)
            xload(1)
            if stage < 9:
                for n_ in range(total):
                    g = make_gen(n_)
                    try:
                        _, _, val = step(g, ())
                        out_toks.append(val)
                    except _Stop:
                        out_toks.append(early(n_))
                    xload(n_ + 2)
            else:
                cur = make_gen(0)
                step(cur, ("front_done",))
                for n_ in range(total):
                    step(cur, ("mid_done",))
                    nxt = make_gen(n_ + 1) if n_ + 1 < total else None
                    cur_done = False
                    nxt_done = nxt is None
                    while not (cur_done and nxt_done):
                        if not cur_done:
                            m_, fin, val = step(cur, ("t",))
                            if fin:
                                cur_done = True
                                out_toks.append(val)
                        if not nxt_done:
                            m_, fin, val = step(nxt, ("f", "front_done"))
                            if m_ == "front_done":
                                nxt_done = True
                    xload(n_ + 2)
                    cur = nxt
        S.wait_all("sp", out_toks[-4:] + dbg_outs + [(sm_, S.dcnt[sm_]) for sm_ in sem_outs])
        S.emit()
    return nc


_CACHE = {}


def kernel(**inputs):
    sh, per = host_prep(inputs)
    if "nc" not in _CACHE:
        _CACHE["nc"] = build()
    nc = _CACHE["nc"]
    in_maps = []
    for core in range(8):
        d = dict(sh)
        d.update(per[core])
        in_maps.append(d)
    res = run_bass_kernel_spmd(nc, in_maps, core_ids=list(range(8)))
    out = np.concatenate([np.asarray(r["out"]).reshape(2, 2048, 1024) for r in res.results], axis=0)
    return out.astype(np.float32)
```

```python
import math
import numpy as np
import concourse.bass as bass
import concourse.mybir as mybir
from concourse.bass_utils import run_bass_kernel_spmd
from contextlib import ExitStack

F32 = mybir.dt.float32
BF16 = mybir.dt.bfloat16
AF = mybir.ActivationFunctionType
ALU = mybir.AluOpType
AX = mybir.AxisListType

COMPUTE = ("pe", "act", "dve", "pool")
NEGM = -4096.0
NRES = 2968
CQ, CKV, CG, CC, CS = 0, 512, 1024, 1048, 1304


class Buf:
    __slots__ = ("w", "r")

    def __init__(self):
        self.w = None
        self.r = {}


class Sched:
    ANNOTATE = False

    def __init__(self, nc, es):
        self.nc = nc
        self.es = es
        self.prog = {e: [] for e in COMPUTE + ("sp",)}
        self.cnt = {e: 0 for e in COMPUTE}
        self.sems = {}
        for e in COMPUTE:
            self.sems[e] = es.enter_context(nc.semaphore("sem_" + e))
        self.known = {e: {} for e in self.prog}
        self.snap = {}
        self.dcnt = {}
        self.pending = {e: False for e in COMPUTE}
        self.last = {}

    def dma_sem(self, name):
        self.sems[name] = self.es.enter_context(self.nc.semaphore("sem_" + name))
        self.dcnt[name] = 0
        return name

    @staticmethod
    def _flat(bs):
        out = []
        for b in bs:
            if isinstance(b, (list, tuple)):
                out.extend(Sched._flat(b))
            else:
                out.append(b)
        return out

    def op(self, eng, fn, reads=(), writes=(), inc=True, dsem=None):
        reads = self._flat(reads)
        writes = self._flat(writes)
        need = {}

        def req(tok, same_ok):
            if tok is None:
                return
            k, v = tok
            if same_ok and k == eng and eng == "pe":
                return
            if need.get(k, 0) < v:
                need[k] = v

        for b in reads:
            req(b.w, False)
        for b in writes:
            req(b.w, True)
            for k, v in b.r.items():
                req((k, v), True)
        kn = self.known[eng]
        waits = []
        for k, v in need.items():
            if kn.get(k, 0) < v:
                waits.append((k, v))
                kn[k] = v
                sn = self.snap.get((k, v))
                if sn is not None:
                    for k2, v2 in sn.items():
                        if kn.get(k2, 0) < v2:
                            kn[k2] = v2
        if dsem is not None:
            self.dcnt[dsem] += 16
            tok = (dsem, self.dcnt[dsem])
            incspec = (dsem, 16)
        elif inc:
            self.cnt[eng] += 1
            tok = (eng, self.cnt[eng])
            incspec = (eng, 1)
            self.pending[eng] = False
            self.snap[tok] = dict(kn)
        else:
            tok = (eng, self.cnt[eng] + 1)
            incspec = None
            self.pending[eng] = True
        self.last[tok[0]] = tok[1]
        for b in writes:
            b.w = tok
            b.r = {}
        for b in reads:
            if b.w is tok:
                continue
            if b.r.get(tok[0], 0) < tok[1]:
                b.r[tok[0]] = tok[1]
        note = None
        if Sched.ANNOTATE:
            import sys as _sys
            f_ = _sys._getframe(1)
            while f_ is not None and f_.f_code.co_name not in ("tile_body2", "attn_gen", "rwkv_gen", "build", "finish", "pv", "five", "rest_chunk", "wsload"):
                f_ = f_.f_back
            note = f"L{f_.f_lineno}" if f_ is not None else None
        self.prog[eng].append((waits, fn, incspec, note))
        return tok

    def wait_all(self, eng, toks):
        kn = self.known[eng]
        waits = []
        mx = {}
        for k, v in toks:
            if mx.get(k, 0) < v:
                mx[k] = v
        for k, v in mx.items():
            if kn.get(k, 0) < v:
                waits.append((k, v))
                kn[k] = v
        self.prog[eng].append((waits, None, None, None))

    def barrier(self):
        for e in COMPUTE:
            if self.pending[e]:
                self.op(e, lambda en: en.nop(), (), ())
        toks = list(self.last.items())
        for e in self.prog:
            self.wait_all(e, toks)

    def emit(self):
        nc = self.nc
        for e in COMPUTE:
            if self.pending[e]:
                self.op(e, lambda en: en.nop(), (), ())
        sems = self.sems
        prog = self.prog

        def run(engname):
            def f(e):
                for waits, fn, incspec, note in prog[engname]:
                    for k, v in waits:
                        e.wait_ge(sems[k], v)
                    if fn is None:
                        continue
                    ins = fn(e)
                    if note is not None:
                        ins.annotate(note)
                    if incspec is not None:
                        ins.then_inc(sems[incspec[0]], incspec[1])
            return f

        with nc.Block() as block:
            block.sync(run("sp"))
            block.tensor(run("pe"))
            block.scalar(run("act"))
            block.vector(run("dve"))
            block.gpsimd(run("pool"))


def _t5_bucket(dist):
    n = np.maximum(dist, 0)
    nf = np.maximum(n, 16).astype(np.float32)
    large = 16 + (np.log(nf / np.float32(16)) / np.float32(math.log(128 / 16)) * np.float32(16)).astype(np.int32)
    return np.where(n < 16, n, np.minimum(large, 31))


def _perms():
    r = lambda a, b: list(range(a, b))
    res = (r(0, 512)
           + r(768, 832) + r(1024, 1088) + r(832, 896) + r(1088, 1152) + r(896, 1024) + r(1152, 1280)
           + r(1280, 1304)
           + r(512, 576) + r(640, 704) + r(576, 640) + r(704, 768)
           + r(1816, 3480))
    rest = r(1304, 1816) + r(3480, 3992) + r(3992, 5016) + r(5016, 6040)
    assert len(res) == NRES and len(rest) == 3072
    return np.array(res), np.array(rest)


def host_prep(inp):
    f = lambda k: np.ascontiguousarray(np.asarray(inp[k], dtype=np.float32))
    sh = {}
    pres, prest = _perms()
    w_in = f("w_in")[0]
    sh["w_res"] = np.ascontiguousarray(w_in[:, pres])
    sh["w_rest"] = np.ascontiguousarray(w_in[:, prest])
    sh["w_ada"] = f("w_ada")[0]
    sh["w_out_a"] = f("w_out_a")[0]
    sh["w_out_b"] = f("w_out_b")[0]
    sh["w_o"] = f("w_o")[0]
    sh["w1k"] = f("cmp_k_w1")[0]
    sh["w1v"] = f("cmp_v_w1")[0]
    col = lambda v, n: np.ascontiguousarray(v.reshape(n, 128).T)
    sh["b_ada"] = col(f("b_ada")[0], 24)
    sh["g_norm"] = col(f("norm_gain")[0], 8)
    sh["mu"] = col(f("shift_mu")[0], 13)
    vec4 = np.stack([col(f(k)[0].reshape(-1), 4) for k in ("k_k", "k_a", "r_k", "ln_x_b")], 1)
    sh["vec4"] = np.ascontiguousarray(vec4)
    rep = lambda v: np.ascontiguousarray(np.broadcast_to(v[None, :], (128, v.shape[0])))
    kng = f("k_norm_gain")[0]
    sh["bc_small"] = np.concatenate([rep(f("q_norm_gain")[0]), rep(kng[1]), rep(kng[2])], 1)
    sh["ln_w_bc"] = rep(f("ln_x_w")[0])
    sh["kgc"] = np.ascontiguousarray(kng[0].reshape(64, 1))
    sh["w0a0"] = np.ascontiguousarray(np.stack([f("w0")[0], f("a0")[0]], 0))
    sh["lora"] = np.ascontiguousarray(np.concatenate([f("w_lora_up")[0], f("a_lora_up")[0]], 0))
    w2 = lambda k: f(k)[0].reshape(2, 128, 64).transpose(1, 0, 2)
    sh["w2"] = np.ascontiguousarray(np.stack([w2("cmp_k_w2"), w2("cmp_v_w2")], 1))
    sh["peT"] = np.ascontiguousarray(np.concatenate([f("cmp_pos_k")[0].T, f("cmp_pos_v")[0].T], 0))
    tbl = f("rel_bias")
    k = np.arange(128)[:, None]
    q = np.arange(128)[None, :]
    tb = np.zeros((2, 2, 128, 4, 128), np.float32)
    for v, dist in enumerate((q - k, 128 + q - k)):
        bk = _t5_bucket(dist)
        for g in range(2):
            for h in range(4):
                tb[v, g, :, h, :] = tbl[bk, g * 4 + h]
    sh["tblDS"] = tb.reshape(2, 2, 128, 512)
    mk = np.zeros((128, 4, 128), np.float32)
    mk[np.broadcast_to(((q - k) < 0)[:, None, :], mk.shape)] = NEGM
    sh["maskD"] = mk.reshape(128, 512)
    c31 = np.zeros((2, 128, 4, 128), np.float32)
    for g in range(2):
        for h in range(4):
            c31[g, :, h, :] = tbl[31, g * 4 + h]
    sh["c31"] = c31.reshape(2, 128, 512)
    p = np.arange(16)[:, None]
    distc = q - 16 * p + 113
    bkc = _t5_bucket(distc)
    tc = np.zeros((2, 16, 4, 128), np.float32)
    for g in range(2):
        for h in range(4):
            tc[g, :, h, :] = tbl[bkc, g * 4 + h]
    sh["tblC"] = tc.reshape(2, 16, 512)
    mc = np.zeros((16, 4, 128), np.float32)
    mc[np.broadcast_to((distc < 0)[:, None, :], mc.shape)] = NEGM
    sh["maskC"] = mc.reshape(16, 512)
    sh["ident"] = np.eye(128, dtype=np.float32)
    far = np.where(k <= q, NEGM, 0.0).astype(np.float32)
    mus = (k < q).astype(np.float32)
    mui = (k <= q).astype(np.float32)
    mls = (k > q).astype(np.float32)
    sh["masks"] = np.ascontiguousarray(np.stack([far, mus, mui, mls], 1))
    z = np.zeros((16, 256), np.float32)
    z[np.arange(16), np.arange(16) + 119] = 1.0
    sh["zsh"] = z
    e = np.zeros((32, 2048), np.float32)
    e[np.arange(2048) // 64, np.arange(2048)] = -NEGM
    sh["emat"] = e
    mi = np.zeros((128, 32), np.float32)
    for j in range(32):
        for a in range(4):
            for b in range(2):
                n = 4 * j + a - b
                if 0 <= n < 127:
                    mi[n, j] += 1.0
    sh["mimp"] = mi
    ka = np.zeros((128, 8, 2, 32), np.float32)
    for i in range(8, 16):
        for qq in range(128):
            cur = (128 * i + qq) // 64
            for j in range(32):
                forced = (j == 0) or (j == cur) or (j == cur - 1)
                causal = j <= cur
                if forced:
                    ka[qq, i - 8, 0, j] = 0.0
                    ka[qq, i - 8, 1, j] = 1e30
                elif causal:
                    ka[qq, i - 8, 0, j] = 1.0
                else:
                    ka[qq, i - 8, 1, j] = -1e30
    sh["keepadd"] = ka.reshape(128, 512)
    ind2 = np.zeros((128, 2), np.float32)
    ind2[:64, 0] = 1.0
    ind2[64:, 1] = 1.0
    sh["ind2"] = ind2
    indT = np.zeros((8, 4, 128), np.float32)
    for h in range(8):
        indT[h, h // 2, (h % 2) * 64:(h % 2) * 64 + 64] = 1.0
    sh["indT"] = indT.reshape(8, 512)
    x = f("x")
    c = f("c")
    per = []
    for core in range(8):
        d = {"x": np.ascontiguousarray(x[2 * core:2 * core + 2].reshape(4096, 1024)),
             "cT": np.ascontiguousarray(c[2 * core:2 * core + 2].reshape(2, 8, 128).transpose(2, 1, 0))}
        per.append(d)
    return sh, per


class _Stop(Exception):
    pass


def build(nseq=2, ntile=16, dbg=None, stage=9):
    nc = bass.Bass("TRN2", target_bir_lowering=False)
    dbg = dbg or {}
    di = lambda name, shape: nc.dram_tensor(name, shape, F32, kind="ExternalInput").ap()
    x_d = di("x", [4096, 1024])
    cT_d = di("cT", [128, 8, 2])
    w_res_d = di("w_res", [1024, NRES])
    w_rest_d = di("w_rest", [1024, 3072])
    w_ada_d = di("w_ada", [1024, 3072])
    w_out_a_d = di("w_out_a", [512, 1024])
    w_out_b_d = di("w_out_b", [512, 1024])
    w_o_d = di("w_o", [1024, 1024])
    w1k_d = di("w1k", [2048, 256])
    w1v_d = di("w1v", [2048, 256])
    b_ada_d = di("b_ada", [128, 24])
    g_norm_d = di("g_norm", [128, 8])
    mu_d = di("mu", [128, 13])
    vec4_d = di("vec4", [128, 4, 4])
    bc_small_d = di("bc_small", [128, 192])
    ln_w_bc_d = di("ln_w_bc", [128, 512])
    kgc_d = di("kgc", [64, 1])
    w0a0_d = di("w0a0", [2, 512])
    lora_d = di("lora", [128, 512])
    w2_d = di("w2", [128, 2, 2, 64])
    peT_d = di("peT", [128, 32])
    tblDS_d = di("tblDS", [2, 2, 128, 512])
    maskD_d = di("maskD", [128, 512])
    c31_d = di("c31", [2, 128, 512])
    tblC_d = di("tblC", [2, 16, 512])
    maskC_d = di("maskC", [16, 512])
    ident_d = di("ident", [128, 128])
    masks_d = di("masks", [128, 4, 128])
    zsh_d = di("zsh", [16, 256])
    emat_d = di("emat", [32, 2048])
    mimp_d = di("mimp", [128, 32])
    keepadd_d = di("keepadd", [128, 512])
    ind2_d = di("ind2", [128, 2])
    indT_d = di("indT", [8, 512])
    out_d = nc.dram_tensor("out", [4096, 1024], F32, kind="ExternalOutput").ap()
    wrest_s = nc.dram_tensor("wrest_s", [24, 128, 1024], BF16, kind="Internal").ap()
    wog_s = nc.dram_tensor("wog_s", [128, 8192], BF16, kind="Internal").ap()

    with ExitStack() as es:
        S = Sched(nc, es)
        _n = [0]

        def sb(shape, dt, name=None):
            _n[0] += 1
            return es.enter_context(nc.sbuf_tensor("s_" + (name or f"sb{_n[0]}"), shape, dt))

        def psb(name):
            return es.enter_context(nc.psum_tensor(name, [128, 512], F32))

        dbg_outs = []

        def dump(name, ap, reads, dt=F32):
            if name not in dbg:
                return
            d = nc.dram_tensor("dbg_" + name, list(ap.shape), dt, kind="ExternalOutput").ap()
            dbg_outs.append(S.op("sp", lambda e: e.dma_start(out=d, in_=ap), reads, (), dsem=sem_dbg))

        def MM(out, lhsT, rhs, start=True, stop=True, r=(), w=(), inc=True, sgc=False):
            if sgc:
                return S.op("pe", lambda e: e.matmul(out, lhsT=lhsT, rhs=rhs, start=start, stop=stop, skip_group_check=True), r, w, inc=inc)
            return S.op("pe", lambda e: e.matmul(out, lhsT=lhsT, rhs=rhs, start=start, stop=stop), r, w, inc=inc)

        def TR(out, in_, ident, r=(), w=(), inc=True):
            return S.op("pe", lambda e: e.transpose(out=out, in_=in_, identity=ident), r, w, inc=inc)

        def ACT(out, in_, func, r=(), w=(), bias=None, scale=None, accum=None):
            kw = {}
            if bias is not None:
                kw["bias"] = bias
            if scale is not None:
                kw["scale"] = scale
            if accum is not None:
                kw["accum_out"] = accum
            return S.op("act", lambda e: e.activation(out=out, in_=in_, func=func, **kw), r, w)

        def TS(eng, out, in0, s1, s2, op0, op1=None, r=(), w=()):
            if op1 is None:
                return S.op(eng, lambda e: e.tensor_scalar(out=out, in0=in0, scalar1=s1, scalar2=None, op0=op0), r, w)
            return S.op(eng, lambda e: e.tensor_scalar(out=out, in0=in0, scalar1=s1, scalar2=s2, op0=op0, op1=op1), r, w)

        def TT(eng, out, in0, in1, op, r=(), w=()):
            return S.op(eng, lambda e: e.tensor_tensor(out=out, in0=in0, in1=in1, op=op), r, w)

        def STT(out, in0, scalar, in1, op0, op1, r=(), w=()):
            return S.op("dve", lambda e: e.scalar_tensor_tensor(out=out, in0=in0, scalar=scalar, in1=in1, op0=op0, op1=op1), r, w)

        def CP(eng, out, in_, r=(), w=()):
            if eng == "act":
                return S.op("act", lambda e: e.copy(out=out, in_=in_), r, w)
            return S.op(eng, lambda e: e.tensor_copy(out=out, in_=in_), r, w)

        def MSET(eng, ap, val, w=()):
            return S.op(eng, lambda e: e.memset(ap, val), (), w)

        def DMA(out, in_, sem, r=(), w=(), eng="sp"):
            return S.op(eng, lambda e: e.dma_start(out=out, in_=in_), r, w, dsem=sem)

        def bcast(ap, shape, axis):
            return ap.unsqueeze(axis).to_broadcast(shape)

        sem_dbg = S.dma_sem("dbg")
        sem_stg = [S.dma_sem("stg0"), S.dma_sem("stg1")]
        sem_scr = S.dma_sem("scr")
        sem_x = [S.dma_sem("x0"), S.dma_sem("x1")]
        sem_xr = S.dma_sem("xr")
        sem_ws = [S.dma_sem(f"ws{i}") for i in range(4)]
        sem_outs = [S.dma_sem("out0"), S.dma_sem("out1")]
        sem_wog = S.dma_sem("wog")

        PS = [psb(f"ps{i}") for i in range(8)]
        PSB = [Buf() for _ in range(8)]
        rot = {"proj": [0, 1], "sc": [2, 3], "acc": [4, 5], "rw": [6, 7], "tl": [4, 5, 6, 7]}
        rotc = {k: 0 for k in rot}

        def bank(cls):
            i = rot[cls][rotc[cls] % len(rot[cls])]
            rotc[cls] += 1
            return PS[i], PSB[i]

        NSLOT = 41
        AR = sb([128, NSLOT * 256], F32, "arena")
        SLB = [Buf() for _ in range(NSLOT)]

        def slot(start, shape, dt, P0=0):
            el = 4 if dt == F32 else 2
            n = int(np.prod(shape[1:]))
            nsl = (n * el + 1023) // 1024
            assert start + nsl <= NSLOT
            base = AR[:] if dt == F32 else AR[:].bitcast(BF16)
            o = start * 1024 // el
            ap = base[P0:P0 + shape[0], o:o + n]
            if len(shape) > 2:
                names = " ".join(f"d{i}" for i in range(len(shape) - 1))
                kw = {f"d{i}": shape[i + 1] for i in range(len(shape) - 1)}
                ap = ap.rearrange(f"p ({names}) -> p {names}", **kw)
            return ap, SLB[start:start + nsl]

        Wres = sb([128, 8, NRES], BF16, "Wres"); bWres = Buf()
        Wouta = sb([128, 4, 1024], BF16, "Wouta"); bWouta = Buf()
        Woutb = sb([128, 4, 1024], BF16, "Woutb"); bWoutb = Buf()
        Wog = sb([128, 8, 1024], BF16, "Wog"); bWog = Buf()
        W1c = sb([128, 32, 256], BF16, "W1c"); bW1c = Buf()
        W2c = sb([128, 2, 2, 64], BF16, "W2c"); bW2c = Buf()
        Lora = sb([128, 512], BF16, "Lora"); bLora = Buf()
        identf = sb([128, 128], F32, "identf"); bidf = Buf()
        identb = sb([128, 128], BF16, "identb"); bidb = Buf()
        masks = sb([128, 4, 128], BF16, "masks"); bmasks = Buf()
        biasDS = sb([128, 2, 2, 512], BF16, "biasDS"); bbias = Buf()
        emat = sb([64, 2048], BF16, "emat"); bemat = Buf()
        zsh = sb([128, 256], BF16, "zsh"); bzsh = Buf()
        biasC = sb([128, 2, 512], BF16, "biasC"); bbiasC = Buf()
        w0a0 = sb([128, 512], F32, "w0a0"); bw0a0 = Buf()
        bmisc = Buf()
        mimp = sb([128, 32], F32, "mimp"); bmimp = Buf()
        keepadd = sb([128, 8, 2, 32], F32, "keepadd"); bka = Buf()
        ind2 = sb([128, 2], F32, "ind2"); bind2 = Buf()
        indT = sb([8, 4, 128], F32, "indT"); bindT = Buf()
        ones_f = sb([128, 128], F32, "ones_f"); bones = Buf()
        bc_small = sb([128, 192], F32, "bc_small"); bbcs = Buf()
        ln_w_bc = sb([128, 512], F32, "ln_w_bc"); blnw = Buf()
        vec4 = sb([128, 4, 4], F32, "vec4"); bvec4 = Buf()
        mucol = sb([128, 2, 13], F32, "mucol"); bmu = Buf()
        kgc = sb([64, 1], F32, "kgc"); bkgc = Buf()
        gcol = sb([128, 8], F32, "gcol"); bgcol = Buf()
        badaT = sb([128, 24], F32, "badaT"); bbada = Buf()
        cTt = sb([128, 8, 2], F32, "cTt"); bcT = Buf()
        modT = sb([128, 24, 2], F32, "modT"); bmod = Buf()
        gsT = sb([128, 2, 8], F32, "gsT"); bgs = Buf()
        hb2 = sb([128, 2, 2], F32, "hb2"); bhb2 = Buf()
        cneg = sb([128, 16], F32, "cneg"); bcneg = Buf()
        peTb = sb([128, 32], BF16, "peTb"); bpeT = Buf()
        siluc = sb([128, 8, 2], F32, "siluc"); bsc = Buf()
        gtmp = sb([128, 16], F32, "gtmp"); bgtmp = Buf()

        stg = []; bstg = []
        for i_ in range(2):
            a_, b_ = slot(16 * i_, [128, 4096], F32)
            stg.append(a_); bstg.append(b_)
        kT = sb([128, 2, 2048], BF16, "kT"); bkT = [Buf() for _ in range(16)]
        Vcf = sb([128, 4160], BF16, "Vc"); bVc = [Buf() for _ in range(16)]
        Vc = Vcf[:].rearrange("p (a b c d) -> p a b c d", a=16, b=2, c=2)
        stgb = kT[:].rearrange("p a b -> p (a b)"); bstgb = bkT
        gate_bc = Vcf[:].bitcast(F32)[:, 0:2048].rearrange("p (s n) -> p s n", s=2); bgbc = bVc

        ldn = [0]
        sem_lds = [S.dma_sem(f"ld{i}") for i in range(8)]

        def ld(out, in_, w):
            sm = sem_lds[ldn[0] % 8]
            ldn[0] += 1
            if S.dcnt[sm] > 0:
                S.wait_all("sp", [(sm, S.dcnt[sm])])
            return DMA(out, in_, sm, w=w)

        ld(identf[:], ident_d, [bidf])
        CP("dve", identb[:], identf[:], [bidf], [bidb])
        ld(stg[0][:, 0:512].rearrange("p (a b) -> p a b", a=4), masks_d, [bstg[0]])
        CP("dve", masks[:], stg[0][:, 0:512].rearrange("p (a b) -> p a b", a=4), [bstg[0]], [bmasks])
        ld(mimp[:], mimp_d, [bmimp])
        ld(keepadd[:].rearrange("p a b c -> p (a b c)"), keepadd_d, [bka])
        ld(ind2[:], ind2_d, [bind2])
        ld(indT[:].rearrange("p a b -> p (a b)"), indT_d, [bindT])
        ld(bc_small[:], bc_small_d, [bbcs])
        ld(ln_w_bc[:], ln_w_bc_d, [blnw])
        ld(vec4[:], vec4_d, [bvec4])
        ld(mucol[:, 0, :], mu_d, [bmu])
        TS("dve", mucol[:, 1, :], mucol[:, 0, :], -1.0, 1.0, ALU.mult, ALU.add, [bmu], [bmu])
        ld(kgc[:], kgc_d, [bkgc])
        MSET("pool", w0a0[:], 0.0, [bw0a0])
        ld(w0a0[0:1, :], w0a0_d[0:1, :], [bw0a0])
        ld(w0a0[64:65, :], w0a0_d[1:2, :], [bw0a0])
        MSET("pool", emat[:], 0.0, [bemat])
        MSET("pool", zsh[:], 0.0, [bzsh])
        MSET("pool", biasC[:].rearrange("p a b -> p (a b)"), 0.0, [bbiasC])
        ld(gcol[:], g_norm_d, [bgcol])
        ld(badaT[:], b_ada_d, [bbada])
        ld(cTt[:], cT_d, [bcT])
        MSET("pool", ones_f[:], 1.0, [bones])
        MSET("pool", cneg[:], -0.5, [bcneg])
        ld(stg[1][0:16, 0:256], zsh_d, [bstg[1]])
        CP("dve", zsh[0:16, :], stg[1][0:16, 0:256], [bstg[1]], [bzsh])
        ld(stg[1][0:32, 0:2048], emat_d, [bstg[1]])
        CP("dve", emat[0:32, :], stg[1][0:32, 0:2048], [bstg[1]], [bemat])
        ld(stg[1][:, 2048:2560], lora_d, [bstg[1]])
        CP("dve", Lora[:], stg[1][:, 2048:2560], [bstg[1]], [bLora])
        ld(stg[1][:, 2560:2816].rearrange("p (a b c) -> p a b c", a=2, b=2), w2_d, [bstg[1]])
        CP("dve", W2c[:], stg[1][:, 2560:2816].rearrange("p (a b c) -> p a b c", a=2, b=2), [bstg[1]], [bW2c])
        ld(stg[1][:, 2816:2848], peT_d, [bstg[1]])
        CP("dve", peTb[:], stg[1][:, 2816:2848], [bstg[1]], [bpeT])
        for g in range(2):
            ld(stg[0][:, 0:512], c31_d[g], [bstg[0]])
            for v in range(2):
                ld(stg[1][:, 0:512], tblDS_d[v, g], [bstg[1]])
                TT("dve", stg[1][:, 0:512], stg[1][:, 0:512], stg[0][:, 0:512], ALU.subtract, [bstg[0], bstg[1]], [bstg[1]])
                if v == 0:
                    ld(stg[1][:, 512:1024], maskD_d, [bstg[1]])
                    STT(biasDS[:, v, g, :], stg[1][:, 0:512], 8.0, stg[1][:, 512:1024], ALU.mult, ALU.add, [bstg[1]], [bbias])
                else:
                    TS("dve", biasDS[:, v, g, :], stg[1][:, 0:512], 8.0, None, ALU.mult, None, [bstg[1]], [bbias])
            ld(stg[1][0:16, 0:512], tblC_d[g], [bstg[1]])
            ld(stg[1][0:16, 512:1024], maskC_d, [bstg[1]])
            TT("dve", stg[1][0:16, 0:512], stg[1][0:16, 0:512], stg[0][0:16, 0:512], ALU.subtract, [bstg[0], bstg[1]], [bstg[1]])
            STT(biasC[0:16, g, :], stg[1][0:16, 0:512], 8.0, stg[1][0:16, 512:1024], ALU.mult, ALU.add, [bstg[1]], [bbiasC])

        def stage_load(i, src_ap, ncols, nk=8):
            view = stg[i][:, 0:nk * ncols].rearrange("p (k n) -> p k n", k=nk)
            DMA(view, src_ap, sem_stg[i], w=[bstg[i]])
            return view

        si = 0
        for c0 in range(0, NRES, 512):
            n = min(512, NRES - c0)
            v = stage_load(si, w_res_d[:, c0:c0 + n].rearrange("(k p) n -> p k n", p=128), n)
            CP("dve" if si == 0 else "act", Wres[:, :, c0:c0 + n], v, [bstg[si]], [bWres])
            si ^= 1
        for c in range(6):
            v = stage_load(si, w_rest_d[:, c * 512:(c + 1) * 512].rearrange("(k p) n -> p k n", p=128), 512)
            sv = stgb[:, 0:4096].rearrange("p (k n) -> p k n", k=8)
            CP("dve" if si == 0 else "act", sv, v, [bstg[si]], [bstgb])
            for j_ in range(4):
                DMA(wrest_s[4 * c + j_].rearrange("p (k n) -> p k n", k=8),
                    stgb[:, 0:4096].rearrange("p (k j n) -> p k j n", k=8, j=4)[:, :, j_, :], sem_scr, r=[bstgb], w=[Buf()])
            si ^= 1
        for (wd_, Wt, bW) in ((w_out_a_d, Wouta, bWouta), (w_out_b_d, Woutb, bWoutb)):
            v = stage_load(si, wd_.rearrange("(k p) n -> p k n", p=128), 1024, nk=4)
            CP("dve" if si == 0 else "act", Wt[:], v, [bstg[si]], [bW])
            si ^= 1
        for (wd_, lo) in ((w1k_d, 0), (w1v_d, 64)):
            for hh in range(2):
                view = stg[si][lo:lo + 64, 0:4096].rearrange("p (k n) -> p k n", k=16)
                DMA(view, wd_[hh * 1024:(hh + 1) * 1024, :].rearrange("(k p) n -> p k n", p=64), sem_stg[si], w=[bstg[si]])
                CP("dve" if si == 0 else "act", W1c[lo:lo + 64, hh * 16:(hh + 1) * 16, :], view, [bstg[si]], [bW1c])
                si ^= 1
        ACT(siluc[:], cTt[:], AF.Tanh, [bcT], [bsc], scale=0.5)
        TS("dve", siluc[:], siluc[:], 0.5, 0.5, ALU.mult, ALU.add, [bsc], [bsc])
        TT("dve", siluc[:], siluc[:], cTt[:], ALU.mult, [bsc, bcT], [bsc])
        pm, bpm = bank("proj")
        silucb = sb([128, 8, 2], BF16, "silucb"); bscb = Buf()
        CP("dve", silucb[:], siluc[:], [bsc], [bscb])
        for c in range(6):
            v = stage_load(si, w_ada_d[:, c * 512:(c + 1) * 512].rearrange("(k p) n -> p k n", p=128), 512)
            vb = stgb[:, 0:4096].rearrange("p (k n) -> p k n", k=8)
            CP("dve" if si == 0 else "act", vb, v, [bstg[si]], [bstgb])
            for jj in range(4):
                j = c * 4 + jj
                for kc in range(8):
                    MM(pm[:, j * 2:j * 2 + 2], vb[:, kc, jj * 128:(jj + 1) * 128], silucb[:, kc, :], start=(kc == 0), stop=(kc == 7),
                       r=[bstgb, bscb], w=[bpm], inc=(kc == 7))
            si ^= 1
        TT("dve", modT[:], pm[:, 0:48].rearrange("p (j b) -> p j b", b=2), bcast(badaT[:], [128, 24, 2], 2), ALU.add, [bpm, bbada], [bmod])
        for s in range(2):
            STT(gsT[:, s, :], modT[:, 8:16, s], 1.0, gcol[:], ALU.add, ALU.mult, [bmod, bgcol], [bgs])
        CP("dve", gtmp[:].rearrange("p (s j) -> p s j", s=2), modT[:, 16:24, :].rearrange("p j s -> p s j"), [bmod], [bgtmp])
        for q4 in range(4):
            pg, bpg = bank("proj")
            for jq in range(4):
                qq = q4 * 4 + jq
                MM(pg[0:1, jq * 128:(jq + 1) * 128], gtmp[:, qq:qq + 1], identf[:], r=[bgtmp, bidf], w=[bpg], inc=(jq == 3))
            CP("dve", stg[1][0:1, q4 * 512:(q4 + 1) * 512], pg[0:1, 0:512], [bpg], [bstg[1]])
        for s in range(2):
            for hh in range(2):
                pb_, bpb_ = bank("proj")
                MM(pb_[:, :], ones_f[0:1, :], stg[1][0:1, s * 1024 + hh * 512: s * 1024 + hh * 512 + 512], r=[bones, bstg[1]], w=[bpb_])
                TS("dve", gate_bc[:, s, hh * 512:(hh + 1) * 512], pb_[:, :], 0.5, None, ALU.mult, None, [bpb_], [bgbc])
        for s in (1, 0):
            for hh in range(2):
                v = stage_load(0, w_o_d[:, hh * 512:(hh + 1) * 512].rearrange("(k p) n -> p k n", p=128), 512)
                TT("dve", Wog[:, :, hh * 512:(hh + 1) * 512], v, bcast(gate_bc[:, s, hh * 512:(hh + 1) * 512], [128, 8, 512], 1), ALU.mult,
                   [bstg[0], bgbc], [bWog])
            if s == 1:
                DMA(wog_s, Wog[:].rearrange("p k n -> p (k n)"), sem_scr, r=[bWog], w=[Buf()])
        for kv in range(2):
            lo = kv * 64
            ph, bph = bank("proj")
            for jh in range(2):
                for pos in range(32):
                    MM(ph[:, jh:jh + 1], W1c[lo:lo + 64, pos, jh * 128:(jh + 1) * 128], peTb[lo:lo + 64, pos:pos + 1],
                       start=(pos == 0), stop=(pos == 31), r=[bW1c, bpeT], w=[bph], inc=(pos == 31))
            CP("dve", hb2[:, kv, :], ph[:, 0:2], [bph], [bhb2])
        S.barrier()
        print("SBUF remaining before main alloc:", nc.sbuf_bytes_remaining)

        xt = [sb([128, 1024], F32, f"xt{i}") for i in range(2)]; bxt = [Buf(), Buf()]
        hT = [sb([128, 8, 130], BF16, f"hT{i}") for i in range(2)]; bhT = [Buf(), Buf()]
        for i_ in range(2):
            MSET("pool", hT[i_][:].rearrange("p a b -> p (a b)"), 0.0, [bhT[i_]])
        ynsa = sb([128, 512], F32, "ynsa"); bynsa = Buf()
        yn = sb([128, 512], F32, "yn"); byn = Buf()
        st12 = sb([128, 16], F32, "st12"); bst12 = Buf()
        rs12 = sb([128, 16], F32, "rs12"); brs12 = Buf()
        MSET("pool", Vcf[:], 1.0, bVc)
        gsig = sb([128, 3, 8], F32, "gsig"); bgsig = Buf()
        kvc = sb([128, 2, 144], BF16, "kvc"); bkvc = Buf()
        kcT = sb([64, 2, 128], BF16, "kcT"); bkcT = Buf()
        vcT = sb([64, 2, 128], F32, "vcT"); bvcT = Buf()
        vca = sb([128, 2, 65], F32, "vca"); bvca = Buf()
        MSET("pool", vca[:].rearrange("p a b -> p (a b)"), 1.0, [bvca])
        hu = sb([128, 64], F32, "hu"); bhu = Buf()
        hw_ = sb([128, 64], F32, "hw_"); bhw = Buf()
        hid = sb([128, 64], BF16, "hid"); bhid = Buf()
        kcs = sb([64, 48], F32, "kcs"); bkcs = Buf()
        coef = sb([128, 16], F32, "coef"); bcoef = Buf()
        impr = sb([128, 2, 32], F32, "impr"); bimpr = Buf()
        imp2 = sb([128, 32], F32, "imp2"); bimp2 = Buf()
        m8a = sb([128, 8], F32, "m8a"); bm8a = Buf()
        m8b = sb([128, 8], F32, "m8b"); bm8b = Buf()
        nsel = sb([128, 2, 32], F32, "nsel"); bnsel = Buf()
        nselT = sb([64, 2, 128], BF16, "nselT"); bnselT = Buf()
        MSET("pool", nselT[:].rearrange("p a b -> p (a b)"), 0.0, [bnselT])
        wdad = sb([128, 128], F32, "wdad"); bwdad = Buf()
        wdadb = sb([128, 128], BF16, "wdadb"); bwdadb = Buf()
        eLC = sb([128, 4], F32, "eLC"); beLC = Buf()
        rn8 = sb([128, 8], F32, "rn8"); brn8 = Buf()
        rn8T = sb([8, 128], F32, "rn8T"); brn8T = Buf()
        sbon = sb([128, 8], F32, "sbon"); bsbon = Buf()
        Pst = sb([128, 4, 64], F32, "Pst"); bPst = Buf()
        Pb = sb([128, 4, 64], BF16, "Pb"); bPb = Buf()
        lnst = sb([128, 32], F32, "lnst"); blnst = Buf()
        NWS = 4
        WS = [sb([128, 8, 128], BF16, f"WS{i}") for i in range(NWS)]; bWS = [Buf() for _ in range(NWS)]
        sq, bsq = slot(0, [128, 1024], F32)
        xs, bxs = sq, bsq
        qn2, bqn2 = slot(4, [128, 8, 2, 64], BF16)
        qT2, bqT2 = slot(35, [128, 8, 128], BF16)
        kn2, bkn2 = slot(8, [128, 2, 2, 64], BF16)
        PT = []; bPT = []
        NPT = 2
        for i_ in range(NPT):
            a_, b_ = slot(37 + i_, [128, 512], BF16)
            PT.append(a_); bPT.append(b_)
        PcT, bPcT = slot(39, [128, 512], F32)
        silA, bsilA = slot(15, [128, 4, 128], F32)
        silB, bsilB = slot(17, [128, 4, 128], F32)
        thA, bthA = slot(19, [128, 8, 128], BF16)
        thB, bthB = slot(21, [128, 8, 128], BF16)
        mg1, bmg1 = slot(23, [128, 8, 128], F32)
        mg2, bmg2 = slot(27, [128, 8, 128], F32)
        mgT, bmgT = slot(31, [128, 8, 128], BF16)
        yaT, byaT = slot(33, [128, 4, 128], BF16)
        ybT, bybT = slot(34, [128, 4, 128], BF16)
        rT, brT = slot(4, [128, 4, 128], F32)
        kTr, bkTr = slot(6, [128, 4, 128], F32)
        vT, bvT = slot(8, [128, 4, 128], F32)
        lwT, blw = slot(10, [128, 4, 128], F32)
        LT, bLT = slot(12, [128, 4, 128], F32)
        asg, basg = slot(14, [128, 4, 128], F32)
        e1, be1 = slot(16, [128, 4, 128], F32)
        e2, be2 = slot(18, [128, 4, 128], F32)
        e3, be3 = slot(20, [128, 4, 128], F32)
        kkn, bkkn = slot(22, [128, 4, 128], F32)
        kmod, bkmod = slot(24, [128, 4, 128], F32)
        tmpA, btmpA = slot(26, [128, 4, 128], F32)
        At, bAt = slot(28, [128, 4, 128], BF16)
        Bt, bBt = slot(29, [128, 4, 128], BF16)
        Kt, bKt = slot(30, [128, 4, 128], BF16)
        Rt, bRt = slot(31, [128, 4, 128], BF16)
        Bh, bBh = slot(32, [128, 512], BF16)
        Kh, bKh = slot(33, [128, 512], BF16)
        Vt, bVt = slot(34, [128, 512], BF16)
        Qm = []; bQm = []; QmT = []; bQmT = []; Xm = []; bXm = []
        for st_ in (10, 12):
            a_, b_ = slot(st_, [128, 8, 128], BF16); Qm.append(a_); bQm.append(b_)
        for st_ in (14, 18):
            a_, b_ = slot(st_, [128, 8, 128], BF16); QmT.append(a_); bQmT.append(b_)
        for st_ in (20, 22):
            a_, b_ = slot(st_, [128, 8, 128], BF16); Xm.append(a_); bXm.append(b_)
        AakT, bAak = slot(24, [128, 8, 128], BF16)
        MrbT, bMrb = slot(4, [128, 8, 128], BF16)
        MrkT, bMrk = slot(6, [128, 8, 128], BF16)
        rhs0, brhs0 = slot(8, [128, 8, 64], BF16)
        Ub, bUb = slot(9, [128, 8, 64], BF16)

        def f4(t):
            return t.rearrange("p c t -> p (c t)")

        def REDUCE(out, in_, r, w):
            return S.op("dve", lambda e: e.tensor_reduce(out=out, in_=in_, axis=AX.X, op=ALU.add), r, w)

        def MAX8(out, in_, r, w):
            return S.op("dve", lambda e: e.max(out=out, in_=in_), r, w)

        def MREP(out, rep, vals, r, w):
            return S.op("dve", lambda e: e.match_replace(out=out, in_to_replace=rep, in_values=vals, imm_value=-3.0e38), r, w)

        def RECIP(out, in_, r, w):
            return S.op("dve", lambda e: e.reciprocal(out=out, in_=in_), r, w)

        def SCAN(out, d0, d1, r, w):
            return S.op("dve", lambda e: e.tensor_tensor_scan(out=out, data0=d0, data1=d1, initial=0.0, op0=ALU.mult, op1=ALU.add), r, w)

        print("SBUF remaining:", nc.sbuf_bytes_remaining)
        ws_i = [0]

        def chk(n):
            if stage <= n:
                raise _Stop()

        def tile_body2(s, i):
            T = s * 16 + i
            yield "f"
            tok0 = T * 128
            yield "f"
            xb_ = T % 2
            yield "f"
            x_t = xt[xb_]; bx = bxt[xb_]
            yield "f"
            hcur = hT[T % 2]; bh = bhT[T % 2]
            yield "f"
            hprev = hT[(T + 1) % 2]; bhp = bhT[(T + 1) % 2]
            yield "f"
            ACT(sq[:], x_t[:], AF.Square, [bx], [bsq, bst12], accum=st12[:, 0:1])
            yield "f"
            TS("dve", st12[:, 0:1], st12[:, 0:1], 1.0 / 1024, 1e-6, ALU.mult, ALU.add, [bst12], [bst12])
            yield "f"
            TT("pool", rs12[:, 0:1], st12[:, 0:1], cneg[:, 0:1], ALU.pow, [bst12, bcneg], [brs12])
            yield "f"
            TS("dve", xs[:], x_t[:], rs12[:, 0:1], None, ALU.mult, None, [bx, brs12], [bxs])
            yield "f"
            import os as _os
            yield "f"
            _sk = _os.environ.get("SKIP", "")
            yield "f"
            if i == 0:
                if "m" not in _sk:
                    MSET("pool", hcur[:, :, 0:1], 0.0, [bh])
            else:
                CP("pool", hcur[:, :, 0:1], hprev[:, :, 128:129], [bhp], [bh])
            yield "f"
            for half in range(2):
                pp, bp = bank("proj")
                for j in range(4):
                    kc = half * 4 + j
                    TR(pp[:, j * 128:(j + 1) * 128], xs[:, kc * 128:(kc + 1) * 128], identf[:], [bxs, bidf], [bp], inc=(j == 3))
                for j in range(4):
                    kc = half * 4 + j
                    if "a" in _sk:
                        ACT(hcur[:, kc, 1:129], pp[:, j * 128:(j + 1) * 128], AF.Identity, [bp, bgs, bmod], [bh])
                    elif "b" in _sk:
                        ACT(hcur[:, kc, 2:130], pp[:, j * 128:(j + 1) * 128], AF.Identity, [bp, bgs, bmod], [bh],
                            bias=modT[:, kc, s:s + 1], scale=gsT[:, s, kc:kc + 1])
                    else:
                        ACT(hcur[:, kc, 1:129], pp[:, j * 128:(j + 1) * 128], AF.Identity, [bp, bgs, bmod], [bh],
                            bias=modT[:, kc, s:s + 1], scale=gsT[:, s, kc:kc + 1])
            yield "f"
            dump(f"hT_{T}", hcur[:], [bh], BF16)
            yield "f"
            chk(1)
            yield "f"

            pq, bpq = bank("proj")
            yield "f"
            for kc in range(8):
                MM(pq[:, :], hcur[:, kc, 1:129], Wres[:, kc, CQ:CQ + 512], start=(kc == 0), stop=(kc == 7), r=[bh, bWres], w=[bpq], inc=(kc == 7))
            yield "f"
            ACT(sq[:, 0:512], pq[:, :], AF.Square, [bpq], [bsq])
            yield "f"
            REDUCE(st12[:, 0:8], sq[:, 0:512].rearrange("p (h d) -> p h d", h=8), [bsq], [bst12])
            yield "f"
            pkv, bpkv = bank("proj")
            yield "f"
            for kc in range(8):
                MM(pkv[:, :], hcur[:, kc, 1:129], Wres[:, kc, CKV:CKV + 512], start=(kc == 0), stop=(kc == 7), r=[bh, bWres], w=[bpkv], inc=(kc == 7))
            yield "f"
            ACT(sq[:, 512:768], pkv[:, 0:256], AF.Square, [bpkv], [bsq])
            yield "f"
            REDUCE(st12[:, 8:12], sq[:, 512:768].rearrange("p (h d) -> p h d", h=4), [bsq], [bst12])
            yield "f"
            TS("dve", st12[:, 0:12], st12[:, 0:12], 1.0 / 64, 1e-6, ALU.mult, ALU.add, [bst12], [bst12])
            yield "f"
            TT("pool", rs12[:, 0:12], st12[:, 0:12], cneg[:, 0:12], ALU.pow, [bst12, bcneg], [brs12])
            yield "f"
            chk(1.2)
            yield "f"
            for h in range(8):
                STT(qn2[:, h, :, :], bcast(pq[:, h * 64:(h + 1) * 64], [128, 2, 64], 1), rs12[:, h:h + 1],
                    bcast(bc_small[:, 0:64], [128, 2, 64], 1), ALU.mult, ALU.mult, [bpq, brs12, bbcs], [bqn2])
            yield "f"
            for gg in range(2):
                for br in range(2):
                    c0 = gg * 128 + br * 64
                    STT(kn2[:, gg, br, :], pkv[:, c0:c0 + 64], rs12[:, 8 + gg * 2 + br:9 + gg * 2 + br],
                        bc_small[:, 64 + br * 64:128 + br * 64], ALU.mult, ALU.mult, [bpkv, brs12, bbcs], [bkn2])
            yield "f"
            CP("act", Vc[:, i, :, :, 0:64], pkv[:, 256:512].rearrange("p (b g d) -> p b g d", b=2, g=2), [bpkv], [bVc[i]])
            yield "f"
            chk(1.4)
            yield "f"
            for _ in range(24):
                yield "f"
            pt, bpt = bank("proj")
            yield "f"
            ptb = pt[:].bitcast(BF16)
            yield "f"
            for h in range(8):
                TR(ptb[:, h * 128:(h + 1) * 128], qn2[:, h, :, :].rearrange("p c d -> p (c d)"), identb[:], [bqn2, bidb], [bpt], inc=(h == 7))
            yield "f"
            CP("act", qT2[:].rearrange("p h q -> p (h q)"), ptb[:, 0:1024], [bpt], [bqT2])
            yield "f"
            pt2, bpt2 = bank("proj")
            yield "f"
            pt2b = pt2[:].bitcast(BF16)
            yield "f"
            for gg in range(2):
                TR(pt2b[:, gg * 128:(gg + 1) * 128], kn2[:, gg, :, :].rearrange("p c d -> p (c d)"), identb[:], [bkn2, bidb], [bpt2], inc=(gg == 1))
            yield "f"
            CP("dve", kT[:, :, i * 128:(i + 1) * 128], pt2b[:, 0:256].rearrange("p (g t) -> p g t", g=2), [bpt2], [bkT[i]])
            yield "f"
            chk(1.6)
            yield "f"
            pgt, bpgt = bank("proj")
            yield "f"
            for kc in range(8):
                MM(pgt[:, 0:24], hcur[:, kc, 1:129], Wres[:, kc, CG:CG + 24], start=(kc == 0), stop=(kc == 7), r=[bh, bWres], w=[bpgt], inc=(kc == 7))
            yield "f"
            ACT(gsig[:].rearrange("p a b -> p (a b)"), pgt[:, 0:24], AF.Tanh, [bpgt], [bgsig], scale=0.5)
            yield "f"
            TS("dve", gsig[:].rearrange("p a b -> p (a b)"), gsig[:].rearrange("p a b -> p (a b)"), 0.5, 0.5, ALU.mult, ALU.add, [bgsig], [bgsig])
            yield "f"
            pcm, bpcm = bank("proj")
            yield "f"
            for gg in range(2):
                for kc in range(8):
                    MM(pcm[:, gg * 128:(gg + 1) * 128], Wres[:, kc, CC + gg * 128:CC + (gg + 1) * 128], hcur[:, kc, 1:129],
                       start=(kc == 0), stop=(kc == 7), r=[bh, bWres], w=[bpcm], inc=(kc == 7 and gg == 1))
            yield "f"
            CP("pool", kvc[:, :, 0:16], kvc[:, :, 128:144], [bkvc], [bkvc])
            yield "f"
            CP("act", kvc[:, :, 16:144], pcm[:, 0:256].rearrange("p (g t) -> p g t", g=2), [bpcm], [bkvc])
            yield "f"
            chk(1.8)
            yield "f"
            m0 = 1 if i == 0 else 0
            yield "f"
            nm = 8 - m0
            yield "f"
            for kv in range(2):
                lo = kv * 64
                phd, bphd = bank("proj")
                for jh in range(2):
                    for pos in range(32):
                        MM(phd[:, jh * 16:jh * 16 + 16].rearrange("p (g m) -> p g m", g=2), W1c[lo:lo + 64, pos, jh * 128:(jh + 1) * 128],
                           kvc[lo:lo + 64, :, pos:pos + 113:16], start=(pos == 0), stop=(pos == 31), r=[bW1c, bkvc], w=[bphd],
                           inc=(pos == 31))
                for jh in range(2):
                    reg = (kv * 2 + jh) * 16
                    ACT(hu[:, reg:reg + 16], phd[:, jh * 16:jh * 16 + 16], AF.Identity, [bphd, bhb2], [bhu], bias=hb2[:, kv, jh:jh + 1])
            yield "f"
            chk(1.85)
            yield "f"
            TT("dve", hw_[:], hu[:], hu[:], ALU.mult, [bhu], [bhw])
            yield "f"
            TS("dve", hw_[:], hw_[:], 0.044715, 1.0, ALU.mult, ALU.add, [bhw], [bhw])
            yield "f"
            TT("dve", hw_[:], hw_[:], hu[:], ALU.mult, [bhw, bhu], [bhw])
            yield "f"
            ACT(hw_[:], hw_[:], AF.Tanh, [bhw], [bhw], scale=math.sqrt(2.0 / math.pi))
            yield "f"
            STT(hid[:], hw_[:], 1.0, hu[:], ALU.add, ALU.mult, [bhw, bhu], [bhid])
            yield "f"
            chk(1.9)
            yield "f"
            for _ in range(6):
                yield "f"
            pc2, bpc2 = bank("proj")
            yield "f"
            for kv in range(2):
                for jh in range(2):
                    reg = (kv * 2 + jh) * 16
                    MM(pc2[0:64, kv * 16:(kv + 1) * 16], W2c[:, kv, jh, :], hid[:, reg:reg + 16], start=(jh == 0), stop=(jh == 1),
                       r=[bW2c, bhid], w=[bpc2], inc=(jh == 1))
            yield "f"
            TS("dve", kcs[:, 0:16], pc2[0:64, 0:16], 0.5, None, ALU.mult, None, [bpc2], [bkcs])
            yield "f"
            TT("dve", kcs[:, 16:32], kcs[:, 0:16], kcs[:, 0:16], ALU.mult, [bkcs], [bkcs])
            yield "f"
            MM(pc2[0:64, 64:80], ones_f[0:64, 0:64], kcs[:, 16:32], r=[bones, bkcs], w=[bpc2])
            yield "f"
            TS("dve", kcs[:, 32:48], pc2[0:64, 64:80], 1.0 / 64, 1e-6, ALU.mult, ALU.add, [bpc2], [bkcs])
            yield "f"
            TT("pool", kcs[:, 16:32], kcs[:, 32:48], cneg[0:64, 0:16], ALU.pow, [bkcs, bcneg], [bkcs])
            yield "f"
            TT("dve", kcs[:, 0:16], kcs[:, 0:16], kcs[:, 16:32], ALU.mult, [bkcs], [bkcs])
            yield "f"
            n0 = 8 * i - 1 + m0
            yield "f"
            TS("dve", kcT[:, :, n0:n0 + nm], kcs[:, 0:16].rearrange("p (g m) -> p g m", g=2)[:, :, m0:8], kgc[:, 0:1], None, ALU.mult, None,
               [bkcs, bkgc], [bkcT])
            yield "f"
            TS("dve", vcT[:, :, n0:n0 + nm], pc2[0:64, 16:32].rearrange("p (g m) -> p g m", g=2)[:, :, m0:8], 0.5, None, ALU.mult, None,
               [bpc2], [bvcT])
            yield "f"
            nv = 8 * i + 7
            yield "f"
            chk(1.95)
            yield "f"
            for _ in range(16):
                yield "f"
            pvt, bpvt = bank("proj")
            yield "f"
            for gg in range(2):
                TR(pvt[0:nv, gg * 64:(gg + 1) * 64], vcT[:, gg, 0:nv], identf[0:64, 0:64], [bvcT, bidf], [bpvt], inc=(gg == 1))
            yield "f"
            CP("dve", vca[0:nv, :, 0:64], pvt[0:nv, 0:128].rearrange("p (g d) -> p g d", g=2), [bpvt], [bvca])
            yield "f"
            dump(f"kcT_{T}", kcT[:], [bkcT], BF16)
            yield "f"
            dump(f"vca_{T}", vca[:], [bvca])
            yield "f"
            dump(f"qT2_{T}", qT2[:], [bqT2], BF16)
            yield "f"
            chk(2)
            yield "f"

            yield "front_done"
            def attn_gen():
                first_y = {0: True, 1: True}

                def finish(acc, bacc, br, gg):
                    accv = acc[:, 0:260].rearrange("p (h e) -> p h e", h=4)
                    c0 = br * 4
                    TS("dve", coef[:, c0:c0 + 4], accv[:, :, 64], 1e-30, None, ALU.max, None, [bacc], [bcoef])
                    RECIP(coef[:, c0:c0 + 4], coef[:, c0:c0 + 4], [bcoef], [bcoef])
                    if br == 0:
                        CP("dve", coef[:, 12:16], coef[:, 0:4], [bcoef], [bcoef])
                    gbr = {0: 0, 1: 1, 2: 2}[br]
                    TT("dve", coef[:, c0:c0 + 4], coef[:, c0:c0 + 4], gsig[:, gbr, gg * 4:(gg + 1) * 4], ALU.mult, [bcoef, bgsig], [bcoef])
                    yv = ynsa[:, gg * 256:(gg + 1) * 256].rearrange("p (h d) -> p h d", h=4)
                    cb = bcast(coef[:, c0:c0 + 4], [128, 4, 64], 2)
                    if first_y[gg]:
                        TT("dve", yv, accv[:, :, 0:64], cb, ALU.mult, [bacc, bcoef], [bynsa])
                        first_y[gg] = False
                    else:
                        for h in range(4):
                            STT(yv[:, h, :], accv[:, h, 0:64], coef[:, c0 + h:c0 + h + 1], yv[:, h, :], ALU.mult, ALU.add, [bacc, bcoef, bynsa], [bynsa])

                def pv(acc, bacc, Pt_, bP, vrhs, bv, first, last, K=128):
                    for h in range(4):
                        MM(acc[:, h * 65:(h + 1) * 65], Pt_[0:K, h * 128:(h + 1) * 128], vrhs, start=(first and h == 0), stop=(last and h == 3), r=[bP] + bv, w=[bacc],
                           inc=(h == 3), sgc=True)

                pti = [0]
                for gg in range(2):
                    sc, bsc_ = bank("sc")
                    MM(sc[0:nv, :], kcT[:, gg, 0:nv], qT2[0:64, gg * 4:(gg + 1) * 4, :].rearrange("p h q -> p (h q)"), start=True, stop=False,
                       r=[bkcT, bqT2], w=[bsc_], inc=False)
                    off = 128 - 8 * i
                    MM(sc[0:nv, :], zsh[:, off:off + nv], biasC[:, gg, :], start=False, stop=True, r=[bzsh], w=[bsc_])
                    ACT(PcT[0:nv, :], sc[0:nv, :], AF.Exp, [bsc_], [bPcT], scale=0.125)
                    acc, bacc = bank("acc")
                    for h in range(4):
                        MM(acc[:, h * 65:(h + 1) * 65], PcT[0:nv, h * 128:(h + 1) * 128], vca[0:nv, gg, :], r=[bPcT, bvca], w=[bacc], inc=False)
                    for h in range(4):
                        MM(acc[:, 320 + h * 32:320 + (h + 1) * 32], PcT[0:nv, h * 128:(h + 1) * 128], mimp[0:nv, :], r=[bPcT, bmimp], w=[bacc], inc=(h == 3))
                    finish(acc, bacc, 0, gg)
                    yield
                    if i >= 8:
                        iv = impr[:, gg, :]
                        TS("dve", iv, acc[:, 320:352], coef[:, 12:13], None, ALU.mult, None, [bacc, bcoef], [bimpr])
                        for h in range(1, 4):
                            STT(iv, acc[:, 320 + h * 32:352 + h * 32], coef[:, 12 + h:13 + h], iv, ALU.mult, ALU.add, [bacc, bcoef, bimpr], [bimpr])
                        TT("dve", iv, iv, keepadd[:, i - 8, 0, :], ALU.mult, [bimpr, bka], [bimpr])
                        TT("dve", iv, iv, keepadd[:, i - 8, 1, :], ALU.add, [bimpr, bka], [bimpr])
                        MAX8(m8a[:], iv, [bimpr], [bm8a])
                        MREP(imp2[:], m8a[:], iv, [bimpr, bm8a], [bimp2])
                        MAX8(m8b[:], imp2[:], [bimp2], [bm8b])
                        TS("dve", nsel[:, gg, :], iv, m8b[:, 7:8], 1.0, ALU.is_ge, ALU.subtract, [bimpr, bm8b], [bnsel])
                dump(f"nsel_{T}", nsel[:], [bnsel])
                for br in (2, 1):
                    if br == 1 and i >= 8:
                        for gg in range(2):
                            pn, bpn = bank("sc")
                            TR(pn[0:32, 0:128], nsel[:, gg, :], identf[:], [bnsel, bidf], [bpn])
                            CP("dve", nselT[0:32, gg, :], pn[0:32, 0:128], [bpn], [bnselT])
                    for gg in range(2):
                        lo = 0 if br == 1 else 64
                        j0 = 0 if br == 1 else max(0, i - 4)
                        acc, bacc = bank("acc")
                        prev_ = None
                        for j in range(j0, i + 1):
                            sc, bsc_ = bank("sc")
                            extra = []
                            if j == i:
                                extra.append((identb[:], biasDS[:, 0, gg, :], [bidb, bbias]))
                            if j == i - 1:
                                extra.append((identb[:], biasDS[:, 1, gg, :], [bidb, bbias]))
                            if br == 2 and j == i - 4:
                                extra.append((identb[:], bcast(masks[:, 0, :], [128, 4, 128], 1), [bidb, bmasks]))
                            if br == 1 and i >= 8:
                                extra.append((emat[:, j * 128:(j + 1) * 128], bcast(nselT[:, gg, :], [64, 4, 128], 1), [bemat, bnselT]))
                            MM(sc[:, :], kT[lo:lo + 64, gg, j * 128:(j + 1) * 128], qT2[lo:lo + 64, gg * 4:(gg + 1) * 4, :].rearrange("p h q -> p (h q)"),
                               start=True, stop=(len(extra) == 0), r=[bkT[j], bqT2], w=[bsc_], inc=(len(extra) == 0))
                            for ei, (l_, r_, bb_) in enumerate(extra):
                                lastx = ei == len(extra) - 1
                                MM(sc[:, :].rearrange("p (h q) -> p h q", h=4) if len(r_.shape) == 3 else sc[:, :], l_, r_, start=False, stop=lastx,
                                   r=bb_, w=[bsc_], inc=lastx)
                            Pt_ = PT[pti[0] % NPT]; bP = bPT[pti[0] % NPT]; pti[0] += 1
                            ACT(Pt_[:, :], sc[:, :], AF.Exp, [bsc_], [bP], scale=0.125)
                            if prev_ is not None:
                                pv(acc, bacc, prev_[0], prev_[1], Vc[:, prev_[2], br - 1, gg, :], [bVc[prev_[2]]], prev_[2] == j0, False)
                            prev_ = (Pt_, bP, j)
                            yield
                        pv(acc, bacc, prev_[0], prev_[1], Vc[:, prev_[2], br - 1, gg, :], [bVc[prev_[2]]], prev_[2] == j0, True)
                        finish(acc, bacc, br, gg)
                        yield
                dump(f"ynsa_{T}", ynsa[:], [bynsa])
                chk(3)

                yield
            def rwkv_gen():
                yield
                for c3 in range(0, 13, 3):
                    ps_, bps_ = bank("rw")
                    ncs = min(3, 13 - c3)
                    for cc in range(ncs):
                        c = c3 + cc
                        for kc in range(8):
                            MM(ps_[:, cc * 129:(cc + 1) * 129], Wres[:, kc, CS + c * 128:CS + (c + 1) * 128], hcur[:, kc, 0:129],
                               start=(kc == 0), stop=(kc == 7), r=[bh, bWres], w=[bps_], inc=(kc == 7))
                    for cc in range(ncs):
                        c = c3 + cc
                        if c < 4:
                            dst, bd = rT[:, c, :], brT
                        elif c < 8:
                            dst, bd = kTr[:, c - 4, :], bkTr
                        elif c < 12:
                            dst, bd = vT[:, c - 8, :], bvT
                        else:
                            dst, bd = wdad[:, :], bwdad
                        ACT(dst, ps_[:, cc * 129 + 1:cc * 129 + 129], AF.Identity, [bps_, bmu], [bd], scale=mucol[:, 1, c:c + 1])
                        STT(dst, ps_[:, cc * 129:cc * 129 + 128], mucol[:, 0, c:c + 1], dst, ALU.mult, ALU.add, [bps_, bmu, bd], [bd])
                yield
                dump(f"rT_{T}", rT[:], [brT])
                yield
                dump(f"wdad_{T}", wdad[:], [bwdad])
                yield
                ACT(wdadb[0:64, :], wdad[0:64, :], AF.Tanh, [bwdad], [bwdadb])
                yield
                CP("act", wdadb[64:128, :], wdad[64:128, :], [bwdad], [bwdadb])
                yield
                pz, bpz = bank("rw")
                yield
                pa, bpa = bank("rw")
                yield
                for c in range(4):
                    MM(pz[:, c * 128:(c + 1) * 128], Lora[0:64, c * 128:(c + 1) * 128], wdadb[0:64, :], start=True, stop=False, r=[bLora, bwdadb], w=[bpz], inc=False)
                    MM(pz[:, c * 128:(c + 1) * 128], w0a0[0:64, c * 128:(c + 1) * 128], ones_f[0:64, :], start=False, stop=True, r=[bw0a0, bones], w=[bpz], inc=(c == 3))
                yield
                for c in range(4):
                    MM(pa[:, c * 128:(c + 1) * 128], Lora[64:128, c * 128:(c + 1) * 128], wdadb[64:128, :], start=True, stop=False, r=[bLora, bwdadb], w=[bpa], inc=False)
                    MM(pa[:, c * 128:(c + 1) * 128], w0a0[64:128, c * 128:(c + 1) * 128], ones_f[64:128, :], start=False, stop=True, r=[bw0a0, bones], w=[bpa], inc=(c == 3))
                yield
                f4 = lambda t: t[:].rearrange("p c t -> p (c t)")
                yield
                ACT(f4(lwT), pz[:, :], AF.Tanh, [bpz], [blw], scale=0.5)
                yield
                cexp = math.exp(-0.5) * 0.5
                yield
                TS("dve", f4(lwT), f4(lwT), -cexp, -cexp, ALU.mult, ALU.add, [blw], [blw])
                yield
                ACT(f4(asg), pa[:, :], AF.Tanh, [bpa], [basg], scale=0.5)
                yield
                TS("dve", f4(asg), f4(asg), 0.5, 0.5, ALU.mult, ALU.add, [basg], [basg])
                yield
                for c in range(4):
                    SCAN(LT[:, c, :], ones_f[:, :], lwT[:, c, :], [bones, blw], [bLT])
                yield
                TT("dve", f4(tmpA), f4(LT), f4(lwT), ALU.subtract, [bLT, blw], [btmpA])
                yield
                ACT(f4(e1), f4(tmpA), AF.Exp, [btmpA], [be1])
                yield
                ACT(f4(e2), f4(LT), AF.Exp, [bLT], [be2], scale=-1.0)
                yield
                ACT(f4(e3), f4(LT), AF.Exp, [bLT], [be3])
                yield
                ACT(eLC[:, :], LT[:, :, 127], AF.Exp, [bLT], [beLC])
                yield
                yield
                for c in range(4):
                    TS("dve", kkn[:, c, :], kTr[:, c, :], vec4[:, 0, c:c + 1], None, ALU.mult, None, [bkTr, bvec4], [bkkn])
                yield
                ACT(f4(tmpA), f4(kkn), AF.Square, [bkkn], [btmpA])
                yield
                for _ in range(4):
                    yield
                pk_, bpk_ = bank("rw")
                yield
                for c in range(4):
                    MM(pk_[:, c * 2:c * 2 + 2], tmpA[:, c, :], ind2[:, :], r=[btmpA, bind2], w=[bpk_], inc=(c == 3))
                yield
                TS("dve", rn8[:, :], pk_[:, 0:8], 1e-24, None, ALU.max, None, [bpk_], [brn8])
                yield
                TT("pool", rn8[:, :], rn8[:, :], cneg[:, 0:8], ALU.pow, [brn8, bcneg], [brn8])
                yield
                for _ in range(6):
                    yield
                TR(pk_[0:8, 128:256], rn8[:, :], identf[:], [brn8, bidf], [bpk_])
                yield
                CP("dve", rn8T[:, :], pk_[0:8, 128:256], [bpk_], [brn8T])
                yield
                pr_, bpr_ = bank("rw")
                yield
                for c in range(4):
                    MM(pr_[:, c * 128:(c + 1) * 128], indT[:, c, :], rn8T[:, :], r=[bindT, brn8T], w=[bpr_], inc=(c == 3))
                yield
                TT("dve", f4(kkn), f4(kkn), pr_[:, :], ALU.mult, [bkkn, bpr_], [bkkn])
                yield
                dump(f"kkn_{T}", kkn[:], [bkkn])
                yield
                yield
                for c in range(4):
                    TS("dve", tmpA[:, c, :], asg[:, c, :], -1.0, vec4[:, 1, c:c + 1], ALU.add, ALU.mult, [basg, bvec4], [btmpA])
                yield
                STT(f4(kmod), f4(tmpA), 1.0, f4(kTr), ALU.add, ALU.mult, [btmpA, bkTr], [bkmod])
                yield
                dump(f"kmod_{T}", kmod[:], [bkmod])
                yield
                yield
                for c in range(4):
                    STT(tmpA[:, c, :], rT[:, c, :], vec4[:, 2, c:c + 1], kmod[:, c, :], ALU.mult, ALU.mult, [brT, bvec4, bkmod], [btmpA])
                yield
                for c in range(4):
                    MM(pk_[:, 256 + c * 2:256 + c * 2 + 2], tmpA[:, c, :], ind2[:, :], r=[btmpA, bind2], w=[bpk_], inc=(c == 3))
                yield
                CP("dve", sbon[:, :], pk_[:, 256:264], [bpk_], [bsbon])
                yield
                yield
                STT(f4(At), f4(kkn), -1.0, f4(e1), ALU.mult, ALU.mult, [bkkn, be1], [bAt])
                yield
                TT("dve", f4(tmpA), f4(kkn), f4(asg), ALU.mult, [bkkn, basg], [btmpA])
                yield
                TT("dve", f4(tmpA), f4(tmpA), f4(e2), ALU.mult, [btmpA, be2], [btmpA])
                yield
                CP("act", f4(Bt), f4(tmpA), [btmpA], [bBt])
                yield
                TT("dve", f4(e1), f4(kmod), f4(e2), ALU.mult, [bkmod, be2], [be1])
                yield
                CP("act", f4(Kt), f4(e1), [be1], [bKt])
                yield
                TT("dve", f4(Rt), f4(rT), f4(e3), ALU.mult, [brT, be3], [bRt])
                yield
                yield
                for c in range(4):
                    TS("dve", tmpA[:, c, :], tmpA[:, c, :], eLC[:, c:c + 1], None, ALU.mult, None, [btmpA, beLC], [btmpA])
                    ACT(e1[:, c, :], e1[:, c, :], AF.Identity, [be1, beLC], [be1], scale=eLC[:, c:c + 1])
                yield
                for _ in range(6):
                    yield
                for (src, bsrc, dst, bdst) in ((tmpA, btmpA, Bh, bBh), (e1, be1, Kh, bKh), (vT, bvT, Vt, bVt)):
                    pp, bp = bank("rw")
                    for c in range(4):
                        TR(pp[:, c * 128:(c + 1) * 128], src[:, c, :], identf[:], [bsrc, bidf], [bp], inc=(c == 3))
                    CP("act", dst[:, :], pp[:, :], [bp], [bdst])
                yield
                chk(4)
                yield
                yield
                def hs(t, h):
                    return t[(h % 2) * 64:(h % 2) * 64 + 64, h // 2, :]
                yield

                def five(lt, blt, rt, brt, mask_i, dst, bdst):
                    for par in range(2):
                        pp, bp = bank("rw")
                        for hh in range(4):
                            h = hh * 2 + par
                            MM(pp[:, hh * 128:(hh + 1) * 128], hs(lt, h), hs(rt, h), r=[blt, brt], w=[bp], inc=(hh == 3))
                        TT("dve", dst[:, par:8:2, :], pp[:, :].rearrange("p (h t) -> p h t", h=4), bcast(masks[:, mask_i, :], [128, 4, 128], 1), ALU.mult,
                           [bp, bmasks], [bdst])
                yield

                five(Bt, bBt, At, bAt, 1, Qm[0], bQm[0])
                yield
                five(At, bAt, Bt, bBt, 3, QmT[0], bQmT[0])
                yield
                five(Kt, bKt, At, bAt, 1, AakT, bAak)
                yield
                five(Bt, bBt, Rt, bRt, 2, MrbT, bMrb)
                yield
                five(Kt, bKt, Rt, bRt, 2, MrkT, bMrk)
                yield
                yield
                TT("dve", Xm[0][:], Qm[0][:], bcast(identb[:], [128, 8, 128], 1), ALU.add, [bQm[0], bidb], [bXm[0]])
                yield
                cur = 0
                yield
                for lvl in range(1, 7):
                    nxt = cur ^ 1
                    lastl = lvl == 6
                    for half in range(2):
                        hsl = slice(half * 4, half * 4 + 4)
                        pT_, bpT_ = bank("rw")
                        for hh in range(4):
                            h = half * 4 + hh
                            MM(pT_[:, hh * 128:(hh + 1) * 128], Qm[cur][:, h, :], QmT[cur][:, h, :], r=[bQm[cur], bQmT[cur]], w=[bpT_], inc=(hh == 3))
                        CP("act", QmT[nxt][:, hsl, :], pT_[:, :].rearrange("p (h t) -> p h t", h=4), [bpT_], [bQmT[nxt]])
                        if not lastl:
                            pQ_, bpQ_ = bank("rw")
                            for hh in range(4):
                                h = half * 4 + hh
                                MM(pQ_[:, hh * 128:(hh + 1) * 128], QmT[cur][:, h, :], Qm[cur][:, h, :], r=[bQm[cur], bQmT[cur]], w=[bpQ_], inc=(hh == 3))
                            CP("dve", Qm[nxt][:, hsl, :], pQ_[:, :].rearrange("p (h t) -> p h t", h=4), [bpQ_], [bQm[nxt]])
                        yield
                    for half in range(2):
                        hsl = slice(half * 4, half * 4 + 4)
                        pX_, bpX_ = bank("rw")
                        for hh in range(4):
                            h = half * 4 + hh
                            MM(pX_[:, hh * 128:(hh + 1) * 128], QmT[nxt][:, h, :], Xm[cur][:, h, :], r=[bQmT[nxt], bXm[cur]], w=[bpX_], inc=(hh == 3))
                        TT("dve", Xm[nxt][:, hsl, :], pX_[:, :].rearrange("p (h t) -> p h t", h=4), Xm[cur][:, hsl, :], ALU.add, [bpX_, bXm[cur]], [bXm[nxt]])
                        yield
                    cur = nxt
                yield
                Xf = Xm[cur]; bXf = bXm[cur]
                yield
                chk(5)
                yield
                yield
                if i == 0:
                    MSET("pool", Pst[:], 0.0, [bPst])
                    MSET("pool", Pb[:], 0.0, [bPb])
                yield

                def ph_(t, h):
                    return t[(h % 2) * 64:(h % 2) * 64 + 64, h // 2, :]
                yield

                def vh(t, h):
                    return t[:, h * 64:(h + 1) * 64]
                yield

                for par in range(2):
                    p1, bp1 = bank("rw")
                    for hh in range(4):
                        h = hh * 2 + par
                        MM(p1[:, hh * 64:(hh + 1) * 64], hs(At, h), ph_(Pb, h), start=True, stop=False, r=[bAt, bPb], w=[bp1], inc=False)
                        MM(p1[:, hh * 64:(hh + 1) * 64], AakT[:, h, :], vh(Vt, h), start=False, stop=True, r=[bAak, bVt], w=[bp1], inc=(hh == 3))
                    CP("act", rhs0[:, par:8:2, :], p1[:, 0:256].rearrange("p (h v) -> p h v", h=4), [bp1], [brhs0])
                yield
                chk(5.2)
                yield
                p2, bp2 = bank("rw")
                yield
                for h in range(8):
                    MM(p2[:, h * 64:(h + 1) * 64], Xf[:, h, :], rhs0[:, h, :], r=[bXf, brhs0], w=[bp2], inc=(h == 7))
                yield
                CP("act", Ub[:].rearrange("p h v -> p (h v)"), p2[:, :], [bp2], [bUb])
                yield
                chk(5.4)
                yield
                p3s = []
                yield
                for par in range(2):
                    p3, bp3 = bank("proj")
                    p3s.append((p3, bp3))
                    for hh in range(4):
                        h = hh * 2 + par
                        MM(p3[:, hh * 64:(hh + 1) * 64], hs(Rt, h), ph_(Pb, h), start=True, stop=False, r=[bRt, bPb], w=[bp3], inc=False)
                        MM(p3[:, hh * 64:(hh + 1) * 64], MrkT[:, h, :], vh(Vt, h), start=False, stop=False, r=[bMrk, bVt], w=[bp3], inc=False)
                        MM(p3[:, hh * 64:(hh + 1) * 64], MrbT[:, h, :], Ub[:, h, :], start=False, stop=True, r=[bMrb, bUb], w=[bp3], inc=(hh == 3))
                yield
                chk(5.6)
                yield
                p4, bp4 = bank("rw")
                yield
                for h in range(8):
                    o_ = p4[(h % 2) * 64:(h % 2) * 64 + 64, (h // 2) * 64:(h // 2) * 64 + 64]
                    MM(o_, vh(Bh, h), Ub[:, h, :], start=True, stop=False, r=[bBh, bUb], w=[bp4], inc=False)
                    MM(o_, vh(Kh, h), vh(Vt, h), start=False, stop=True, r=[bKh, bVt], w=[bp4], inc=(h == 7))
                yield
                for c in range(4):
                    STT(Pst[:, c, :], Pst[:, c, :], eLC[:, c:c + 1], p4[:, c * 64:(c + 1) * 64], ALU.mult, ALU.add, [bPst, beLC, bp4], [bPst])
                yield
                CP("act", Pb[:].rearrange("p a b -> p (a b)"), Pst[:].rearrange("p a b -> p (a b)"), [bPst], [bPb])
                yield
                chk(5.8)
                yield
                yield
                yn3 = yn[:, :].rearrange("p (h d) -> p h d", h=8)
                yield
                for par in range(2):
                    p3, bp3 = p3s[par]
                    CP("act", yn3[:, par:8:2, :], p3[:, 0:256].rearrange("p (h d) -> p h d", h=4), [bp3], [byn])
                yield
                ACT(sq[:, 0:512], yn[:, :], AF.Square, [byn], [bsq])
                yield
                REDUCE(lnst[:, 0:8], yn3, [byn], [blnst])
                yield
                REDUCE(lnst[:, 8:16], sq[:, 0:512].rearrange("p (h d) -> p h d", h=8), [bsq], [blnst])
                yield
                chk(5.85)
                yield
                TS("dve", lnst[:, 0:16], lnst[:, 0:16], 1.0 / 64, None, ALU.mult, None, [blnst], [blnst])
                yield
                TT("dve", lnst[:, 16:24], lnst[:, 0:8], lnst[:, 0:8], ALU.mult, [blnst], [blnst])
                yield
                TT("dve", lnst[:, 16:24], lnst[:, 8:16], lnst[:, 16:24], ALU.subtract, [blnst], [blnst])
                yield
                TS("dve", lnst[:, 16:24], lnst[:, 16:24], 0.0, 64e-5, ALU.max, ALU.add, [blnst], [blnst])
                yield
                TT("pool", lnst[:, 24:32], lnst[:, 16:24], cneg[:, 0:8], ALU.pow, [blnst, bcneg], [blnst])
                yield
                STT(lnst[:, 16:24], lnst[:, 0:8], -1.0, lnst[:, 24:32], ALU.mult, ALU.mult, [blnst], [blnst])
                yield
                chk(5.9)
                yield
                for h in range(8):
                    ACT(yn[:, h * 64:(h + 1) * 64], yn[:, h * 64:(h + 1) * 64], AF.Identity, [byn, blnst], [byn],
                        bias=lnst[:, 16 + h:17 + h], scale=lnst[:, 24 + h:25 + h])
                yield
                chk(5.95)
                yield
                TT("dve", yn[:, :], yn[:, :], ln_w_bc[:, :], ALU.mult, [byn, blnw], [byn])
                yield
                chk(5.97)
                yield
                for h in range(8):
                    STT(yn[:, h * 64:(h + 1) * 64], Vt[:, h * 64:(h + 1) * 64], sbon[:, h:h + 1], yn[:, h * 64:(h + 1) * 64], ALU.mult, ALU.add,
                        [bVt, bsbon, byn], [byn])
                yield

            na_ = 2 * (2 + (i + 1) + min(5, i + 1)) + 2
            nr_ = 150
            ga_, gr_ = attn_gen(), rwkv_gen()
            da_ = dr_ = 0
            alive_a = alive_r = True
            while alive_a or alive_r:
                pick_a = alive_a and (not alive_r or da_ * nr_ <= dr_ * na_)
                if pick_a:
                    try:
                        next(ga_); da_ += 1
                    except StopIteration:
                        alive_a = False
                else:
                    try:
                        next(gr_); dr_ += 1
                    except StopIteration:
                        alive_r = False
            yield "mid_done"
            chk(6)
            yield "t"
            def wsload(c):
                k = ws_i[0] % NWS
                ws_i[0] += 1
                DMA(WS[k][:].rearrange("p k n -> p (k n)"), wrest_s[c], sem_ws[k], w=[bWS[k]])
                return WS[k], bWS[k]

            def rest_chunk(c):
                pp, bp = bank("tl")
                for sub in range(2):
                    W_, bW_ = wsload(2 * c + sub)
                    for kc in range(8):
                        MM(pp[:, sub * 128:(sub + 1) * 128], W_[:, kc, :], hcur[:, kc, 1:129], start=(kc == 0), stop=(kc == 7),
                           r=[bW_, bh], w=[bp], inc=(kc == 7 and sub == 1))
                return pp, bp

            for c in range(12):
                pp, bp = rest_chunk(c)
                if c < 4:
                    dst, bd = (silA, bsilA) if c < 2 else (silB, bsilB)
                    dv = dst[:, (c % 2) * 2:(c % 2) * 2 + 2, :].rearrange("p a t -> p (a t)")
                    ACT(dv, pp[:, 0:256], AF.Tanh, [bp], [bd], scale=0.5)
                    STT(dv, dv, 1.0, pp[:, 0:256], ALU.add, ALU.mult, [bd, bp], [bd])
                else:
                    dst, bd = (thA, bthA) if c < 8 else (thB, bthB)
                    cc = (c - 4) % 4
                    ACT(dst[:, cc * 2:cc * 2 + 2, :].rearrange("p a t -> p (a t)"), pp[:, 0:256], AF.Tanh, [bp], [bd], scale=0.5)
            yield "t"
            for (src, bsrc, sil, bsil, dst, bdst, lnb) in ((ynsa, bynsa, silA, bsilA, yaT, byaT, False), (yn, byn, silB, bsilB, ybT, bybT, True)):
                pp, bp = bank("tl")
                for c in range(4):
                    TR(pp[:, c * 128:(c + 1) * 128], src[:, c * 128:(c + 1) * 128], identf[:], [bsrc, bidf], [bp], inc=(c == 3))
                if not lnb:
                    STT(dst[:].rearrange("p c t -> p (c t)"), pp[:, :], 0.5, sil[:].rearrange("p c t -> p (c t)"), ALU.mult, ALU.mult, [bp, bsil], [bdst])
                else:
                    for c in range(4):
                        STT(tmpA[:, c, :], pp[:, c * 128:(c + 1) * 128], vec4[:, 3, c:c + 1], sil[:, c, :], ALU.add, ALU.mult, [bp, bvec4, bsil], [btmpA])
                    ACT(dst[:].rearrange("p c t -> p (c t)"), f4(tmpA), AF.Copy, [btmpA], [bdst], scale=0.5)
            yield "t"
            dump(f"yaT_{T}", yaT[:], [byaT], BF16)
            yield "t"
            dump(f"ybT_{T}", ybT[:], [bybT], BF16)
            yield "t"
            for (yT_, byT_, W_, bW_, th, bth, mg, bmg) in ((yaT, byaT, Wouta, bWouta, thA, bthA, mg1, bmg1), (ybT, bybT, Woutb, bWoutb, thB, bthB, mg2, bmg2)):
                for half in range(2):
                    pp, bp = bank("tl")
                    for mm_ in range(4):
                        mc = half * 4 + mm_
                        for kc in range(4):
                            MM(pp[:, mm_ * 128:(mm_ + 1) * 128], W_[:, kc, mc * 128:(mc + 1) * 128], yT_[:, kc, :], start=(kc == 0), stop=(kc == 3),
                               r=[bW_, byT_], w=[bp], inc=(kc == 3 and mm_ == 3))
                    STT(mg[:, half * 4:(half + 1) * 4, :].rearrange("p a t -> p (a t)"), th[:, half * 4:(half + 1) * 4, :].rearrange("p a t -> p (a t)"), 1.0, pp[:, :],
                        ALU.add, ALU.mult, [bth, bp], [bmg])
            yield "t"
            TT("dve", mgT[:].rearrange("p a t -> p (a t)"), mg1[:].rearrange("p a t -> p (a t)"), mg2[:].rearrange("p a t -> p (a t)"), ALU.add, [bmg1, bmg2], [bmgT])
            yield "t"
            dump(f"mgT_{T}", mgT[:], [bmgT], BF16)
            yield "t"
            if s == 1 and i == 0:
                DMA(Wog[:].rearrange("p k n -> p (k n)"), wog_s, sem_wog, w=[bWog])
            yield "t"
            for half in range(2):
                pp, bp = bank("tl")
                for kc in range(8):
                    MM(pp[:, :], mgT[:, kc, :], Wog[:, kc, half * 512:(half + 1) * 512], start=(kc == 0), stop=(kc == 7), r=[bmgT, bWog], w=[bp], inc=(kc == 7))
                TT("dve", x_t[:, half * 512:(half + 1) * 512], pp[:, :], x_t[:, half * 512:(half + 1) * 512], ALU.add, [bp, bx], [bx])
            yield "t"
            return DMA(out_d[tok0:tok0 + 128, :], x_t[:, :], sem_outs[T % 2], r=[bx], w=[])

        out_toks = []
        total = nseq * ntile
        seq_tiles = [(s, i) for s in range(nseq) for i in range(ntile)]

        def make_gen(n_):
            s_, i_ = seq_tiles[n_]
            T_ = s_ * 16 + i_
            if i_ == 0:
                MSET("pool", kcT[:].rearrange("p a b -> p (a b)"), 0.0, [bkcT])
                MSET("pool", vcT[:].rearrange("p a b -> p (a b)"), 0.0, [bvcT])
                MSET("pool", vca[:, :, 0:64], 0.0, [bvca])
                MSET("pool", kvc[:].rearrange("p a b -> p (a b)"), 0.0, [bkvc])
            return tile_body2(s_, i_)

        def xload(n_):
            if n_ < total:
                s2, i2 = seq_tiles[n_]
                T2 = s2 * 16 + i2
                DMA(xt[T2 % 2][:], x_d[T2 * 128:(T2 + 1) * 128, :], sem_x[T2 % 2], w=[bxt[T2 % 2]])

        def step(g, until):
            while True:
                try:
                    m_ = next(g)
                except StopIteration as e_:
                    return None, True, e_.value
                if m_ in until:
                    return m_, False, None

        def early(n_):
            s_, i_ = seq_tiles[n_]
            T_ = s_ * 16 + i_
            return DMA(out_d[T_ * 128:T_ * 128 + 128, :], xs[:, :], sem_outs[T_ % 2], r=[bxs], w=[])

        if total > 0:
            xload(0)
            xload(1)
            if stage < 9:
                for n_ in range(total):
                    g = make_gen(n_)
                    try:
                        _, _, val = step(g, ())
                        out_toks.append(val)
                    except _Stop:
                        out_toks.append(early(n_))
                    xload(n_ + 2)
            else:
                cur = make_gen(0)
                step(cur, ("front_done",))
                for n_ in range(total):
                    step(cur, ("mid_done",))
                    nxt = make_gen(n_ + 1) if n_ + 1 < total else None
                    cur_done = False
                    nxt_done = nxt is None
                    while not (cur_done and nxt_done):
                        if not cur_done:
                            m_, fin, val = step(cur, ("t",))
                            if fin:
                                cur_done = True
                                out_toks.append(val)
                        if not nxt_done:
                            m_, fin, val = step(nxt, ("f", "front_done"))
                            if m_ == "front_done":
                                nxt_done = True
                    xload(n_ + 2)
                    cur = nxt
        S.wait_all("sp", out_toks[-4:] + dbg_outs + [(sm_, S.dcnt[sm_]) for sm_ in sem_outs])
        S.emit()
    return nc


_CACHE = {}


def kernel(**inputs):
    sh, per = host_prep(inputs)
    if "nc" not in _CACHE:
        _CACHE["nc"] = build()
    nc = _CACHE["nc"]
    in_maps = []
    for core in range(8):
        d = dict(sh)
        d.update(per[core])
        in_maps.append(d)
    res = run_bass_kernel_spmd(nc, in_maps, core_ids=list(range(8)))
    out = np.concatenate([np.asarray(r["out"]).reshape(2, 2048, 1024) for r in res.results], axis=0)
    return out.astype(np.float32)
```

```python
import math
import numpy as np
import concourse.bass as bass
import concourse.mybir as mybir
from concourse.bass_utils import run_bass_kernel_spmd
from contextlib import ExitStack

F32 = mybir.dt.float32
BF16 = mybir.dt.bfloat16
AF = mybir.ActivationFunctionType
ALU = mybir.AluOpType
AX = mybir.AxisListType

COMPUTE = ("pe", "act", "dve", "pool")
NEGM = -4096.0
NRES = 2968
CQ, CKV, CG, CC, CS = 0, 512, 1024, 1048, 1304


class Buf:
    __slots__ = ("w", "r")

    def __init__(self):
        self.w = None
        self.r = {}


class Sched:
    ANNOTATE = False

    def __init__(self, nc, es):
        self.nc = nc
        self.es = es
        self.prog = {e: [] for e in COMPUTE + ("sp",)}
        self.cnt = {e: 0 for e in COMPUTE}
        self.sems = {}
        for e in COMPUTE:
            self.sems[e] = es.enter_context(nc.semaphore("sem_" + e))
        self.known = {e: {} for e in self.prog}
        self.snap = {}
        self.dcnt = {}
        self.pending = {e: False for e in COMPUTE}
        self.last = {}

    def dma_sem(self, name):
        self.sems[name] = self.es.enter_context(self.nc.semaphore("sem_" + name))
        self.dcnt[name] = 0
        return name

    @staticmethod
    def _flat(bs):
        out = []
        for b in bs:
            if isinstance(b, (list, tuple)):
                out.extend(Sched._flat(b))
            else:
                out.append(b)
        return out

    def op(self, eng, fn, reads=(), writes=(), inc=True, dsem=None):
        reads = self._flat(reads)
        writes = self._flat(writes)
        need = {}

        def req(tok, same_ok):
            if tok is None:
                return
            k, v = tok
            if same_ok and k == eng and eng == "pe":
                return
            if need.get(k, 0) < v:
                need[k] = v

        for b in reads:
            req(b.w, False)
        for b in writes:
            req(b.w, True)
            for k, v in b.r.items():
                req((k, v), True)
        kn = self.known[eng]
        waits = []
        for k, v in need.items():
            if kn.get(k, 0) < v:
                waits.append((k, v))
                kn[k] = v
                sn = self.snap.get((k, v))
                if sn is not None:
                    for k2, v2 in sn.items():
                        if kn.get(k2, 0) < v2:
                            kn[k2] = v2
        if dsem is not None:
            self.dcnt[dsem] += 16
            tok = (dsem, self.dcnt[dsem])
            incspec = (dsem, 16)
        elif inc:
            self.cnt[eng] += 1
            tok = (eng, self.cnt[eng])
            incspec = (eng, 1)
            self.pending[eng] = False
            self.snap[tok] = dict(kn)
        else:
            tok = (eng, self.cnt[eng] + 1)
            incspec = None
            self.pending[eng] = True
        self.last[tok[0]] = tok[1]
        for b in writes:
            b.w = tok
            b.r = {}
        for b in reads:
            if b.w is tok:
                continue
            if b.r.get(tok[0], 0) < tok[1]:
                b.r[tok[0]] = tok[1]
        note = None
        if Sched.ANNOTATE:
            import sys as _sys
            f_ = _sys._getframe(1)
            while f_ is not None and f_.f_code.co_name not in ("tile_body2", "attn_gen", "rwkv_gen", "build", "finish", "pv", "five", "rest_chunk", "wsload"):
                f_ = f_.f_back
            note = f"L{f_.f_lineno}" if f_ is not None else None
        self.prog[eng].append((waits, fn, incspec, note))
        return tok

    def wait_all(self, eng, toks):
        kn = self.known[eng]
        waits = []
        mx = {}
        for k, v in toks:
            if mx.get(k, 0) < v:
                mx[k] = v
        for k, v in mx.items():
            if kn.get(k, 0) < v:
                waits.append((k, v))
                kn[k] = v
        self.prog[eng].append((waits, None, None, None))

    def barrier(self):
        for e in COMPUTE:
            if self.pending[e]:
                self.op(e, lambda en: en.nop(), (), ())
        toks = list(self.last.items())
        for e in self.prog:
            self.wait_all(e, toks)

    def emit(self):
        nc = self.nc
        for e in COMPUTE:
            if self.pending[e]:
                self.op(e, lambda en: en.nop(), (), ())
        sems = self.sems
        prog = self.prog

        def run(engname):
            def f(e):
                for waits, fn, incspec, note in prog[engname]:
                    for k, v in waits:
                        e.wait_ge(sems[k], v)
                    if fn is None:
                        continue
                    ins = fn(e)
                    if note is not None:
                        ins.annotate(note)
                    if incspec is not None:
                        ins.then_inc(sems[incspec[0]], incspec[1])
            return f

        with nc.Block() as block:
            block.sync(run("sp"))
            block.tensor(run("pe"))
            block.scalar(run("act"))
            block.vector(run("dve"))
            block.gpsimd(run("pool"))


def _t5_bucket(dist):
    n = np.maximum(dist, 0)
    nf = np.maximum(n, 16).astype(np.float32)
    large = 16 + (np.log(nf / np.float32(16)) / np.float32(math.log(128 / 16)) * np.float32(16)).astype(np.int32)
    return np.where(n < 16, n, np.minimum(large, 31))


def _perms():
    r = lambda a, b: list(range(a, b))
    res = (r(0, 512)
           + r(768, 832) + r(1024, 1088) + r(832, 896) + r(1088, 1152) + r(896, 1024) + r(1152, 1280)
           + r(1280, 1304)
           + r(512, 576) + r(640, 704) + r(576, 640) + r(704, 768)
           + r(1816, 3480))
    rest = r(1304, 1816) + r(3480, 3992) + r(3992, 5016) + r(5016, 6040)
    assert len(res) == NRES and len(rest) == 3072
    return np.array(res), np.array(rest)


def host_prep(inp):
    f = lambda k: np.ascontiguousarray(np.asarray(inp[k], dtype=np.float32))
    sh = {}
    pres, prest = _perms()
    w_in = f("w_in")[0]
    sh["w_res"] = np.ascontiguousarray(w_in[:, pres])
    sh["w_rest"] = np.ascontiguousarray(w_in[:, prest])
    sh["w_ada"] = f("w_ada")[0]
    sh["w_out_a"] = f("w_out_a")[0]
    sh["w_out_b"] = f("w_out_b")[0]
    sh["w_o"] = f("w_o")[0]
    sh["w1k"] = f("cmp_k_w1")[0]
    sh["w1v"] = f("cmp_v_w1")[0]
    col = lambda v, n: np.ascontiguousarray(v.reshape(n, 128).T)
    sh["b_ada"] = col(f("b_ada")[0], 24)
    sh["g_norm"] = col(f("norm_gain")[0], 8)
    sh["mu"] = col(f("shift_mu")[0], 13)
    vec4 = np.stack([col(f(k)[0].reshape(-1), 4) for k in ("k_k", "k_a", "r_k", "ln_x_b")], 1)
    sh["vec4"] = np.ascontiguousarray(vec4)
    rep = lambda v: np.ascontiguousarray(np.broadcast_to(v[None, :], (128, v.shape[0])))
    kng = f("k_norm_gain")[0]
    sh["bc_small"] = np.concatenate([rep(f("q_norm_gain")[0]), rep(kng[1]), rep(kng[2])], 1)
    sh["ln_w_bc"] = rep(f("ln_x_w")[0])
    sh["kgc"] = np.ascontiguousarray(kng[0].reshape(64, 1))
    sh["w0a0"] = np.ascontiguousarray(np.stack([f("w0")[0], f("a0")[0]], 0))
    sh["lora"] = np.ascontiguousarray(np.concatenate([f("w_lora_up")[0], f("a_lora_up")[0]], 0))
    w2 = lambda k: f(k)[0].reshape(2, 128, 64).transpose(1, 0, 2)
    sh["w2"] = np.ascontiguousarray(np.stack([w2("cmp_k_w2"), w2("cmp_v_w2")], 1))
    sh["peT"] = np.ascontiguousarray(np.concatenate([f("cmp_pos_k")[0].T, f("cmp_pos_v")[0].T], 0))
    tbl = f("rel_bias")
    k = np.arange(128)[:, None]
    q = np.arange(128)[None, :]
    tb = np.zeros((2, 2, 128, 4, 128), np.float32)
    for v, dist in enumerate((q - k, 128 + q - k)):
        bk = _t5_bucket(dist)
        for g in range(2):
            for h in range(4):
                tb[v, g, :, h, :] = tbl[bk, g * 4 + h]
    sh["tblDS"] = tb.reshape(2, 2, 128, 512)
    mk = np.zeros((128, 4, 128), np.float32)
    mk[np.broadcast_to(((q - k) < 0)[:, None, :], mk.shape)] = NEGM
    sh["maskD"] = mk.reshape(128, 512)
    c31 = np.zeros((2, 128, 4, 128), np.float32)
    for g in range(2):
        for h in range(4):
            c31[g, :, h, :] = tbl[31, g * 4 + h]
    sh["c31"] = c31.reshape(2, 128, 512)
    p = np.arange(16)[:, None]
    distc = q - 16 * p + 113
    bkc = _t5_bucket(distc)
    tc = np.zeros((2, 16, 4, 128), np.float32)
    for g in range(2):
        for h in range(4):
            tc[g, :, h, :] = tbl[bkc, g * 4 + h]
    sh["tblC"] = tc.reshape(2, 16, 512)
    mc = np.zeros((16, 4, 128), np.float32)
    mc[np.broadcast_to((distc < 0)[:, None, :], mc.shape)] = NEGM
    sh["maskC"] = mc.reshape(16, 512)
    sh["ident"] = np.eye(128, dtype=np.float32)
    far = np.where(k <= q, NEGM, 0.0).astype(np.float32)
    mus = (k < q).astype(np.float32)
    mui = (k <= q).astype(np.float32)
    mls = (k > q).astype(np.float32)
    sh["masks"] = np.ascontiguousarray(np.stack([far, mus, mui, mls], 1))
    z = np.zeros((16, 256), np.float32)
    z[np.arange(16), np.arange(16) + 119] = 1.0
    sh["zsh"] = z
    e = np.zeros((32, 2048), np.float32)
    e[np.arange(2048) // 64, np.arange(2048)] = -NEGM
    sh["emat"] = e
    mi = np.zeros((128, 32), np.float32)
    for j in range(32):
        for a in range(4):
            for b in range(2):
                n = 4 * j + a - b
                if 0 <= n < 127:
                    mi[n, j] += 1.0
    sh["mimp"] = mi
    ka = np.zeros((128, 8, 2, 32), np.float32)
    for i in range(8, 16):
        for qq in range(128):
            cur = (128 * i + qq) // 64
            for j in range(32):
                forced = (j == 0) or (j == cur) or (j == cur - 1)
                causal = j <= cur
                if forced:
                    ka[qq, i - 8, 0, j] = 0.0
                    ka[qq, i - 8, 1, j] = 1e30
                elif causal:
                    ka[qq, i - 8, 0, j] = 1.0
                else:
                    ka[qq, i - 8, 1, j] = -1e30
    sh["keepadd"] = ka.reshape(128, 512)
    ind2 = np.zeros((128, 2), np.float32)
    ind2[:64, 0] = 1.0
    ind2[64:, 1] = 1.0
    sh["ind2"] = ind2
    indT = np.zeros((8, 4, 128), np.float32)
    for h in range(8):
        indT[h, h // 2, (h % 2) * 64:(h % 2) * 64 + 64] = 1.0
    sh["indT"] = indT.reshape(8, 512)
    x = f("x")
    c = f("c")
    per = []
    for core in range(8):
        d = {"x": np.ascontiguousarray(x[2 * core:2 * core + 2].reshape(4096, 1024)),
             "cT": np.ascontiguousarray(c[2 * core:2 * core + 2].reshape(2, 8, 128).transpose(2, 1, 0))}
        per.append(d)
    return sh, per


class _Stop(Exception):
    pass


def build(nseq=2, ntile=16, dbg=None, stage=9):
    nc = bass.Bass("TRN2", target_bir_lowering=False)
    dbg = dbg or {}
    di = lambda name, shape: nc.dram_tensor(name, shape, F32, kind="ExternalInput").ap()
    x_d = di("x", [4096, 1024])
    cT_d = di("cT", [128, 8, 2])
    w_res_d = di("w_res", [1024, NRES])
    w_rest_d = di("w_rest", [1024, 3072])
    w_ada_d = di("w_ada", [1024, 3072])
    w_out_a_d = di("w_out_a", [512, 1024])
    w_out_b_d = di("w_out_b", [512, 1024])
    w_o_d = di("w_o", [1024, 1024])
    w1k_d = di("w1k", [2048, 256])
    w1v_d = di("w1v", [2048, 256])
    b_ada_d = di("b_ada", [128, 24])
    g_norm_d = di("g_norm", [128, 8])
    mu_d = di("mu", [128, 13])
    vec4_d = di("vec4", [128, 4, 4])
    bc_small_d = di("bc_small", [128, 192])
    ln_w_bc_d = di("ln_w_bc", [128, 512])
    kgc_d = di("kgc", [64, 1])
    w0a0_d = di("w0a0", [2, 512])
    lora_d = di("lora", [128, 512])
    w2_d = di("w2", [128, 2, 2, 64])
    peT_d = di("peT", [128, 32])
    tblDS_d = di("tblDS", [2, 2, 128, 512])
    maskD_d = di("maskD", [128, 512])
    c31_d = di("c31", [2, 128, 512])
    tblC_d = di("tblC", [2, 16, 512])
    maskC_d = di("maskC", [16, 512])
    ident_d = di("ident", [128, 128])
    masks_d = di("masks", [128, 4, 128])
    zsh_d = di("zsh", [16, 256])
    emat_d = di("emat", [32, 2048])
    mimp_d = di("mimp", [128, 32])
    keepadd_d = di("keepadd", [128, 512])
    ind2_d = di("ind2", [128, 2])
    indT_d = di("indT", [8, 512])
    out_d = nc.dram_tensor("out", [4096, 1024], F32, kind="ExternalOutput").ap()
    wrest_s = nc.dram_tensor("wrest_s", [24, 128, 1024], BF16, kind="Internal").ap()
    wog_s = nc.dram_tensor("wog_s", [128, 8192], BF16, kind="Internal").ap()

    with ExitStack() as es:
        S = Sched(nc, es)
        _n = [0]

        def sb(shape, dt, name=None):
            _n[0] += 1
            return es.enter_context(nc.sbuf_tensor("s_" + (name or f"sb{_n[0]}"), shape, dt))

        def psb(name):
            return es.enter_context(nc.psum_tensor(name, [128, 512], F32))

        dbg_outs = []

        def dump(name, ap, reads, dt=F32):
            if name not in dbg:
                return
            d = nc.dram_tensor("dbg_" + name, list(ap.shape), dt, kind="ExternalOutput").ap()
            dbg_outs.append(S.op("sp", lambda e: e.dma_start(out=d, in_=ap), reads, (), dsem=sem_dbg))

        def MM(out, lhsT, rhs, start=True, stop=True, r=(), w=(), inc=True, sgc=False):
            if sgc:
                return S.op("pe", lambda e: e.matmul(out, lhsT=lhsT, rhs=rhs, start=start, stop=stop, skip_group_check=True), r, w, inc=inc)
            return S.op("pe", lambda e: e.matmul(out, lhsT=lhsT, rhs=rhs, start=start, stop=stop), r, w, inc=inc)

        def TR(out, in_, ident, r=(), w=(), inc=True):
            return S.op("pe", lambda e: e.transpose(out=out, in_=in_, identity=ident), r, w, inc=inc)

        def ACT(out, in_, func, r=(), w=(), bias=None, scale=None, accum=None):
            kw = {}
            if bias is not None:
                kw["bias"] = bias
            if scale is not None:
                kw["scale"] = scale
            if accum is not None:
                kw["accum_out"] = accum
            return S.op("act", lambda e: e.activation(out=out, in_=in_, func=func, **kw), r, w)

        def TS(eng, out, in0, s1, s2, op0, op1=None, r=(), w=()):
            if op1 is None:
                return S.op(eng, lambda e: e.tensor_scalar(out=out, in0=in0, scalar1=s1, scalar2=None, op0=op0), r, w)
            return S.op(eng, lambda e: e.tensor_scalar(out=out, in0=in0, scalar1=s1, scalar2=s2, op0=op0, op1=op1), r, w)

        def TT(eng, out, in0, in1, op, r=(), w=()):
            return S.op(eng, lambda e: e.tensor_tensor(out=out, in0=in0, in1=in1, op=op), r, w)

        def STT(out, in0, scalar, in1, op0, op1, r=(), w=()):
            return S.op("dve", lambda e: e.scalar_tensor_tensor(out=out, in0=in0, scalar=scalar, in1=in1, op0=op0, op1=op1), r, w)

        def CP(eng, out, in_, r=(), w=()):
            if eng == "act":
                return S.op("act", lambda e: e.copy(out=out, in_=in_), r, w)
            return S.op(eng, lambda e: e.tensor_copy(out=out, in_=in_), r, w)

        def MSET(eng, ap, val, w=()):
            return S.op(eng, lambda e: e.memset(ap, val), (), w)

        def DMA(out, in_, sem, r=(), w=(), eng="sp"):
            return S.op(eng, lambda e: e.dma_start(out=out, in_=in_), r, w, dsem=sem)

        def bcast(ap, shape, axis):
            return ap.unsqueeze(axis).to_broadcast(shape)

        sem_dbg = S.dma_sem("dbg")
        sem_stg = [S.dma_sem("stg0"), S.dma_sem("stg1")]
        sem_scr = S.dma_sem("scr")
        sem_x = [S.dma_sem("x0"), S.dma_sem("x1")]
        sem_xr = S.dma_sem("xr")
        sem_ws = [S.dma_sem(f"ws{i}") for i in range(4)]
        sem_outs = [S.dma_sem("out0"), S.dma_sem("out1")]
        sem_wog = S.dma_sem("wog")

        PS = [psb(f"ps{i}") for i in range(8)]
        PSB = [Buf() for _ in range(8)]
        rot = {"proj": [0, 1], "sc": [2, 3], "acc": [4, 5], "rw": [6, 7], "tl": [4, 5, 6, 7]}
        rotc = {k: 0 for k in rot}

        def bank(cls):
            i = rot[cls][rotc[cls] % len(rot[cls])]
            rotc[cls] += 1
            return PS[i], PSB[i]

        NSLOT = 41
        AR = sb([128, NSLOT * 256], F32, "arena")
        SLB = [Buf() for _ in range(NSLOT)]

        def slot(start, shape, dt, P0=0):
            el = 4 if dt == F32 else 2
            n = int(np.prod(shape[1:]))
            nsl = (n * el + 1023) // 1024
            assert start + nsl <= NSLOT
            base = AR[:] if dt == F32 else AR[:].bitcast(BF16)
            o = start * 1024 // el
            ap = base[P0:P0 + shape[0], o:o + n]
            if len(shape) > 2:
                names = " ".join(f"d{i}" for i in range(len(shape) - 1))
                kw = {f"d{i}": shape[i + 1] for i in range(len(shape) - 1)}
                ap = ap.rearrange(f"p ({names}) -> p {names}", **kw)
            return ap, SLB[start:start + nsl]

        Wres = sb([128, 8, NRES], BF16, "Wres"); bWres = Buf()
        Wouta = sb([128, 4, 1024], BF16, "Wouta"); bWouta = Buf()
        Woutb = sb([128, 4, 1024], BF16, "Woutb"); bWoutb = Buf()
        Wog = sb([128, 8, 1024], BF16, "Wog"); bWog = Buf()
        W1c = sb([128, 32, 256], BF16, "W1c"); bW1c = Buf()
        W2c = sb([128, 2, 2, 64], BF16, "W2c"); bW2c = Buf()
        Lora = sb([128, 512], BF16, "Lora"); bLora = Buf()
        identf = sb([128, 128], F32, "identf"); bidf = Buf()
        identb = sb([128, 128], BF16, "identb"); bidb = Buf()
        masks = sb([128, 4, 128], BF16, "masks"); bmasks = Buf()
        biasDS = sb([128, 2, 2, 512], BF16, "biasDS"); bbias = Buf()
        emat = sb([64, 2048], BF16, "emat"); bemat = Buf()
        zsh = sb([128, 256], BF16, "zsh"); bzsh = Buf()
        biasC = sb([128, 2, 512], BF16, "biasC"); bbiasC = Buf()
        w0a0 = sb([128, 512], F32, "w0a0"); bw0a0 = Buf()
        bmisc = Buf()
        mimp = sb([128, 32], F32, "mimp"); bmimp = Buf()
        keepadd = sb([128, 8, 2, 32], F32, "keepadd"); bka = Buf()
        ind2 = sb([128, 2], F32, "ind2"); bind2 = Buf()
        indT = sb([8, 4, 128], F32, "indT"); bindT = Buf()
        ones_f = sb([128, 128], F32, "ones_f"); bones = Buf()
        bc_small = sb([128, 192], F32, "bc_small"); bbcs = Buf()
        ln_w_bc = sb([128, 512], F32, "ln_w_bc"); blnw = Buf()
        vec4 = sb([128, 4, 4], F32, "vec4"); bvec4 = Buf()
        mucol = sb([128, 2, 13], F32, "mucol"); bmu = Buf()
        kgc = sb([64, 1], F32, "kgc"); bkgc = Buf()
        gcol = sb([128, 8], F32, "gcol"); bgcol = Buf()
        badaT = sb([128, 24], F32, "badaT"); bbada = Buf()
        cTt = sb([128, 8, 2], F32, "cTt"); bcT = Buf()
        modT = sb([128, 24, 2], F32, "modT"); bmod = Buf()
        gsT = sb([128, 2, 8], F32, "gsT"); bgs = Buf()
        hb2 = sb([128, 2, 2], F32, "hb2"); bhb2 = Buf()
        cneg = sb([128, 16], F32, "cneg"); bcneg = Buf()
        peTb = sb([128, 32], BF16, "peTb"); bpeT = Buf()
        siluc = sb([128, 8, 2], F32, "siluc"); bsc = Buf()
        gtmp = sb([128, 16], F32, "gtmp"); bgtmp = Buf()

        stg = []; bstg = []
        for i_ in range(2):
            a_, b_ = slot(16 * i_, [128, 4096], F32)
            stg.append(a_); bstg.append(b_)
        kT = sb([128, 2, 2048], BF16, "kT"); bkT = [Buf() for _ in range(16)]
        Vcf = sb([128, 4160], BF16, "Vc"); bVc = [Buf() for _ in range(16)]
        Vc = Vcf[:].rearrange("p (a b c d) -> p a b c d", a=16, b=2, c=2)
        stgb = kT[:].rearrange("p a b -> p (a b)"); bstgb = bkT
        gate_bc = Vcf[:].bitcast(F32)[:, 0:2048].rearrange("p (s n) -> p s n", s=2); bgbc = bVc

        ldn = [0]
        sem_lds = [S.dma_sem(f"ld{i}") for i in range(8)]

        def ld(out, in_, w):
            sm = sem_lds[ldn[0] % 8]
            ldn[0] += 1
            if S.dcnt[sm] > 0:
                S.wait_all("sp", [(sm, S.dcnt[sm])])
            return DMA(out, in_, sm, w=w)

        ld(identf[:], ident_d, [bidf])
        CP("dve", identb[:], identf[:], [bidf], [bidb])
        ld(stg[0][:, 0:512].rearrange("p (a b) -> p a b", a=4), masks_d, [bstg[0]])
        CP("dve", masks[:], stg[0][:, 0:512].rearrange("p (a b) -> p a b", a=4), [bstg[0]], [bmasks])
        ld(mimp[:], mimp_d, [bmimp])
        ld(keepadd[:].rearrange("p a b c -> p (a b c)"), keepadd_d, [bka])
        ld(ind2[:], ind2_d, [bind2])
        ld(indT[:].rearrange("p a b -> p (a b)"), indT_d, [bindT])
        ld(bc_small[:], bc_small_d, [bbcs])
        ld(ln_w_bc[:], ln_w_bc_d, [blnw])
        ld(vec4[:], vec4_d, [bvec4])
        ld(mucol[:, 0, :], mu_d, [bmu])
        TS("dve", mucol[:, 1, :], mucol[:, 0, :], -1.0, 1.0, ALU.mult, ALU.add, [bmu], [bmu])
        ld(kgc[:], kgc_d, [bkgc])
        MSET("pool", w0a0[:], 0.0, [bw0a0])
        ld(w0a0[0:1, :], w0a0_d[0:1, :], [bw0a0])
        ld(w0a0[64:65, :], w0a0_d[1:2, :], [bw0a0])
        MSET("pool", emat[:], 0.0, [bemat])
        MSET("pool", zsh[:], 0.0, [bzsh])
        MSET("pool", biasC[:].rearrange("p a b -> p (a b)"), 0.0, [bbiasC])
        ld(gcol[:], g_norm_d, [bgcol])
        ld(badaT[:], b_ada_d, [bbada])
        ld(cTt[:], cT_d, [bcT])
        MSET("pool", ones_f[:], 1.0, [bones])
        MSET("pool", cneg[:], -0.5, [bcneg])
        ld(stg[1][0:16, 0:256], zsh_d, [bstg[1]])
        CP("dve", zsh[0:16, :], stg[1][0:16, 0:256], [bstg[1]], [bzsh])
        ld(stg[1][0:32, 0:2048], emat_d, [bstg[1]])
        CP("dve", emat[0:32, :], stg[1][0:32, 0:2048], [bstg[1]], [bemat])
        ld(stg[1][:, 2048:2560], lora_d, [bstg[1]])
        CP("dve", Lora[:], stg[1][:, 2048:2560], [bstg[1]], [bLora])
        ld(stg[1][:, 2560:2816].rearrange("p (a b c) -> p a b c", a=2, b=2), w2_d, [bstg[1]])
        CP("dve", W2c[:], stg[1][:, 2560:2816].rearrange("p (a b c) -> p a b c", a=2, b=2), [bstg[1]], [bW2c])
        ld(stg[1][:, 2816:2848], peT_d, [bstg[1]])
        CP("dve", peTb[:], stg[1][:, 2816:2848], [bstg[1]], [bpeT])
        for g in range(2):
            ld(stg[0][:, 0:512], c31_d[g], [bstg[0]])
            for v in range(2):
                ld(stg[1][:, 0:512], tblDS_d[v, g], [bstg[1]])
                TT("dve", stg[1][:, 0:512], stg[1][:, 0:512], stg[0][:, 0:512], ALU.subtract, [bstg[0], bstg[1]], [bstg[1]])
                if v == 0:
                    ld(stg[1][:, 512:1024], maskD_d, [bstg[1]])
                    STT(biasDS[:, v, g, :], stg[1][:, 0:512], 8.0, stg[1][:, 512:1024], ALU.mult, ALU.add, [bstg[1]], [bbias])
                else:
                    TS("dve", biasDS[:, v, g, :], stg[1][:, 0:512], 8.0, None, ALU.mult, None, [bstg[1]], [bbias])
            ld(stg[1][0:16, 0:512], tblC_d[g], [bstg[1]])
            ld(stg[1][0:16, 512:1024], maskC_d, [bstg[1]])
            TT("dve", stg[1][0:16, 0:512], stg[1][0:16, 0:512], stg[0][0:16, 0:512], ALU.subtract, [bstg[0], bstg[1]], [bstg[1]])
            STT(biasC[0:16, g, :], stg[1][0:16, 0:512], 8.0, stg[1][0:16, 512:1024], ALU.mult, ALU.add, [bstg[1]], [bbiasC])

        def stage_load(i, src_ap, ncols, nk=8):
            view = stg[i][:, 0:nk * ncols].rearrange("p (k n) -> p k n", k=nk)
            DMA(view, src_ap, sem_stg[i], w=[bstg[i]])
            return view

        si = 0
        for c0 in range(0, NRES, 512):
            n = min(512, NRES - c0)
            v = stage_load(si, w_res_d[:, c0:c0 + n].rearrange("(k p) n -> p k n", p=128), n)
            CP("dve" if si == 0 else "act", Wres[:, :, c0:c0 + n], v, [bstg[si]], [bWres])
            si ^= 1
        for c in range(6):
            v = stage_load(si, w_rest_d[:, c * 512:(c + 1) * 512].rearrange("(k p) n -> p k n", p=128), 512)
            sv = stgb[:, 0:4096].rearrange("p (k n) -> p k n", k=8)
            CP("dve" if si == 0 else "act", sv, v, [bstg[si]], [bstgb])
            for j_ in range(4):
                DMA(wrest_s[4 * c + j_].rearrange("p (k n) -> p k n", k=8),
                    stgb[:, 0:4096].rearrange("p (k j n) -> p k j n", k=8, j=4)[:, :, j_, :], sem_scr, r=[bstgb], w=[Buf()])
            si ^= 1
        for (wd_, Wt, bW) in ((w_out_a_d, Wouta, bWouta), (w_out_b_d, Woutb, bWoutb)):
            v = stage_load(si, wd_.rearrange("(k p) n -> p k n", p=128), 1024, nk=4)
            CP("dve" if si == 0 else "act", Wt[:], v, [bstg[si]], [bW])
            si ^= 1
        for (wd_, lo) in ((w1k_d, 0), (w1v_d, 64)):
            for hh in range(2):
                view = stg[si][lo:lo + 64, 0:4096].rearrange("p (k n) -> p k n", k=16)
                DMA(view, wd_[hh * 1024:(hh + 1) * 1024, :].rearrange("(k p) n -> p k n", p=64), sem_stg[si], w=[bstg[si]])
                CP("dve" if si == 0 else "act", W1c[lo:lo + 64, hh * 16:(hh + 1) * 16, :], view, [bstg[si]], [bW1c])
                si ^= 1
        ACT(siluc[:], cTt[:], AF.Tanh, [bcT], [bsc], scale=0.5)
        TS("dve", siluc[:], siluc[:], 0.5, 0.5, ALU.mult, ALU.add, [bsc], [bsc])
        TT("dve", siluc[:], siluc[:], cTt[:], ALU.mult, [bsc, bcT], [bsc])
        pm, bpm = bank("proj")
        silucb = sb([128, 8, 2], BF16, "silucb"); bscb = Buf()
        CP("dve", silucb[:], siluc[:], [bsc], [bscb])
        for c in range(6):
            v = stage_load(si, w_ada_d[:, c * 512:(c + 1) * 512].rearrange("(k p) n -> p k n", p=128), 512)
            vb = stgb[:, 0:4096].rearrange("p (k n) -> p k n", k=8)
            CP("dve" if si == 0 else "act", vb, v, [bstg[si]], [bstgb])
            for jj in range(4):
                j = c * 4 + jj
                for kc in range(8):
                    MM(pm[:, j * 2:j * 2 + 2], vb[:, kc, jj * 128:(jj + 1) * 128], silucb[:, kc, :], start=(kc == 0), stop=(kc == 7),
                       r=[bstgb, bscb], w=[bpm], inc=(kc == 7))
            si ^= 1
        TT("dve", modT[:], pm[:, 0:48].rearrange("p (j b) -> p j b", b=2), bcast(badaT[:], [128, 24, 2], 2), ALU.add, [bpm, bbada], [bmod])
        for s in range(2):
            STT(gsT[:, s, :], modT[:, 8:16, s], 1.0, gcol[:], ALU.add, ALU.mult, [bmod, bgcol], [bgs])
        CP("dve", gtmp[:].rearrange("p (s j) -> p s j", s=2), modT[:, 16:24, :].rearrange("p j s -> p s j"), [bmod], [bgtmp])
        for q4 in range(4):
            pg, bpg = bank("proj")
            for jq in range(4):
                qq = q4 * 4 + jq
                MM(pg[0:1, jq * 128:(jq + 1) * 128], gtmp[:, qq:qq + 1], identf[:], r=[bgtmp, bidf], w=[bpg], inc=(jq == 3))
            CP("dve", stg[1][0:1, q4 * 512:(q4 + 1) * 512], pg[0:1, 0:512], [bpg], [bstg[1]])
        for s in range(2):
            for hh in range(2):
                pb_, bpb_ = bank("proj")
                MM(pb_[:, :], ones_f[0:1, :], stg[1][0:1, s * 1024 + hh * 512: s * 1024 + hh * 512 + 512], r=[bones, bstg[1]], w=[bpb_])
                TS("dve", gate_bc[:, s, hh * 512:(hh + 1) * 512], pb_[:, :], 0.5, None, ALU.mult, None, [bpb_], [bgbc])
        for s in (1, 0):
            for hh in range(2):
                v = stage_load(0, w_o_d[:, hh * 512:(hh + 1) * 512].rearrange("(k p) n -> p k n", p=128), 512)
                TT("dve", Wog[:, :, hh * 512:(hh + 1) * 512], v, bcast(gate_bc[:, s, hh * 512:(hh + 1) * 512], [128, 8, 512], 1), ALU.mult,
                   [bstg[0], bgbc], [bWog])
            if s == 1:
                DMA(wog_s, Wog[:].rearrange("p k n -> p (k n)"), sem_scr, r=[bWog], w=[Buf()])
        for kv in range(2):
            lo = kv * 64
            ph, bph = bank("proj")
            for jh in range(2):
                for pos in range(32):
                    MM(ph[:, jh:jh + 1], W1c[lo:lo + 64, pos, jh * 128:(jh + 1) * 128], peTb[lo:lo + 64, pos:pos + 1],
                       start=(pos == 0), stop=(pos == 31), r=[bW1c, bpeT], w=[bph], inc=(pos == 31))
            CP("dve", hb2[:, kv, :], ph[:, 0:2], [bph], [bhb2])
        S.barrier()
        print("SBUF remaining before main alloc:", nc.sbuf_bytes_remaining)

        xt = [sb([128, 1024], F32, f"xt{i}") for i in range(2)]; bxt = [Buf(), Buf()]
        hT = [sb([128, 8, 130], BF16, f"hT{i}") for i in range(2)]; bhT = [Buf(), Buf()]
        for i_ in range(2):
            MSET("pool", hT[i_][:].rearrange("p a b -> p (a b)"), 0.0, [bhT[i_]])
        ynsa = sb([128, 512], F32, "ynsa"); bynsa = Buf()
        yn = sb([128, 512], F32, "yn"); byn = Buf()
        st12 = sb([128, 16], F32, "st12"); bst12 = Buf()
        rs12 = sb([128, 16], F32, "rs12"); brs12 = Buf()
        MSET("pool", Vcf[:], 1.0, bVc)
        gsig = sb([128, 3, 8], F32, "gsig"); bgsig = Buf()
        kvc = sb([128, 2, 144], BF16, "kvc"); bkvc = Buf()
        kcT = sb([64, 2, 128], BF16, "kcT"); bkcT = Buf()
        vcT = sb([64, 2, 128], F32, "vcT"); bvcT = Buf()
        vca = sb([128, 2, 65], F32, "vca"); bvca = Buf()
        MSET("pool", vca[:].rearrange("p a b -> p (a b)"), 1.0, [bvca])
        hu = sb([128, 64], F32, "hu"); bhu = Buf()
        hw_ = sb([128, 64], F32, "hw_"); bhw = Buf()
        hid = sb([128, 64], BF16, "hid"); bhid = Buf()
        kcs = sb([64, 48], F32, "kcs"); bkcs = Buf()
        coef = sb([128, 16], F32, "coef"); bcoef = Buf()
        impr = sb([128, 2, 32], F32, "impr"); bimpr = Buf()
        imp2 = sb([128, 32], F32, "imp2"); bimp2 = Buf()
        m8a = sb([128, 8], F32, "m8a"); bm8a = Buf()
        m8b = sb([128, 8], F32, "m8b"); bm8b = Buf()
        nsel = sb([128, 2, 32], F32, "nsel"); bnsel = Buf()
        nselT = sb([64, 2, 128], BF16, "nselT"); bnselT = Buf()
        MSET("pool", nselT[:].rearrange("p a b -> p (a b)"), 0.0, [bnselT])
        wdad = sb([128, 128], F32, "wdad"); bwdad = Buf()
        wdadb = sb([128, 128], BF16, "wdadb"); bwdadb = Buf()
        eLC = sb([128, 4], F32, "eLC"); beLC = Buf()
        rn8 = sb([128, 8], F32, "rn8"); brn8 = Buf()
        rn8T = sb([8, 128], F32, "rn8T"); brn8T = Buf()
        sbon = sb([128, 8], F32, "sbon"); bsbon = Buf()
        Pst = sb([128, 4, 64], F32, "Pst"); bPst = Buf()
        Pb = sb([128, 4, 64], BF16, "Pb"); bPb = Buf()
        lnst = sb([128, 32], F32, "lnst"); blnst = Buf()
        NWS = 4
        WS = [sb([128, 8, 128], BF16, f"WS{i}") for i in range(NWS)]; bWS = [Buf() for _ in range(NWS)]
        sq, bsq = slot(0, [128, 1024], F32)
        xs, bxs = sq, bsq
        qn2, bqn2 = slot(4, [128, 8, 2, 64], BF16)
        qT2, bqT2 = slot(35, [128, 8, 128], BF16)
        kn2, bkn2 = slot(8, [128, 2, 2, 64], BF16)
        PT = []; bPT = []
        NPT = 2
        for i_ in range(NPT):
            a_, b_ = slot(37 + i_, [128, 512], BF16)
            PT.append(a_); bPT.append(b_)
        PcT, bPcT = slot(39, [128, 512], F32)
        silA, bsilA = slot(15, [128, 4, 128], F32)
        silB, bsilB = slot(17, [128, 4, 128], F32)
        thA, bthA = slot(19, [128, 8, 128], BF16)
        thB, bthB = slot(21, [128, 8, 128], BF16)
        mg1, bmg1 = slot(23, [128, 8, 128], F32)
        mg2, bmg2 = slot(27, [128, 8, 128], F32)
        mgT, bmgT = slot(31, [128, 8, 128], BF16)
        yaT, byaT = slot(33, [128, 4, 128], BF16)
        ybT, bybT = slot(34, [128, 4, 128], BF16)
        rT, brT = slot(4, [128, 4, 128], F32)
        kTr, bkTr = slot(6, [128, 4, 128], F32)
        vT, bvT = slot(8, [128, 4, 128], F32)
        lwT, blw = slot(10, [128, 4, 128], F32)
        LT, bLT = slot(12, [128, 4, 128], F32)
        asg, basg = slot(14, [128, 4, 128], F32)
        e1, be1 = slot(16, [128, 4, 128], F32)
        e2, be2 = slot(18, [128, 4, 128], F32)
        e3, be3 = slot(20, [128, 4, 128], F32)
        kkn, bkkn = slot(22, [128, 4, 128], F32)
        kmod, bkmod = slot(24, [128, 4, 128], F32)
        tmpA, btmpA = slot(26, [128, 4, 128], F32)
        At, bAt = slot(28, [128, 4, 128], BF16)
        Bt, bBt = slot(29, [128, 4, 128], BF16)
        Kt, bKt = slot(30, [128, 4, 128], BF16)
        Rt, bRt = slot(31, [128, 4, 128], BF16)
        Bh, bBh = slot(32, [128, 512], BF16)
        Kh, bKh = slot(33, [128, 512], BF16)
        Vt, bVt = slot(34, [128, 512], BF16)
        Qm = []; bQm = []; QmT = []; bQmT = []; Xm = []; bXm = []
        for st_ in (10, 12):
            a_, b_ = slot(st_, [128, 8, 128], BF16); Qm.append(a_); bQm.append(b_)
        for st_ in (14, 18):
            a_, b_ = slot(st_, [128, 8, 128], BF16); QmT.append(a_); bQmT.append(b_)
        for st_ in (20, 22):
            a_, b_ = slot(st_, [128, 8, 128], BF16); Xm.append(a_); bXm.append(b_)
        AakT, bAak = slot(24, [128, 8, 128], BF16)
        MrbT, bMrb = slot(4, [128, 8, 128], BF16)
        MrkT, bMrk = slot(6, [128, 8, 128], BF16)
        rhs0, brhs0 = slot(8, [128, 8, 64], BF16)
        Ub, bUb = slot(9, [128, 8, 64], BF16)

        def f4(t):
            return t.rearrange("p c t -> p (c t)")

        def REDUCE(out, in_, r, w):
            return S.op("dve", lambda e: e.tensor_reduce(out=out, in_=in_, axis=AX.X, op=ALU.add), r, w)

        def MAX8(out, in_, r, w):
            return S.op("dve", lambda e: e.max(out=out, in_=in_), r, w)

        def MREP(out, rep, vals, r, w):
            return S.op("dve", lambda e: e.match_replace(out=out, in_to_replace=rep, in_values=vals, imm_value=-3.0e38), r, w)

        def RECIP(out, in_, r, w):
            return S.op("dve", lambda e: e.reciprocal(out=out, in_=in_), r, w)

        def SCAN(out, d0, d1, r, w):
            return S.op("dve", lambda e: e.tensor_tensor_scan(out=out, data0=d0, data1=d1, initial=0.0, op0=ALU.mult, op1=ALU.add), r, w)

        print("SBUF remaining:", nc.sbuf_bytes_remaining)
        ws_i = [0]

        def chk(n):
            if stage <= n:
                raise _Stop()

        def tile_body2(s, i):
            T = s * 16 + i
            yield "f"
            tok0 = T * 128
            yield "f"
            xb_ = T % 2
            yield "f"
            x_t = xt[xb_]; bx = bxt[xb_]
            yield "f"
            hcur = hT[T % 2]; bh = bhT[T % 2]
            yield "f"
            hprev = hT[(T + 1) % 2]; bhp = bhT[(T + 1) % 2]
            yield "f"
            ACT(sq[:], x_t[:], AF.Square, [bx], [bsq, bst12], accum=st12[:, 0:1])
            yield "f"
            TS("dve", st12[:, 0:1], st12[:, 0:1], 1.0 / 1024, 1e-6, ALU.mult, ALU.add, [bst12], [bst12])
            yield "f"
            TT("pool", rs12[:, 0:1], st12[:, 0:1], cneg[:, 0:1], ALU.pow, [bst12, bcneg], [brs12])
            yield "f"
            TS("dve", xs[:], x_t[:], rs12[:, 0:1], None, ALU.mult, None, [bx, brs12], [bxs])
            yield "f"
            import os as _os
            yield "f"
            _sk = _os.environ.get("SKIP", "")
            yield "f"
            if i == 0:
                if "m" not in _sk:
                    MSET("pool", hcur[:, :, 0:1], 0.0, [bh])
            else:
                CP("pool", hcur[:, :, 0:1], hprev[:, :, 128:129], [bhp], [bh])
            yield "f"
            for half in range(2):
                pp, bp = bank("proj")
                for j in range(4):
                    kc = half * 4 + j
                    TR(pp[:, j * 128:(j + 1) * 128], xs[:, kc * 128:(kc + 1) * 128], identf[:], [bxs, bidf], [bp], inc=(j == 3))
                for j in range(4):
                    kc = half * 4 + j
                    if "a" in _sk:
                        ACT(hcur[:, kc, 1:129], pp[:, j * 128:(j + 1) * 128], AF.Identity, [bp, bgs, bmod], [bh])
                    elif "b" in _sk:
                        ACT(hcur[:, kc, 2:130], pp[:, j * 128:(j + 1) * 128], AF.Identity, [bp, bgs, bmod], [bh],
                            bias=modT[:, kc, s:s + 1], scale=gsT[:, s, kc:kc + 1])
                    else:
                        ACT(hcur[:, kc, 1:129], pp[:, j * 128:(j + 1) * 128], AF.Identity, [bp, bgs, bmod], [bh],
                            bias=modT[:, kc, s:s + 1], scale=gsT[:, s, kc:kc + 1])
            yield "f"
            dump(f"hT_{T}", hcur[:], [bh], BF16)
            yield "f"
            chk(1)
            yield "f"

            pq, bpq = bank("proj")
            yield "f"
            for kc in range(8):
                MM(pq[:, :], hcur[:, kc, 1:129], Wres[:, kc, CQ:CQ + 512], start=(kc == 0), stop=(kc == 7), r=[bh, bWres], w=[bpq], inc=(kc == 7))
            yield "f"
            ACT(sq[:, 0:512], pq[:, :], AF.Square, [bpq], [bsq])
            yield "f"
            REDUCE(st12[:, 0:8], sq[:, 0:512].rearrange("p (h d) -> p h d", h=8), [bsq], [bst12])
            yield "f"
            pkv, bpkv = bank("proj")
            yield "f"
            for kc in range(8):
                MM(pkv[:, :], hcur[:, kc, 1:129], Wres[:, kc, CKV:CKV + 512], start=(kc == 0), stop=(kc == 7), r=[bh, bWres], w=[bpkv], inc=(kc == 7))
            yield "f"
            ACT(sq[:, 512:768], pkv[:, 0:256], AF.Square, [bpkv], [bsq])
            yield "f"
            REDUCE(st12[:, 8:12], sq[:, 512:768].rearrange("p (h d) -> p h d", h=4), [bsq], [bst12])
            yield "f"
            TS("dve", st12[:, 0:12], st12[:, 0:12], 1.0 / 64, 1e-6, ALU.mult, ALU.add, [bst12], [bst12])
            yield "f"
            TT("pool", rs12[:, 0:12], st12[:, 0:12], cneg[:, 0:12], ALU.pow, [bst12, bcneg], [brs12])
            yield "f"
            chk(1.2)
            yield "f"
            for h in range(8):
                STT(qn2[:, h, :, :], bcast(pq[:, h * 64:(h + 1) * 64], [128, 2, 64], 1), rs12[:, h:h + 1],
                    bcast(bc_small[:, 0:64], [128, 2, 64], 1), ALU.mult, ALU.mult, [bpq, brs12, bbcs], [bqn2])
            yield "f"
            for gg in range(2):
                for br in range(2):
                    c0 = gg * 128 + br * 64
                    STT(kn2[:, gg, br, :], pkv[:, c0:c0 + 64], rs12[:, 8 + gg * 2 + br:9 + gg * 2 + br],
                        bc_small[:, 64 + br * 64:128 + br * 64], ALU.mult, ALU.mult, [bpkv, brs12, bbcs], [bkn2])
            yield "f"
            CP("act", Vc[:, i, :, :, 0:64], pkv[:, 256:512].rearrange("p (b g d) -> p b g d", b=2, g=2), [bpkv], [bVc[i]])
            yield "f"
            chk(1.4)
            yield "f"
            for _ in range(24):
                yield "f"
            pt, bpt = bank("proj")
            yield "f"
            ptb = pt[:].bitcast(BF16)
            yield "f"
            for h in range(8):
                TR(ptb[:, h * 128:(h + 1) * 128], qn2[:, h, :, :].rearrange("p c d -> p (c d)"), identb[:], [bqn2, bidb], [bpt], inc=(h == 7))
            yield "f"
            CP("act", qT2[:].rearrange("p h q -> p (h q)"), ptb[:, 0:1024], [bpt], [bqT2])
            yield "f"
            pt2, bpt2 = bank("proj")
            yield "f"
            pt2b = pt2[:].bitcast(BF16)
            yield "f"
            for gg in range(2):
                TR(pt2b[:, gg * 128:(gg + 1) * 128], kn2[:, gg, :, :].rearrange("p c d -> p (c d)"), identb[:], [bkn2, bidb], [bpt2], inc=(gg == 1))
            yield "f"
            CP("dve", kT[:, :, i * 128:(i + 1) * 128], pt2b[:, 0:256].rearrange("p (g t) -> p g t", g=2), [bpt2], [bkT[i]])
            yield "f"
            chk(1.6)
            yield "f"
            pgt, bpgt = bank("proj")
            yield "f"
            for kc in range(8):
                MM(pgt[:, 0:24], hcur[:, kc, 1:129], Wres[:, kc, CG:CG + 24], start=(kc == 0), stop=(kc == 7), r=[bh, bWres], w=[bpgt], inc=(kc == 7))
            yield "f"
            ACT(gsig[:].rearrange("p a b -> p (a b)"), pgt[:, 0:24], AF.Tanh, [bpgt], [bgsig], scale=0.5)
            yield "f"
            TS("dve", gsig[:].rearrange("p a b -> p (a b)"), gsig[:].rearrange("p a b -> p (a b)"), 0.5, 0.5, ALU.mult, ALU.add, [bgsig], [bgsig])
            yield "f"
            pcm, bpcm = bank("proj")
            yield "f"
            for gg in range(2):
                for kc in range(8):
                    MM(pcm[:, gg * 128:(gg + 1) * 128], Wres[:, kc, CC + gg * 128:CC + (gg + 1) * 128], hcur[:, kc, 1:129],
                       start=(kc == 0), stop=(kc == 7), r=[bh, bWres], w=[bpcm], inc=(kc == 7 and gg == 1))
            yield "f"
            CP("pool", kvc[:, :, 0:16], kvc[:, :, 128:144], [bkvc], [bkvc])
            yield "f"
            CP("act", kvc[:, :, 16:144], pcm[:, 0:256].rearrange("p (g t) -> p g t", g=2), [bpcm], [bkvc])
            yield "f"
            chk(1.8)
            yield "f"
            m0 = 1 if i == 0 else 0
            yield "f"
            nm = 8 - m0
            yield "f"
            for kv in range(2):
                lo = kv * 64
                phd, bphd = bank("proj")
                for jh in range(2):
                    for pos in range(32):
                        MM(phd[:, jh * 16:jh * 16 + 16].rearrange("p (g m) -> p g m", g=2), W1c[lo:lo + 64, pos, jh * 128:(jh + 1) * 128],
                           kvc[lo:lo + 64, :, pos:pos + 113:16], start=(pos == 0), stop=(pos == 31), r=[bW1c, bkvc], w=[bphd],
                           inc=(pos == 31))
                for jh in range(2):
                    reg = (kv * 2 + jh) * 16
                    ACT(hu[:, reg:reg + 16], phd[:, jh * 16:jh * 16 + 16], AF.Identity, [bphd, bhb2], [bhu], bias=hb2[:, kv, jh:jh + 1])
            yield "f"
            chk(1.85)
            yield "f"
            TT("dve", hw_[:], hu[:], hu[:], ALU.mult, [bhu], [bhw])
            yield "f"
            TS("dve", hw_[:], hw_[:], 0.044715, 1.0, ALU.mult, ALU.add, [bhw], [bhw])
            yield "f"
            TT("dve", hw_[:], hw_[:], hu[:], ALU.mult, [bhw, bhu], [bhw])
            yield "f"
            ACT(hw_[:], hw_[:], AF.Tanh, [bhw], [bhw], scale=math.sqrt(2.0 / math.pi))
            yield "f"
            STT(hid[:], hw_[:], 1.0, hu[:], ALU.add, ALU.mult, [bhw, bhu], [bhid])
            yield "f"
            chk(1.9)
            yield "f"
            for _ in range(6):
                yield "f"
            pc2, bpc2 = bank("proj")
            yield "f"
            for kv in range(2):
                for jh in range(2):
                    reg = (kv * 2 + jh) * 16
                    MM(pc2[0:64, kv * 16:(kv + 1) * 16], W2c[:, kv, jh, :], hid[:, reg:reg + 16], start=(jh == 0), stop=(jh == 1),
                       r=[bW2c, bhid], w=[bpc2], inc=(jh == 1))
            yield "f"
            TS("dve", kcs[:, 0:16], pc2[0:64, 0:16], 0.5, None, ALU.mult, None, [bpc2], [bkcs])
            yield "f"
            TT("dve", kcs[:, 16:32], kcs[:, 0:16], kcs[:, 0:16], ALU.mult, [bkcs], [bkcs])
            yield "f"
            MM(pc2[0:64, 64:80], ones_f[0:64, 0:64], kcs[:, 16:32], r=[bones, bkcs], w=[bpc2])
            yield "f"
            TS("dve", kcs[:, 32:48], pc2[0:64, 64:80], 1.0 / 64, 1e-6, ALU.mult, ALU.add, [bpc2], [bkcs])
            yield "f"
            TT("pool", kcs[:, 16:32], kcs[:, 32:48], cneg[0:64, 0:16], ALU.pow, [bkcs, bcneg], [bkcs])
            yield "f"
            TT("dve", kcs[:, 0:16], kcs[:, 0:16], kcs[:, 16:32], ALU.mult, [bkcs], [bkcs])
            yield "f"
            n0 = 8 * i - 1 + m0
            yield "f"
            TS("dve", kcT[:, :, n0:n0 + nm], kcs[:, 0:16].rearrange("p (g m) -> p g m", g=2)[:, :, m0:8], kgc[:, 0:1], None, ALU.mult, None,
               [bkcs, bkgc], [bkcT])
            yield "f"
            TS("dve", vcT[:, :, n0:n0 + nm], pc2[0:64, 16:32].rearrange("p (g m) -> p g m", g=2)[:, :, m0:8], 0.5, None, ALU.mult, None,
               [bpc2], [bvcT])
            yield "f"
            nv = 8 * i + 7
            yield "f"
            chk(1.95)
            yield "f"
            for _ in range(16):
                yield "f"
            pvt, bpvt = bank("proj")
            yield "f"
            for gg in range(2):
                TR(pvt[0:nv, gg * 64:(gg + 1) * 64], vcT[:, gg, 0:nv], identf[0:64, 0:64], [bvcT, bidf], [bpvt], inc=(gg == 1))
            yield "f"
            CP("dve", vca[0:nv, :, 0:64], pvt[0:nv, 0:128].rearrange("p (g d) -> p g d", g=2), [bpvt], [bvca])
            yield "f"
            dump(f"kcT_{T}", kcT[:], [bkcT], BF16)
            yield "f"
            dump(f"vca_{T}", vca[:], [bvca])
            yield "f"
            dump(f"qT2_{T}", qT2[:], [bqT2], BF16)
            yield "f"
            chk(2)
            yield "f"

            yield "front_done"
            def attn_gen():
                first_y = {0: True, 1: True}

                def finish(acc, bacc, br, gg):
                    accv = acc[:, 0:260].rearrange("p (h e) -> p h e", h=4)
                    c0 = br * 4
                    TS("dve", coef[:, c0:c0 + 4], accv[:, :, 64], 1e-30, None, ALU.max, None, [bacc], [bcoef])
                    RECIP(coef[:, c0:c0 + 4], coef[:, c0:c0 + 4], [bcoef], [bcoef])
                    if br == 0:
                        CP("dve", coef[:, 12:16], coef[:, 0:4], [bcoef], [bcoef])
                    gbr = {0: 0, 1: 1, 2: 2}[br]
                    TT("dve", coef[:, c0:c0 + 4], coef[:, c0:c0 + 4], gsig[:, gbr, gg * 4:(gg + 1) * 4], ALU.mult, [bcoef, bgsig], [bcoef])
                    yv = ynsa[:, gg * 256:(gg + 1) * 256].rearrange("p (h d) -> p h d", h=4)
                    cb = bcast(coef[:, c0:c0 + 4], [128, 4, 64], 2)
                    if first_y[gg]:
                        TT("dve", yv, accv[:, :, 0:64], cb, ALU.mult, [bacc, bcoef], [bynsa])
                        first_y[gg] = False
                    else:
                        for h in range(4):
                            STT(yv[:, h, :], accv[:, h, 0:64], coef[:, c0 + h:c0 + h + 1], yv[:, h, :], ALU.mult, ALU.add, [bacc, bcoef, bynsa], [bynsa])

                def pv(acc, bacc, Pt_, bP, vrhs, bv, first, last, K=128):
                    for h in range(4):
                        MM(acc[:, h * 65:(h + 1) * 65], Pt_[0:K, h * 128:(h + 1) * 128], vrhs, start=(first and h == 0), stop=(last and h == 3), r=[bP] + bv, w=[bacc],
                           inc=(h == 3), sgc=True)

                pti = [0]
                for gg in range(2):
                    sc, bsc_ = bank("sc")
                    MM(sc[0:nv, :], kcT[:, gg, 0:nv], qT2[0:64, gg * 4:(gg + 1) * 4, :].rearrange("p h q -> p (h q)"), start=True, stop=False,
                       r=[bkcT, bqT2], w=[bsc_], inc=False)
                    off = 128 - 8 * i
                    MM(sc[0:nv, :], zsh[:, off:off + nv], biasC[:, gg, :], start=False, stop=True, r=[bzsh], w=[bsc_])
                    ACT(PcT[0:nv, :], sc[0:nv, :], AF.Exp, [bsc_], [bPcT], scale=0.125)
                    acc, bacc = bank("acc")
                    for h in range(4):
                        MM(acc[:, h * 65:(h + 1) * 65], PcT[0:nv, h * 128:(h + 1) * 128], vca[0:nv, gg, :], r=[bPcT, bvca], w=[bacc], inc=False)
                    for h in range(4):
                        MM(acc[:, 320 + h * 32:320 + (h + 1) * 32], PcT[0:nv, h * 128:(h + 1) * 128], mimp[0:nv, :], r=[bPcT, bmimp], w=[bacc], inc=(h == 3))
                    finish(acc, bacc, 0, gg)
                    yield
                    if i >= 8:
                        iv = impr[:, gg, :]
                        TS("dve", iv, acc[:, 320:352], coef[:, 12:13], None, ALU.mult, None, [bacc, bcoef], [bimpr])
                        for h in range(1, 4):
                            STT(iv, acc[:, 320 + h * 32:352 + h * 32], coef[:, 12 + h:13 + h], iv, ALU.mult, ALU.add, [bacc, bcoef, bimpr], [bimpr])
                        TT("dve", iv, iv, keepadd[:, i - 8, 0, :], ALU.mult, [bimpr, bka], [bimpr])
                        TT("dve", iv, iv, keepadd[:, i - 8, 1, :], ALU.add, [bimpr, bka], [bimpr])
                        MAX8(m8a[:], iv, [bimpr], [bm8a])
                        MREP(imp2[:], m8a[:], iv, [bimpr, bm8a], [bimp2])
                        MAX8(m8b[:], imp2[:], [bimp2], [bm8b])
                        TS("dve", nsel[:, gg, :], iv, m8b[:, 7:8], 1.0, ALU.is_ge, ALU.subtract, [bimpr, bm8b], [bnsel])
                dump(f"nsel_{T}", nsel[:], [bnsel])
                for br in (2, 1):
                    if br == 1 and i >= 8:
                        for gg in range(2):
                            pn, bpn = bank("sc")
                            TR(pn[0:32, 0:128], nsel[:, gg, :], identf[:], [bnsel, bidf], [bpn])
                            CP("dve", nselT[0:32, gg, :], pn[0:32, 0:128], [bpn], [bnselT])
                    for gg in range(2):
                        lo = 0 if br == 1 else 64
                        j0 = 0 if br == 1 else max(0, i - 4)
                        acc, bacc = bank("acc")
                        prev_ = None
                        for j in range(j0, i + 1):
                            sc, bsc_ = bank("sc")
                            extra = []
                            if j == i:
                                extra.append((identb[:], biasDS[:, 0, gg, :], [bidb, bbias]))
                            if j == i - 1:
                                extra.append((identb[:], biasDS[:, 1, gg, :], [bidb, bbias]))
                            if br == 2 and j == i - 4:
                                extra.append((identb[:], bcast(masks[:, 0, :], [128, 4, 128], 1), [bidb, bmasks]))
                            if br == 1 and i >= 8:
                                extra.append((emat[:, j * 128:(j + 1) * 128], bcast(nselT[:, gg, :], [64, 4, 128], 1), [bemat, bnselT]))
                            MM(sc[:, :], kT[lo:lo + 64, gg, j * 128:(j + 1) * 128], qT2[lo:lo + 64, gg * 4:(gg + 1) * 4, :].rearrange("p h q -> p (h q)"),
                               start=True, stop=(len(extra) == 0), r=[bkT[j], bqT2], w=[bsc_], inc=(len(extra) == 0))
                            for ei, (l_, r_, bb_) in enumerate(extra):
                                lastx = ei == len(extra) - 1
                                MM(sc[:, :].rearrange("p (h q) -> p h q", h=4) if len(r_.shape) == 3 else sc[:, :], l_, r_, start=False, stop=lastx,
                                   r=bb_, w=[bsc_], inc=lastx)
                            Pt_ = PT[pti[0] % NPT]; bP = bPT[pti[0] % NPT]; pti[0] += 1
                            ACT(Pt_[:, :], sc[:, :], AF.Exp, [bsc_], [bP], scale=0.125)
                            if prev_ is not None:
                                pv(acc, bacc, prev_[0], prev_[1], Vc[:, prev_[2], br - 1, gg, :], [bVc[prev_[2]]], prev_[2] == j0, False)
                            prev_ = (Pt_, bP, j)
                            yield
                        pv(acc, bacc, prev_[0], prev_[1], Vc[:, prev_[2], br - 1, gg, :], [bVc[prev_[2]]], prev_[2] == j0, True)
                        finish(acc, bacc, br, gg)
                        yield
                dump(f"ynsa_{T}", ynsa[:], [bynsa])
                chk(3)

                yield
            def rwkv_gen():
                yield
                for c3 in range(0, 13, 3):
                    ps_, bps_ = bank("rw")
                    ncs = min(3, 13 - c3)
                    for cc in range(ncs):
                        c = c3 + cc
                        for kc in range(8):
                            MM(ps_[:, cc * 129:(cc + 1) * 129], Wres[:, kc, CS + c * 128:CS + (c + 1) * 128], hcur[:, kc, 0:129],
                               start=(kc == 0), stop=(kc == 7), r=[bh, bWres], w=[bps_], inc=(kc == 7))
                    for cc in range(ncs):
                        c = c3 + cc
                        if c < 4:
                            dst, bd = rT[:, c, :], brT
                        elif c < 8:
                            dst, bd = kTr[:, c - 4, :], bkTr
                        elif c < 12:
                            dst, bd = vT[:, c - 8, :], bvT
                        else:
                            dst, bd = wdad[:, :], bwdad
                        ACT(dst, ps_[:, cc * 129 + 1:cc * 129 + 129], AF.Identity, [bps_, bmu], [bd], scale=mucol[:, 1, c:c + 1])
                        STT(dst, ps_[:, cc * 129:cc * 129 + 128], mucol[:, 0, c:c + 1], dst, ALU.mult, ALU.add, [bps_, bmu, bd], [bd])
                    yield
                yield
                dump(f"rT_{T}", rT[:], [brT])
                yield
                dump(f"wdad_{T}", wdad[:], [bwdad])
                yield
                ACT(wdadb[0:64, :], wdad[0:64, :], AF.Tanh, [bwdad], [bwdadb])
                yield
                CP("act", wdadb[64:128, :], wdad[64:128, :], [bwdad], [bwdadb])
                yield
                pz, bpz = bank("rw")
                yield
                pa, bpa = bank("rw")
                yield
                for c in range(4):
                    MM(pz[:, c * 128:(c + 1) * 128], Lora[0:64, c * 128:(c + 1) * 128], wdadb[0:64, :], start=True, stop=False, r=[bLora, bwdadb], w=[bpz], inc=False)
                    MM(pz[:, c * 128:(c + 1) * 128], w0a0[0:64, c * 128:(c + 1) * 128], ones_f[0:64, :], start=False, stop=True, r=[bw0a0, bones], w=[bpz], inc=(c == 3))
                yield
                for c in range(4):
                    MM(pa[:, c * 128:(c + 1) * 128], Lora[64:128, c * 128:(c + 1) * 128], wdadb[64:128, :], start=True, stop=False, r=[bLora, bwdadb], w=[bpa], inc=False)
                    MM(pa[:, c * 128:(c + 1) * 128], w0a0[64:128, c * 128:(c + 1) * 128], ones_f[64:128, :], start=False, stop=True, r=[bw0a0, bones], w=[bpa], inc=(c == 3))
                yield
                f4 = lambda t: t[:].rearrange("p c t -> p (c t)")
                yield
                ACT(f4(lwT), pz[:, :], AF.Tanh, [bpz], [blw], scale=0.5)
                yield
                cexp = math.exp(-0.5) * 0.5
                yield
                TS("dve", f4(lwT), f4(lwT), -cexp, -cexp, ALU.mult, ALU.add, [blw], [blw])
                yield
                ACT(f4(asg), pa[:, :], AF.Tanh, [bpa], [basg], scale=0.5)
                yield
                TS("dve", f4(asg), f4(asg), 0.5, 0.5, ALU.mult, ALU.add, [basg], [basg])
                yield
                for c in range(4):
                    SCAN(LT[:, c, :], ones_f[:, :], lwT[:, c, :], [bones, blw], [bLT])
                yield
                TT("dve", f4(tmpA), f4(LT), f4(lwT), ALU.subtract, [bLT, blw], [btmpA])
                yield
                ACT(f4(e1), f4(tmpA), AF.Exp, [btmpA], [be1])
                yield
                ACT(f4(e2), f4(LT), AF.Exp, [bLT], [be2], scale=-1.0)
                yield
                ACT(f4(e3), f4(LT), AF.Exp, [bLT], [be3])
                yield
                ACT(eLC[:, :], LT[:, :, 127], AF.Exp, [bLT], [beLC])
                yield
                yield
                for c in range(4):
                    TS("dve", kkn[:, c, :], kTr[:, c, :], vec4[:, 0, c:c + 1], None, ALU.mult, None, [bkTr, bvec4], [bkkn])
                yield
                ACT(f4(tmpA), f4(kkn), AF.Square, [bkkn], [btmpA])
                yield
                for _ in range(4):
                    yield
                pk_, bpk_ = bank("rw")
                yield
                for c in range(4):
                    MM(pk_[:, c * 2:c * 2 + 2], tmpA[:, c, :], ind2[:, :], r=[btmpA, bind2], w=[bpk_], inc=(c == 3))
                yield
                TS("dve", rn8[:, :], pk_[:, 0:8], 1e-24, None, ALU.max, None, [bpk_], [brn8])
                yield
                TT("pool", rn8[:, :], rn8[:, :], cneg[:, 0:8], ALU.pow, [brn8, bcneg], [brn8])
                yield
                for _ in range(6):
                    yield
                TR(pk_[0:8, 128:256], rn8[:, :], identf[:], [brn8, bidf], [bpk_])
                yield
                CP("dve", rn8T[:, :], pk_[0:8, 128:256], [bpk_], [brn8T])
                yield
                pr_, bpr_ = bank("rw")
                yield
                for c in range(4):
                    MM(pr_[:, c * 128:(c + 1) * 128], indT[:, c, :], rn8T[:, :], r=[bindT, brn8T], w=[bpr_], inc=(c == 3))
                yield
                TT("dve", f4(kkn), f4(kkn), pr_[:, :], ALU.mult, [bkkn, bpr_], [bkkn])
                yield
                dump(f"kkn_{T}", kkn[:], [bkkn])
                yield
                yield
                for c in range(4):
                    TS("dve", tmpA[:, c, :], asg[:, c, :], -1.0, vec4[:, 1, c:c + 1], ALU.add, ALU.mult, [basg, bvec4], [btmpA])
                yield
                STT(f4(kmod), f4(tmpA), 1.0, f4(kTr), ALU.add, ALU.mult, [btmpA, bkTr], [bkmod])
                yield
                dump(f"kmod_{T}", kmod[:], [bkmod])
                yield
                yield
                for c in range(4):
                    STT(tmpA[:, c, :], rT[:, c, :], vec4[:, 2, c:c + 1], kmod[:, c, :], ALU.mult, ALU.mult, [brT, bvec4, bkmod], [btmpA])
                yield
                for c in range(4):
                    MM(pk_[:, 256 + c * 2:256 + c * 2 + 2], tmpA[:, c, :], ind2[:, :], r=[btmpA, bind2], w=[bpk_], inc=(c == 3))
                yield
                CP("dve", sbon[:, :], pk_[:, 256:264], [bpk_], [bsbon])
                yield
                yield
                STT(f4(At), f4(kkn), -1.0, f4(e1), ALU.mult, ALU.mult, [bkkn, be1], [bAt])
                yield
                TT("dve", f4(tmpA), f4(kkn), f4(asg), ALU.mult, [bkkn, basg], [btmpA])
                yield
                TT("dve", f4(tmpA), f4(tmpA), f4(e2), ALU.mult, [btmpA, be2], [btmpA])
                yield
                CP("act", f4(Bt), f4(tmpA), [btmpA], [bBt])
                yield
                TT("dve", f4(e1), f4(kmod), f4(e2), ALU.mult, [bkmod, be2], [be1])
                yield
                CP("act", f4(Kt), f4(e1), [be1], [bKt])
                yield
                TT("dve", f4(Rt), f4(rT), f4(e3), ALU.mult, [brT, be3], [bRt])
                yield
                yield
                for c in range(4):
                    TS("dve", tmpA[:, c, :], tmpA[:, c, :], eLC[:, c:c + 1], None, ALU.mult, None, [btmpA, beLC], [btmpA])
                    ACT(e1[:, c, :], e1[:, c, :], AF.Identity, [be1, beLC], [be1], scale=eLC[:, c:c + 1])
                yield
                for _ in range(6):
                    yield
                for (src, bsrc, dst, bdst) in ((tmpA, btmpA, Bh, bBh), (e1, be1, Kh, bKh), (vT, bvT, Vt, bVt)):
                    pp, bp = bank("rw")
                    for c in range(4):
                        TR(pp[:, c * 128:(c + 1) * 128], src[:, c, :], identf[:], [bsrc, bidf], [bp], inc=(c == 3))
                    CP("act", dst[:, :], pp[:, :], [bp], [bdst])
                yield
                chk(4)
                yield
                yield
                def hs(t, h):
                    return t[(h % 2) * 64:(h % 2) * 64 + 64, h // 2, :]
                yield

                def five(lt, blt, rt, brt, mask_i, dst, bdst):
                    for par in range(2):
                        pp, bp = bank("rw")
                        for hh in range(4):
                            h = hh * 2 + par
                            MM(pp[:, hh * 128:(hh + 1) * 128], hs(lt, h), hs(rt, h), r=[blt, brt], w=[bp], inc=(hh == 3))
                        TT("dve", dst[:, par:8:2, :], pp[:, :].rearrange("p (h t) -> p h t", h=4), bcast(masks[:, mask_i, :], [128, 4, 128], 1), ALU.mult,
                           [bp, bmasks], [bdst])
                yield

                five(Bt, bBt, At, bAt, 1, Qm[0], bQm[0])
                yield
                five(At, bAt, Bt, bBt, 3, QmT[0], bQmT[0])
                yield
                five(Kt, bKt, At, bAt, 1, AakT, bAak)
                yield
                five(Bt, bBt, Rt, bRt, 2, MrbT, bMrb)
                yield
                five(Kt, bKt, Rt, bRt, 2, MrkT, bMrk)
                yield
                yield
                TT("dve", Xm[0][:], Qm[0][:], bcast(identb[:], [128, 8, 128], 1), ALU.add, [bQm[0], bidb], [bXm[0]])
                yield
                cur = 0
                yield
                for lvl in range(1, 7):
                    nxt = cur ^ 1
                    lastl = lvl == 6
                    for half in range(2):
                        hsl = slice(half * 4, half * 4 + 4)
                        pT_, bpT_ = bank("rw")
                        for hh in range(4):
                            h = half * 4 + hh
                            MM(pT_[:, hh * 128:(hh + 1) * 128], Qm[cur][:, h, :], QmT[cur][:, h, :], r=[bQm[cur], bQmT[cur]], w=[bpT_], inc=(hh == 3))
                        CP("act", QmT[nxt][:, hsl, :], pT_[:, :].rearrange("p (h t) -> p h t", h=4), [bpT_], [bQmT[nxt]])
                        if not lastl:
                            pQ_, bpQ_ = bank("rw")
                            for hh in range(4):
                                h = half * 4 + hh
                                MM(pQ_[:, hh * 128:(hh + 1) * 128], QmT[cur][:, h, :], Qm[cur][:, h, :], r=[bQm[cur], bQmT[cur]], w=[bpQ_], inc=(hh == 3))
                            CP("dve", Qm[nxt][:, hsl, :], pQ_[:, :].rearrange("p (h t) -> p h t", h=4), [bpQ_], [bQm[nxt]])
                        yield
                    for half in range(2):
                        hsl = slice(half * 4, half * 4 + 4)
                        pX_, bpX_ = bank("rw")
                        for hh in range(4):
                            h = half * 4 + hh
                            MM(pX_[:, hh * 128:(hh + 1) * 128], QmT[nxt][:, h, :], Xm[cur][:, h, :], r=[bQmT[nxt], bXm[cur]], w=[bpX_], inc=(hh == 3))
                        TT("dve", Xm[nxt][:, hsl, :], pX_[:, :].rearrange("p (h t) -> p h t", h=4), Xm[cur][:, hsl, :], ALU.add, [bpX_, bXm[cur]], [bXm[nxt]])
                        yield
                    cur = nxt
                yield
                Xf = Xm[cur]; bXf = bXm[cur]
                yield
                chk(5)
                yield
                yield
                if i == 0:
                    MSET("pool", Pst[:], 0.0, [bPst])
                    MSET("pool", Pb[:], 0.0, [bPb])
                yield

                def ph_(t, h):
                    return t[(h % 2) * 64:(h % 2) * 64 + 64, h // 2, :]
                yield

                def vh(t, h):
                    return t[:, h * 64:(h + 1) * 64]
                yield

                for par in range(2):
                    p1, bp1 = bank("rw")
                    for hh in range(4):
                        h = hh * 2 + par
                        MM(p1[:, hh * 64:(hh + 1) * 64], hs(At, h), ph_(Pb, h), start=True, stop=False, r=[bAt, bPb], w=[bp1], inc=False)
                        MM(p1[:, hh * 64:(hh + 1) * 64], AakT[:, h, :], vh(Vt, h), start=False, stop=True, r=[bAak, bVt], w=[bp1], inc=(hh == 3))
                    CP("act", rhs0[:, par:8:2, :], p1[:, 0:256].rearrange("p (h v) -> p h v", h=4), [bp1], [brhs0])
                yield
                chk(5.2)
                yield
                p2, bp2 = bank("rw")
                yield
                for h in range(8):
                    MM(p2[:, h * 64:(h + 1) * 64], Xf[:, h, :], rhs0[:, h, :], r=[bXf, brhs0], w=[bp2], inc=(h == 7))
                yield
                CP("act", Ub[:].rearrange("p h v -> p (h v)"), p2[:, :], [bp2], [bUb])
                yield
                chk(5.4)
                yield
                p3s = []
                yield
                for par in range(2):
                    p3, bp3 = bank("proj")
                    p3s.append((p3, bp3))
                    for hh in range(4):
                        h = hh * 2 + par
                        MM(p3[:, hh * 64:(hh + 1) * 64], hs(Rt, h), ph_(Pb, h), start=True, stop=False, r=[bRt, bPb], w=[bp3], inc=False)
                        MM(p3[:, hh * 64:(hh + 1) * 64], MrkT[:, h, :], vh(Vt, h), start=False, stop=False, r=[bMrk, bVt], w=[bp3], inc=False)
                        MM(p3[:, hh * 64:(hh + 1) * 64], MrbT[:, h, :], Ub[:, h, :], start=False, stop=True, r=[bMrb, bUb], w=[bp3], inc=(hh == 3))
                yield
                chk(5.6)
                yield
                p4, bp4 = bank("rw")
                yield
                for h in range(8):
                    o_ = p4[(h % 2) * 64:(h % 2) * 64 + 64, (h // 2) * 64:(h // 2) * 64 + 64]
                    MM(o_, vh(Bh, h), Ub[:, h, :], start=True, stop=False, r=[bBh, bUb], w=[bp4], inc=False)
                    MM(o_, vh(Kh, h), vh(Vt, h), start=False, stop=True, r=[bKh, bVt], w=[bp4], inc=(h == 7))
                yield
                for c in range(4):
                    STT(Pst[:, c, :], Pst[:, c, :], eLC[:, c:c + 1], p4[:, c * 64:(c + 1) * 64], ALU.mult, ALU.add, [bPst, beLC, bp4], [bPst])
                yield
                CP("act", Pb[:].rearrange("p a b -> p (a b)"), Pst[:].rearrange("p a b -> p (a b)"), [bPst], [bPb])
                yield
                chk(5.8)
                yield
                yield
                yn3 = yn[:, :].rearrange("p (h d) -> p h d", h=8)
                yield
                for par in range(2):
                    p3, bp3 = p3s[par]
                    CP("act", yn3[:, par:8:2, :], p3[:, 0:256].rearrange("p (h d) -> p h d", h=4), [bp3], [byn])
                yield
                ACT(sq[:, 0:512], yn[:, :], AF.Square, [byn], [bsq])
                yield
                REDUCE(lnst[:, 0:8], yn3, [byn], [blnst])
                yield
                REDUCE(lnst[:, 8:16], sq[:, 0:512].rearrange("p (h d) -> p h d", h=8), [bsq], [blnst])
                yield
                chk(5.85)
                yield
                TS("dve", lnst[:, 0:16], lnst[:, 0:16], 1.0 / 64, None, ALU.mult, None, [blnst], [blnst])
                yield
                TT("dve", lnst[:, 16:24], lnst[:, 0:8], lnst[:, 0:8], ALU.mult, [blnst], [blnst])
                yield
                TT("dve", lnst[:, 16:24], lnst[:, 8:16], lnst[:, 16:24], ALU.subtract, [blnst], [blnst])
                yield
                TS("dve", lnst[:, 16:24], lnst[:, 16:24], 0.0, 64e-5, ALU.max, ALU.add, [blnst], [blnst])
                yield
                TT("pool", lnst[:, 24:32], lnst[:, 16:24], cneg[:, 0:8], ALU.pow, [blnst, bcneg], [blnst])
                yield
                STT(lnst[:, 16:24], lnst[:, 0:8], -1.0, lnst[:, 24:32], ALU.mult, ALU.mult, [blnst], [blnst])
                yield
                chk(5.9)
                yield
                for h in range(8):
                    ACT(yn[:, h * 64:(h + 1) * 64], yn[:, h * 64:(h + 1) * 64], AF.Identity, [byn, blnst], [byn],
                        bias=lnst[:, 16 + h:17 + h], scale=lnst[:, 24 + h:25 + h])
                yield
                chk(5.95)
                yield
                TT("dve", yn[:, :], yn[:, :], ln_w_bc[:, :], ALU.mult, [byn, blnw], [byn])
                yield
                chk(5.97)
                yield
                for h in range(8):
                    STT(yn[:, h * 64:(h + 1) * 64], Vt[:, h * 64:(h + 1) * 64], sbon[:, h:h + 1], yn[:, h * 64:(h + 1) * 64], ALU.mult, ALU.add,
                        [bVt, bsbon, byn], [byn])
                yield

            na_ = 2 * (2 + (i + 1) + min(5, i + 1)) + 2
            nr_ = 150
            ga_, gr_ = attn_gen(), rwkv_gen()
            da_ = dr_ = 0
            alive_a = alive_r = True
            while alive_a or alive_r:
                pick_a = alive_a and (not alive_r or da_ * nr_ <= dr_ * na_)
                if pick_a:
                    try:
                        next(ga_); da_ += 1
                    except StopIteration:
                        alive_a = False
                else:
                    try:
                        next(gr_); dr_ += 1
                    except StopIteration:
                        alive_r = False
            yield "mid_done"
            chk(6)
            yield "t"
            def wsload(c):
                k = ws_i[0] % NWS
                ws_i[0] += 1
                DMA(WS[k][:].rearrange("p k n -> p (k n)"), wrest_s[c], sem_ws[k], w=[bWS[k]])
                return WS[k], bWS[k]

            def rest_chunk(c):
                pp, bp = bank("tl")
                for sub in range(2):
                    W_, bW_ = wsload(2 * c + sub)
                    for kc in range(8):
                        MM(pp[:, sub * 128:(sub + 1) * 128], W_[:, kc, :], hcur[:, kc, 1:129], start=(kc == 0), stop=(kc == 7),
                           r=[bW_, bh], w=[bp], inc=(kc == 7 and sub == 1))
                return pp, bp

            for c in range(12):
                pp, bp = rest_chunk(c)
                if c < 4:
                    dst, bd = (silA, bsilA) if c < 2 else (silB, bsilB)
                    dv = dst[:, (c % 2) * 2:(c % 2) * 2 + 2, :].rearrange("p a t -> p (a t)")
                    ACT(dv, pp[:, 0:256], AF.Tanh, [bp], [bd], scale=0.5)
                    STT(dv, dv, 1.0, pp[:, 0:256], ALU.add, ALU.mult, [bd, bp], [bd])
                else:
                    dst, bd = (thA, bthA) if c < 8 else (thB, bthB)
                    cc = (c - 4) % 4
                    ACT(dst[:, cc * 2:cc * 2 + 2, :].rearrange("p a t -> p (a t)"), pp[:, 0:256], AF.Tanh, [bp], [bd], scale=0.5)
                yield "t"
            yield "t"
            for (src, bsrc, sil, bsil, dst, bdst, lnb) in ((ynsa, bynsa, silA, bsilA, yaT, byaT, False), (yn, byn, silB, bsilB, ybT, bybT, True)):
                pp, bp = bank("tl")
                for c in range(4):
                    TR(pp[:, c * 128:(c + 1) * 128], src[:, c * 128:(c + 1) * 128], identf[:], [bsrc, bidf], [bp], inc=(c == 3))
                if not lnb:
                    STT(dst[:].rearrange("p c t -> p (c t)"), pp[:, :], 0.5, sil[:].rearrange("p c t -> p (c t)"), ALU.mult, ALU.mult, [bp, bsil], [bdst])
                else:
                    for c in range(4):
                        STT(tmpA[:, c, :], pp[:, c * 128:(c + 1) * 128], vec4[:, 3, c:c + 1], sil[:, c, :], ALU.add, ALU.mult, [bp, bvec4, bsil], [btmpA])
                    ACT(dst[:].rearrange("p c t -> p (c t)"), f4(tmpA), AF.Copy, [btmpA], [bdst], scale=0.5)
            yield "t"
            dump(f"yaT_{T}", yaT[:], [byaT], BF16)
            yield "t"
            dump(f"ybT_{T}", ybT[:], [bybT], BF16)
            yield "t"
            for (yT_, byT_, W_, bW_, th, bth, mg, bmg) in ((yaT, byaT, Wouta, bWouta, thA, bthA, mg1, bmg1), (ybT, bybT, Woutb, bWoutb, thB, bthB, mg2, bmg2)):
                for half in range(2):
                    pp, bp = bank("tl")
                    for mm_ in range(4):
                        mc = half * 4 + mm_
                        for kc in range(4):
                            MM(pp[:, mm_ * 128:(mm_ + 1) * 128], W_[:, kc, mc * 128:(mc + 1) * 128], yT_[:, kc, :], start=(kc == 0), stop=(kc == 3),
                               r=[bW_, byT_], w=[bp], inc=(kc == 3 and mm_ == 3))
                    STT(mg[:, half * 4:(half + 1) * 4, :].rearrange("p a t -> p (a t)"), th[:, half * 4:(half + 1) * 4, :].rearrange("p a t -> p (a t)"), 1.0, pp[:, :],
                        ALU.add, ALU.mult, [bth, bp], [bmg])
                    yield "t"
            yield "t"
            TT("dve", mgT[:].rearrange("p a t -> p (a t)"), mg1[:].rearrange("p a t -> p (a t)"), mg2[:].rearrange("p a t -> p (a t)"), ALU.add, [bmg1, bmg2], [bmgT])
            yield "t"
            dump(f"mgT_{T}", mgT[:], [bmgT], BF16)
            yield "t"
            if s == 1 and i == 0:
                DMA(Wog[:].rearrange("p k n -> p (k n)"), wog_s, sem_wog, w=[bWog])
            yield "t"
            for half in range(2):
                pp, bp = bank("tl")
                for kc in range(8):
                    MM(pp[:, :], mgT[:, kc, :], Wog[:, kc, half * 512:(half + 1) * 512], start=(kc == 0), stop=(kc == 7), r=[bmgT, bWog], w=[bp], inc=(kc == 7))
                TT("dve", x_t[:, half * 512:(half + 1) * 512], pp[:, :], x_t[:, half * 512:(half + 1) * 512], ALU.add, [bp, bx], [bx])
                yield "t"
            yield "t"
            return DMA(out_d[tok0:tok0 + 128, :], x_t[:, :], sem_outs[T % 2], r=[bx], w=[])

        out_toks = []
        total = nseq * ntile
        seq_tiles = [(s, i) for s in range(nseq) for i in range(ntile)]

        def make_gen(n_):
            s_, i_ = seq_tiles[n_]
            T_ = s_ * 16 + i_
            if i_ == 0:
                MSET("pool", kcT[:].rearrange("p a b -> p (a b)"), 0.0, [bkcT])
                MSET("pool", vcT[:].rearrange("p a b -> p (a b)"), 0.0, [bvcT])
                MSET("pool", vca[:, :, 0:64], 0.0, [bvca])
                MSET("pool", kvc[:].rearrange("p a b -> p (a b)"), 0.0, [bkvc])
            return tile_body2(s_, i_)

        def xload(n_):
            if n_ < total:
                s2, i2 = seq_tiles[n_]
                T2 = s2 * 16 + i2
                DMA(xt[T2 % 2][:], x_d[T2 * 128:(T2 + 1) * 128, :], sem_x[T2 % 2], w=[bxt[T2 % 2]])

        def step(g, until):
            while True:
                try:
                    m_ = next(g)
                except StopIteration as e_:
                    return None, True, e_.value
                if m_ in until:
                    return m_, False, None

        def early(n_):
            s_, i_ = seq_tiles[n_]
            T_ = s_ * 16 + i_
            return DMA(out_d[T_ * 128:T_ * 128 + 128, :], xs[:, :], sem_outs[T_ % 2], r=[bxs], w=[])

        if total > 0:
            xload(0)
            xload(1)
            if stage < 9:
                for n_ in range(total):
                    g = make_gen(n_)
                    try:
                        _, _, val = step(g, ())
                        out_toks.append(val)
                    except _Stop:
                        out_toks.append(early(n_))
                    xload(n_ + 2)
            else:
                cur = make_gen(0)
                step(cur, ("front_done",))
                for n_ in range(total):
                    step(cur, ("mid_done",))
                    nxt = make_gen(n_ + 1) if n_ + 1 < total else None
                    cur_done = False
                    nxt_done = nxt is None
                    while not (cur_done and nxt_done):
                        if not cur_done:
                            m_, fin, val = step(cur, ("t",))
                            if fin:
                                cur_done = True
                                out_toks.append(val)
                        if not nxt_done:
                            m_, fin, val = step(nxt, ("f", "front_done"))
                            if m_ == "front_done":
                                nxt_done = True
                    xload(n_ + 2)
                    cur = nxt
        S.wait_all("sp", out_toks[-4:] + dbg_outs + [(sm_, S.dcnt[sm_]) for sm_ in sem_outs])
        S.emit()
    return nc


_CACHE = {}


def kernel(**inputs):
    sh, per = host_prep(inputs)
    if "nc" not in _CACHE:
        _CACHE["nc"] = build()
    nc = _CACHE["nc"]
    in_maps = []
    for core in range(8):
        d = dict(sh)
        d.update(per[core])
        in_maps.append(d)
    res = run_bass_kernel_spmd(nc, in_maps, core_ids=list(range(8)))
    out = np.concatenate([np.asarray(r["out"]).reshape(2, 2048, 1024) for r in res.results], axis=0)
    return out.astype(np.float32)
```

```python
import math
import numpy as np
import concourse.bass as bass
import concourse.mybir as mybir
from concourse.bass_utils import run_bass_kernel_spmd
from contextlib import ExitStack

F32 = mybir.dt.float32
BF16 = mybir.dt.bfloat16
AF = mybir.ActivationFunctionType
ALU = mybir.AluOpType
AX = mybir.AxisListType

COMPUTE = ("pe", "act", "dve", "pool")
NEGM = -4096.0
NRES = 2968
CQ, CKV, CG, CC, CS = 0, 512, 1024, 1048, 1304


class Buf:
    __slots__ = ("w", "r")

    def __init__(self):
        self.w = None
        self.r = {}


class Sched:
    ANNOTATE = False

    def __init__(self, nc, es):
        self.nc = nc
        self.es = es
        self.prog = {e: [] for e in COMPUTE + ("sp",)}
        self.cnt = {e: 0 for e in COMPUTE}
        self.sems = {}
        for e in COMPUTE:
            self.sems[e] = es.enter_context(nc.semaphore("sem_" + e))
        self.known = {e: {} for e in self.prog}
        self.snap = {}
        self.dcnt = {}
        self.pending = {e: False for e in COMPUTE}
        self.last = {}

    def dma_sem(self, name):
        self.sems[name] = self.es.enter_context(self.nc.semaphore("sem_" + name))
        self.dcnt[name] = 0
        return name

    @staticmethod
    def _flat(bs):
        out = []
        for b in bs:
            if isinstance(b, (list, tuple)):
                out.extend(Sched._flat(b))
            else:
                out.append(b)
        return out

    def op(self, eng, fn, reads=(), writes=(), inc=True, dsem=None):
        reads = self._flat(reads)
        writes = self._flat(writes)
        need = {}

        def req(tok, same_ok):
            if tok is None:
                return
            k, v = tok
            if same_ok and k == eng and eng == "pe":
                return
            if need.get(k, 0) < v:
                need[k] = v

        for b in reads:
            req(b.w, False)
        for b in writes:
            req(b.w, True)
            for k, v in b.r.items():
                req((k, v), True)
        kn = self.known[eng]
        waits = []
        for k, v in need.items():
            if kn.get(k, 0) < v:
                waits.append((k, v))
                kn[k] = v
                sn = self.snap.get((k, v))
                if sn is not None:
                    for k2, v2 in sn.items():
                        if kn.get(k2, 0) < v2:
                            kn[k2] = v2
        if dsem is not None:
            self.dcnt[dsem] += 16
            tok = (dsem, self.dcnt[dsem])
            incspec = (dsem, 16)
        elif inc:
            self.cnt[eng] += 1
            tok = (eng, self.cnt[eng])
            incspec = (eng, 1)
            self.pending[eng] = False
            self.snap[tok] = dict(kn)
        else:
            tok = (eng, self.cnt[eng] + 1)
            incspec = None
            self.pending[eng] = True
        self.last[tok[0]] = tok[1]
        for b in writes:
            b.w = tok
            b.r = {}
        for b in reads:
            if b.w is tok:
                continue
            if b.r.get(tok[0], 0) < tok[1]:
                b.r[tok[0]] = tok[1]
        note = None
        if Sched.ANNOTATE:
            import sys as _sys
            f_ = _sys._getframe(1)
            while f_ is not None and f_.f_code.co_name not in ("tile_body2", "attn_gen", "rwkv_gen", "build", "finish", "pv", "five", "rest_chunk", "wsload"):
                f_ = f_.f_back
            note = f"L{f_.f_lineno}" if f_ is not None else None
        self.prog[eng].append((waits, fn, incspec, note))
        return tok

    def wait_all(self, eng, toks):
        kn = self.known[eng]
        waits = []
        mx = {}
        for k, v in toks:
            if mx.get(k, 0) < v:
                mx[k] = v
        for k, v in mx.items():
            if kn.get(k, 0) < v:
                waits.append((k, v))
                kn[k] = v
        self.prog[eng].append((waits, None, None, None))

    def barrier(self):
        for e in COMPUTE:
            if self.pending[e]:
                self.op(e, lambda en: en.nop(), (), ())
        toks = list(self.last.items())
        for e in self.prog:
            self.wait_all(e, toks)

    def emit(self):
        nc = self.nc
        for e in COMPUTE:
            if self.pending[e]:
                self.op(e, lambda en: en.nop(), (), ())
        sems = self.sems
        prog = self.prog

        def run(engname):
            def f(e):
                for waits, fn, incspec, note in prog[engname]:
                    for k, v in waits:
                        e.wait_ge(sems[k], v)
                    if fn is None:
                        continue
                    ins = fn(e)
                    if note is not None:
                        ins.annotate(note)
                    if incspec is not None:
                        ins.then_inc(sems[incspec[0]], incspec[1])
            return f

        with nc.Block() as block:
            block.sync(run("sp"))
            block.tensor(run("pe"))
            block.scalar(run("act"))
            block.vector(run("dve"))
            block.gpsimd(run("pool"))


def _t5_bucket(dist):
    n = np.maximum(dist, 0)
    nf = np.maximum(n, 16).astype(np.float32)
    large = 16 + (np.log(nf / np.float32(16)) / np.float32(math.log(128 / 16)) * np.float32(16)).astype(np.int32)
    return np.where(n < 16, n, np.minimum(large, 31))


def _perms():
    r = lambda a, b: list(range(a, b))
    res = (r(0, 512)
           + r(768, 832) + r(1024, 1088) + r(832, 896) + r(1088, 1152) + r(896, 1024) + r(1152, 1280)
           + r(1280, 1304)
           + r(512, 576) + r(640, 704) + r(576, 640) + r(704, 768)
           + r(1816, 3480))
    rest = r(1304, 1816) + r(3480, 3992) + r(3992, 5016) + r(5016, 6040)
    assert len(res) == NRES and len(rest) == 3072
    return np.array(res), np.array(rest)


def host_prep(inp):
    f = lambda k: np.ascontiguousarray(np.asarray(inp[k], dtype=np.float32))
    sh = {}
    pres, prest = _perms()
    w_in = f("w_in")[0]
    sh["w_res"] = np.ascontiguousarray(w_in[:, pres])
    sh["w_rest"] = np.ascontiguousarray(w_in[:, prest])
    sh["w_ada"] = f("w_ada")[0]
    sh["w_out_a"] = f("w_out_a")[0]
    sh["w_out_b"] = f("w_out_b")[0]
    sh["w_o"] = f("w_o")[0]
    sh["w1k"] = f("cmp_k_w1")[0]
    sh["w1v"] = f("cmp_v_w1")[0]
    col = lambda v, n: np.ascontiguousarray(v.reshape(n, 128).T)
    sh["b_ada"] = col(f("b_ada")[0], 24)
    sh["g_norm"] = col(f("norm_gain")[0], 8)
    sh["mu"] = col(f("shift_mu")[0], 13)
    vec4 = np.stack([col(f(k)[0].reshape(-1), 4) for k in ("k_k", "k_a", "r_k", "ln_x_b")], 1)
    sh["vec4"] = np.ascontiguousarray(vec4)
    rep = lambda v: np.ascontiguousarray(np.broadcast_to(v[None, :], (128, v.shape[0])))
    kng = f("k_norm_gain")[0]
    sh["bc_small"] = np.concatenate([rep(f("q_norm_gain")[0]), rep(kng[1]), rep(kng[2])], 1)
    sh["ln_w_bc"] = rep(f("ln_x_w")[0])
    sh["kgc"] = np.ascontiguousarray(kng[0].reshape(64, 1))
    sh["w0a0"] = np.ascontiguousarray(np.stack([f("w0")[0], f("a0")[0]], 0))
    sh["lora"] = np.ascontiguousarray(np.concatenate([f("w_lora_up")[0], f("a_lora_up")[0]], 0))
    w2 = lambda k: f(k)[0].reshape(2, 128, 64).transpose(1, 0, 2)
    sh["w2"] = np.ascontiguousarray(np.stack([w2("cmp_k_w2"), w2("cmp_v_w2")], 1))
    sh["peT"] = np.ascontiguousarray(np.concatenate([f("cmp_pos_k")[0].T, f("cmp_pos_v")[0].T], 0))
    tbl = f("rel_bias")
    k = np.arange(128)[:, None]
    q = np.arange(128)[None, :]
    tb = np.zeros((2, 2, 128, 4, 128), np.float32)
    for v, dist in enumerate((q - k, 128 + q - k)):
        bk = _t5_bucket(dist)
        for g in range(2):
            for h in range(4):
                tb[v, g, :, h, :] = tbl[bk, g * 4 + h]
    sh["tblDS"] = tb.reshape(2, 2, 128, 512)
    mk = np.zeros((128, 4, 128), np.float32)
    mk[np.broadcast_to(((q - k) < 0)[:, None, :], mk.shape)] = NEGM
    sh["maskD"] = mk.reshape(128, 512)
    c31 = np.zeros((2, 128, 4, 128), np.float32)
    for g in range(2):
        for h in range(4):
            c31[g, :, h, :] = tbl[31, g * 4 + h]
    sh["c31"] = c31.reshape(2, 128, 512)
    p = np.arange(16)[:, None]
    distc = q - 16 * p + 113
    bkc = _t5_bucket(distc)
    tc = np.zeros((2, 16, 4, 128), np.float32)
    for g in range(2):
        for h in range(4):
            tc[g, :, h, :] = tbl[bkc, g * 4 + h]
    sh["tblC"] = tc.reshape(2, 16, 512)
    mc = np.zeros((16, 4, 128), np.float32)
    mc[np.broadcast_to((distc < 0)[:, None, :], mc.shape)] = NEGM
    sh["maskC"] = mc.reshape(16, 512)
    sh["ident"] = np.eye(128, dtype=np.float32)
    far = np.where(k <= q, NEGM, 0.0).astype(np.float32)
    mus = (k < q).astype(np.float32)
    mui = (k <= q).astype(np.float32)
    mls = (k > q).astype(np.float32)
    sh["masks"] = np.ascontiguousarray(np.stack([far, mus, mui, mls], 1))
    z = np.zeros((16, 256), np.float32)
    z[np.arange(16), np.arange(16) + 119] = 1.0
    sh["zsh"] = z
    e = np.zeros((32, 2048), np.float32)
    e[np.arange(2048) // 64, np.arange(2048)] = -NEGM
    sh["emat"] = e
    mi = np.zeros((128, 32), np.float32)
    for j in range(32):
        for a in range(4):
            for b in range(2):
                n = 4 * j + a - b
                if 0 <= n < 127:
                    mi[n, j] += 1.0
    sh["mimp"] = mi
    ka = np.zeros((128, 8, 2, 32), np.float32)
    for i in range(8, 16):
        for qq in range(128):
            cur = (128 * i + qq) // 64
            for j in range(32):
                forced = (j == 0) or (j == cur) or (j == cur - 1)
                causal = j <= cur
                if forced:
                    ka[qq, i - 8, 0, j] = 0.0
                    ka[qq, i - 8, 1, j] = 1e30
                elif causal:
                    ka[qq, i - 8, 0, j] = 1.0
                else:
                    ka[qq, i - 8, 1, j] = -1e30
    sh["keepadd"] = ka.reshape(128, 512)
    ind2 = np.zeros((128, 2), np.float32)
    ind2[:64, 0] = 1.0
    ind2[64:, 1] = 1.0
    sh["ind2"] = ind2
    indT = np.zeros((8, 4, 128), np.float32)
    for h in range(8):
        indT[h, h // 2, (h % 2) * 64:(h % 2) * 64 + 64] = 1.0
    sh["indT"] = indT.reshape(8, 512)
    x = f("x")
    c = f("c")
    per = []
    for core in range(8):
        d = {"x": np.ascontiguousarray(x[2 * core:2 * core + 2].reshape(4096, 1024)),
             "cT": np.ascontiguousarray(c[2 * core:2 * core + 2].reshape(2, 8, 128).transpose(2, 1, 0))}
        per.append(d)
    return sh, per


class _Stop(Exception):
    pass


def build(nseq=2, ntile=16, dbg=None, stage=9):
    nc = bass.Bass("TRN2", target_bir_lowering=False)
    dbg = dbg or {}
    di = lambda name, shape: nc.dram_tensor(name, shape, F32, kind="ExternalInput").ap()
    x_d = di("x", [4096, 1024])
    cT_d = di("cT", [128, 8, 2])
    w_res_d = di("w_res", [1024, NRES])
    w_rest_d = di("w_rest", [1024, 3072])
    w_ada_d = di("w_ada", [1024, 3072])
    w_out_a_d = di("w_out_a", [512, 1024])
    w_out_b_d = di("w_out_b", [512, 1024])
    w_o_d = di("w_o", [1024, 1024])
    w1k_d = di("w1k", [2048, 256])
    w1v_d = di("w1v", [2048, 256])
    b_ada_d = di("b_ada", [128, 24])
    g_norm_d = di("g_norm", [128, 8])
    mu_d = di("mu", [128, 13])
    vec4_d = di("vec4", [128, 4, 4])
    bc_small_d = di("bc_small", [128, 192])
    ln_w_bc_d = di("ln_w_bc", [128, 512])
    kgc_d = di("kgc", [64, 1])
    w0a0_d = di("w0a0", [2, 512])
    lora_d = di("lora", [128, 512])
    w2_d = di("w2", [128, 2, 2, 64])
    peT_d = di("peT", [128, 32])
    tblDS_d = di("tblDS", [2, 2, 128, 512])
    maskD_d = di("maskD", [128, 512])
    c31_d = di("c31", [2, 128, 512])
    tblC_d = di("tblC", [2, 16, 512])
    maskC_d = di("maskC", [16, 512])
    ident_d = di("ident", [128, 128])
    masks_d = di("masks", [128, 4, 128])
    zsh_d = di("zsh", [16, 256])
    emat_d = di("emat", [32, 2048])
    mimp_d = di("mimp", [128, 32])
    keepadd_d = di("keepadd", [128, 512])
    ind2_d = di("ind2", [128, 2])
    indT_d = di("indT", [8, 512])
    out_d = nc.dram_tensor("out", [4096, 1024], F32, kind="ExternalOutput").ap()
    wrest_s = nc.dram_tensor("wrest_s", [24, 128, 1024], BF16, kind="Internal").ap()
    wog_s = nc.dram_tensor("wog_s", [128, 8192], BF16, kind="Internal").ap()

    with ExitStack() as es:
        S = Sched(nc, es)
        _n = [0]

        def sb(shape, dt, name=None):
            _n[0] += 1
            return es.enter_context(nc.sbuf_tensor("s_" + (name or f"sb{_n[0]}"), shape, dt))

        def psb(name):
            return es.enter_context(nc.psum_tensor(name, [128, 512], F32))

        dbg_outs = []

        def dump(name, ap, reads, dt=F32):
            if name not in dbg:
                return
            d = nc.dram_tensor("dbg_" + name, list(ap.shape), dt, kind="ExternalOutput").ap()
            dbg_outs.append(S.op("sp", lambda e: e.dma_start(out=d, in_=ap), reads, (), dsem=sem_dbg))

        def MM(out, lhsT, rhs, start=True, stop=True, r=(), w=(), inc=True, sgc=False):
            if sgc:
                return S.op("pe", lambda e: e.matmul(out, lhsT=lhsT, rhs=rhs, start=start, stop=stop, skip_group_check=True), r, w, inc=inc)
            return S.op("pe", lambda e: e.matmul(out, lhsT=lhsT, rhs=rhs, start=start, stop=stop), r, w, inc=inc)

        def TR(out, in_, ident, r=(), w=(), inc=True):
            return S.op("pe", lambda e: e.transpose(out=out, in_=in_, identity=ident), r, w, inc=inc)

        def ACT(out, in_, func, r=(), w=(), bias=None, scale=None, accum=None):
            kw = {}
            if bias is not None:
                kw["bias"] = bias
            if scale is not None:
                kw["scale"] = scale
            if accum is not None:
                kw["accum_out"] = accum
            return S.op("act", lambda e: e.activation(out=out, in_=in_, func=func, **kw), r, w)

        def TS(eng, out, in0, s1, s2, op0, op1=None, r=(), w=()):
            if op1 is None:
                return S.op(eng, lambda e: e.tensor_scalar(out=out, in0=in0, scalar1=s1, scalar2=None, op0=op0), r, w)
            return S.op(eng, lambda e: e.tensor_scalar(out=out, in0=in0, scalar1=s1, scalar2=s2, op0=op0, op1=op1), r, w)

        def TT(eng, out, in0, in1, op, r=(), w=()):
            return S.op(eng, lambda e: e.tensor_tensor(out=out, in0=in0, in1=in1, op=op), r, w)

        def STT(out, in0, scalar, in1, op0, op1, r=(), w=()):
            return S.op("dve", lambda e: e.scalar_tensor_tensor(out=out, in0=in0, scalar=scalar, in1=in1, op0=op0, op1=op1), r, w)

        def CP(eng, out, in_, r=(), w=()):
            if eng == "act":
                return S.op("act", lambda e: e.copy(out=out, in_=in_), r, w)
            return S.op(eng, lambda e: e.tensor_copy(out=out, in_=in_), r, w)

        def MSET(eng, ap, val, w=()):
            return S.op(eng, lambda e: e.memset(ap, val), (), w)

        def DMA(out, in_, sem, r=(), w=(), eng="sp"):
            return S.op(eng, lambda e: e.dma_start(out=out, in_=in_), r, w, dsem=sem)

        def bcast(ap, shape, axis):
            return ap.unsqueeze(axis).to_broadcast(shape)

        sem_dbg = S.dma_sem("dbg")
        sem_stg = [S.dma_sem("stg0"), S.dma_sem("stg1")]
        sem_scr = S.dma_sem("scr")
        sem_x = [S.dma_sem("x0"), S.dma_sem("x1")]
        sem_xr = S.dma_sem("xr")
        sem_ws = [S.dma_sem(f"ws{i}") for i in range(4)]
        sem_outs = [S.dma_sem("out0"), S.dma_sem("out1")]
        sem_wog = S.dma_sem("wog")

        PS = [psb(f"ps{i}") for i in range(8)]
        PSB = [Buf() for _ in range(8)]
        rot = {"proj": [0, 1], "sc": [2, 3], "acc": [4, 5], "rw": [6, 7], "tl": [4, 5, 6, 7]}
        rotc = {k: 0 for k in rot}

        def bank(cls):
            i = rot[cls][rotc[cls] % len(rot[cls])]
            rotc[cls] += 1
            return PS[i], PSB[i]

        NSLOT = 41
        AR = sb([128, NSLOT * 256], F32, "arena")
        SLB = [Buf() for _ in range(NSLOT)]

        def slot(start, shape, dt, P0=0):
            el = 4 if dt == F32 else 2
            n = int(np.prod(shape[1:]))
            nsl = (n * el + 1023) // 1024
            assert start + nsl <= NSLOT
            base = AR[:] if dt == F32 else AR[:].bitcast(BF16)
            o = start * 1024 // el
            ap = base[P0:P0 + shape[0], o:o + n]
            if len(shape) > 2:
                names = " ".join(f"d{i}" for i in range(len(shape) - 1))
                kw = {f"d{i}": shape[i + 1] for i in range(len(shape) - 1)}
                ap = ap.rearrange(f"p ({names}) -> p {names}", **kw)
            return ap, SLB[start:start + nsl]

        Wres = sb([128, 8, NRES], BF16, "Wres"); bWres = Buf()
        Wouta = sb([128, 4, 1024], BF16, "Wouta"); bWouta = Buf()
        Woutb = sb([128, 4, 1024], BF16, "Woutb"); bWoutb = Buf()
        Wog = sb([128, 8, 1024], BF16, "Wog"); bWog = Buf()
        W1c = sb([128, 32, 256], BF16, "W1c"); bW1c = Buf()
        W2c = sb([128, 2, 2, 64], BF16, "W2c"); bW2c = Buf()
        Lora = sb([128, 512], BF16, "Lora"); bLora = Buf()
        identf = sb([128, 128], F32, "identf"); bidf = Buf()
        identb = sb([128, 128], BF16, "identb"); bidb = Buf()
        masks = sb([128, 4, 128], BF16, "masks"); bmasks = Buf()
        biasDS = sb([128, 2, 2, 512], BF16, "biasDS"); bbias = Buf()
        emat = sb([64, 2048], BF16, "emat"); bemat = Buf()
        zsh = sb([128, 256], BF16, "zsh"); bzsh = Buf()
        biasC = sb([128, 2, 512], BF16, "biasC"); bbiasC = Buf()
        w0a0 = sb([128, 512], F32, "w0a0"); bw0a0 = Buf()
        bmisc = Buf()
        mimp = sb([128, 32], F32, "mimp"); bmimp = Buf()
        keepadd = sb([128, 8, 2, 32], F32, "keepadd"); bka = Buf()
        ind2 = sb([128, 2], F32, "ind2"); bind2 = Buf()
        indT = sb([8, 4, 128], F32, "indT"); bindT = Buf()
        ones_f = sb([128, 128], F32, "ones_f"); bones = Buf()
        bc_small = sb([128, 192], F32, "bc_small"); bbcs = Buf()
        ln_w_bc = sb([128, 512], F32, "ln_w_bc"); blnw = Buf()
        vec4 = sb([128, 4, 4], F32, "vec4"); bvec4 = Buf()
        mucol = sb([128, 2, 13], F32, "mucol"); bmu = Buf()
        kgc = sb([64, 1], F32, "kgc"); bkgc = Buf()
        gcol = sb([128, 8], F32, "gcol"); bgcol = Buf()
        badaT = sb([128, 24], F32, "badaT"); bbada = Buf()
        cTt = sb([128, 8, 2], F32, "cTt"); bcT = Buf()
        modT = sb([128, 24, 2], F32, "modT"); bmod = Buf()
        gsT = sb([128, 2, 8], F32, "gsT"); bgs = Buf()
        hb2 = sb([128, 2, 2], F32, "hb2"); bhb2 = Buf()
        cneg = sb([128, 16], F32, "cneg"); bcneg = Buf()
        peTb = sb([128, 32], BF16, "peTb"); bpeT = Buf()
        siluc = sb([128, 8, 2], F32, "siluc"); bsc = Buf()
        gtmp = sb([128, 16], F32, "gtmp"); bgtmp = Buf()

        stg = []; bstg = []
        for i_ in range(2):
            a_, b_ = slot(16 * i_, [128, 4096], F32)
            stg.append(a_); bstg.append(b_)
        kT = sb([128, 2, 2048], BF16, "kT"); bkT = [Buf() for _ in range(16)]
        Vcf = sb([128, 4160], BF16, "Vc"); bVc = [Buf() for _ in range(16)]
        Vc = Vcf[:].rearrange("p (a b c d) -> p a b c d", a=16, b=2, c=2)
        stgb = kT[:].rearrange("p a b -> p (a b)"); bstgb = bkT
        gate_bc = Vcf[:].bitcast(F32)[:, 0:2048].rearrange("p (s n) -> p s n", s=2); bgbc = bVc

        ldn = [0]
        sem_lds = [S.dma_sem(f"ld{i}") for i in range(8)]

        def ld(out, in_, w):
            sm = sem_lds[ldn[0] % 8]
            ldn[0] += 1
            if S.dcnt[sm] > 0:
                S.wait_all("sp", [(sm, S.dcnt[sm])])
            return DMA(out, in_, sm, w=w)

        ld(identf[:], ident_d, [bidf])
        CP("dve", identb[:], identf[:], [bidf], [bidb])
        ld(stg[0][:, 0:512].rearrange("p (a b) -> p a b", a=4), masks_d, [bstg[0]])
        CP("dve", masks[:], stg[0][:, 0:512].rearrange("p (a b) -> p a b", a=4), [bstg[0]], [bmasks])
        ld(mimp[:], mimp_d, [bmimp])
        ld(keepadd[:].rearrange("p a b c -> p (a b c)"), keepadd_d, [bka])
        ld(ind2[:], ind2_d, [bind2])
        ld(indT[:].rearrange("p a b -> p (a b)"), indT_d, [bindT])
        ld(bc_small[:], bc_small_d, [bbcs])
        ld(ln_w_bc[:], ln_w_bc_d, [blnw])
        ld(vec4[:], vec4_d, [bvec4])
        ld(mucol[:, 0, :], mu_d, [bmu])
        TS("dve", mucol[:, 1, :], mucol[:, 0, :], -1.0, 1.0, ALU.mult, ALU.add, [bmu], [bmu])
        ld(kgc[:], kgc_d, [bkgc])
        MSET("pool", w0a0[:], 0.0, [bw0a0])
        ld(w0a0[0:1, :], w0a0_d[0:1, :], [bw0a0])
        ld(w0a0[64:65, :], w0a0_d[1:2, :], [bw0a0])
        MSET("pool", emat[:], 0.0, [bemat])
        MSET("pool", zsh[:], 0.0, [bzsh])
        MSET("pool", biasC[:].rearrange("p a b -> p (a b)"), 0.0, [bbiasC])
        ld(gcol[:], g_norm_d, [bgcol])
        ld(badaT[:], b_ada_d, [bbada])
        ld(cTt[:], cT_d, [bcT])
        MSET("pool", ones_f[:], 1.0, [bones])
        MSET("pool", cneg[:], -0.5, [bcneg])
        ld(stg[1][0:16, 0:256], zsh_d, [bstg[1]])
        CP("dve", zsh[0:16, :], stg[1][0:16, 0:256], [bstg[1]], [bzsh])
        ld(stg[1][0:32, 0:2048], emat_d, [bstg[1]])
        CP("dve", emat[0:32, :], stg[1][0:32, 0:2048], [bstg[1]], [bemat])
        ld(stg[1][:, 2048:2560], lora_d, [bstg[1]])
        CP("dve", Lora[:], stg[1][:, 2048:2560], [bstg[1]], [bLora])
        ld(stg[1][:, 2560:2816].rearrange("p (a b c) -> p a b c", a=2, b=2), w2_d, [bstg[1]])
        CP("dve", W2c[:], stg[1][:, 2560:2816].rearrange("p (a b c) -> p a b c", a=2, b=2), [bstg[1]], [bW2c])
        ld(stg[1][:, 2816:2848], peT_d, [bstg[1]])
        CP("dve", peTb[:], stg[1][:, 2816:2848], [bstg[1]], [bpeT])
        for g in range(2):
            ld(stg[0][:, 0:512], c31_d[g], [bstg[0]])
            for v in range(2):
                ld(stg[1][:, 0:512], tblDS_d[v, g], [bstg[1]])
                TT("dve", stg[1][:, 0:512], stg[1][:, 0:512], stg[0][:, 0:512], ALU.subtract, [bstg[0], bstg[1]], [bstg[1]])
                if v == 0:
                    ld(stg[1][:, 512:1024], maskD_d, [bstg[1]])
                    STT(biasDS[:, v, g, :], stg[1][:, 0:512], 8.0, stg[1][:, 512:1024], ALU.mult, ALU.add, [bstg[1]], [bbias])
                else:
                    TS("dve", biasDS[:, v, g, :], stg[1][:, 0:512], 8.0, None, ALU.mult, None, [bstg[1]], [bbias])
            ld(stg[1][0:16, 0:512], tblC_d[g], [bstg[1]])
            ld(stg[1][0:16, 512:1024], maskC_d, [bstg[1]])
            TT("dve", stg[1][0:16, 0:512], stg[1][0:16, 0:512], stg[0][0:16, 0:512], ALU.subtract, [bstg[0], bstg[1]], [bstg[1]])
            STT(biasC[0:16, g, :], stg[1][0:16, 0:512], 8.0, stg[1][0:16, 512:1024], ALU.mult, ALU.add, [bstg[1]], [bbiasC])

        def stage_load(i, src_ap, ncols, nk=8):
            view = stg[i][:, 0:nk * ncols].rearrange("p (k n) -> p k n", k=nk)
            DMA(view, src_ap, sem_stg[i], w=[bstg[i]])
            return view

        si = 0
        for c0 in range(0, NRES, 512):
            n = min(512, NRES - c0)
            v = stage_load(si, w_res_d[:, c0:c0 + n].rearrange("(k p) n -> p k n", p=128), n)
            CP("dve" if si == 0 else "act", Wres[:, :, c0:c0 + n], v, [bstg[si]], [bWres])
            si ^= 1
        for c in range(6):
            v = stage_load(si, w_rest_d[:, c * 512:(c + 1) * 512].rearrange("(k p) n -> p k n", p=128), 512)
            sv = stgb[:, 0:4096].rearrange("p (k n) -> p k n", k=8)
            CP("dve" if si == 0 else "act", sv, v, [bstg[si]], [bstgb])
            for j_ in range(4):
                DMA(wrest_s[4 * c + j_].rearrange("p (k n) -> p k n", k=8),
                    stgb[:, 0:4096].rearrange("p (k j n) -> p k j n", k=8, j=4)[:, :, j_, :], sem_scr, r=[bstgb], w=[Buf()])
            si ^= 1
        for (wd_, Wt, bW) in ((w_out_a_d, Wouta, bWouta), (w_out_b_d, Woutb, bWoutb)):
            v = stage_load(si, wd_.rearrange("(k p) n -> p k n", p=128), 1024, nk=4)
            CP("dve" if si == 0 else "act", Wt[:], v, [bstg[si]], [bW])
            si ^= 1
        for (wd_, lo) in ((w1k_d, 0), (w1v_d, 64)):
            for hh in range(2):
                view = stg[si][lo:lo + 64, 0:4096].rearrange("p (k n) -> p k n", k=16)
                DMA(view, wd_[hh * 1024:(hh + 1) * 1024, :].rearrange("(k p) n -> p k n", p=64), sem_stg[si], w=[bstg[si]])
                CP("dve" if si == 0 else "act", W1c[lo:lo + 64, hh * 16:(hh + 1) * 16, :], view, [bstg[si]], [bW1c])
                si ^= 1
        ACT(siluc[:], cTt[:], AF.Tanh, [bcT], [bsc], scale=0.5)
        TS("dve", siluc[:], siluc[:], 0.5, 0.5, ALU.mult, ALU.add, [bsc], [bsc])
        TT("dve", siluc[:], siluc[:], cTt[:], ALU.mult, [bsc, bcT], [bsc])
        pm, bpm = bank("proj")
        silucb = sb([128, 8, 2], BF16, "silucb"); bscb = Buf()
        CP("dve", silucb[:], siluc[:], [bsc], [bscb])
        for c in range(6):
            v = stage_load(si, w_ada_d[:, c * 512:(c + 1) * 512].rearrange("(k p) n -> p k n", p=128), 512)
            vb = stgb[:, 0:4096].rearrange("p (k n) -> p k n", k=8)
            CP("dve" if si == 0 else "act", vb, v, [bstg[si]], [bstgb])
            for jj in range(4):
                j = c * 4 + jj
                for kc in range(8):
                    MM(pm[:, j * 2:j * 2 + 2], vb[:, kc, jj * 128:(jj + 1) * 128], silucb[:, kc, :], start=(kc == 0), stop=(kc == 7),
                       r=[bstgb, bscb], w=[bpm], inc=(kc == 7))
            si ^= 1
        TT("dve", modT[:], pm[:, 0:48].rearrange("p (j b) -> p j b", b=2), bcast(badaT[:], [128, 24, 2], 2), ALU.add, [bpm, bbada], [bmod])
        for s in range(2):
            STT(gsT[:, s, :], modT[:, 8:16, s], 1.0, gcol[:], ALU.add, ALU.mult, [bmod, bgcol], [bgs])
        CP("dve", gtmp[:].rearrange("p (s j) -> p s j", s=2), modT[:, 16:24, :].rearrange("p j s -> p s j"), [bmod], [bgtmp])
        for q4 in range(4):
            pg, bpg = bank("proj")
            for jq in range(4):
                qq = q4 * 4 + jq
                MM(pg[0:1, jq * 128:(jq + 1) * 128], gtmp[:, qq:qq + 1], identf[:], r=[bgtmp, bidf], w=[bpg], inc=(jq == 3))
            CP("dve", stg[1][0:1, q4 * 512:(q4 + 1) * 512], pg[0:1, 0:512], [bpg], [bstg[1]])
        for s in range(2):
            for hh in range(2):
                pb_, bpb_ = bank("proj")
                MM(pb_[:, :], ones_f[0:1, :], stg[1][0:1, s * 1024 + hh * 512: s * 1024 + hh * 512 + 512], r=[bones, bstg[1]], w=[bpb_])
                TS("dve", gate_bc[:, s, hh * 512:(hh + 1) * 512], pb_[:, :], 0.5, None, ALU.mult, None, [bpb_], [bgbc])
        for s in (1, 0):
            for hh in range(2):
                v = stage_load(0, w_o_d[:, hh * 512:(hh + 1) * 512].rearrange("(k p) n -> p k n", p=128), 512)
                TT("dve", Wog[:, :, hh * 512:(hh + 1) * 512], v, bcast(gate_bc[:, s, hh * 512:(hh + 1) * 512], [128, 8, 512], 1), ALU.mult,
                   [bstg[0], bgbc], [bWog])
            if s == 1:
                DMA(wog_s, Wog[:].rearrange("p k n -> p (k n)"), sem_scr, r=[bWog], w=[Buf()])
        for kv in range(2):
            lo = kv * 64
            ph, bph = bank("proj")
            for jh in range(2):
                for pos in range(32):
                    MM(ph[:, jh:jh + 1], W1c[lo:lo + 64, pos, jh * 128:(jh + 1) * 128], peTb[lo:lo + 64, pos:pos + 1],
                       start=(pos == 0), stop=(pos == 31), r=[bW1c, bpeT], w=[bph], inc=(pos == 31))
            CP("dve", hb2[:, kv, :], ph[:, 0:2], [bph], [bhb2])
        S.barrier()
        print("SBUF remaining before main alloc:", nc.sbuf_bytes_remaining)

        xt = [sb([128, 1024], F32, f"xt{i}") for i in range(2)]; bxt = [Buf(), Buf()]
        hT = [sb([128, 8, 130], BF16, f"hT{i}") for i in range(2)]; bhT = [Buf(), Buf()]
        for i_ in range(2):
            MSET("pool", hT[i_][:].rearrange("p a b -> p (a b)"), 0.0, [bhT[i_]])
        ynsa = sb([128, 512], F32, "ynsa"); bynsa = Buf()
        yn = sb([128, 512], F32, "yn"); byn = Buf()
        st12 = sb([128, 16], F32, "st12"); bst12 = Buf()
        rs12 = sb([128, 16], F32, "rs12"); brs12 = Buf()
        MSET("pool", Vcf[:], 1.0, bVc)
        gsig = sb([128, 3, 8], F32, "gsig"); bgsig = Buf()
        kvc = sb([128, 2, 144], BF16, "kvc"); bkvc = Buf()
        kcT = sb([64, 2, 128], BF16, "kcT"); bkcT = Buf()
        vcT = sb([64, 2, 128], F32, "vcT"); bvcT = Buf()
        vca = sb([128, 2, 65], F32, "vca"); bvca = Buf()
        MSET("pool", vca[:].rearrange("p a b -> p (a b)"), 1.0, [bvca])
        hu = sb([128, 64], F32, "hu"); bhu = Buf()
        hw_ = sb([128, 64], F32, "hw_"); bhw = Buf()
        hid = sb([128, 64], BF16, "hid"); bhid = Buf()
        kcs = sb([64, 48], F32, "kcs"); bkcs = Buf()
        coef = sb([128, 16], F32, "coef"); bcoef = Buf()
        impr = sb([128, 2, 32], F32, "impr"); bimpr = Buf()
        imp2 = sb([128, 32], F32, "imp2"); bimp2 = Buf()
        m8a = sb([128, 8], F32, "m8a"); bm8a = Buf()
        m8b = sb([128, 8], F32, "m8b"); bm8b = Buf()
        nsel = sb([128, 2, 32], F32, "nsel"); bnsel = Buf()
        nselT = sb([64, 2, 128], BF16, "nselT"); bnselT = Buf()
        MSET("pool", nselT[:].rearrange("p a b -> p (a b)"), 0.0, [bnselT])
        wdad = sb([128, 128], F32, "wdad"); bwdad = Buf()
        wdadb = sb([128, 128], BF16, "wdadb"); bwdadb = Buf()
        eLC = sb([128, 4], F32, "eLC"); beLC = Buf()
        rn8 = sb([128, 8], F32, "rn8"); brn8 = Buf()
        rn8T = sb([8, 128], F32, "rn8T"); brn8T = Buf()
        sbon = sb([128, 8], F32, "sbon"); bsbon = Buf()
        Pst = sb([128, 4, 64], F32, "Pst"); bPst = Buf()
        Pb = sb([128, 4, 64], BF16, "Pb"); bPb = Buf()
        lnst = sb([128, 32], F32, "lnst"); blnst = Buf()
        NWS = 4
        WS = [sb([128, 8, 128], BF16, f"WS{i}") for i in range(NWS)]; bWS = [Buf() for _ in range(NWS)]
        sq, bsq = slot(0, [128, 1024], F32)
        xs, bxs = sq, bsq
        qn2, bqn2 = slot(4, [128, 8, 2, 64], BF16)
        qT2, bqT2 = slot(35, [128, 8, 128], BF16)
        kn2, bkn2 = slot(8, [128, 2, 2, 64], BF16)
        PT = []; bPT = []
        NPT = 2
        for i_ in range(NPT):
            a_, b_ = slot(37 + i_, [128, 512], BF16)
            PT.append(a_); bPT.append(b_)
        PcT, bPcT = slot(39, [128, 512], F32)
        silA, bsilA = slot(15, [128, 4, 128], F32)
        silB, bsilB = slot(17, [128, 4, 128], F32)
        thA, bthA = slot(19, [128, 8, 128], BF16)
        thB, bthB = slot(21, [128, 8, 128], BF16)
        mg1, bmg1 = slot(23, [128, 8, 128], F32)
        mg2, bmg2 = slot(27, [128, 8, 128], F32)
        mgT, bmgT = slot(31, [128, 8, 128], BF16)
        yaT, byaT = slot(33, [128, 4, 128], BF16)
        ybT, bybT = slot(34, [128, 4, 128], BF16)
        rT, brT = slot(4, [128, 4, 128], F32)
        kTr, bkTr = slot(6, [128, 4, 128], F32)
        vT, bvT = slot(8, [128, 4, 128], F32)
        lwT, blw = slot(10, [128, 4, 128], F32)
        LT, bLT = slot(12, [128, 4, 128], F32)
        asg, basg = slot(14, [128, 4, 128], F32)
        e1, be1 = slot(16, [128, 4, 128], F32)
        e2, be2 = slot(18, [128, 4, 128], F32)
        e3, be3 = slot(20, [128, 4, 128], F32)
        kkn, bkkn = slot(22, [128, 4, 128], F32)
        kmod, bkmod = slot(24, [128, 4, 128], F32)
        tmpA, btmpA = slot(26, [128, 4, 128], F32)
        At, bAt = slot(28, [128, 4, 128], BF16)
        Bt, bBt = slot(29, [128, 4, 128], BF16)
        Kt, bKt = slot(30, [128, 4, 128], BF16)
        Rt, bRt = slot(31, [128, 4, 128], BF16)
        Bh, bBh = slot(32, [128, 512], BF16)
        Kh, bKh = slot(33, [128, 512], BF16)
        Vt, bVt = slot(34, [128, 512], BF16)
        Qm = []; bQm = []; QmT = []; bQmT = []; Xm = []; bXm = []
        for st_ in (10, 12):
            a_, b_ = slot(st_, [128, 8, 128], BF16); Qm.append(a_); bQm.append(b_)
        for st_ in (14, 18):
            a_, b_ = slot(st_, [128, 8, 128], BF16); QmT.append(a_); bQmT.append(b_)
        for st_ in (20, 22):
            a_, b_ = slot(st_, [128, 8, 128], BF16); Xm.append(a_); bXm.append(b_)
        AakT, bAak = slot(24, [128, 8, 128], BF16)
        MrbT, bMrb = slot(4, [128, 8, 128], BF16)
        MrkT, bMrk = slot(6, [128, 8, 128], BF16)
        rhs0, brhs0 = slot(8, [128, 8, 64], BF16)
        Ub, bUb = slot(9, [128, 8, 64], BF16)

        def f4(t):
            return t.rearrange("p c t -> p (c t)")

        def REDUCE(out, in_, r, w):
            return S.op("dve", lambda e: e.tensor_reduce(out=out, in_=in_, axis=AX.X, op=ALU.add), r, w)

        def MAX8(out, in_, r, w):
            return S.op("dve", lambda e: e.max(out=out, in_=in_), r, w)

        def MREP(out, rep, vals, r, w):
            return S.op("dve", lambda e: e.match_replace(out=out, in_to_replace=rep, in_values=vals, imm_value=-3.0e38), r, w)

        def RECIP(out, in_, r, w):
            return S.op("dve", lambda e: e.reciprocal(out=out, in_=in_), r, w)

        def SCAN(out, d0, d1, r, w):
            return S.op("dve", lambda e: e.tensor_tensor_scan(out=out, data0=d0, data1=d1, initial=0.0, op0=ALU.mult, op1=ALU.add), r, w)

        print("SBUF remaining:", nc.sbuf_bytes_remaining)
        ws_i = [0]

        def chk(n):
            if stage <= n:
                raise _Stop()

        def tile_body2(s, i):
            T = s * 16 + i
            yield "f"
            tok0 = T * 128
            yield "f"
            xb_ = T % 2
            yield "f"
            x_t = xt[xb_]; bx = bxt[xb_]
            yield "f"
            hcur = hT[T % 2]; bh = bhT[T % 2]
            yield "f"
            hprev = hT[(T + 1) % 2]; bhp = bhT[(T + 1) % 2]
            yield "f"
            ACT(sq[:], x_t[:], AF.Square, [bx], [bsq, bst12], accum=st12[:, 0:1])
            yield "f"
            TS("dve", st12[:, 0:1], st12[:, 0:1], 1.0 / 1024, 1e-6, ALU.mult, ALU.add, [bst12], [bst12])
            yield "f"
            TT("pool", rs12[:, 0:1], st12[:, 0:1], cneg[:, 0:1], ALU.pow, [bst12, bcneg], [brs12])
            yield "f"
            TS("dve", xs[:], x_t[:], rs12[:, 0:1], None, ALU.mult, None, [bx, brs12], [bxs])
            yield "f"
            import os as _os
            yield "f"
            _sk = _os.environ.get("SKIP", "")
            yield "f"
            if i == 0:
                if "m" not in _sk:
                    MSET("pool", hcur[:, :, 0:1], 0.0, [bh])
            else:
                CP("pool", hcur[:, :, 0:1], hprev[:, :, 128:129], [bhp], [bh])
            yield "f"
            for half in range(2):
                pp, bp = bank("proj")
                for j in range(4):
                    kc = half * 4 + j
                    TR(pp[:, j * 128:(j + 1) * 128], xs[:, kc * 128:(kc + 1) * 128], identf[:], [bxs, bidf], [bp], inc=(j == 3))
                for j in range(4):
                    kc = half * 4 + j
                    if "a" in _sk:
                        ACT(hcur[:, kc, 1:129], pp[:, j * 128:(j + 1) * 128], AF.Identity, [bp, bgs, bmod], [bh])
                    elif "b" in _sk:
                        ACT(hcur[:, kc, 2:130], pp[:, j * 128:(j + 1) * 128], AF.Identity, [bp, bgs, bmod], [bh],
                            bias=modT[:, kc, s:s + 1], scale=gsT[:, s, kc:kc + 1])
                    else:
                        ACT(hcur[:, kc, 1:129], pp[:, j * 128:(j + 1) * 128], AF.Identity, [bp, bgs, bmod], [bh],
                            bias=modT[:, kc, s:s + 1], scale=gsT[:, s, kc:kc + 1])
            yield "f"
            dump(f"hT_{T}", hcur[:], [bh], BF16)
            yield "f"
            chk(1)
            yield "f"

            pq, bpq = bank("proj")
            yield "f"
            for kc in range(8):
                MM(pq[:, :], hcur[:, kc, 1:129], Wres[:, kc, CQ:CQ + 512], start=(kc == 0), stop=(kc == 7), r=[bh, bWres], w=[bpq], inc=(kc == 7))
            yield "f"
            ACT(sq[:, 0:512], pq[:, :], AF.Square, [bpq], [bsq])
            yield "f"
            REDUCE(st12[:, 0:8], sq[:, 0:512].rearrange("p (h d) -> p h d", h=8), [bsq], [bst12])
            yield "f"
            pkv, bpkv = bank("proj")
            yield "f"
            for kc in range(8):
                MM(pkv[:, :], hcur[:, kc, 1:129], Wres[:, kc, CKV:CKV + 512], start=(kc == 0), stop=(kc == 7), r=[bh, bWres], w=[bpkv], inc=(kc == 7))
            yield "f"
            ACT(sq[:, 512:768], pkv[:, 0:256], AF.Square, [bpkv], [bsq])
            yield "f"
            REDUCE(st12[:, 8:12], sq[:, 512:768].rearrange("p (h d) -> p h d", h=4), [bsq], [bst12])
            yield "f"
            TS("dve", st12[:, 0:12], st12[:, 0:12], 1.0 / 64, 1e-6, ALU.mult, ALU.add, [bst12], [bst12])
            yield "f"
            TT("pool", rs12[:, 0:12], st12[:, 0:12], cneg[:, 0:12], ALU.pow, [bst12, bcneg], [brs12])
            yield "f"
            chk(1.2)
            yield "f"
            for h in range(8):
                STT(qn2[:, h, :, :], bcast(pq[:, h * 64:(h + 1) * 64], [128, 2, 64], 1), rs12[:, h:h + 1],
                    bcast(bc_small[:, 0:64], [128, 2, 64], 1), ALU.mult, ALU.mult, [bpq, brs12, bbcs], [bqn2])
            yield "f"
            for gg in range(2):
                for br in range(2):
                    c0 = gg * 128 + br * 64
                    STT(kn2[:, gg, br, :], pkv[:, c0:c0 + 64], rs12[:, 8 + gg * 2 + br:9 + gg * 2 + br],
                        bc_small[:, 64 + br * 64:128 + br * 64], ALU.mult, ALU.mult, [bpkv, brs12, bbcs], [bkn2])
            yield "f"
            CP("act", Vc[:, i, :, :, 0:64], pkv[:, 256:512].rearrange("p (b g d) -> p b g d", b=2, g=2), [bpkv], [bVc[i]])
            yield "f"
            chk(1.4)
            yield "f"
            for _ in range(24):
                yield "f"
            pt, bpt = bank("proj")
            yield "f"
            ptb = pt[:].bitcast(BF16)
            yield "f"
            for h in range(8):
                TR(ptb[:, h * 128:(h + 1) * 128], qn2[:, h, :, :].rearrange("p c d -> p (c d)"), identb[:], [bqn2, bidb], [bpt], inc=(h == 7))
            yield "f"
            CP("act", qT2[:].rearrange("p h q -> p (h q)"), ptb[:, 0:1024], [bpt], [bqT2])
            yield "f"
            pt2, bpt2 = bank("proj")
            yield "f"
            pt2b = pt2[:].bitcast(BF16)
            yield "f"
            for gg in range(2):
                TR(pt2b[:, gg * 128:(gg + 1) * 128], kn2[:, gg, :, :].rearrange("p c d -> p (c d)"), identb[:], [bkn2, bidb], [bpt2], inc=(gg == 1))
            yield "f"
            CP("dve", kT[:, :, i * 128:(i + 1) * 128], pt2b[:, 0:256].rearrange("p (g t) -> p g t", g=2), [bpt2], [bkT[i]])
            yield "f"
            chk(1.6)
            yield "f"
            pgt, bpgt = bank("proj")
            yield "f"
            for kc in range(8):
                MM(pgt[:, 0:24], hcur[:, kc, 1:129], Wres[:, kc, CG:CG + 24], start=(kc == 0), stop=(kc == 7), r=[bh, bWres], w=[bpgt], inc=(kc == 7))
            yield "f"
            ACT(gsig[:].rearrange("p a b -> p (a b)"), pgt[:, 0:24], AF.Tanh, [bpgt], [bgsig], scale=0.5)
            yield "f"
            TS("dve", gsig[:].rearrange("p a b -> p (a b)"), gsig[:].rearrange("p a b -> p (a b)"), 0.5, 0.5, ALU.mult, ALU.add, [bgsig], [bgsig])
            yield "f"
            pcm, bpcm = bank("proj")
            yield "f"
            for gg in range(2):
                for kc in range(8):
                    MM(pcm[:, gg * 128:(gg + 1) * 128], Wres[:, kc, CC + gg * 128:CC + (gg + 1) * 128], hcur[:, kc, 1:129],
                       start=(kc == 0), stop=(kc == 7), r=[bh, bWres], w=[bpcm], inc=(kc == 7 and gg == 1))
            yield "f"
            CP("pool", kvc[:, :, 0:16], kvc[:, :, 128:144], [bkvc], [bkvc])
            yield "f"
            CP("act", kvc[:, :, 16:144], pcm[:, 0:256].rearrange("p (g t) -> p g t", g=2), [bpcm], [bkvc])
            yield "f"
            chk(1.8)
            yield "f"
            m0 = 1 if i == 0 else 0
            yield "f"
            nm = 8 - m0
            yield "f"
            for kv in range(2):
                lo = kv * 64
                phd, bphd = bank("proj")
                for jh in range(2):
                    for pos in range(32):
                        MM(phd[:, jh * 16:jh * 16 + 16].rearrange("p (g m) -> p g m", g=2), W1c[lo:lo + 64, pos, jh * 128:(jh + 1) * 128],
                           kvc[lo:lo + 64, :, pos:pos + 113:16], start=(pos == 0), stop=(pos == 31), r=[bW1c, bkvc], w=[bphd],
                           inc=(pos == 31))
                for jh in range(2):
                    reg = (kv * 2 + jh) * 16
                    ACT(hu[:, reg:reg + 16], phd[:, jh * 16:jh * 16 + 16], AF.Identity, [bphd, bhb2], [bhu], bias=hb2[:, kv, jh:jh + 1])
            yield "f"
            chk(1.85)
            yield "f"
            TT("dve", hw_[:], hu[:], hu[:], ALU.mult, [bhu], [bhw])
            yield "f"
            TS("dve", hw_[:], hw_[:], 0.044715, 1.0, ALU.mult, ALU.add, [bhw], [bhw])
            yield "f"
            TT("dve", hw_[:], hw_[:], hu[:], ALU.mult, [bhw, bhu], [bhw])
            yield "f"
            ACT(hw_[:], hw_[:], AF.Tanh, [bhw], [bhw], scale=math.sqrt(2.0 / math.pi))
            yield "f"
            STT(hid[:], hw_[:], 1.0, hu[:], ALU.add, ALU.mult, [bhw, bhu], [bhid])
            yield "f"
            chk(1.9)
            yield "f"
            for _ in range(6):
                yield "f"
            pc2, bpc2 = bank("proj")
            yield "f"
            for kv in range(2):
                for jh in range(2):
                    reg = (kv * 2 + jh) * 16
                    MM(pc2[0:64, kv * 16:(kv + 1) * 16], W2c[:, kv, jh, :], hid[:, reg:reg + 16], start=(jh == 0), stop=(jh == 1),
                       r=[bW2c, bhid], w=[bpc2], inc=(jh == 1))
            yield "f"
            TS("dve", kcs[:, 0:16], pc2[0:64, 0:16], 0.5, None, ALU.mult, None, [bpc2], [bkcs])
            yield "f"
            TT("dve", kcs[:, 16:32], kcs[:, 0:16], kcs[:, 0:16], ALU.mult, [bkcs], [bkcs])
            yield "f"
            MM(pc2[0:64, 64:80], ones_f[0:64, 0:64], kcs[:, 16:32], r=[bones, bkcs], w=[bpc2])
            yield "f"
            TS("dve", kcs[:, 32:48], pc2[0:64, 64:80], 1.0 / 64, 1e-6, ALU.mult, ALU.add, [bpc2], [bkcs])
            yield "f"
            TT("pool", kcs[:, 16:32], kcs[:, 32:48], cneg[0:64, 0:16], ALU.pow, [bkcs, bcneg], [bkcs])
            yield "f"
            TT("dve", kcs[:, 0:16], kcs[:, 0:16], kcs[:, 16:32], ALU.mult, [bkcs], [bkcs])
            yield "f"
            n0 = 8 * i - 1 + m0
            yield "f"
            TS("dve", kcT[:, :, n0:n0 + nm], kcs[:, 0:16].rearrange("p (g m) -> p g m", g=2)[:, :, m0:8], kgc[:, 0:1], None, ALU.mult, None,
               [bkcs, bkgc], [bkcT])
            yield "f"
            TS("dve", vcT[:, :, n0:n0 + nm], pc2[0:64, 16:32].rearrange("p (g m) -> p g m", g=2)[:, :, m0:8], 0.5, None, ALU.mult, None,
               [bpc2], [bvcT])
            yield "f"
            nv = 8 * i + 7
            yield "f"
            chk(1.95)
            yield "f"
            for _ in range(16):
                yield "f"
            pvt, bpvt = bank("proj")
            yield "f"
            for gg in range(2):
                TR(pvt[0:nv, gg * 64:(gg + 1) * 64], vcT[:, gg, 0:nv], identf[0:64, 0:64], [bvcT, bidf], [bpvt], inc=(gg == 1))
            yield "f"
            CP("dve", vca[0:nv, :, 0:64], pvt[0:nv, 0:128].rearrange("p (g d) -> p g d", g=2), [bpvt], [bvca])
            yield "f"
            dump(f"kcT_{T}", kcT[:], [bkcT], BF16)
            yield "f"
            dump(f"vca_{T}", vca[:], [bvca])
            yield "f"
            dump(f"qT2_{T}", qT2[:], [bqT2], BF16)
            yield "f"
            chk(2)
            yield "f"

            yield "front_done"
            def attn_gen():
                first_y = {0: True, 1: True}

                def finish(acc, bacc, br, gg):
                    accv = acc[:, 0:260].rearrange("p (h e) -> p h e", h=4)
                    c0 = br * 4
                    TS("dve", coef[:, c0:c0 + 4], accv[:, :, 64], 1e-30, None, ALU.max, None, [bacc], [bcoef])
                    RECIP(coef[:, c0:c0 + 4], coef[:, c0:c0 + 4], [bcoef], [bcoef])
                    if br == 0:
                        CP("dve", coef[:, 12:16], coef[:, 0:4], [bcoef], [bcoef])
                    gbr = {0: 0, 1: 1, 2: 2}[br]
                    TT("dve", coef[:, c0:c0 + 4], coef[:, c0:c0 + 4], gsig[:, gbr, gg * 4:(gg + 1) * 4], ALU.mult, [bcoef, bgsig], [bcoef])
                    yv = ynsa[:, gg * 256:(gg + 1) * 256].rearrange("p (h d) -> p h d", h=4)
                    cb = bcast(coef[:, c0:c0 + 4], [128, 4, 64], 2)
                    if first_y[gg]:
                        TT("dve", yv, accv[:, :, 0:64], cb, ALU.mult, [bacc, bcoef], [bynsa])
                        first_y[gg] = False
                    else:
                        for h in range(4):
                            STT(yv[:, h, :], accv[:, h, 0:64], coef[:, c0 + h:c0 + h + 1], yv[:, h, :], ALU.mult, ALU.add, [bacc, bcoef, bynsa], [bynsa])

                def pv(acc, bacc, Pt_, bP, vrhs, bv, first, last, K=128):
                    for h in range(4):
                        MM(acc[:, h * 65:(h + 1) * 65], Pt_[0:K, h * 128:(h + 1) * 128], vrhs, start=(first and h == 0), stop=(last and h == 3), r=[bP] + bv, w=[bacc],
                           inc=(h == 3), sgc=True)

                pti = [0]
                for gg in range(2):
                    sc, bsc_ = bank("sc")
                    MM(sc[0:nv, :], kcT[:, gg, 0:nv], qT2[0:64, gg * 4:(gg + 1) * 4, :].rearrange("p h q -> p (h q)"), start=True, stop=False,
                       r=[bkcT, bqT2], w=[bsc_], inc=False)
                    off = 128 - 8 * i
                    MM(sc[0:nv, :], zsh[:, off:off + nv], biasC[:, gg, :], start=False, stop=True, r=[bzsh], w=[bsc_])
                    ACT(PcT[0:nv, :], sc[0:nv, :], AF.Exp, [bsc_], [bPcT], scale=0.125)
                    acc, bacc = bank("acc")
                    for h in range(4):
                        MM(acc[:, h * 65:(h + 1) * 65], PcT[0:nv, h * 128:(h + 1) * 128], vca[0:nv, gg, :], r=[bPcT, bvca], w=[bacc], inc=False)
                    for h in range(4):
                        MM(acc[:, 320 + h * 32:320 + (h + 1) * 32], PcT[0:nv, h * 128:(h + 1) * 128], mimp[0:nv, :], r=[bPcT, bmimp], w=[bacc], inc=(h == 3))
                    finish(acc, bacc, 0, gg)
                    yield
                    if i >= 8:
                        iv = impr[:, gg, :]
                        TS("dve", iv, acc[:, 320:352], coef[:, 12:13], None, ALU.mult, None, [bacc, bcoef], [bimpr])
                        for h in range(1, 4):
                            STT(iv, acc[:, 320 + h * 32:352 + h * 32], coef[:, 12 + h:13 + h], iv, ALU.mult, ALU.add, [bacc, bcoef, bimpr], [bimpr])
                        TT("dve", iv, iv, keepadd[:, i - 8, 0, :], ALU.mult, [bimpr, bka], [bimpr])
                        TT("dve", iv, iv, keepadd[:, i - 8, 1, :], ALU.add, [bimpr, bka], [bimpr])
                        MAX8(m8a[:], iv, [bimpr], [bm8a])
                        MREP(imp2[:], m8a[:], iv, [bimpr, bm8a], [bimp2])
                        MAX8(m8b[:], imp2[:], [bimp2], [bm8b])
                        TS("dve", nsel[:, gg, :], iv, m8b[:, 7:8], 1.0, ALU.is_ge, ALU.subtract, [bimpr, bm8b], [bnsel])
                dump(f"nsel_{T}", nsel[:], [bnsel])
                for br in (2, 1):
                    if br == 1 and i >= 8:
                        for gg in range(2):
                            pn, bpn = bank("sc")
                            TR(pn[0:32, 0:128], nsel[:, gg, :], identf[:], [bnsel, bidf], [bpn])
                            CP("dve", nselT[0:32, gg, :], pn[0:32, 0:128], [bpn], [bnselT])
                    for gg in range(2):
                        lo = 0 if br == 1 else 64
                        j0 = 0 if br == 1 else max(0, i - 4)
                        acc, bacc = bank("acc")
                        prev_ = None
                        for j in range(j0, i + 1):
                            sc, bsc_ = bank("sc")
                            extra = []
                            if j == i:
                                extra.append((identb[:], biasDS[:, 0, gg, :], [bidb, bbias]))
                            if j == i - 1:
                                extra.append((identb[:], biasDS[:, 1, gg, :], [bidb, bbias]))
                            if br == 2 and j == i - 4:
                                extra.append((identb[:], bcast(masks[:, 0, :], [128, 4, 128], 1), [bidb, bmasks]))
                            if br == 1 and i >= 8:
                                extra.append((emat[:, j * 128:(j + 1) * 128], bcast(nselT[:, gg, :], [64, 4, 128], 1), [bemat, bnselT]))
                            MM(sc[:, :], kT[lo:lo + 64, gg, j * 128:(j + 1) * 128], qT2[lo:lo + 64, gg * 4:(gg + 1) * 4, :].rearrange("p h q -> p (h q)"),
                               start=True, stop=(len(extra) == 0), r=[bkT[j], bqT2], w=[bsc_], inc=(len(extra) == 0))
                            for ei, (l_, r_, bb_) in enumerate(extra):
                                lastx = ei == len(extra) - 1
                                MM(sc[:, :].rearrange("p (h q) -> p h q", h=4) if len(r_.shape) == 3 else sc[:, :], l_, r_, start=False, stop=lastx,
                                   r=bb_, w=[bsc_], inc=lastx)
                            Pt_ = PT[pti[0] % NPT]; bP = bPT[pti[0] % NPT]; pti[0] += 1
                            ACT(Pt_[:, :], sc[:, :], AF.Exp, [bsc_], [bP], scale=0.125)
                            if prev_ is not None:
                                pv(acc, bacc, prev_[0], prev_[1], Vc[:, prev_[2], br - 1, gg, :], [bVc[prev_[2]]], prev_[2] == j0, False)
                            prev_ = (Pt_, bP, j)
                            yield
                        pv(acc, bacc, prev_[0], prev_[1], Vc[:, prev_[2], br - 1, gg, :], [bVc[prev_[2]]], prev_[2] == j0, True)
                        finish(acc, bacc, br, gg)
                        yield
                dump(f"ynsa_{T}", ynsa[:], [bynsa])
                chk(3)

                yield
            def rwkv_gen():
                yield
                for c3 in range(0, 13, 3):
                    ps_, bps_ = bank("rw")
                    ncs = min(3, 13 - c3)
                    for cc in range(ncs):
                        c = c3 + cc
                        for kc in range(8):
                            MM(ps_[:, cc * 129:(cc + 1) * 129], Wres[:, kc, CS + c * 128:CS + (c + 1) * 128], hcur[:, kc, 0:129],
                               start=(kc == 0), stop=(kc == 7), r=[bh, bWres], w=[bps_], inc=(kc == 7))
                    for cc in range(ncs):
                        c = c3 + cc
                        if c < 4:
                            dst, bd = rT[:, c, :], brT
                        elif c < 8:
                            dst, bd = kTr[:, c - 4, :], bkTr
                        elif c < 12:
                            dst, bd = vT[:, c - 8, :], bvT
                        else:
                            dst, bd = wdad[:, :], bwdad
                        ACT(dst, ps_[:, cc * 129 + 1:cc * 129 + 129], AF.Identity, [bps_, bmu], [bd], scale=mucol[:, 1, c:c + 1])
                        STT(dst, ps_[:, cc * 129:cc * 129 + 128], mucol[:, 0, c:c + 1], dst, ALU.mult, ALU.add, [bps_, bmu, bd], [bd])
                    yield
                yield
                dump(f"rT_{T}", rT[:], [brT])
                yield
                dump(f"wdad_{T}", wdad[:], [bwdad])
                yield
                ACT(wdadb[0:64, :], wdad[0:64, :], AF.Tanh, [bwdad], [bwdadb])
                yield
                CP("act", wdadb[64:128, :], wdad[64:128, :], [bwdad], [bwdadb])
                yield
                pz, bpz = bank("rw")
                yield
                pa, bpa = bank("rw")
                yield
                for c in range(4):
                    MM(pz[:, c * 128:(c + 1) * 128], Lora[0:64, c * 128:(c + 1) * 128], wdadb[0:64, :], start=True, stop=False, r=[bLora, bwdadb], w=[bpz], inc=False)
                    MM(pz[:, c * 128:(c + 1) * 128], w0a0[0:64, c * 128:(c + 1) * 128], ones_f[0:64, :], start=False, stop=True, r=[bw0a0, bones], w=[bpz], inc=(c == 3))
                yield
                for c in range(4):
                    MM(pa[:, c * 128:(c + 1) * 128], Lora[64:128, c * 128:(c + 1) * 128], wdadb[64:128, :], start=True, stop=False, r=[bLora, bwdadb], w=[bpa], inc=False)
                    MM(pa[:, c * 128:(c + 1) * 128], w0a0[64:128, c * 128:(c + 1) * 128], ones_f[64:128, :], start=False, stop=True, r=[bw0a0, bones], w=[bpa], inc=(c == 3))
                yield
                f4 = lambda t: t[:].rearrange("p c t -> p (c t)")
                yield
                ACT(f4(lwT), pz[:, :], AF.Tanh, [bpz], [blw], scale=0.5)
                yield
                cexp = math.exp(-0.5) * 0.5
                yield
                TS("dve", f4(lwT), f4(lwT), -cexp, -cexp, ALU.mult, ALU.add, [blw], [blw])
                yield
                ACT(f4(asg), pa[:, :], AF.Tanh, [bpa], [basg], scale=0.5)
                yield
                TS("dve", f4(asg), f4(asg), 0.5, 0.5, ALU.mult, ALU.add, [basg], [basg])
                yield
                for c in range(4):
                    SCAN(LT[:, c, :], ones_f[:, :], lwT[:, c, :], [bones, blw], [bLT])
                yield
                TT("dve", f4(tmpA), f4(LT), f4(lwT), ALU.subtract, [bLT, blw], [btmpA])
                yield
                ACT(f4(e1), f4(tmpA), AF.Exp, [btmpA], [be1])
                yield
                ACT(f4(e2), f4(LT), AF.Exp, [bLT], [be2], scale=-1.0)
                yield
                ACT(f4(e3), f4(LT), AF.Exp, [bLT], [be3])
                yield
                ACT(eLC[:, :], LT[:, :, 127], AF.Exp, [bLT], [beLC])
                yield
                yield
                for c in range(4):
                    TS("dve", kkn[:, c, :], kTr[:, c, :], vec4[:, 0, c:c + 1], None, ALU.mult, None, [bkTr, bvec4], [bkkn])
                yield
                ACT(f4(tmpA), f4(kkn), AF.Square, [bkkn], [btmpA])
                yield
                for _ in range(4):
                    yield
                pk_, bpk_ = bank("rw")
                yield
                for c in range(4):
                    MM(pk_[:, c * 2:c * 2 + 2], tmpA[:, c, :], ind2[:, :], r=[btmpA, bind2], w=[bpk_], inc=(c == 3))
                yield
                TS("dve", rn8[:, :], pk_[:, 0:8], 1e-24, None, ALU.max, None, [bpk_], [brn8])
                yield
                TT("pool", rn8[:, :], rn8[:, :], cneg[:, 0:8], ALU.pow, [brn8, bcneg], [brn8])
                yield
                for _ in range(6):
                    yield
                TR(pk_[0:8, 128:256], rn8[:, :], identf[:], [brn8, bidf], [bpk_])
                yield
                CP("dve", rn8T[:, :], pk_[0:8, 128:256], [bpk_], [brn8T])
                yield
                pr_, bpr_ = bank("rw")
                yield
                for c in range(4):
                    MM(pr_[:, c * 128:(c + 1) * 128], indT[:, c, :], rn8T[:, :], r=[bindT, brn8T], w=[bpr_], inc=(c == 3))
                yield
                TT("dve", f4(kkn), f4(kkn), pr_[:, :], ALU.mult, [bkkn, bpr_], [bkkn])
                yield
                dump(f"kkn_{T}", kkn[:], [bkkn])
                yield
                yield
                for c in range(4):
                    TS("dve", tmpA[:, c, :], asg[:, c, :], -1.0, vec4[:, 1, c:c + 1], ALU.add, ALU.mult, [basg, bvec4], [btmpA])
                yield
                STT(f4(kmod), f4(tmpA), 1.0, f4(kTr), ALU.add, ALU.mult, [btmpA, bkTr], [bkmod])
                yield
                dump(f"kmod_{T}", kmod[:], [bkmod])
                yield
                yield
                for c in range(4):
                    STT(tmpA[:, c, :], rT[:, c, :], vec4[:, 2, c:c + 1], kmod[:, c, :], ALU.mult, ALU.mult, [brT, bvec4, bkmod], [btmpA])
                yield
                for c in range(4):
                    MM(pk_[:, 256 + c * 2:256 + c * 2 + 2], tmpA[:, c, :], ind2[:, :], r=[btmpA, bind2], w=[bpk_], inc=(c == 3))
                yield
                CP("dve", sbon[:, :], pk_[:, 256:264], [bpk_], [bsbon])
                yield
                yield
                STT(f4(At), f4(kkn), -1.0, f4(e1), ALU.mult, ALU.mult, [bkkn, be1], [bAt])
                yield
                TT("dve", f4(tmpA), f4(kkn), f4(asg), ALU.mult, [bkkn, basg], [btmpA])
                yield
                TT("dve", f4(tmpA), f4(tmpA), f4(e2), ALU.mult, [btmpA, be2], [btmpA])
                yield
                CP("act", f4(Bt), f4(tmpA), [btmpA], [bBt])
                yield
                TT("dve", f4(e1), f4(kmod), f4(e2), ALU.mult, [bkmod, be2], [be1])
                yield
                CP("act", f4(Kt), f4(e1), [be1], [bKt])
                yield
                TT("dve", f4(Rt), f4(rT), f4(e3), ALU.mult, [brT, be3], [bRt])
                yield
                yield
                for c in range(4):
                    TS("dve", tmpA[:, c, :], tmpA[:, c, :], eLC[:, c:c + 1], None, ALU.mult, None, [btmpA, beLC], [btmpA])
                    ACT(e1[:, c, :], e1[:, c, :], AF.Identity, [be1, beLC], [be1], scale=eLC[:, c:c + 1])
                yield
                for _ in range(6):
                    yield
                for (src, bsrc, dst, bdst) in ((tmpA, btmpA, Bh, bBh), (e1, be1, Kh, bKh), (vT, bvT, Vt, bVt)):
                    pp, bp = bank("rw")
                    for c in range(4):
                        TR(pp[:, c * 128:(c + 1) * 128], src[:, c, :], identf[:], [bsrc, bidf], [bp], inc=(c == 3))
                    CP("act", dst[:, :], pp[:, :], [bp], [bdst])
                yield
                chk(4)
                yield
                yield
                def hs(t, h):
                    return t[(h % 2) * 64:(h % 2) * 64 + 64, h // 2, :]
                yield

                def five(lt, blt, rt, brt, mask_i, dst, bdst):
                    for par in range(2):
                        pp, bp = bank("rw")
                        for hh in range(4):
                            h = hh * 2 + par
                            MM(pp[:, hh * 128:(hh + 1) * 128], hs(lt, h), hs(rt, h), r=[blt, brt], w=[bp], inc=(hh == 3))
                        TT("dve", dst[:, par:8:2, :], pp[:, :].rearrange("p (h t) -> p h t", h=4), bcast(masks[:, mask_i, :], [128, 4, 128], 1), ALU.mult,
                           [bp, bmasks], [bdst])
                yield

                five(Bt, bBt, At, bAt, 1, Qm[0], bQm[0])
                yield
                five(At, bAt, Bt, bBt, 3, QmT[0], bQmT[0])
                yield
                five(Kt, bKt, At, bAt, 1, AakT, bAak)
                yield
                five(Bt, bBt, Rt, bRt, 2, MrbT, bMrb)
                yield
                five(Kt, bKt, Rt, bRt, 2, MrkT, bMrk)
                yield
                yield
                TT("dve", Xm[0][:], Qm[0][:], bcast(identb[:], [128, 8, 128], 1), ALU.add, [bQm[0], bidb], [bXm[0]])
                yield
                cur = 0
                yield
                for lvl in range(1, 7):
                    nxt = cur ^ 1
                    lastl = lvl == 6
                    for half in range(2):
                        hsl = slice(half * 4, half * 4 + 4)
                        pT_, bpT_ = bank("rw")
                        for hh in range(4):
                            h = half * 4 + hh
                            MM(pT_[:, hh * 128:(hh + 1) * 128], Qm[cur][:, h, :], QmT[cur][:, h, :], r=[bQm[cur], bQmT[cur]], w=[bpT_], inc=(hh == 3))
                        CP("act", QmT[nxt][:, hsl, :], pT_[:, :].rearrange("p (h t) -> p h t", h=4), [bpT_], [bQmT[nxt]])
                        if not lastl:
                            pQ_, bpQ_ = bank("rw")
                            for hh in range(4):
                                h = half * 4 + hh
                                MM(pQ_[:, hh * 128:(hh + 1) * 128], QmT[cur][:, h, :], Qm[cur][:, h, :], r=[bQm[cur], bQmT[cur]], w=[bpQ_], inc=(hh == 3))
                            CP("dve", Qm[nxt][:, hsl, :], pQ_[:, :].rearrange("p (h t) -> p h t", h=4), [bpQ_], [bQm[nxt]])
                        yield
                    for half in range(2):
                        hsl = slice(half * 4, half * 4 + 4)
                        pX_, bpX_ = bank("rw")
                        for hh in range(4):
                            h = half * 4 + hh
                            MM(pX_[:, hh * 128:(hh + 1) * 128], QmT[nxt][:, h, :], Xm[cur][:, h, :], r=[bQmT[nxt], bXm[cur]], w=[bpX_], inc=(hh == 3))
                        TT("dve", Xm[nxt][:, hsl, :], pX_[:, :].rearrange("p (h t) -> p h t", h=4), Xm[cur][:, hsl, :], ALU.add, [bpX_, bXm[cur]], [bXm[nxt]])
                        yield
                    cur = nxt
                yield
                Xf = Xm[cur]; bXf = bXm[cur]
                yield
                chk(5)
                yield
                yield
                if i == 0:
                    MSET("pool", Pst[:], 0.0, [bPst])
                    MSET("pool", Pb[:], 0.0, [bPb])
                yield

                def ph_(t, h):
                    return t[(h % 2) * 64:(h % 2) * 64 + 64, h // 2, :]
                yield

                def vh(t, h):
                    return t[:, h * 64:(h + 1) * 64]
                yield

                for par in range(2):
                    p1, bp1 = bank("rw")
                    for hh in range(4):
                        h = hh * 2 + par
                        MM(p1[:, hh * 64:(hh + 1) * 64], hs(At, h), ph_(Pb, h), start=True, stop=False, r=[bAt, bPb], w=[bp1], inc=False)
                        MM(p1[:, hh * 64:(hh + 1) * 64], AakT[:, h, :], vh(Vt, h), start=False, stop=True, r=[bAak, bVt], w=[bp1], inc=(hh == 3))
                    CP("act", rhs0[:, par:8:2, :], p1[:, 0:256].rearrange("p (h v) -> p h v", h=4), [bp1], [brhs0])
                yield
                chk(5.2)
                yield
                p2, bp2 = bank("rw")
                yield
                for h in range(8):
                    MM(p2[:, h * 64:(h + 1) * 64], Xf[:, h, :], rhs0[:, h, :], r=[bXf, brhs0], w=[bp2], inc=(h == 7))
                yield
                CP("act", Ub[:].rearrange("p h v -> p (h v)"), p2[:, :], [bp2], [bUb])
                yield
                chk(5.4)
                yield
                p3s = []
                yield
                for par in range(2):
                    p3, bp3 = bank("proj")
                    p3s.append((p3, bp3))
                    for hh in range(4):
                        h = hh * 2 + par
                        MM(p3[:, hh * 64:(hh + 1) * 64], hs(Rt, h), ph_(Pb, h), start=True, stop=False, r=[bRt, bPb], w=[bp3], inc=False)
                        MM(p3[:, hh * 64:(hh + 1) * 64], MrkT[:, h, :], vh(Vt, h), start=False, stop=False, r=[bMrk, bVt], w=[bp3], inc=False)
                        MM(p3[:, hh * 64:(hh + 1) * 64], MrbT[:, h, :], Ub[:, h, :], start=False, stop=True, r=[bMrb, bUb], w=[bp3], inc=(hh == 3))
                yield
                chk(5.6)
                yield
                p4, bp4 = bank("rw")
                yield
                for h in range(8):
                    o_ = p4[(h % 2) * 64:(h % 2) * 64 + 64, (h // 2) * 64:(h // 2) * 64 + 64]
                    MM(o_, vh(Bh, h), Ub[:, h, :], start=True, stop=False, r=[bBh, bUb], w=[bp4], inc=False)
                    MM(o_, vh(Kh, h), vh(Vt, h), start=False, stop=True, r=[bKh, bVt], w=[bp4], inc=(h == 7))
                yield
                for c in range(4):
                    STT(Pst[:, c, :], Pst[:, c, :], eLC[:, c:c + 1], p4[:, c * 64:(c + 1) * 64], ALU.mult, ALU.add, [bPst, beLC, bp4], [bPst])
                yield
                CP("act", Pb[:].rearrange("p a b -> p (a b)"), Pst[:].rearrange("p a b -> p (a b)"), [bPst], [bPb])
                yield
                chk(5.8)
                yield
                yield
                yn3 = yn[:, :].rearrange("p (h d) -> p h d", h=8)
                yield
                for par in range(2):
                    p3, bp3 = p3s[par]
                    CP("act", yn3[:, par:8:2, :], p3[:, 0:256].rearrange("p (h d) -> p h d", h=4), [bp3], [byn])
                yield
                ACT(sq[:, 0:512], yn[:, :], AF.Square, [byn], [bsq])
                yield
                REDUCE(lnst[:, 0:8], yn3, [byn], [blnst])
                yield
                REDUCE(lnst[:, 8:16], sq[:, 0:512].rearrange("p (h d) -> p h d", h=8), [bsq], [blnst])
                yield
                chk(5.85)
                yield
                TS("dve", lnst[:, 0:16], lnst[:, 0:16], 1.0 / 64, None, ALU.mult, None, [blnst], [blnst])
                yield
                TT("dve", lnst[:, 16:24], lnst[:, 0:8], lnst[:, 0:8], ALU.mult, [blnst], [blnst])
                yield
                TT("dve", lnst[:, 16:24], lnst[:, 8:16], lnst[:, 16:24], ALU.subtract, [blnst], [blnst])
                yield
                TS("dve", lnst[:, 16:24], lnst[:, 16:24], 0.0, 64e-5, ALU.max, ALU.add, [blnst], [blnst])
                yield
                TT("pool", lnst[:, 24:32], lnst[:, 16:24], cneg[:, 0:8], ALU.pow, [blnst, bcneg], [blnst])
                yield
                STT(lnst[:, 16:24], lnst[:, 0:8], -1.0, lnst[:, 24:32], ALU.mult, ALU.mult, [blnst], [blnst])
                yield
                chk(5.9)
                yield
                for h in range(8):
                    ACT(yn[:, h * 64:(h + 1) * 64], yn[:, h * 64:(h + 1) * 64], AF.Identity, [byn, blnst], [byn],
                        bias=lnst[:, 16 + h:17 + h], scale=lnst[:, 24 + h:25 + h])
                yield
                chk(5.95)
                yield
                TT("dve", yn[:, :], yn[:, :], ln_w_bc[:, :], ALU.mult, [byn, blnw], [byn])
                yield
                chk(5.97)
                yield
                for h in range(8):
                    STT(yn[:, h * 64:(h + 1) * 64], Vt[:, h * 64:(h + 1) * 64], sbon[:, h:h + 1], yn[:, h * 64:(h + 1) * 64], ALU.mult, ALU.add,
                        [bVt, bsbon, byn], [byn])
                yield

            na_ = 2 * (2 + (i + 1) + min(5, i + 1)) + 2
            nr_ = 150
            ga_, gr_ = attn_gen(), rwkv_gen()
            da_ = dr_ = 0
            alive_a = alive_r = True
            while alive_a or alive_r:
                pick_a = alive_a and (not alive_r or da_ * nr_ <= dr_ * na_)
                if pick_a:
                    try:
                        next(ga_); da_ += 1
                    except StopIteration:
                        alive_a = False
                else:
                    try:
                        next(gr_); dr_ += 1
                    except StopIteration:
                        alive_r = False
            yield "mid_done"
            chk(6)
            yield "t"
            def wsload(c):
                k = ws_i[0] % NWS
                ws_i[0] += 1
                DMA(WS[k][:].rearrange("p k n -> p (k n)"), wrest_s[c], sem_ws[k], w=[bWS[k]])
                return WS[k], bWS[k]

            def rest_chunk(c):
                pp, bp = bank("tl")
                for sub in range(2):
                    W_, bW_ = wsload(2 * c + sub)
                    for kc in range(8):
                        MM(pp[:, sub * 128:(sub + 1) * 128], W_[:, kc, :], hcur[:, kc, 1:129], start=(kc == 0), stop=(kc == 7),
                           r=[bW_, bh], w=[bp], inc=(kc == 7 and sub == 1))
                return pp, bp

            for c in range(12):
                pp, bp = rest_chunk(c)
                if c < 4:
                    dst, bd = (silA, bsilA) if c < 2 else (silB, bsilB)
                    dv = dst[:, (c % 2) * 2:(c % 2) * 2 + 2, :].rearrange("p a t -> p (a t)")
                    ACT(dv, pp[:, 0:256], AF.Tanh, [bp], [bd], scale=0.5)
                    STT(dv, dv, 1.0, pp[:, 0:256], ALU.add, ALU.mult, [bd, bp], [bd])
                else:
                    dst, bd = (thA, bthA) if c < 8 else (thB, bthB)
                    cc = (c - 4) % 4
                    ACT(dst[:, cc * 2:cc * 2 + 2, :].rearrange("p a t -> p (a t)"), pp[:, 0:256], AF.Tanh, [bp], [bd], scale=0.5)
                yield "t"
            yield "t"
            for (src, bsrc, sil, bsil, dst, bdst, lnb) in ((ynsa, bynsa, silA, bsilA, yaT, byaT, False), (yn, byn, silB, bsilB, ybT, bybT, True)):
                pp, bp = bank("tl")
                for c in range(4):
                    TR(pp[:, c * 128:(c + 1) * 128], src[:, c * 128:(c + 1) * 128], identf[:], [bsrc, bidf], [bp], inc=(c == 3))
                if not lnb:
                    STT(dst[:].rearrange("p c t -> p (c t)"), pp[:, :], 0.5, sil[:].rearrange("p c t -> p (c t)"), ALU.mult, ALU.mult, [bp, bsil], [bdst])
                else:
                    for c in range(4):
                        STT(tmpA[:, c, :], pp[:, c * 128:(c + 1) * 128], vec4[:, 3, c:c + 1], sil[:, c, :], ALU.add, ALU.mult, [bp, bvec4, bsil], [btmpA])
                    ACT(dst[:].rearrange("p c t -> p (c t)"), f4(tmpA), AF.Copy, [btmpA], [bdst], scale=0.5)
            yield "t"
            dump(f"yaT_{T}", yaT[:], [byaT], BF16)
            yield "t"
            dump(f"ybT_{T}", ybT[:], [bybT], BF16)
            yield "t"
            for (yT_, byT_, W_, bW_, th, bth, mg, bmg) in ((yaT, byaT, Wouta, bWouta, thA, bthA, mg1, bmg1), (ybT, bybT, Woutb, bWoutb, thB, bthB, mg2, bmg2)):
                for half in range(2):
                    pp, bp = bank("tl")
                    for mm_ in range(4):
                        mc = half * 4 + mm_
                        for kc in range(4):
                            MM(pp[:, mm_ * 128:(mm_ + 1) * 128], W_[:, kc, mc * 128:(mc + 1) * 128], yT_[:, kc, :], start=(kc == 0), stop=(kc == 3),
                               r=[bW_, byT_], w=[bp], inc=(kc == 3 and mm_ == 3))
                    STT(mg[:, half * 4:(half + 1) * 4, :].rearrange("p a t -> p (a t)"), th[:, half * 4:(half + 1) * 4, :].rearrange("p a t -> p (a t)"), 1.0, pp[:, :],
                        ALU.add, ALU.mult, [bth, bp], [bmg])
                    yield "t"
            yield "t"
            TT("dve", mgT[:].rearrange("p a t -> p (a t)"), mg1[:].rearrange("p a t -> p (a t)"), mg2[:].rearrange("p a t -> p (a t)"), ALU.add, [bmg1, bmg2], [bmgT])
            yield "t"
            dump(f"mgT_{T}", mgT[:], [bmgT], BF16)
            yield "t"
            if s == 1 and i == 0:
                DMA(Wog[:].rearrange("p k n -> p (k n)"), wog_s, sem_wog, w=[bWog])
            yield "t"
            for half in range(2):
                pp, bp = bank("tl")
                for kc in range(8):
                    MM(pp[:, :], mgT[:, kc, :], Wog[:, kc, half * 512:(half + 1) * 512], start=(kc == 0), stop=(kc == 7), r=[bmgT, bWog], w=[bp], inc=(kc == 7))
                TT("dve", x_t[:, half * 512:(half + 1) * 512], pp[:, :], x_t[:, half * 512:(half + 1) * 512], ALU.add, [bp, bx], [bx])
                yield "t"
            yield "t"
            return DMA(out_d[tok0:tok0 + 128, :], x_t[:, :], sem_outs[T % 2], r=[bx], w=[])

        out_toks = []
        FRONT_PER_TAIL = 3
        total = nseq * ntile
        seq_tiles = [(s, i) for s in range(nseq) for i in range(ntile)]

        def make_gen(n_):
            s_, i_ = seq_tiles[n_]
            T_ = s_ * 16 + i_
            if i_ == 0:
                MSET("pool", kcT[:].rearrange("p a b -> p (a b)"), 0.0, [bkcT])
                MSET("pool", vcT[:].rearrange("p a b -> p (a b)"), 0.0, [bvcT])
                MSET("pool", vca[:, :, 0:64], 0.0, [bvca])
                MSET("pool", kvc[:].rearrange("p a b -> p (a b)"), 0.0, [bkvc])
            return tile_body2(s_, i_)

        def xload(n_):
            if n_ < total:
                s2, i2 = seq_tiles[n_]
                T2 = s2 * 16 + i2
                DMA(xt[T2 % 2][:], x_d[T2 * 128:(T2 + 1) * 128, :], sem_x[T2 % 2], w=[bxt[T2 % 2]])

        def step(g, until):
            while True:
                try:
                    m_ = next(g)
                except StopIteration as e_:
                    return None, True, e_.value
                if m_ in until:
                    return m_, False, None

        def early(n_):
            s_, i_ = seq_tiles[n_]
            T_ = s_ * 16 + i_
            return DMA(out_d[T_ * 128:T_ * 128 + 128, :], xs[:, :], sem_outs[T_ % 2], r=[bxs], w=[])

        if total > 0:
            xload(0)
            xload(1)
            if stage < 9:
                for n_ in range(total):
                    g = make_gen(n_)
                    try:
                        _, _, val = step(g, ())
                        out_toks.append(val)
                    except _Stop:
                        out_toks.append(early(n_))
                    xload(n_ + 2)
            else:
                cur = make_gen(0)
                step(cur, ("front_done",))
                for n_ in range(total):
                    step(cur, ("mid_done",))
                    nxt = make_gen(n_ + 1) if n_ + 1 < total else None
                    cur_done = False
                    nxt_done = nxt is None
                    while not (cur_done and nxt_done):
                        if not cur_done:
                            m_, fin, val = step(cur, ("t",))
                            if fin:
                                cur_done = True
                                out_toks.append(val)
                        for _r in range(1 if cur_done else FRONT_PER_TAIL):
                            if not nxt_done:
                                m_, fin, val = step(nxt, ("f", "front_done"))
                                if m_ == "front_done":
                                    nxt_done = True
                    xload(n_ + 2)
                    cur = nxt
        S.wait_all("sp", out_toks[-4:] + dbg_outs + [(sm_, S.dcnt[sm_]) for sm_ in sem_outs])
        S.emit()
    return nc


_CACHE = {}


def kernel(**inputs):
    sh, per = host_prep(inputs)
    if "nc" not in _CACHE:
        _CACHE["nc"] = build()
    nc = _CACHE["nc"]
    in_maps = []
    for core in range(8):
        d = dict(sh)
        d.update(per[core])
        in_maps.append(d)
    res = run_bass_kernel_spmd(nc, in_maps, core_ids=list(range(8)))
    out = np.concatenate([np.asarray(r["out"]).reshape(2, 2048, 1024) for r in res.results], axis=0)
    return out.astype(np.float32)
```

```python
import math
import numpy as np
import concourse.bass as bass
import concourse.mybir as mybir
from concourse.bass_utils import run_bass_kernel_spmd
from contextlib import ExitStack

F32 = mybir.dt.float32
BF16 = mybir.dt.bfloat16
AF = mybir.ActivationFunctionType
ALU = mybir.AluOpType
AX = mybir.AxisListType

COMPUTE = ("pe", "act", "dve", "pool")
NEGM = -4096.0
NRES = 2968
CQ, CKV, CG, CC, CS = 0, 512, 1024, 1048, 1304


class Buf:
    __slots__ = ("w", "r")

    def __init__(self):
        self.w = None
        self.r = {}


class Sched:
    ANNOTATE = False

    def __init__(self, nc, es):
        self.nc = nc
        self.es = es
        self.prog = {e: [] for e in COMPUTE + ("sp",)}
        self.cnt = {e: 0 for e in COMPUTE}
        self.sems = {}
        for e in COMPUTE:
            self.sems[e] = es.enter_context(nc.semaphore("sem_" + e))
        self.known = {e: {} for e in self.prog}
        self.snap = {}
        self.dcnt = {}
        self.pending = {e: False for e in COMPUTE}
        self.last = {}

    def dma_sem(self, name):
        self.sems[name] = self.es.enter_context(self.nc.semaphore("sem_" + name))
        self.dcnt[name] = 0
        return name

    @staticmethod
    def _flat(bs):
        out = []
        for b in bs:
            if isinstance(b, (list, tuple)):
                out.extend(Sched._flat(b))
            else:
                out.append(b)
        return out

    def op(self, eng, fn, reads=(), writes=(), inc=True, dsem=None):
        reads = self._flat(reads)
        writes = self._flat(writes)
        need = {}

        def req(tok, same_ok):
            if tok is None:
                return
            k, v = tok
            if same_ok and k == eng and eng == "pe":
                return
            if need.get(k, 0) < v:
                need[k] = v

        for b in reads:
            req(b.w, False)
        for b in writes:
            req(b.w, True)
            for k, v in b.r.items():
                req((k, v), True)
        kn = self.known[eng]
        waits = []
        for k, v in need.items():
            if kn.get(k, 0) < v:
                waits.append((k, v))
                kn[k] = v
                sn = self.snap.get((k, v))
                if sn is not None:
                    for k2, v2 in sn.items():
                        if kn.get(k2, 0) < v2:
                            kn[k2] = v2
        if dsem is not None:
            self.dcnt[dsem] += 16
            tok = (dsem, self.dcnt[dsem])
            incspec = (dsem, 16)
        elif inc:
            self.cnt[eng] += 1
            tok = (eng, self.cnt[eng])
            incspec = (eng, 1)
            self.pending[eng] = False
            self.snap[tok] = dict(kn)
        else:
            tok = (eng, self.cnt[eng] + 1)
            incspec = None
            self.pending[eng] = True
        self.last[tok[0]] = tok[1]
        for b in writes:
            b.w = tok
            b.r = {}
        for b in reads:
            if b.w is tok:
                continue
            if b.r.get(tok[0], 0) < tok[1]:
                b.r[tok[0]] = tok[1]
        note = None
        if Sched.ANNOTATE:
            import sys as _sys
            f_ = _sys._getframe(1)
            while f_ is not None and f_.f_code.co_name not in ("tile_body2", "attn_gen", "rwkv_gen", "build", "finish", "pv", "five", "rest_chunk", "wsload"):
                f_ = f_.f_back
            note = f"L{f_.f_lineno}" if f_ is not None else None
        self.prog[eng].append((waits, fn, incspec, note))
        return tok

    def wait_all(self, eng, toks):
        kn = self.known[eng]
        waits = []
        mx = {}
        for k, v in toks:
            if mx.get(k, 0) < v:
                mx[k] = v
        for k, v in mx.items():
            if kn.get(k, 0) < v:
                waits.append((k, v))
                kn[k] = v
        self.prog[eng].append((waits, None, None, None))

    def barrier(self):
        for e in COMPUTE:
            if self.pending[e]:
                self.op(e, lambda en: en.nop(), (), ())
        toks = list(self.last.items())
        for e in self.prog:
            self.wait_all(e, toks)

    def emit(self):
        nc = self.nc
        for e in COMPUTE:
            if self.pending[e]:
                self.op(e, lambda en: en.nop(), (), ())
        sems = self.sems
        prog = self.prog

        def run(engname):
            def f(e):
                for waits, fn, incspec, note in prog[engname]:
                    for k, v in waits:
                        e.wait_ge(sems[k], v)
                    if fn is None:
                        continue
                    ins = fn(e)
                    if note is not None:
                        ins.annotate(note)
                    if incspec is not None:
                        ins.then_inc(sems[incspec[0]], incspec[1])
            return f

        with nc.Block() as block:
            block.sync(run("sp"))
            block.tensor(run("pe"))
            block.scalar(run("act"))
            block.vector(run("dve"))
            block.gpsimd(run("pool"))


def _t5_bucket(dist):
    n = np.maximum(dist, 0)
    nf = np.maximum(n, 16).astype(np.float32)
    large = 16 + (np.log(nf / np.float32(16)) / np.float32(math.log(128 / 16)) * np.float32(16)).astype(np.int32)
    return np.where(n < 16, n, np.minimum(large, 31))


def _perms():
    r = lambda a, b: list(range(a, b))
    res = (r(0, 512)
           + r(768, 832) + r(1024, 1088) + r(832, 896) + r(1088, 1152) + r(896, 1024) + r(1152, 1280)
           + r(1280, 1304)
           + r(512, 576) + r(640, 704) + r(576, 640) + r(704, 768)
           + r(1816, 3480))
    rest = r(1304, 1816) + r(3480, 3992) + r(3992, 5016) + r(5016, 6040)
    assert len(res) == NRES and len(rest) == 3072
    return np.array(res), np.array(rest)


def host_prep(inp):
    f = lambda k: np.ascontiguousarray(np.asarray(inp[k], dtype=np.float32))
    sh = {}
    pres, prest = _perms()
    w_in = f("w_in")[0]
    sh["w_res"] = np.ascontiguousarray(w_in[:, pres])
    sh["w_rest"] = np.ascontiguousarray(w_in[:, prest])
    sh["w_ada"] = f("w_ada")[0]
    sh["w_out_a"] = f("w_out_a")[0]
    sh["w_out_b"] = f("w_out_b")[0]
    sh["w_o"] = f("w_o")[0]
    sh["w1k"] = f("cmp_k_w1")[0]
    sh["w1v"] = f("cmp_v_w1")[0]
    col = lambda v, n: np.ascontiguousarray(v.reshape(n, 128).T)
    sh["b_ada"] = col(f("b_ada")[0], 24)
    sh["g_norm"] = col(f("norm_gain")[0], 8)
    sh["mu"] = col(f("shift_mu")[0], 13)
    vec4 = np.stack([col(f(k)[0].reshape(-1), 4) for k in ("k_k", "k_a", "r_k", "ln_x_b")], 1)
    sh["vec4"] = np.ascontiguousarray(vec4)
    rep = lambda v: np.ascontiguousarray(np.broadcast_to(v[None, :], (128, v.shape[0])))
    kng = f("k_norm_gain")[0]
    sh["bc_small"] = np.concatenate([rep(f("q_norm_gain")[0]), rep(kng[1]), rep(kng[2])], 1)
    sh["ln_w_bc"] = rep(f("ln_x_w")[0])
    sh["kgc"] = np.ascontiguousarray(kng[0].reshape(64, 1))
    sh["w0a0"] = np.ascontiguousarray(np.stack([f("w0")[0], f("a0")[0]], 0))
    sh["lora"] = np.ascontiguousarray(np.concatenate([f("w_lora_up")[0], f("a_lora_up")[0]], 0))
    w2 = lambda k: f(k)[0].reshape(2, 128, 64).transpose(1, 0, 2)
    sh["w2"] = np.ascontiguousarray(np.stack([w2("cmp_k_w2"), w2("cmp_v_w2")], 1))
    sh["peT"] = np.ascontiguousarray(np.concatenate([f("cmp_pos_k")[0].T, f("cmp_pos_v")[0].T], 0))
    tbl = f("rel_bias")
    k = np.arange(128)[:, None]
    q = np.arange(128)[None, :]
    tb = np.zeros((2, 2, 128, 4, 128), np.float32)
    for v, dist in enumerate((q - k, 128 + q - k)):
        bk = _t5_bucket(dist)
        for g in range(2):
            for h in range(4):
                tb[v, g, :, h, :] = tbl[bk, g * 4 + h]
    sh["tblDS"] = tb.reshape(2, 2, 128, 512)
    mk = np.zeros((128, 4, 128), np.float32)
    mk[np.broadcast_to(((q - k) < 0)[:, None, :], mk.shape)] = NEGM
    sh["maskD"] = mk.reshape(128, 512)
    c31 = np.zeros((2, 128, 4, 128), np.float32)
    for g in range(2):
        for h in range(4):
            c31[g, :, h, :] = tbl[31, g * 4 + h]
    sh["c31"] = c31.reshape(2, 128, 512)
    p = np.arange(16)[:, None]
    distc = q - 16 * p + 113
    bkc = _t5_bucket(distc)
    tc = np.zeros((2, 16, 4, 128), np.float32)
    for g in range(2):
        for h in range(4):
            tc[g, :, h, :] = tbl[bkc, g * 4 + h]
    sh["tblC"] = tc.reshape(2, 16, 512)
    mc = np.zeros((16, 4, 128), np.float32)
    mc[np.broadcast_to((distc < 0)[:, None, :], mc.shape)] = NEGM
    sh["maskC"] = mc.reshape(16, 512)
    sh["ident"] = np.eye(128, dtype=np.float32)
    far = np.where(k <= q, NEGM, 0.0).astype(np.float32)
    mus = (k < q).astype(np.float32)
    mui = (k <= q).astype(np.float32)
    mls = (k > q).astype(np.float32)
    sh["masks"] = np.ascontiguousarray(np.stack([far, mus, mui, mls], 1))
    z = np.zeros((16, 256), np.float32)
    z[np.arange(16), np.arange(16) + 119] = 1.0
    sh["zsh"] = z
    e = np.zeros((32, 2048), np.float32)
    e[np.arange(2048) // 64, np.arange(2048)] = -NEGM
    sh["emat"] = e
    mi = np.zeros((128, 32), np.float32)
    for j in range(32):
        for a in range(4):
            for b in range(2):
                n = 4 * j + a - b
                if 0 <= n < 127:
                    mi[n, j] += 1.0
    sh["mimp"] = mi
    ka = np.zeros((128, 8, 2, 32), np.float32)
    for i in range(8, 16):
        for qq in range(128):
            cur = (128 * i + qq) // 64
            for j in range(32):
                forced = (j == 0) or (j == cur) or (j == cur - 1)
                causal = j <= cur
                if forced:
                    ka[qq, i - 8, 0, j] = 0.0
                    ka[qq, i - 8, 1, j] = 1e30
                elif causal:
                    ka[qq, i - 8, 0, j] = 1.0
                else:
                    ka[qq, i - 8, 1, j] = -1e30
    sh["keepadd"] = ka.reshape(128, 512)
    ind2 = np.zeros((128, 2), np.float32)
    ind2[:64, 0] = 1.0
    ind2[64:, 1] = 1.0
    sh["ind2"] = ind2
    indT = np.zeros((8, 4, 128), np.float32)
    for h in range(8):
        indT[h, h // 2, (h % 2) * 64:(h % 2) * 64 + 64] = 1.0
    sh["indT"] = indT.reshape(8, 512)
    x = f("x")
    c = f("c")
    per = []
    for core in range(8):
        d = {"x": np.ascontiguousarray(x[2 * core:2 * core + 2].reshape(4096, 1024)),
             "cT": np.ascontiguousarray(c[2 * core:2 * core + 2].reshape(2, 8, 128).transpose(2, 1, 0))}
        per.append(d)
    return sh, per


class _Stop(Exception):
    pass


def build(nseq=2, ntile=16, dbg=None, stage=9):
    nc = bass.Bass("TRN2", target_bir_lowering=False)
    dbg = dbg or {}
    di = lambda name, shape: nc.dram_tensor(name, shape, F32, kind="ExternalInput").ap()
    x_d = di("x", [4096, 1024])
    cT_d = di("cT", [128, 8, 2])
    w_res_d = di("w_res", [1024, NRES])
    w_rest_d = di("w_rest", [1024, 3072])
    w_ada_d = di("w_ada", [1024, 3072])
    w_out_a_d = di("w_out_a", [512, 1024])
    w_out_b_d = di("w_out_b", [512, 1024])
    w_o_d = di("w_o", [1024, 1024])
    w1k_d = di("w1k", [2048, 256])
    w1v_d = di("w1v", [2048, 256])
    b_ada_d = di("b_ada", [128, 24])
    g_norm_d = di("g_norm", [128, 8])
    mu_d = di("mu", [128, 13])
    vec4_d = di("vec4", [128, 4, 4])
    bc_small_d = di("bc_small", [128, 192])
    ln_w_bc_d = di("ln_w_bc", [128, 512])
    kgc_d = di("kgc", [64, 1])
    w0a0_d = di("w0a0", [2, 512])
    lora_d = di("lora", [128, 512])
    w2_d = di("w2", [128, 2, 2, 64])
    peT_d = di("peT", [128, 32])
    tblDS_d = di("tblDS", [2, 2, 128, 512])
    maskD_d = di("maskD", [128, 512])
    c31_d = di("c31", [2, 128, 512])
    tblC_d = di("tblC", [2, 16, 512])
    maskC_d = di("maskC", [16, 512])
    ident_d = di("ident", [128, 128])
    masks_d = di("masks", [128, 4, 128])
    zsh_d = di("zsh", [16, 256])
    emat_d = di("emat", [32, 2048])
    mimp_d = di("mimp", [128, 32])
    keepadd_d = di("keepadd", [128, 512])
    ind2_d = di("ind2", [128, 2])
    indT_d = di("indT", [8, 512])
    out_d = nc.dram_tensor("out", [4096, 1024], F32, kind="ExternalOutput").ap()
    wrest_s = nc.dram_tensor("wrest_s", [24, 128, 1024], BF16, kind="Internal").ap()
    wog_s = nc.dram_tensor("wog_s", [128, 8192], BF16, kind="Internal").ap()

    with ExitStack() as es:
        S = Sched(nc, es)
        _n = [0]

        def sb(shape, dt, name=None):
            _n[0] += 1
            return es.enter_context(nc.sbuf_tensor("s_" + (name or f"sb{_n[0]}"), shape, dt))

        def psb(name):
            return es.enter_context(nc.psum_tensor(name, [128, 512], F32))

        dbg_outs = []

        def dump(name, ap, reads, dt=F32):
            if name not in dbg:
                return
            d = nc.dram_tensor("dbg_" + name, list(ap.shape), dt, kind="ExternalOutput").ap()
            dbg_outs.append(S.op("sp", lambda e: e.dma_start(out=d, in_=ap), reads, (), dsem=sem_dbg))

        def MM(out, lhsT, rhs, start=True, stop=True, r=(), w=(), inc=True, sgc=False):
            if sgc:
                return S.op("pe", lambda e: e.matmul(out, lhsT=lhsT, rhs=rhs, start=start, stop=stop, skip_group_check=True), r, w, inc=inc)
            return S.op("pe", lambda e: e.matmul(out, lhsT=lhsT, rhs=rhs, start=start, stop=stop), r, w, inc=inc)

        def TR(out, in_, ident, r=(), w=(), inc=True):
            return S.op("pe", lambda e: e.transpose(out=out, in_=in_, identity=ident), r, w, inc=inc)

        def ACT(out, in_, func, r=(), w=(), bias=None, scale=None, accum=None):
            kw = {}
            if bias is not None:
                kw["bias"] = bias
            if scale is not None:
                kw["scale"] = scale
            if accum is not None:
                kw["accum_out"] = accum
            return S.op("act", lambda e: e.activation(out=out, in_=in_, func=func, **kw), r, w)

        def TS(eng, out, in0, s1, s2, op0, op1=None, r=(), w=()):
            if op1 is None:
                return S.op(eng, lambda e: e.tensor_scalar(out=out, in0=in0, scalar1=s1, scalar2=None, op0=op0), r, w)
            return S.op(eng, lambda e: e.tensor_scalar(out=out, in0=in0, scalar1=s1, scalar2=s2, op0=op0, op1=op1), r, w)

        def TT(eng, out, in0, in1, op, r=(), w=()):
            return S.op(eng, lambda e: e.tensor_tensor(out=out, in0=in0, in1=in1, op=op), r, w)

        def STT(out, in0, scalar, in1, op0, op1, r=(), w=()):
            return S.op("dve", lambda e: e.scalar_tensor_tensor(out=out, in0=in0, scalar=scalar, in1=in1, op0=op0, op1=op1), r, w)

        def CP(eng, out, in_, r=(), w=()):
            if eng == "act":
                return S.op("act", lambda e: e.copy(out=out, in_=in_), r, w)
            return S.op(eng, lambda e: e.tensor_copy(out=out, in_=in_), r, w)

        def MSET(eng, ap, val, w=()):
            return S.op(eng, lambda e: e.memset(ap, val), (), w)

        def DMA(out, in_, sem, r=(), w=(), eng="sp"):
            return S.op(eng, lambda e: e.dma_start(out=out, in_=in_), r, w, dsem=sem)

        def bcast(ap, shape, axis):
            return ap.unsqueeze(axis).to_broadcast(shape)

        sem_dbg = S.dma_sem("dbg")
        sem_stg = [S.dma_sem("stg0"), S.dma_sem("stg1")]
        sem_scr = S.dma_sem("scr")
        sem_x = [S.dma_sem("x0"), S.dma_sem("x1")]
        sem_xr = S.dma_sem("xr")
        sem_ws = [S.dma_sem(f"ws{i}") for i in range(4)]
        sem_outs = [S.dma_sem("out0"), S.dma_sem("out1")]
        sem_wog = S.dma_sem("wog")

        PS = [psb(f"ps{i}") for i in range(8)]
        PSB = [Buf() for _ in range(8)]
        rot = {"proj": [0, 1], "sc": [2, 3], "acc": [4, 5], "rw": [6, 7], "tl": [4, 5, 6, 7]}
        rotc = {k: 0 for k in rot}

        def bank(cls):
            i = rot[cls][rotc[cls] % len(rot[cls])]
            rotc[cls] += 1
            return PS[i], PSB[i]

        NSLOT = 41
        AR = sb([128, NSLOT * 256], F32, "arena")
        SLB = [Buf() for _ in range(NSLOT)]

        def slot(start, shape, dt, P0=0):
            el = 4 if dt == F32 else 2
            n = int(np.prod(shape[1:]))
            nsl = (n * el + 1023) // 1024
            assert start + nsl <= NSLOT
            base = AR[:] if dt == F32 else AR[:].bitcast(BF16)
            o = start * 1024 // el
            ap = base[P0:P0 + shape[0], o:o + n]
            if len(shape) > 2:
                names = " ".join(f"d{i}" for i in range(len(shape) - 1))
                kw = {f"d{i}": shape[i + 1] for i in range(len(shape) - 1)}
                ap = ap.rearrange(f"p ({names}) -> p {names}", **kw)
            return ap, SLB[start:start + nsl]

        Wres = sb([128, 8, NRES], BF16, "Wres"); bWres = Buf()
        Wouta = sb([128, 4, 1024], BF16, "Wouta"); bWouta = Buf()
        Woutb = sb([128, 4, 1024], BF16, "Woutb"); bWoutb = Buf()
        Wog = sb([128, 8, 1024], BF16, "Wog"); bWog = Buf()
        W1c = sb([128, 32, 256], BF16, "W1c"); bW1c = Buf()
        W2c = sb([128, 2, 2, 64], BF16, "W2c"); bW2c = Buf()
        Lora = sb([128, 512], BF16, "Lora"); bLora = Buf()
        identf = sb([128, 128], F32, "identf"); bidf = Buf()
        identb = sb([128, 128], BF16, "identb"); bidb = Buf()
        masks = sb([128, 4, 128], BF16, "masks"); bmasks = Buf()
        biasDS = sb([128, 2, 2, 512], BF16, "biasDS"); bbias = Buf()
        emat = sb([64, 2048], BF16, "emat"); bemat = Buf()
        zsh = sb([128, 256], BF16, "zsh"); bzsh = Buf()
        biasC = sb([128, 2, 512], BF16, "biasC"); bbiasC = Buf()
        w0a0 = sb([128, 512], F32, "w0a0"); bw0a0 = Buf()
        bmisc = Buf()
        mimp = sb([128, 32], F32, "mimp"); bmimp = Buf()
        keepadd = sb([128, 8, 2, 32], F32, "keepadd"); bka = Buf()
        ind2 = sb([128, 2], F32, "ind2"); bind2 = Buf()
        indT = sb([8, 4, 128], F32, "indT"); bindT = Buf()
        ones_f = sb([128, 128], F32, "ones_f"); bones = Buf()
        bc_small = sb([128, 192], F32, "bc_small"); bbcs = Buf()
        ln_w_bc = sb([128, 512], F32, "ln_w_bc"); blnw = Buf()
        vec4 = sb([128, 4, 4], F32, "vec4"); bvec4 = Buf()
        mucol = sb([128, 2, 13], F32, "mucol"); bmu = Buf()
        kgc = sb([64, 1], F32, "kgc"); bkgc = Buf()
        gcol = sb([128, 8], F32, "gcol"); bgcol = Buf()
        badaT = sb([128, 24], F32, "badaT"); bbada = Buf()
        cTt = sb([128, 8, 2], F32, "cTt"); bcT = Buf()
        modT = sb([128, 24, 2], F32, "modT"); bmod = Buf()
        gsT = sb([128, 2, 8], F32, "gsT"); bgs = Buf()
        hb2 = sb([128, 2, 2], F32, "hb2"); bhb2 = Buf()
        cneg = sb([128, 16], F32, "cneg"); bcneg = Buf()
        peTb = sb([128, 32], BF16, "peTb"); bpeT = Buf()
        siluc = sb([128, 8, 2], F32, "siluc"); bsc = Buf()
        gtmp = sb([128, 16], F32, "gtmp"); bgtmp = Buf()

        stg = []; bstg = []
        for i_ in range(2):
            a_, b_ = slot(16 * i_, [128, 4096], F32)
            stg.append(a_); bstg.append(b_)
        kT = sb([128, 2, 2048], BF16, "kT"); bkT = [Buf() for _ in range(16)]
        Vcf = sb([128, 4160], BF16, "Vc"); bVc = [Buf() for _ in range(16)]
        Vc = Vcf[:].rearrange("p (a b c d) -> p a b c d", a=16, b=2, c=2)
        stgb = kT[:].rearrange("p a b -> p (a b)"); bstgb = bkT
        gate_bc = Vcf[:].bitcast(F32)[:, 0:2048].rearrange("p (s n) -> p s n", s=2); bgbc = bVc

        ldn = [0]
        sem_lds = [S.dma_sem(f"ld{i}") for i in range(8)]

        def ld(out, in_, w):
            sm = sem_lds[ldn[0] % 8]
            ldn[0] += 1
            if S.dcnt[sm] > 0:
                S.wait_all("sp", [(sm, S.dcnt[sm])])
            return DMA(out, in_, sm, w=w)

        ld(identf[:], ident_d, [bidf])
        CP("dve", identb[:], identf[:], [bidf], [bidb])
        ld(stg[0][:, 0:512].rearrange("p (a b) -> p a b", a=4), masks_d, [bstg[0]])
        CP("dve", masks[:], stg[0][:, 0:512].rearrange("p (a b) -> p a b", a=4), [bstg[0]], [bmasks])
        ld(mimp[:], mimp_d, [bmimp])
        ld(keepadd[:].rearrange("p a b c -> p (a b c)"), keepadd_d, [bka])
        ld(ind2[:], ind2_d, [bind2])
        ld(indT[:].rearrange("p a b -> p (a b)"), indT_d, [bindT])
        ld(bc_small[:], bc_small_d, [bbcs])
        ld(ln_w_bc[:], ln_w_bc_d, [blnw])
        ld(vec4[:], vec4_d, [bvec4])
        ld(mucol[:, 0, :], mu_d, [bmu])
        TS("dve", mucol[:, 1, :], mucol[:, 0, :], -1.0, 1.0, ALU.mult, ALU.add, [bmu], [bmu])
        ld(kgc[:], kgc_d, [bkgc])
        MSET("pool", w0a0[:], 0.0, [bw0a0])
        ld(w0a0[0:1, :], w0a0_d[0:1, :], [bw0a0])
        ld(w0a0[64:65, :], w0a0_d[1:2, :], [bw0a0])
        MSET("pool", emat[:], 0.0, [bemat])
        MSET("pool", zsh[:], 0.0, [bzsh])
        MSET("pool", biasC[:].rearrange("p a b -> p (a b)"), 0.0, [bbiasC])
        ld(gcol[:], g_norm_d, [bgcol])
        ld(badaT[:], b_ada_d, [bbada])
        ld(cTt[:], cT_d, [bcT])
        MSET("pool", ones_f[:], 1.0, [bones])
        MSET("pool", cneg[:], -0.5, [bcneg])
        ld(stg[1][0:16, 0:256], zsh_d, [bstg[1]])
        CP("dve", zsh[0:16, :], stg[1][0:16, 0:256], [bstg[1]], [bzsh])
        ld(stg[1][0:32, 0:2048], emat_d, [bstg[1]])
        CP("dve", emat[0:32, :], stg[1][0:32, 0:2048], [bstg[1]], [bemat])
        ld(stg[1][:, 2048:2560], lora_d, [bstg[1]])
        CP("dve", Lora[:], stg[1][:, 2048:2560], [bstg[1]], [bLora])
        ld(stg[1][:, 2560:2816].rearrange("p (a b c) -> p a b c", a=2, b=2), w2_d, [bstg[1]])
        CP("dve", W2c[:], stg[1][:, 2560:2816].rearrange("p (a b c) -> p a b c", a=2, b=2), [bstg[1]], [bW2c])
        ld(stg[1][:, 2816:2848], peT_d, [bstg[1]])
        CP("dve", peTb[:], stg[1][:, 2816:2848], [bstg[1]], [bpeT])
        for g in range(2):
            ld(stg[0][:, 0:512], c31_d[g], [bstg[0]])
            for v in range(2):
                ld(stg[1][:, 0:512], tblDS_d[v, g], [bstg[1]])
                TT("dve", stg[1][:, 0:512], stg[1][:, 0:512], stg[0][:, 0:512], ALU.subtract, [bstg[0], bstg[1]], [bstg[1]])
                if v == 0:
                    ld(stg[1][:, 512:1024], maskD_d, [bstg[1]])
                    STT(biasDS[:, v, g, :], stg[1][:, 0:512], 8.0, stg[1][:, 512:1024], ALU.mult, ALU.add, [bstg[1]], [bbias])
                else:
                    TS("dve", biasDS[:, v, g, :], stg[1][:, 0:512], 8.0, None, ALU.mult, None, [bstg[1]], [bbias])
            ld(stg[1][0:16, 0:512], tblC_d[g], [bstg[1]])
            ld(stg[1][0:16, 512:1024], maskC_d, [bstg[1]])
            TT("dve", stg[1][0:16, 0:512], stg[1][0:16, 0:512], stg[0][0:16, 0:512], ALU.subtract, [bstg[0], bstg[1]], [bstg[1]])
            STT(biasC[0:16, g, :], stg[1][0:16, 0:512], 8.0, stg[1][0:16, 512:1024], ALU.mult, ALU.add, [bstg[1]], [bbiasC])

        def stage_load(i, src_ap, ncols, nk=8):
            view = stg[i][:, 0:nk * ncols].rearrange("p (k n) -> p k n", k=nk)
            DMA(view, src_ap, sem_stg[i], w=[bstg[i]])
            return view

        si = 0
        for c0 in range(0, NRES, 512):
            n = min(512, NRES - c0)
            v = stage_load(si, w_res_d[:, c0:c0 + n].rearrange("(k p) n -> p k n", p=128), n)
            CP("dve" if si == 0 else "act", Wres[:, :, c0:c0 + n], v, [bstg[si]], [bWres])
            si ^= 1
        for c in range(6):
            v = stage_load(si, w_rest_d[:, c * 512:(c + 1) * 512].rearrange("(k p) n -> p k n", p=128), 512)
            sv = stgb[:, 0:4096].rearrange("p (k n) -> p k n", k=8)
            CP("dve" if si == 0 else "act", sv, v, [bstg[si]], [bstgb])
            for j_ in range(4):
                DMA(wrest_s[4 * c + j_].rearrange("p (k n) -> p k n", k=8),
                    stgb[:, 0:4096].rearrange("p (k j n) -> p k j n", k=8, j=4)[:, :, j_, :], sem_scr, r=[bstgb], w=[Buf()])
            si ^= 1
        for (wd_, Wt, bW) in ((w_out_a_d, Wouta, bWouta), (w_out_b_d, Woutb, bWoutb)):
            v = stage_load(si, wd_.rearrange("(k p) n -> p k n", p=128), 1024, nk=4)
            CP("dve" if si == 0 else "act", Wt[:], v, [bstg[si]], [bW])
            si ^= 1
        for (wd_, lo) in ((w1k_d, 0), (w1v_d, 64)):
            for hh in range(2):
                view = stg[si][lo:lo + 64, 0:4096].rearrange("p (k n) -> p k n", k=16)
                DMA(view, wd_[hh * 1024:(hh + 1) * 1024, :].rearrange("(k p) n -> p k n", p=64), sem_stg[si], w=[bstg[si]])
                CP("dve" if si == 0 else "act", W1c[lo:lo + 64, hh * 16:(hh + 1) * 16, :], view, [bstg[si]], [bW1c])
                si ^= 1
        ACT(siluc[:], cTt[:], AF.Tanh, [bcT], [bsc], scale=0.5)
        TS("dve", siluc[:], siluc[:], 0.5, 0.5, ALU.mult, ALU.add, [bsc], [bsc])
        TT("dve", siluc[:], siluc[:], cTt[:], ALU.mult, [bsc, bcT], [bsc])
        pm, bpm = bank("proj")
        silucb = sb([128, 8, 2], BF16, "silucb"); bscb = Buf()
        CP("dve", silucb[:], siluc[:], [bsc], [bscb])
        for c in range(6):
            v = stage_load(si, w_ada_d[:, c * 512:(c + 1) * 512].rearrange("(k p) n -> p k n", p=128), 512)
            vb = stgb[:, 0:4096].rearrange("p (k n) -> p k n", k=8)
            CP("dve" if si == 0 else "act", vb, v, [bstg[si]], [bstgb])
            for jj in range(4):
                j = c * 4 + jj
                for kc in range(8):
                    MM(pm[:, j * 2:j * 2 + 2], vb[:, kc, jj * 128:(jj + 1) * 128], silucb[:, kc, :], start=(kc == 0), stop=(kc == 7),
                       r=[bstgb, bscb], w=[bpm], inc=(kc == 7))
            si ^= 1
        TT("dve", modT[:], pm[:, 0:48].rearrange("p (j b) -> p j b", b=2), bcast(badaT[:], [128, 24, 2], 2), ALU.add, [bpm, bbada], [bmod])
        for s in range(2):
            STT(gsT[:, s, :], modT[:, 8:16, s], 1.0, gcol[:], ALU.add, ALU.mult, [bmod, bgcol], [bgs])
        CP("dve", gtmp[:].rearrange("p (s j) -> p s j", s=2), modT[:, 16:24, :].rearrange("p j s -> p s j"), [bmod], [bgtmp])
        for q4 in range(4):
            pg, bpg = bank("proj")
            for jq in range(4):
                qq = q4 * 4 + jq
                MM(pg[0:1, jq * 128:(jq + 1) * 128], gtmp[:, qq:qq + 1], identf[:], r=[bgtmp, bidf], w=[bpg], inc=(jq == 3))
            CP("dve", stg[1][0:1, q4 * 512:(q4 + 1) * 512], pg[0:1, 0:512], [bpg], [bstg[1]])
        for s in range(2):
            for hh in range(2):
                pb_, bpb_ = bank("proj")
                MM(pb_[:, :], ones_f[0:1, :], stg[1][0:1, s * 1024 + hh * 512: s * 1024 + hh * 512 + 512], r=[bones, bstg[1]], w=[bpb_])
                TS("dve", gate_bc[:, s, hh * 512:(hh + 1) * 512], pb_[:, :], 0.5, None, ALU.mult, None, [bpb_], [bgbc])
        for s in (1, 0):
            for hh in range(2):
                v = stage_load(0, w_o_d[:, hh * 512:(hh + 1) * 512].rearrange("(k p) n -> p k n", p=128), 512)
                TT("dve", Wog[:, :, hh * 512:(hh + 1) * 512], v, bcast(gate_bc[:, s, hh * 512:(hh + 1) * 512], [128, 8, 512], 1), ALU.mult,
                   [bstg[0], bgbc], [bWog])
            if s == 1:
                DMA(wog_s, Wog[:].rearrange("p k n -> p (k n)"), sem_scr, r=[bWog], w=[Buf()])
        for kv in range(2):
            lo = kv * 64
            ph, bph = bank("proj")
            for jh in range(2):
                for pos in range(32):
                    MM(ph[:, jh:jh + 1], W1c[lo:lo + 64, pos, jh * 128:(jh + 1) * 128], peTb[lo:lo + 64, pos:pos + 1],
                       start=(pos == 0), stop=(pos == 31), r=[bW1c, bpeT], w=[bph], inc=(pos == 31))
            CP("dve", hb2[:, kv, :], ph[:, 0:2], [bph], [bhb2])
        S.barrier()
        print("SBUF remaining before main alloc:", nc.sbuf_bytes_remaining)

        xt = [sb([128, 1024], F32, f"xt{i}") for i in range(2)]; bxt = [Buf(), Buf()]
        hT = [sb([128, 8, 130], BF16, f"hT{i}") for i in range(2)]; bhT = [Buf(), Buf()]
        for i_ in range(2):
            MSET("pool", hT[i_][:].rearrange("p a b -> p (a b)"), 0.0, [bhT[i_]])
        ynsa = sb([128, 512], F32, "ynsa"); bynsa = Buf()
        yn = sb([128, 512], F32, "yn"); byn = Buf()
        st12 = sb([128, 16], F32, "st12"); bst12 = Buf()
        rs12 = sb([128, 16], F32, "rs12"); brs12 = Buf()
        MSET("pool", Vcf[:], 1.0, bVc)
        gsig = sb([128, 3, 8], F32, "gsig"); bgsig = Buf()
        kvc = sb([128, 2, 144], BF16, "kvc"); bkvc = Buf()
        kcT = sb([64, 2, 128], BF16, "kcT"); bkcT = Buf()
        vcT = sb([64, 2, 128], F32, "vcT"); bvcT = Buf()
        vca = sb([128, 2, 65], F32, "vca"); bvca = Buf()
        MSET("pool", vca[:].rearrange("p a b -> p (a b)"), 1.0, [bvca])
        hu = sb([128, 64], F32, "hu"); bhu = Buf()
        hw_ = sb([128, 64], F32, "hw_"); bhw = Buf()
        hid = sb([128, 64], BF16, "hid"); bhid = Buf()
        kcs = sb([64, 48], F32, "kcs"); bkcs = Buf()
        coef = sb([128, 16], F32, "coef"); bcoef = Buf()
        impr = sb([128, 2, 32], F32, "impr"); bimpr = Buf()
        imp2 = sb([128, 32], F32, "imp2"); bimp2 = Buf()
        m8a = sb([128, 8], F32, "m8a"); bm8a = Buf()
        m8b = sb([128, 8], F32, "m8b"); bm8b = Buf()
        nsel = sb([128, 2, 32], F32, "nsel"); bnsel = Buf()
        nselT = sb([64, 2, 128], BF16, "nselT"); bnselT = Buf()
        MSET("pool", nselT[:].rearrange("p a b -> p (a b)"), 0.0, [bnselT])
        wdad = sb([128, 128], F32, "wdad"); bwdad = Buf()
        wdadb = sb([128, 128], BF16, "wdadb"); bwdadb = Buf()
        eLC = sb([128, 4], F32, "eLC"); beLC = Buf()
        rn8 = sb([128, 8], F32, "rn8"); brn8 = Buf()
        rn8T = sb([8, 128], F32, "rn8T"); brn8T = Buf()
        sbon = sb([128, 8], F32, "sbon"); bsbon = Buf()
        Pst = sb([128, 4, 64], F32, "Pst"); bPst = Buf()
        Pb = sb([128, 4, 64], BF16, "Pb"); bPb = Buf()
        lnst = sb([128, 32], F32, "lnst"); blnst = Buf()
        NWS = 4
        WS = [sb([128, 8, 128], BF16, f"WS{i}") for i in range(NWS)]; bWS = [Buf() for _ in range(NWS)]
        sq, bsq = slot(0, [128, 1024], F32)
        xs, bxs = sq, bsq
        qn2, bqn2 = slot(4, [128, 8, 2, 64], BF16)
        qT2, bqT2 = slot(35, [128, 8, 128], BF16)
        kn2, bkn2 = slot(8, [128, 2, 2, 64], BF16)
        PT = []; bPT = []
        NPT = 2
        for i_ in range(NPT):
            a_, b_ = slot(37 + i_, [128, 512], BF16)
            PT.append(a_); bPT.append(b_)
        PcT, bPcT = slot(39, [128, 512], F32)
        silA, bsilA = slot(15, [128, 4, 128], F32)
        silB, bsilB = slot(17, [128, 4, 128], F32)
        thA, bthA = slot(19, [128, 8, 128], BF16)
        thB, bthB = slot(21, [128, 8, 128], BF16)
        mg1, bmg1 = slot(23, [128, 8, 128], F32)
        mg2, bmg2 = slot(27, [128, 8, 128], F32)
        mgT, bmgT = slot(31, [128, 8, 128], BF16)
        yaT, byaT = slot(33, [128, 4, 128], BF16)
        ybT, bybT = slot(34, [128, 4, 128], BF16)
        rT, brT = slot(4, [128, 4, 128], F32)
        kTr, bkTr = slot(6, [128, 4, 128], F32)
        vT, bvT = slot(8, [128, 4, 128], F32)
        lwT, blw = slot(10, [128, 4, 128], F32)
        LT, bLT = slot(12, [128, 4, 128], F32)
        asg, basg = slot(14, [128, 4, 128], F32)
        e1, be1 = slot(16, [128, 4, 128], F32)
        e2, be2 = slot(18, [128, 4, 128], F32)
        e3, be3 = slot(20, [128, 4, 128], F32)
        kkn, bkkn = slot(22, [128, 4, 128], F32)
        kmod, bkmod = slot(24, [128, 4, 128], F32)
        tmpA, btmpA = slot(26, [128, 4, 128], F32)
        At, bAt = slot(28, [128, 4, 128], BF16)
        Bt, bBt = slot(29, [128, 4, 128], BF16)
        Kt, bKt = slot(30, [128, 4, 128], BF16)
        Rt, bRt = slot(31, [128, 4, 128], BF16)
        Bh, bBh = slot(32, [128, 512], BF16)
        Kh, bKh = slot(33, [128, 512], BF16)
        Vt, bVt = slot(34, [128, 512], BF16)
        Qm = []; bQm = []; QmT = []; bQmT = []; Xm = []; bXm = []
        for st_ in (10, 12):
            a_, b_ = slot(st_, [128, 8, 128], BF16); Qm.append(a_); bQm.append(b_)
        for st_ in (14, 18):
            a_, b_ = slot(st_, [128, 8, 128], BF16); QmT.append(a_); bQmT.append(b_)
        for st_ in (20, 22):
            a_, b_ = slot(st_, [128, 8, 128], BF16); Xm.append(a_); bXm.append(b_)
        AakT, bAak = slot(24, [128, 8, 128], BF16)
        MrbT, bMrb = slot(4, [128, 8, 128], BF16)
        MrkT, bMrk = slot(6, [128, 8, 128], BF16)
        rhs0, brhs0 = slot(8, [128, 8, 64], BF16)
        Ub, bUb = slot(9, [128, 8, 64], BF16)

        def f4(t):
            return t.rearrange("p c t -> p (c t)")

        def REDUCE(out, in_, r, w):
            return S.op("dve", lambda e: e.tensor_reduce(out=out, in_=in_, axis=AX.X, op=ALU.add), r, w)

        def MAX8(out, in_, r, w):
            return S.op("dve", lambda e: e.max(out=out, in_=in_), r, w)

        def MREP(out, rep, vals, r, w):
            return S.op("dve", lambda e: e.match_replace(out=out, in_to_replace=rep, in_values=vals, imm_value=-3.0e38), r, w)

        def RECIP(out, in_, r, w):
            return S.op("dve", lambda e: e.reciprocal(out=out, in_=in_), r, w)

        def SCAN(out, d0, d1, r, w):
            return S.op("dve", lambda e: e.tensor_tensor_scan(out=out, data0=d0, data1=d1, initial=0.0, op0=ALU.mult, op1=ALU.add), r, w)

        print("SBUF remaining:", nc.sbuf_bytes_remaining)
        ws_i = [0]

        def chk(n):
            if stage <= n:
                raise _Stop()

        def tile_body2(s, i):
            T = s * 16 + i
            yield "f"
            tok0 = T * 128
            yield "f"
            xb_ = T % 2
            yield "f"
            x_t = xt[xb_]; bx = bxt[xb_]
            yield "f"
            hcur = hT[T % 2]; bh = bhT[T % 2]
            yield "f"
            hprev = hT[(T + 1) % 2]; bhp = bhT[(T + 1) % 2]
            yield "f"
            ACT(sq[:], x_t[:], AF.Square, [bx], [bsq, bst12], accum=st12[:, 0:1])
            yield "f"
            TS("dve", st12[:, 0:1], st12[:, 0:1], 1.0 / 1024, 1e-6, ALU.mult, ALU.add, [bst12], [bst12])
            yield "f"
            TT("pool", rs12[:, 0:1], st12[:, 0:1], cneg[:, 0:1], ALU.pow, [bst12, bcneg], [brs12])
            yield "f"
            TS("dve", xs[:], x_t[:], rs12[:, 0:1], None, ALU.mult, None, [bx, brs12], [bxs])
            yield "f"
            import os as _os
            yield "f"
            _sk = _os.environ.get("SKIP", "")
            yield "f"
            if i == 0:
                if "m" not in _sk:
                    MSET("pool", hcur[:, :, 0:1], 0.0, [bh])
            else:
                CP("pool", hcur[:, :, 0:1], hprev[:, :, 128:129], [bhp], [bh])
            yield "f"
            for half in range(2):
                pp, bp = bank("proj")
                for j in range(4):
                    kc = half * 4 + j
                    TR(pp[:, j * 128:(j + 1) * 128], xs[:, kc * 128:(kc + 1) * 128], identf[:], [bxs, bidf], [bp], inc=(j == 3))
                for j in range(4):
                    kc = half * 4 + j
                    if "a" in _sk:
                        ACT(hcur[:, kc, 1:129], pp[:, j * 128:(j + 1) * 128], AF.Identity, [bp, bgs, bmod], [bh])
                    elif "b" in _sk:
                        ACT(hcur[:, kc, 2:130], pp[:, j * 128:(j + 1) * 128], AF.Identity, [bp, bgs, bmod], [bh],
                            bias=modT[:, kc, s:s + 1], scale=gsT[:, s, kc:kc + 1])
                    else:
                        ACT(hcur[:, kc, 1:129], pp[:, j * 128:(j + 1) * 128], AF.Identity, [bp, bgs, bmod], [bh],
                            bias=modT[:, kc, s:s + 1], scale=gsT[:, s, kc:kc + 1])
            yield "f"
            dump(f"hT_{T}", hcur[:], [bh], BF16)
            yield "f"
            chk(1)
            yield "f"

            pq, bpq = bank("proj")
            yield "f"
            for kc in range(8):
                MM(pq[:, :], hcur[:, kc, 1:129], Wres[:, kc, CQ:CQ + 512], start=(kc == 0), stop=(kc == 7), r=[bh, bWres], w=[bpq], inc=(kc == 7))
            yield "f"
            ACT(sq[:, 0:512], pq[:, :], AF.Square, [bpq], [bsq])
            yield "f"
            REDUCE(st12[:, 0:8], sq[:, 0:512].rearrange("p (h d) -> p h d", h=8), [bsq], [bst12])
            yield "f"
            pkv, bpkv = bank("proj")
            yield "f"
            for kc in range(8):
                MM(pkv[:, :], hcur[:, kc, 1:129], Wres[:, kc, CKV:CKV + 512], start=(kc == 0), stop=(kc == 7), r=[bh, bWres], w=[bpkv], inc=(kc == 7))
            yield "f"
            ACT(sq[:, 512:768], pkv[:, 0:256], AF.Square, [bpkv], [bsq])
            yield "f"
            REDUCE(st12[:, 8:12], sq[:, 512:768].rearrange("p (h d) -> p h d", h=4), [bsq], [bst12])
            yield "f"
            TS("dve", st12[:, 0:12], st12[:, 0:12], 1.0 / 64, 1e-6, ALU.mult, ALU.add, [bst12], [bst12])
            yield "f"
            TT("pool", rs12[:, 0:12], st12[:, 0:12], cneg[:, 0:12], ALU.pow, [bst12, bcneg], [brs12])
            yield "f"
            chk(1.2)
            yield "f"
            for h in range(8):
                STT(qn2[:, h, :, :], bcast(pq[:, h * 64:(h + 1) * 64], [128, 2, 64], 1), rs12[:, h:h + 1],
                    bcast(bc_small[:, 0:64], [128, 2, 64], 1), ALU.mult, ALU.mult, [bpq, brs12, bbcs], [bqn2])
            yield "f"
            for gg in range(2):
                for br in range(2):
                    c0 = gg * 128 + br * 64
                    STT(kn2[:, gg, br, :], pkv[:, c0:c0 + 64], rs12[:, 8 + gg * 2 + br:9 + gg * 2 + br],
                        bc_small[:, 64 + br * 64:128 + br * 64], ALU.mult, ALU.mult, [bpkv, brs12, bbcs], [bkn2])
            yield "f"
            CP("act", Vc[:, i, :, :, 0:64], pkv[:, 256:512].rearrange("p (b g d) -> p b g d", b=2, g=2), [bpkv], [bVc[i]])
            yield "f"
            chk(1.4)
            yield "f"
            for _ in range(14):
                yield "f"
            pt, bpt = bank("proj")
            yield "f"
            ptb = pt[:].bitcast(BF16)
            yield "f"
            for h in range(8):
                TR(ptb[:, h * 128:(h + 1) * 128], qn2[:, h, :, :].rearrange("p c d -> p (c d)"), identb[:], [bqn2, bidb], [bpt], inc=(h == 7))
            yield "f"
            CP("act", qT2[:].rearrange("p h q -> p (h q)"), ptb[:, 0:1024], [bpt], [bqT2])
            yield "f"
            pt2, bpt2 = bank("proj")
            yield "f"
            pt2b = pt2[:].bitcast(BF16)
            yield "f"
            for gg in range(2):
                TR(pt2b[:, gg * 128:(gg + 1) * 128], kn2[:, gg, :, :].rearrange("p c d -> p (c d)"), identb[:], [bkn2, bidb], [bpt2], inc=(gg == 1))
            yield "f"
            CP("dve", kT[:, :, i * 128:(i + 1) * 128], pt2b[:, 0:256].rearrange("p (g t) -> p g t", g=2), [bpt2], [bkT[i]])
            yield "f"
            chk(1.6)
            yield "f"
            pgt, bpgt = bank("proj")
            yield "f"
            for kc in range(8):
                MM(pgt[:, 0:24], hcur[:, kc, 1:129], Wres[:, kc, CG:CG + 24], start=(kc == 0), stop=(kc == 7), r=[bh, bWres], w=[bpgt], inc=(kc == 7))
            yield "f"
            ACT(gsig[:].rearrange("p a b -> p (a b)"), pgt[:, 0:24], AF.Tanh, [bpgt], [bgsig], scale=0.5)
            yield "f"
            TS("dve", gsig[:].rearrange("p a b -> p (a b)"), gsig[:].rearrange("p a b -> p (a b)"), 0.5, 0.5, ALU.mult, ALU.add, [bgsig], [bgsig])
            yield "f"
            pcm, bpcm = bank("proj")
            yield "f"
            for gg in range(2):
                for kc in range(8):
                    MM(pcm[:, gg * 128:(gg + 1) * 128], Wres[:, kc, CC + gg * 128:CC + (gg + 1) * 128], hcur[:, kc, 1:129],
                       start=(kc == 0), stop=(kc == 7), r=[bh, bWres], w=[bpcm], inc=(kc == 7 and gg == 1))
            yield "f"
            CP("pool", kvc[:, :, 0:16], kvc[:, :, 128:144], [bkvc], [bkvc])
            yield "f"
            CP("act", kvc[:, :, 16:144], pcm[:, 0:256].rearrange("p (g t) -> p g t", g=2), [bpcm], [bkvc])
            yield "f"
            chk(1.8)
            yield "f"
            m0 = 1 if i == 0 else 0
            yield "f"
            nm = 8 - m0
            yield "f"
            for kv in range(2):
                lo = kv * 64
                phd, bphd = bank("proj")
                for jh in range(2):
                    for pos in range(32):
                        MM(phd[:, jh * 16:jh * 16 + 16].rearrange("p (g m) -> p g m", g=2), W1c[lo:lo + 64, pos, jh * 128:(jh + 1) * 128],
                           kvc[lo:lo + 64, :, pos:pos + 113:16], start=(pos == 0), stop=(pos == 31), r=[bW1c, bkvc], w=[bphd],
                           inc=(pos == 31))
                for jh in range(2):
                    reg = (kv * 2 + jh) * 16
                    ACT(hu[:, reg:reg + 16], phd[:, jh * 16:jh * 16 + 16], AF.Identity, [bphd, bhb2], [bhu], bias=hb2[:, kv, jh:jh + 1])
            yield "f"
            chk(1.85)
            yield "f"
            TT("dve", hw_[:], hu[:], hu[:], ALU.mult, [bhu], [bhw])
            yield "f"
            TS("dve", hw_[:], hw_[:], 0.044715, 1.0, ALU.mult, ALU.add, [bhw], [bhw])
            yield "f"
            TT("dve", hw_[:], hw_[:], hu[:], ALU.mult, [bhw, bhu], [bhw])
            yield "f"
            ACT(hw_[:], hw_[:], AF.Tanh, [bhw], [bhw], scale=math.sqrt(2.0 / math.pi))
            yield "f"
            STT(hid[:], hw_[:], 1.0, hu[:], ALU.add, ALU.mult, [bhw, bhu], [bhid])
            yield "f"
            chk(1.9)
            yield "f"
            for _ in range(6):
                yield "f"
            pc2, bpc2 = bank("proj")
            yield "f"
            for kv in range(2):
                for jh in range(2):
                    reg = (kv * 2 + jh) * 16
                    MM(pc2[0:64, kv * 16:(kv + 1) * 16], W2c[:, kv, jh, :], hid[:, reg:reg + 16], start=(jh == 0), stop=(jh == 1),
                       r=[bW2c, bhid], w=[bpc2], inc=(jh == 1))
            yield "f"
            TS("dve", kcs[:, 0:16], pc2[0:64, 0:16], 0.5, None, ALU.mult, None, [bpc2], [bkcs])
            yield "f"
            TT("dve", kcs[:, 16:32], kcs[:, 0:16], kcs[:, 0:16], ALU.mult, [bkcs], [bkcs])
            yield "f"
            MM(pc2[0:64, 64:80], ones_f[0:64, 0:64], kcs[:, 16:32], r=[bones, bkcs], w=[bpc2])
            yield "f"
            TS("dve", kcs[:, 32:48], pc2[0:64, 64:80], 1.0 / 64, 1e-6, ALU.mult, ALU.add, [bpc2], [bkcs])
            yield "f"
            TT("pool", kcs[:, 16:32], kcs[:, 32:48], cneg[0:64, 0:16], ALU.pow, [bkcs, bcneg], [bkcs])
            yield "f"
            TT("dve", kcs[:, 0:16], kcs[:, 0:16], kcs[:, 16:32], ALU.mult, [bkcs], [bkcs])
            yield "f"
            n0 = 8 * i - 1 + m0
            yield "f"
            TS("dve", kcT[:, :, n0:n0 + nm], kcs[:, 0:16].rearrange("p (g m) -> p g m", g=2)[:, :, m0:8], kgc[:, 0:1], None, ALU.mult, None,
               [bkcs, bkgc], [bkcT])
            yield "f"
            TS("dve", vcT[:, :, n0:n0 + nm], pc2[0:64, 16:32].rearrange("p (g m) -> p g m", g=2)[:, :, m0:8], 0.5, None, ALU.mult, None,
               [bpc2], [bvcT])
            yield "f"
            nv = 8 * i + 7
            yield "f"
            chk(1.95)
            yield "f"
            for _ in range(10):
                yield "f"
            pvt, bpvt = bank("proj")
            yield "f"
            for gg in range(2):
                TR(pvt[0:nv, gg * 64:(gg + 1) * 64], vcT[:, gg, 0:nv], identf[0:64, 0:64], [bvcT, bidf], [bpvt], inc=(gg == 1))
            yield "f"
            CP("dve", vca[0:nv, :, 0:64], pvt[0:nv, 0:128].rearrange("p (g d) -> p g d", g=2), [bpvt], [bvca])
            yield "f"
            dump(f"kcT_{T}", kcT[:], [bkcT], BF16)
            yield "f"
            dump(f"vca_{T}", vca[:], [bvca])
            yield "f"
            dump(f"qT2_{T}", qT2[:], [bqT2], BF16)
            yield "f"
            chk(2)
            yield "f"

            yield "front_done"
            def attn_gen():
                first_y = {0: True, 1: True}

                def finish(acc, bacc, br, gg):
                    accv = acc[:, 0:260].rearrange("p (h e) -> p h e", h=4)
                    c0 = br * 4
                    TS("dve", coef[:, c0:c0 + 4], accv[:, :, 64], 1e-30, None, ALU.max, None, [bacc], [bcoef])
                    RECIP(coef[:, c0:c0 + 4], coef[:, c0:c0 + 4], [bcoef], [bcoef])
                    if br == 0:
                        CP("dve", coef[:, 12:16], coef[:, 0:4], [bcoef], [bcoef])
                    gbr = {0: 0, 1: 1, 2: 2}[br]
                    TT("dve", coef[:, c0:c0 + 4], coef[:, c0:c0 + 4], gsig[:, gbr, gg * 4:(gg + 1) * 4], ALU.mult, [bcoef, bgsig], [bcoef])
                    yv = ynsa[:, gg * 256:(gg + 1) * 256].rearrange("p (h d) -> p h d", h=4)
                    cb = bcast(coef[:, c0:c0 + 4], [128, 4, 64], 2)
                    if first_y[gg]:
                        TT("dve", yv, accv[:, :, 0:64], cb, ALU.mult, [bacc, bcoef], [bynsa])
                        first_y[gg] = False
                    else:
                        for h in range(4):
                            STT(yv[:, h, :], accv[:, h, 0:64], coef[:, c0 + h:c0 + h + 1], yv[:, h, :], ALU.mult, ALU.add, [bacc, bcoef, bynsa], [bynsa])

                def pv(acc, bacc, Pt_, bP, vrhs, bv, first, last, K=128):
                    for h in range(4):
                        MM(acc[:, h * 65:(h + 1) * 65], Pt_[0:K, h * 128:(h + 1) * 128], vrhs, start=(first and h == 0), stop=(last and h == 3), r=[bP] + bv, w=[bacc],
                           inc=(h == 3), sgc=True)

                pti = [0]
                for gg in range(2):
                    sc, bsc_ = bank("sc")
                    MM(sc[0:nv, :], kcT[:, gg, 0:nv], qT2[0:64, gg * 4:(gg + 1) * 4, :].rearrange("p h q -> p (h q)"), start=True, stop=False,
                       r=[bkcT, bqT2], w=[bsc_], inc=False)
                    off = 128 - 8 * i
                    MM(sc[0:nv, :], zsh[:, off:off + nv], biasC[:, gg, :], start=False, stop=True, r=[bzsh], w=[bsc_])
                    ACT(PcT[0:nv, :], sc[0:nv, :], AF.Exp, [bsc_], [bPcT], scale=0.125)
                    acc, bacc = bank("acc")
                    for h in range(4):
                        MM(acc[:, h * 65:(h + 1) * 65], PcT[0:nv, h * 128:(h + 1) * 128], vca[0:nv, gg, :], r=[bPcT, bvca], w=[bacc], inc=False)
                    for h in range(4):
                        MM(acc[:, 320 + h * 32:320 + (h + 1) * 32], PcT[0:nv, h * 128:(h + 1) * 128], mimp[0:nv, :], r=[bPcT, bmimp], w=[bacc], inc=(h == 3))
                    finish(acc, bacc, 0, gg)
                    yield
                    if i >= 8:
                        iv = impr[:, gg, :]
                        TS("dve", iv, acc[:, 320:352], coef[:, 12:13], None, ALU.mult, None, [bacc, bcoef], [bimpr])
                        for h in range(1, 4):
                            STT(iv, acc[:, 320 + h * 32:352 + h * 32], coef[:, 12 + h:13 + h], iv, ALU.mult, ALU.add, [bacc, bcoef, bimpr], [bimpr])
                        TT("dve", iv, iv, keepadd[:, i - 8, 0, :], ALU.mult, [bimpr, bka], [bimpr])
                        TT("dve", iv, iv, keepadd[:, i - 8, 1, :], ALU.add, [bimpr, bka], [bimpr])
                        MAX8(m8a[:], iv, [bimpr], [bm8a])
                        MREP(imp2[:], m8a[:], iv, [bimpr, bm8a], [bimp2])
                        MAX8(m8b[:], imp2[:], [bimp2], [bm8b])
                        TS("dve", nsel[:, gg, :], iv, m8b[:, 7:8], 1.0, ALU.is_ge, ALU.subtract, [bimpr, bm8b], [bnsel])
                dump(f"nsel_{T}", nsel[:], [bnsel])
                for br in (2, 1):
                    if br == 1 and i >= 8:
                        for gg in range(2):
                            pn, bpn = bank("sc")
                            TR(pn[0:32, 0:128], nsel[:, gg, :], identf[:], [bnsel, bidf], [bpn])
                            CP("dve", nselT[0:32, gg, :], pn[0:32, 0:128], [bpn], [bnselT])
                    for gg in range(2):
                        lo = 0 if br == 1 else 64
                        j0 = 0 if br == 1 else max(0, i - 4)
                        acc, bacc = bank("acc")
                        prev_ = None
                        for j in range(j0, i + 1):
                            sc, bsc_ = bank("sc")
                            extra = []
                            if j == i:
                                extra.append((identb[:], biasDS[:, 0, gg, :], [bidb, bbias]))
                            if j == i - 1:
                                extra.append((identb[:], biasDS[:, 1, gg, :], [bidb, bbias]))
                            if br == 2 and j == i - 4:
                                extra.append((identb[:], bcast(masks[:, 0, :], [128, 4, 128], 1), [bidb, bmasks]))
                            if br == 1 and i >= 8:
                                extra.append((emat[:, j * 128:(j + 1) * 128], bcast(nselT[:, gg, :], [64, 4, 128], 1), [bemat, bnselT]))
                            MM(sc[:, :], kT[lo:lo + 64, gg, j * 128:(j + 1) * 128], qT2[lo:lo + 64, gg * 4:(gg + 1) * 4, :].rearrange("p h q -> p (h q)"),
                               start=True, stop=(len(extra) == 0), r=[bkT[j], bqT2], w=[bsc_], inc=(len(extra) == 0))
                            for ei, (l_, r_, bb_) in enumerate(extra):
                                lastx = ei == len(extra) - 1
                                MM(sc[:, :].rearrange("p (h q) -> p h q", h=4) if len(r_.shape) == 3 else sc[:, :], l_, r_, start=False, stop=lastx,
                                   r=bb_, w=[bsc_], inc=lastx)
                            Pt_ = PT[pti[0] % NPT]; bP = bPT[pti[0] % NPT]; pti[0] += 1
                            ACT(Pt_[:, :], sc[:, :], AF.Exp, [bsc_], [bP], scale=0.125)
                            if prev_ is not None:
                                pv(acc, bacc, prev_[0], prev_[1], Vc[:, prev_[2], br - 1, gg, :], [bVc[prev_[2]]], prev_[2] == j0, False)
                            prev_ = (Pt_, bP, j)
                            yield
                        pv(acc, bacc, prev_[0], prev_[1], Vc[:, prev_[2], br - 1, gg, :], [bVc[prev_[2]]], prev_[2] == j0, True)
                        finish(acc, bacc, br, gg)
                        yield
                dump(f"ynsa_{T}", ynsa[:], [bynsa])
                chk(3)

                yield
            def rwkv_gen():
                yield
                for c3 in range(0, 13, 3):
                    ps_, bps_ = bank("rw")
                    ncs = min(3, 13 - c3)
                    for cc in range(ncs):
                        c = c3 + cc
                        for kc in range(8):
                            MM(ps_[:, cc * 129:(cc + 1) * 129], Wres[:, kc, CS + c * 128:CS + (c + 1) * 128], hcur[:, kc, 0:129],
                               start=(kc == 0), stop=(kc == 7), r=[bh, bWres], w=[bps_], inc=(kc == 7))
                    for cc in range(ncs):
                        c = c3 + cc
                        if c < 4:
                            dst, bd = rT[:, c, :], brT
                        elif c < 8:
                            dst, bd = kTr[:, c - 4, :], bkTr
                        elif c < 12:
                            dst, bd = vT[:, c - 8, :], bvT
                        else:
                            dst, bd = wdad[:, :], bwdad
                        ACT(dst, ps_[:, cc * 129 + 1:cc * 129 + 129], AF.Identity, [bps_, bmu], [bd], scale=mucol[:, 1, c:c + 1])
                        STT(dst, ps_[:, cc * 129:cc * 129 + 128], mucol[:, 0, c:c + 1], dst, ALU.mult, ALU.add, [bps_, bmu, bd], [bd])
                    yield
                yield
                dump(f"rT_{T}", rT[:], [brT])
                yield
                dump(f"wdad_{T}", wdad[:], [bwdad])
                yield
                ACT(wdadb[0:64, :], wdad[0:64, :], AF.Tanh, [bwdad], [bwdadb])
                yield
                CP("act", wdadb[64:128, :], wdad[64:128, :], [bwdad], [bwdadb])
                yield
                pz, bpz = bank("rw")
                yield
                pa, bpa = bank("rw")
                yield
                for c in range(4):
                    MM(pz[:, c * 128:(c + 1) * 128], Lora[0:64, c * 128:(c + 1) * 128], wdadb[0:64, :], start=True, stop=False, r=[bLora, bwdadb], w=[bpz], inc=False)
                    MM(pz[:, c * 128:(c + 1) * 128], w0a0[0:64, c * 128:(c + 1) * 128], ones_f[0:64, :], start=False, stop=True, r=[bw0a0, bones], w=[bpz], inc=(c == 3))
                yield
                for c in range(4):
                    MM(pa[:, c * 128:(c + 1) * 128], Lora[64:128, c * 128:(c + 1) * 128], wdadb[64:128, :], start=True, stop=False, r=[bLora, bwdadb], w=[bpa], inc=False)
                    MM(pa[:, c * 128:(c + 1) * 128], w0a0[64:128, c * 128:(c + 1) * 128], ones_f[64:128, :], start=False, stop=True, r=[bw0a0, bones], w=[bpa], inc=(c == 3))
                yield
                f4 = lambda t: t[:].rearrange("p c t -> p (c t)")
                yield
                ACT(f4(lwT), pz[:, :], AF.Tanh, [bpz], [blw], scale=0.5)
                yield
                cexp = math.exp(-0.5) * 0.5
                yield
                TS("dve", f4(lwT), f4(lwT), -cexp, -cexp, ALU.mult, ALU.add, [blw], [blw])
                yield
                ACT(f4(asg), pa[:, :], AF.Tanh, [bpa], [basg], scale=0.5)
                yield
                TS("dve", f4(asg), f4(asg), 0.5, 0.5, ALU.mult, ALU.add, [basg], [basg])
                yield
                for c in range(4):
                    SCAN(LT[:, c, :], ones_f[:, :], lwT[:, c, :], [bones, blw], [bLT])
                yield
                TT("dve", f4(tmpA), f4(LT), f4(lwT), ALU.subtract, [bLT, blw], [btmpA])
                yield
                ACT(f4(e1), f4(tmpA), AF.Exp, [btmpA], [be1])
                yield
                ACT(f4(e2), f4(LT), AF.Exp, [bLT], [be2], scale=-1.0)
                yield
                ACT(f4(e3), f4(LT), AF.Exp, [bLT], [be3])
                yield
                ACT(eLC[:, :], LT[:, :, 127], AF.Exp, [bLT], [beLC])
                yield
                yield
                for c in range(4):
                    TS("dve", kkn[:, c, :], kTr[:, c, :], vec4[:, 0, c:c + 1], None, ALU.mult, None, [bkTr, bvec4], [bkkn])
                yield
                ACT(f4(tmpA), f4(kkn), AF.Square, [bkkn], [btmpA])
                yield
                for _ in range(4):
                    yield
                pk_, bpk_ = bank("rw")
                yield
                for c in range(4):
                    MM(pk_[:, c * 2:c * 2 + 2], tmpA[:, c, :], ind2[:, :], r=[btmpA, bind2], w=[bpk_], inc=(c == 3))
                yield
                TS("dve", rn8[:, :], pk_[:, 0:8], 1e-24, None, ALU.max, None, [bpk_], [brn8])
                yield
                TT("pool", rn8[:, :], rn8[:, :], cneg[:, 0:8], ALU.pow, [brn8, bcneg], [brn8])
                yield
                for _ in range(6):
                    yield
                TR(pk_[0:8, 128:256], rn8[:, :], identf[:], [brn8, bidf], [bpk_])
                yield
                CP("dve", rn8T[:, :], pk_[0:8, 128:256], [bpk_], [brn8T])
                yield
                pr_, bpr_ = bank("rw")
                yield
                for c in range(4):
                    MM(pr_[:, c * 128:(c + 1) * 128], indT[:, c, :], rn8T[:, :], r=[bindT, brn8T], w=[bpr_], inc=(c == 3))
                yield
                TT("dve", f4(kkn), f4(kkn), pr_[:, :], ALU.mult, [bkkn, bpr_], [bkkn])
                yield
                dump(f"kkn_{T}", kkn[:], [bkkn])
                yield
                yield
                for c in range(4):
                    TS("dve", tmpA[:, c, :], asg[:, c, :], -1.0, vec4[:, 1, c:c + 1], ALU.add, ALU.mult, [basg, bvec4], [btmpA])
                yield
                STT(f4(kmod), f4(tmpA), 1.0, f4(kTr), ALU.add, ALU.mult, [btmpA, bkTr], [bkmod])
                yield
                dump(f"kmod_{T}", kmod[:], [bkmod])
                yield
                yield
                for c in range(4):
                    STT(tmpA[:, c, :], rT[:, c, :], vec4[:, 2, c:c + 1], kmod[:, c, :], ALU.mult, ALU.mult, [brT, bvec4, bkmod], [btmpA])
                yield
                for c in range(4):
                    MM(pk_[:, 256 + c * 2:256 + c * 2 + 2], tmpA[:, c, :], ind2[:, :], r=[btmpA, bind2], w=[bpk_], inc=(c == 3))
                yield
                CP("dve", sbon[:, :], pk_[:, 256:264], [bpk_], [bsbon])
                yield
                yield
                STT(f4(At), f4(kkn), -1.0, f4(e1), ALU.mult, ALU.mult, [bkkn, be1], [bAt])
                yield
                TT("dve", f4(tmpA), f4(kkn), f4(asg), ALU.mult, [bkkn, basg], [btmpA])
                yield
                TT("dve", f4(tmpA), f4(tmpA), f4(e2), ALU.mult, [btmpA, be2], [btmpA])
                yield
                CP("act", f4(Bt), f4(tmpA), [btmpA], [bBt])
                yield
                TT("dve", f4(e1), f4(kmod), f4(e2), ALU.mult, [bkmod, be2], [be1])
                yield
                CP("act", f4(Kt), f4(e1), [be1], [bKt])
                yield
                TT("dve", f4(Rt), f4(rT), f4(e3), ALU.mult, [brT, be3], [bRt])
                yield
                yield
                for c in range(4):
                    TS("dve", tmpA[:, c, :], tmpA[:, c, :], eLC[:, c:c + 1], None, ALU.mult, None, [btmpA, beLC], [btmpA])
                    ACT(e1[:, c, :], e1[:, c, :], AF.Identity, [be1, beLC], [be1], scale=eLC[:, c:c + 1])
                yield
                for _ in range(6):
                    yield
                for (src, bsrc, dst, bdst) in ((tmpA, btmpA, Bh, bBh), (e1, be1, Kh, bKh), (vT, bvT, Vt, bVt)):
                    pp, bp = bank("rw")
                    for c in range(4):
                        TR(pp[:, c * 128:(c + 1) * 128], src[:, c, :], identf[:], [bsrc, bidf], [bp], inc=(c == 3))
                    CP("act", dst[:, :], pp[:, :], [bp], [bdst])
                yield
                chk(4)
                yield
                yield
                def hs(t, h):
                    return t[(h % 2) * 64:(h % 2) * 64 + 64, h // 2, :]
                yield

                def five(lt, blt, rt, brt, mask_i, dst, bdst):
                    for par in range(2):
                        pp, bp = bank("rw")
                        for hh in range(4):
                            h = hh * 2 + par
                            MM(pp[:, hh * 128:(hh + 1) * 128], hs(lt, h), hs(rt, h), r=[blt, brt], w=[bp], inc=(hh == 3))
                        TT("dve", dst[:, par:8:2, :], pp[:, :].rearrange("p (h t) -> p h t", h=4), bcast(masks[:, mask_i, :], [128, 4, 128], 1), ALU.mult,
                           [bp, bmasks], [bdst])
                yield

                five(Bt, bBt, At, bAt, 1, Qm[0], bQm[0])
                yield
                five(At, bAt, Bt, bBt, 3, QmT[0], bQmT[0])
                yield
                five(Kt, bKt, At, bAt, 1, AakT, bAak)
                yield
                five(Bt, bBt, Rt, bRt, 2, MrbT, bMrb)
                yield
                five(Kt, bKt, Rt, bRt, 2, MrkT, bMrk)
                yield
                yield
                TT("dve", Xm[0][:], Qm[0][:], bcast(identb[:], [128, 8, 128], 1), ALU.add, [bQm[0], bidb], [bXm[0]])
                yield
                cur = 0
                yield
                for lvl in range(1, 7):
                    nxt = cur ^ 1
                    lastl = lvl == 6
                    for half in range(2):
                        hsl = slice(half * 4, half * 4 + 4)
                        pT_, bpT_ = bank("rw")
                        for hh in range(4):
                            h = half * 4 + hh
                            MM(pT_[:, hh * 128:(hh + 1) * 128], Qm[cur][:, h, :], QmT[cur][:, h, :], r=[bQm[cur], bQmT[cur]], w=[bpT_], inc=(hh == 3))
                        CP("act", QmT[nxt][:, hsl, :], pT_[:, :].rearrange("p (h t) -> p h t", h=4), [bpT_], [bQmT[nxt]])
                        if not lastl:
                            pQ_, bpQ_ = bank("rw")
                            for hh in range(4):
                                h = half * 4 + hh
                                MM(pQ_[:, hh * 128:(hh + 1) * 128], QmT[cur][:, h, :], Qm[cur][:, h, :], r=[bQm[cur], bQmT[cur]], w=[bpQ_], inc=(hh == 3))
                            CP("dve", Qm[nxt][:, hsl, :], pQ_[:, :].rearrange("p (h t) -> p h t", h=4), [bpQ_], [bQm[nxt]])
                        yield
                    for half in range(2):
                        hsl = slice(half * 4, half * 4 + 4)
                        pX_, bpX_ = bank("rw")
                        for hh in range(4):
                            h = half * 4 + hh
                            MM(pX_[:, hh * 128:(hh + 1) * 128], QmT[nxt][:, h, :], Xm[cur][:, h, :], r=[bQmT[nxt], bXm[cur]], w=[bpX_], inc=(hh == 3))
                        TT("dve", Xm[nxt][:, hsl, :], pX_[:, :].rearrange("p (h t) -> p h t", h=4), Xm[cur][:, hsl, :], ALU.add, [bpX_, bXm[cur]], [bXm[nxt]])
                        yield
                    cur = nxt
                yield
                Xf = Xm[cur]; bXf = bXm[cur]
                yield
                chk(5)
                yield
                yield
                if i == 0:
                    MSET("pool", Pst[:], 0.0, [bPst])
                    MSET("pool", Pb[:], 0.0, [bPb])
                yield

                def ph_(t, h):
                    return t[(h % 2) * 64:(h % 2) * 64 + 64, h // 2, :]
                yield

                def vh(t, h):
                    return t[:, h * 64:(h + 1) * 64]
                yield

                for par in range(2):
                    p1, bp1 = bank("rw")
                    for hh in range(4):
                        h = hh * 2 + par
                        MM(p1[:, hh * 64:(hh + 1) * 64], hs(At, h), ph_(Pb, h), start=True, stop=False, r=[bAt, bPb], w=[bp1], inc=False)
                        MM(p1[:, hh * 64:(hh + 1) * 64], AakT[:, h, :], vh(Vt, h), start=False, stop=True, r=[bAak, bVt], w=[bp1], inc=(hh == 3))
                    CP("act", rhs0[:, par:8:2, :], p1[:, 0:256].rearrange("p (h v) -> p h v", h=4), [bp1], [brhs0])
                yield
                chk(5.2)
                yield
                p2, bp2 = bank("rw")
                yield
                for h in range(8):
                    MM(p2[:, h * 64:(h + 1) * 64], Xf[:, h, :], rhs0[:, h, :], r=[bXf, brhs0], w=[bp2], inc=(h == 7))
                yield
                CP("act", Ub[:].rearrange("p h v -> p (h v)"), p2[:, :], [bp2], [bUb])
                yield
                chk(5.4)
                yield
                p3s = []
                yield
                for par in range(2):
                    p3, bp3 = bank("proj")
                    p3s.append((p3, bp3))
                    for hh in range(4):
                        h = hh * 2 + par
                        MM(p3[:, hh * 64:(hh + 1) * 64], hs(Rt, h), ph_(Pb, h), start=True, stop=False, r=[bRt, bPb], w=[bp3], inc=False)
                        MM(p3[:, hh * 64:(hh + 1) * 64], MrkT[:, h, :], vh(Vt, h), start=False, stop=False, r=[bMrk, bVt], w=[bp3], inc=False)
                        MM(p3[:, hh * 64:(hh + 1) * 64], MrbT[:, h, :], Ub[:, h, :], start=False, stop=True, r=[bMrb, bUb], w=[bp3], inc=(hh == 3))
                yield
                chk(5.6)
                yield
                p4, bp4 = bank("rw")
                yield
                for h in range(8):
                    o_ = p4[(h % 2) * 64:(h % 2) * 64 + 64, (h // 2) * 64:(h // 2) * 64 + 64]
                    MM(o_, vh(Bh, h), Ub[:, h, :], start=True, stop=False, r=[bBh, bUb], w=[bp4], inc=False)
                    MM(o_, vh(Kh, h), vh(Vt, h), start=False, stop=True, r=[bKh, bVt], w=[bp4], inc=(h == 7))
                yield
                for c in range(4):
                    STT(Pst[:, c, :], Pst[:, c, :], eLC[:, c:c + 1], p4[:, c * 64:(c + 1) * 64], ALU.mult, ALU.add, [bPst, beLC, bp4], [bPst])
                yield
                CP("act", Pb[:].rearrange("p a b -> p (a b)"), Pst[:].rearrange("p a b -> p (a b)"), [bPst], [bPb])
                yield
                chk(5.8)
                yield
                yield
                yn3 = yn[:, :].rearrange("p (h d) -> p h d", h=8)
                yield
                for par in range(2):
                    p3, bp3 = p3s[par]
                    CP("act", yn3[:, par:8:2, :], p3[:, 0:256].rearrange("p (h d) -> p h d", h=4), [bp3], [byn])
                yield
                ACT(sq[:, 0:512], yn[:, :], AF.Square, [byn], [bsq])
                yield
                REDUCE(lnst[:, 0:8], yn3, [byn], [blnst])
                yield
                REDUCE(lnst[:, 8:16], sq[:, 0:512].rearrange("p (h d) -> p h d", h=8), [bsq], [blnst])
                yield
                chk(5.85)
                yield
                TS("dve", lnst[:, 0:16], lnst[:, 0:16], 1.0 / 64, None, ALU.mult, None, [blnst], [blnst])
                yield
                TT("dve", lnst[:, 16:24], lnst[:, 0:8], lnst[:, 0:8], ALU.mult, [blnst], [blnst])
                yield
                TT("dve", lnst[:, 16:24], lnst[:, 8:16], lnst[:, 16:24], ALU.subtract, [blnst], [blnst])
                yield
                TS("dve", lnst[:, 16:24], lnst[:, 16:24], 0.0, 64e-5, ALU.max, ALU.add, [blnst], [blnst])
                yield
                TT("pool", lnst[:, 24:32], lnst[:, 16:24], cneg[:, 0:8], ALU.pow, [blnst, bcneg], [blnst])
                yield
                STT(lnst[:, 16:24], lnst[:, 0:8], -1.0, lnst[:, 24:32], ALU.mult, ALU.mult, [blnst], [blnst])
                yield
                chk(5.9)
                yield
                for h in range(8):
                    ACT(yn[:, h * 64:(h + 1) * 64], yn[:, h * 64:(h + 1) * 64], AF.Identity, [byn, blnst], [byn],
                        bias=lnst[:, 16 + h:17 + h], scale=lnst[:, 24 + h:25 + h])
                yield
                chk(5.95)
                yield
                TT("dve", yn[:, :], yn[:, :], ln_w_bc[:, :], ALU.mult, [byn, blnw], [byn])
                yield
                chk(5.97)
                yield
                for h in range(8):
                    STT(yn[:, h * 64:(h + 1) * 64], Vt[:, h * 64:(h + 1) * 64], sbon[:, h:h + 1], yn[:, h * 64:(h + 1) * 64], ALU.mult, ALU.add,
                        [bVt, bsbon, byn], [byn])
                yield

            na_ = 2 * (2 + (i + 1) + min(5, i + 1)) + 2
            nr_ = 150
            ga_, gr_ = attn_gen(), rwkv_gen()
            da_ = dr_ = 0
            alive_a = alive_r = True
            while alive_a or alive_r:
                pick_a = alive_a and (not alive_r or da_ * nr_ <= dr_ * na_)
                if pick_a:
                    try:
                        next(ga_); da_ += 1
                    except StopIteration:
                        alive_a = False
                else:
                    try:
                        next(gr_); dr_ += 1
                    except StopIteration:
                        alive_r = False
            yield "mid_done"
            chk(6)
            yield "t"
            def wsload(c):
                k = ws_i[0] % NWS
                ws_i[0] += 1
                DMA(WS[k][:].rearrange("p k n -> p (k n)"), wrest_s[c], sem_ws[k], w=[bWS[k]])
                return WS[k], bWS[k]

            def rest_chunk(c):
                pp, bp = bank("tl")
                for sub in range(2):
                    W_, bW_ = wsload(2 * c + sub)
                    for kc in range(8):
                        MM(pp[:, sub * 128:(sub + 1) * 128], W_[:, kc, :], hcur[:, kc, 1:129], start=(kc == 0), stop=(kc == 7),
                           r=[bW_, bh], w=[bp], inc=(kc == 7 and sub == 1))
                return pp, bp

            for c in range(12):
                pp, bp = rest_chunk(c)
                if c < 4:
                    dst, bd = (silA, bsilA) if c < 2 else (silB, bsilB)
                    dv = dst[:, (c % 2) * 2:(c % 2) * 2 + 2, :].rearrange("p a t -> p (a t)")
                    ACT(dv, pp[:, 0:256], AF.Tanh, [bp], [bd], scale=0.5)
                    STT(dv, dv, 1.0, pp[:, 0:256], ALU.add, ALU.mult, [bd, bp], [bd])
                else:
                    dst, bd = (thA, bthA) if c < 8 else (thB, bthB)
                    cc = (c - 4) % 4
                    ACT(dst[:, cc * 2:cc * 2 + 2, :].rearrange("p a t -> p (a t)"), pp[:, 0:256], AF.Tanh, [bp], [bd], scale=0.5)
                yield "t"
            yield "t"
            for (src, bsrc, sil, bsil, dst, bdst, lnb) in ((ynsa, bynsa, silA, bsilA, yaT, byaT, False), (yn, byn, silB, bsilB, ybT, bybT, True)):
                pp, bp = bank("tl")
                for c in range(4):
                    TR(pp[:, c * 128:(c + 1) * 128], src[:, c * 128:(c + 1) * 128], identf[:], [bsrc, bidf], [bp], inc=(c == 3))
                if not lnb:
                    STT(dst[:].rearrange("p c t -> p (c t)"), pp[:, :], 0.5, sil[:].rearrange("p c t -> p (c t)"), ALU.mult, ALU.mult, [bp, bsil], [bdst])
                else:
                    for c in range(4):
                        STT(tmpA[:, c, :], pp[:, c * 128:(c + 1) * 128], vec4[:, 3, c:c + 1], sil[:, c, :], ALU.add, ALU.mult, [bp, bvec4, bsil], [btmpA])
                    ACT(dst[:].rearrange("p c t -> p (c t)"), f4(tmpA), AF.Copy, [btmpA], [bdst], scale=0.5)
            yield "t"
            dump(f"yaT_{T}", yaT[:], [byaT], BF16)
            yield "t"
            dump(f"ybT_{T}", ybT[:], [bybT], BF16)
            yield "t"
            for (yT_, byT_, W_, bW_, th, bth, mg, bmg) in ((yaT, byaT, Wouta, bWouta, thA, bthA, mg1, bmg1), (ybT, bybT, Woutb, bWoutb, thB, bthB, mg2, bmg2)):
                for half in range(2):
                    pp, bp = bank("tl")
                    for mm_ in range(4):
                        mc = half * 4 + mm_
                        for kc in range(4):
                            MM(pp[:, mm_ * 128:(mm_ + 1) * 128], W_[:, kc, mc * 128:(mc + 1) * 128], yT_[:, kc, :], start=(kc == 0), stop=(kc == 3),
                               r=[bW_, byT_], w=[bp], inc=(kc == 3 and mm_ == 3))
                    STT(mg[:, half * 4:(half + 1) * 4, :].rearrange("p a t -> p (a t)"), th[:, half * 4:(half + 1) * 4, :].rearrange("p a t -> p (a t)"), 1.0, pp[:, :],
                        ALU.add, ALU.mult, [bth, bp], [bmg])
                    yield "t"
            yield "t"
            TT("dve", mgT[:].rearrange("p a t -> p (a t)"), mg1[:].rearrange("p a t -> p (a t)"), mg2[:].rearrange("p a t -> p (a t)"), ALU.add, [bmg1, bmg2], [bmgT])
            yield "t"
            dump(f"mgT_{T}", mgT[:], [bmgT], BF16)
            yield "t"
            if s == 1 and i == 0:
                DMA(Wog[:].rearrange("p k n -> p (k n)"), wog_s, sem_wog, w=[bWog])
            yield "t"
            for half in range(2):
                pp, bp = bank("tl")
                for kc in range(8):
                    MM(pp[:, :], mgT[:, kc, :], Wog[:, kc, half * 512:(half + 1) * 512], start=(kc == 0), stop=(kc == 7), r=[bmgT, bWog], w=[bp], inc=(kc == 7))
                TT("dve", x_t[:, half * 512:(half + 1) * 512], pp[:, :], x_t[:, half * 512:(half + 1) * 512], ALU.add, [bp, bx], [bx])
                yield "t"
            yield "t"
            return DMA(out_d[tok0:tok0 + 128, :], x_t[:, :], sem_outs[T % 2], r=[bx], w=[])

        out_toks = []
        FRONT_PER_TAIL = 3
        total = nseq * ntile
        seq_tiles = [(s, i) for s in range(nseq) for i in range(ntile)]

        def make_gen(n_):
            s_, i_ = seq_tiles[n_]
            T_ = s_ * 16 + i_
            if i_ == 0:
                MSET("pool", kcT[:].rearrange("p a b -> p (a b)"), 0.0, [bkcT])
                MSET("pool", vcT[:].rearrange("p a b -> p (a b)"), 0.0, [bvcT])
                MSET("pool", vca[:, :, 0:64], 0.0, [bvca])
                MSET("pool", kvc[:].rearrange("p a b -> p (a b)"), 0.0, [bkvc])
            return tile_body2(s_, i_)

        def xload(n_):
            if n_ < total:
                s2, i2 = seq_tiles[n_]
                T2 = s2 * 16 + i2
                DMA(xt[T2 % 2][:], x_d[T2 * 128:(T2 + 1) * 128, :], sem_x[T2 % 2], w=[bxt[T2 % 2]])

        def step(g, until):
            while True:
                try:
                    m_ = next(g)
                except StopIteration as e_:
                    return None, True, e_.value
                if m_ in until:
                    return m_, False, None

        def early(n_):
            s_, i_ = seq_tiles[n_]
            T_ = s_ * 16 + i_
            return DMA(out_d[T_ * 128:T_ * 128 + 128, :], xs[:, :], sem_outs[T_ % 2], r=[bxs], w=[])

        if total > 0:
            xload(0)
            xload(1)
            if stage < 9:
                for n_ in range(total):
                    g = make_gen(n_)
                    try:
                        _, _, val = step(g, ())
                        out_toks.append(val)
                    except _Stop:
                        out_toks.append(early(n_))
                    xload(n_ + 2)
            else:
                cur = make_gen(0)
                step(cur, ("front_done",))
                for n_ in range(total):
                    step(cur, ("mid_done",))
                    nxt = make_gen(n_ + 1) if n_ + 1 < total else None
                    cur_done = False
                    nxt_done = nxt is None
                    while not (cur_done and nxt_done):
                        if not cur_done:
                            m_, fin, val = step(cur, ("t",))
                            if fin:
                                cur_done = True
                                out_toks.append(val)
                        for _r in range(1 if cur_done else FRONT_PER_TAIL):
                            if not nxt_done:
                                m_, fin, val = step(nxt, ("f", "front_done"))
                                if m_ == "front_done":
                                    nxt_done = True
                    xload(n_ + 2)
                    cur = nxt
        S.wait_all("sp", out_toks[-4:] + dbg_outs + [(sm_, S.dcnt[sm_]) for sm_ in sem_outs])
        S.emit()
    return nc


_CACHE = {}


def kernel(**inputs):
    sh, per = host_prep(inputs)
    if "nc" not in _CACHE:
        _CACHE["nc"] = build()
    nc = _CACHE["nc"]
    in_maps = []
    for core in range(8):
        d = dict(sh)
        d.update(per[core])
        in_maps.append(d)
    res = run_bass_kernel_spmd(nc, in_maps, core_ids=list(range(8)))
    out = np.concatenate([np.asarray(r["out"]).reshape(2, 2048, 1024) for r in res.results], axis=0)
    return out.astype(np.float32)
```
